# Optimizing a Trainium2 kernel written in Bass

```python
import math
import jax
import jax.numpy as jnp
from jax import lax
import numpy as np

D_MODEL = 1024
BATCH = 4
SEQ = 8192
DEPTH = 2

GRID_W = 64
CTX_LEN = 256
N_MOD = 9
FFN_HIDDEN = 2816
NORM_EPS = 1e-6
ROPE_BASE = 10000.0

MLA_HEADS = 8
QK_NOPE = 64
QK_ROPE = 32
V_DIM = 64
Q_RANK = 384
KV_RANK = 256
Q_BLOCK = 128

S5_CH = 512
S5_GROUP = 16
S5_GROUPS = S5_CH // S5_GROUP
S5_STATE = 64
S5_MIN_STEP = 1e-3
S5_MAX_STEP = 1e-1

HY_CH = 512
HY_BANDS = 16
HY_EMB = 1 + 2 * HY_BANDS
HY_FILT_HIDDEN = 64
HY_SHORT = 3
HY_DECAY_TARGET = 1e-2
HY_MIN_DECAY = math.log(HY_DECAY_TARGET) / 1.5
HY_MAX_DECAY = math.log(HY_DECAY_TARGET) / 0.3

LRU_CH = 512
LRU_BLOCKS = 8
LRU_BLOCK = LRU_CH // LRU_BLOCKS
LRU_CONV = 4
LRU_C = 8.0

EVEN_IN = Q_RANK + KV_RANK + QK_ROPE + S5_CH
EVEN_MIX = MLA_HEADS * V_DIM + S5_CH
ODD_IN = 3 * HY_CH + 2 * LRU_CH
ODD_MIX = HY_CH + LRU_CH
N_EVEN = (DEPTH + 1) // 2
N_ODD = DEPTH // 2

kernel_name = 'hybrid_mla_s5_hyena_rglru_diffusion_trunk'


def rmsnorm(x, g):
    xf = x.astype(jnp.float32)
    y = xf * lax.rsqrt(jnp.mean(xf * xf, axis=-1, keepdims=True) + NORM_EPS)
    return (y * g.astype(jnp.float32)).astype(x.dtype)


def modulate(h, shift, scale):
    return h * (1 + scale) + shift


def swiglu(h, w_gate, w_up, w_down):
    return (jax.nn.silu(h @ w_gate) * (h @ w_up)) @ w_down


def adaln_params(cond, w, b):
    m = jax.nn.silu(cond) @ w + b
    return m.reshape(cond.shape[0], 1, N_MOD, D_MODEL)


def ffn_half_step(h, m, k, g, w_gate, w_up, w_down):
    shift, scale, gate = m[:, :, 3 * k], m[:, :, 3 * k + 1], m[:, :, 3 * k + 2]
    hn = modulate(rmsnorm(h, g), shift, scale)
    return h + 0.5 * gate * swiglu(hn, w_gate, w_up, w_down)


def grid_rope_tables(n_tok):
    rows = n_tok // GRID_W
    row = jnp.repeat(jnp.arange(rows, dtype=jnp.float32), GRID_W)
    col = jnp.tile(jnp.arange(GRID_W, dtype=jnp.float32), rows)
    n_freq = QK_ROPE // 4
    inv_freq = ROPE_BASE ** (-jnp.arange(n_freq, dtype=jnp.float32) / n_freq)
    ang = jnp.concatenate([row[:, None] * inv_freq, col[:, None] * inv_freq], axis=-1)
    return jnp.cos(ang), jnp.sin(ang)


def apply_rope(x, cos, sin):
    x1, x2 = jnp.split(x.astype(jnp.float32), 2, axis=-1)
    return jnp.concatenate([x1 * cos - x2 * sin, x2 * cos + x1 * sin], axis=-1).astype(x.dtype)


def depthwise_conv(x, w, b, pad):
    y = lax.conv_general_dilated(x, w[:, None, :].astype(x.dtype), window_strides=(1,), padding=[pad],
                                 dimension_numbers=('NWC', 'WIO', 'NWC'), feature_group_count=x.shape[-1])
    return y + b.astype(x.dtype)


def _affine_combine(e1, e2):
    a1, b1 = e1
    a2, b2 = e2
    return a1 * a2, a2 * b1 + b2


def linear_scan(a, b, h0, reverse):
    if h0 is not None:
        idx = -1 if reverse else 0
        b = b.at[:, idx].add(a[:, idx] * h0)
    _, h = lax.associative_scan(_affine_combine, (a, b), reverse=reverse, axis=1)
    return h


def mla_queries(cq, q_norm, w_uq, cos, sin):
    B, n, _ = cq.shape
    q = (rmsnorm(cq, q_norm) @ w_uq).reshape(B, n, MLA_HEADS, QK_NOPE + QK_ROPE)
    if cos is None:
        return q
    q_rope = apply_rope(q[..., QK_NOPE:], cos[None, :, None], sin[None, :, None])
    return jnp.concatenate([q[..., :QK_NOPE], q_rope], axis=-1)


def mla_keys_values(ckv, kr, kv_norm, w_ukv, cos, sin):
    B, n, _ = ckv.shape
    kv = (rmsnorm(ckv, kv_norm) @ w_ukv).reshape(B, n, MLA_HEADS, QK_NOPE + V_DIM)
    if cos is not None:
        kr = apply_rope(kr, cos[None], sin[None])
    k = jnp.concatenate([kv[..., :QK_NOPE], jnp.broadcast_to(kr[:, :, None, :], (B, n, MLA_HEADS, QK_ROPE))], axis=-1)
    return k, kv[..., QK_NOPE:]


def context_attention(q, k, v):
    scale = (QK_NOPE + QK_ROPE) ** -0.5
    s = jnp.einsum('bqhd,bkhd->bhqk', q, k).astype(jnp.float32) * scale
    p = jax.nn.softmax(s, axis=-1).astype(v.dtype)
    return jnp.einsum('bhqk,bkhd->bqhd', p, v)


def latent_attention(q, k_lat, v_lat, k_ctx, v_ctx):
    B, n, H, dk = q.shape
    n_ctx = k_ctx.shape[1]
    scale = dk ** -0.5
    qb = jnp.moveaxis(q.reshape(B, n // Q_BLOCK, Q_BLOCK, H, dk), 1, 0)

    def one_block(qi):
        s = jnp.concatenate([jnp.einsum('bqhd,bkhd->bhqk', qi, k_ctx),
                             jnp.einsum('bqhd,bkhd->bhqk', qi, k_lat)], axis=-1)
        p = jax.nn.softmax(s.astype(jnp.float32) * scale, axis=-1).astype(v_lat.dtype)
        return (jnp.einsum('bhqk,bkhd->bqhd', p[..., :n_ctx], v_ctx)
                + jnp.einsum('bhqk,bkhd->bqhd', p[..., n_ctx:], v_lat))

    o = lax.map(one_block, qb)
    return jnp.moveaxis(o, 0, 1).reshape(B, n, H, v_lat.shape[-1])


def s5_discretise(lam_re, lam_im, log_step, b_re, b_im):
    f32 = jnp.float32
    lam = lax.complex(lam_re.astype(f32), lam_im.astype(f32))
    step = jnp.exp(log_step.astype(f32))[:, None]
    a_bar = jnp.exp(lam * step)
    b = lax.complex(b_re.astype(f32), b_im.astype(f32))
    b_bar = ((a_bar - 1.0) / lam)[..., None] * b
    return a_bar, b_bar


def s5_stream(u, lam_re, lam_im, log_step, b_re, b_im, c_re, c_im, d_skip, w_glu, b_glu, h0, need_out):
    f32 = jnp.float32
    B, n, _ = u.shape
    uf = u.astype(f32)
    ug = uf.reshape(B, n, S5_GROUPS, S5_GROUP).astype(jnp.complex64)
    y = None
    finals = []
    for dr in range(2):
        a_bar, b_bar = s5_discretise(lam_re[dr], lam_im[dr], log_step[dr], b_re[dr], b_im[dr])
        bu = jnp.einsum('gpc,bngc->bngp', b_bar, ug)
        a = jnp.broadcast_to(a_bar[None, None], (1, n, S5_GROUPS, S5_STATE))
        h = linear_scan(a, bu, None if h0 is None else h0[dr], reverse=dr == 1)
        finals.append(h[:, 0] if dr == 1 else h[:, -1])
        if need_out:
            c_mat = lax.complex(c_re[dr].astype(f32), c_im[dr].astype(f32))
            yd = jnp.real(jnp.einsum('gcp,bngp->bngc', c_mat, h))
            y = yd if y is None else y + yd
    if not need_out:
        return None, finals
    y = y.reshape(B, n, S5_CH) + d_skip.astype(f32) * uf
    y = jax.nn.gelu(y)
    y = y * jax.nn.sigmoid(y @ w_glu.astype(f32) + b_glu.astype(f32))
    return y.astype(u.dtype), finals


def hyena_filters(n, w1, b1, w2, b2, w3, sin_freq):
    f32 = jnp.float32
    t = jnp.linspace(0.0, 1.0, n, dtype=f32)[:, None]
    w = 2.0 * math.pi * jnp.arange(n, dtype=f32)[:, None] / n
    bands = jnp.linspace(1e-4, HY_BANDS - 1, HY_BANDS, dtype=f32)[None, :]
    z = jnp.concatenate([t, jnp.cos(bands * w), -jnp.sin(bands * w)], axis=-1)
    h = jnp.sin(sin_freq[0].astype(f32) * (z @ w1.astype(f32) + b1.astype(f32)))
    h = jnp.sin(sin_freq[1].astype(f32) * (h @ w2.astype(f32) + b2.astype(f32)))
    h = (h @ w3.astype(f32)).reshape(n, 2, HY_CH)
    deltas = jnp.abs(jnp.linspace(HY_MIN_DECAY, HY_MAX_DECAY, HY_CH, dtype=f32))
    return h * jnp.exp(-t * deltas)[:, None, :]


def bidir_fftconv(u, h_fwd, h_bwd, bias):
    n = u.shape[1]
    filt_circ = jnp.concatenate([h_fwd, jnp.zeros_like(h_fwd[:1]), h_bwd[:0:-1]], axis=0)
    uf = u.astype(jnp.float32)
    spec = jnp.fft.rfft(uf, n=2 * n, axis=1) * jnp.fft.rfft(filt_circ, axis=0)[None]
    y = jnp.fft.irfft(spec, n=2 * n, axis=1)[:, :n]
    return (y + uf * bias.astype(jnp.float32)).astype(u.dtype)


def hyena_stream(u, conv_w, conv_b, w1, b1, w2, b2, w3, sin_freq, bias):
    n = u.shape[1]
    u = depthwise_conv(u, conv_w, conv_b, (1, 1))
    x0, x1, v = jnp.split(u, 3, axis=-1)
    filt = hyena_filters(n, w1, b1, w2, b2, w3, sin_freq)
    return x0 * bidir_fftconv(v * x1, filt[:, 0], filt[:, 1], bias)


def rglru_stream(u_x, u_gate, conv_w, conv_b, w_a, b_a, w_x, b_x, lam, h0, need_out):
    f32 = jnp.float32
    B, n, _ = u_x.shape
    xc = depthwise_conv(u_x, conv_w, conv_b, (2, 1)).astype(f32)
    xb = xc.reshape(B, n, LRU_BLOCKS, LRU_BLOCK)
    y = None
    finals = []
    for dr in range(2):
        r = jax.nn.sigmoid(jnp.einsum('bnhi,hij->bnhj', xb, w_a[dr].astype(f32)).reshape(B, n, LRU_CH) + b_a[dr].astype(f32))
        ig = jax.nn.sigmoid(jnp.einsum('bnhi,hij->bnhj', xb, w_x[dr].astype(f32)).reshape(B, n, LRU_CH) + b_x[dr].astype(f32))
        log_a = -LRU_C * r * jax.nn.softplus(-lam[dr].astype(f32))
        a = jnp.exp(log_a)
        b = jnp.sqrt(-jnp.expm1(2.0 * log_a)) * (ig * xc)
        h = linear_scan(a, b, None if h0 is None else h0[dr], reverse=dr == 1)
        finals.append(h[:, 0] if dr == 1 else h[:, -1])
        if need_out:
            y = h if y is None else y + h
    if not need_out:
        return None, finals
    return (y * jax.nn.gelu(u_gate.astype(f32))).astype(u_x.dtype), finals


def even_mixer(h_ctx, h_lat, cos, sin, w_in, q_norm, w_uq, kv_norm, w_ukv,
               s5_lambda_re, s5_lambda_im, s5_log_step, s5_b_re, s5_b_im, s5_c_re, s5_c_im,
               s5_d, s5_w_glu, s5_b_glu, w_out, need_ctx_out):
    cut = [Q_RANK, Q_RANK + KV_RANK, Q_RANK + KV_RANK + QK_ROPE]
    cq_c, ckv_c, kr_c, s_c = jnp.split(h_ctx @ w_in, cut, axis=-1)
    cq_l, ckv_l, kr_l, s_l = jnp.split(h_lat @ w_in, cut, axis=-1)
    B, n, _ = h_lat.shape
    k_ctx, v_ctx = mla_keys_values(ckv_c, kr_c, kv_norm, w_ukv, None, None)
    k_lat, v_lat = mla_keys_values(ckv_l, kr_l, kv_norm, w_ukv, cos, sin)
    q_lat = mla_queries(cq_l, q_norm, w_uq, cos, sin)
    o_lat = latent_attention(q_lat, k_lat, v_lat, k_ctx, v_ctx).reshape(B, n, MLA_HEADS * V_DIM)
    s5p = (s5_lambda_re, s5_lambda_im, s5_log_step, s5_b_re, s5_b_im, s5_c_re, s5_c_im, s5_d, s5_w_glu, s5_b_glu)
    s_ctx_out, ctx_finals = s5_stream(s_c, *s5p, None, need_ctx_out)
    s_lat_out, _ = s5_stream(s_l, *s5p, ctx_finals, True)
    y_lat = jnp.concatenate([o_lat, s_lat_out], axis=-1) @ w_out
    y_ctx = None
    if need_ctx_out:
        n_ctx = h_ctx.shape[1]
        q_ctx = mla_queries(cq_c, q_norm, w_uq, None, None)
        o_ctx = context_attention(q_ctx, k_ctx, v_ctx).reshape(h_ctx.shape[0], n_ctx, MLA_HEADS * V_DIM)
        y_ctx = jnp.concatenate([o_ctx, s_ctx_out], axis=-1) @ w_out
    return y_ctx, y_lat


def odd_mixer(h_ctx, h_lat, w_in, hy_conv_w, hy_conv_b, hy_filt_w1, hy_filt_b1, hy_filt_w2, hy_filt_b2,
              hy_filt_w3, hy_sin_freq, hy_bias, lru_conv_w, lru_conv_b, lru_w_a, lru_b_a, lru_w_x, lru_b_x,
              lru_lambda, w_out, need_ctx_out):
    cut = [3 * HY_CH, 3 * HY_CH + LRU_CH]
    hy_c, lx_c, lg_c = jnp.split(h_ctx @ w_in, cut, axis=-1)
    hy_l, lx_l, lg_l = jnp.split(h_lat @ w_in, cut, axis=-1)
    hyp = (hy_conv_w, hy_conv_b, hy_filt_w1, hy_filt_b1, hy_filt_w2, hy_filt_b2, hy_filt_w3, hy_sin_freq, hy_bias)
    lrup = (lru_conv_w, lru_conv_b, lru_w_a, lru_b_a, lru_w_x, lru_b_x, lru_lambda)
    r_ctx, ctx_finals = rglru_stream(lx_c, lg_c, *lrup, None, need_ctx_out)
    r_lat, _ = rglru_stream(lx_l, lg_l, *lrup, ctx_finals, True)
    y_lat = jnp.concatenate([hyena_stream(hy_l, *hyp), r_lat], axis=-1) @ w_out
    y_ctx = None
    if need_ctx_out:
        y_ctx = jnp.concatenate([hyena_stream(hy_c, *hyp), r_ctx], axis=-1) @ w_out
    return y_ctx, y_lat


def setup_inputs(seed: int = 0) -> dict:
    key = jax.random.key(seed)
    keys = iter(jax.random.split(key, 64))
    f32 = jnp.float32

    def nrm(shape, scale):
        return scale * jax.random.normal(next(keys), shape, f32)

    def gain(shape):
        return 1.0 + nrm(shape, 0.02)

    def unif(shape, lo, hi):
        return jax.random.uniform(next(keys), shape, f32, lo, hi)

    D, F, NE, NO = D_MODEL, FFN_HIDDEN, N_EVEN, N_ODD
    lru_a0 = unif((NO, 2, LRU_CH), 0.9, 0.999) ** (1.0 / LRU_C)
    return {
        'x': nrm((BATCH, SEQ, D), 1.0),
        'c': nrm((BATCH, D), 1.0),
        'ctx': nrm((BATCH, CTX_LEN, D), 1.0),
        'c_ctx': nrm((D,), 1.0),
        'mod_w': nrm((DEPTH, D, N_MOD * D), 0.5 * D ** -0.5),
        'mod_b': nrm((DEPTH, N_MOD * D), 0.02),
        'norm_ffn1': gain((DEPTH, D)),
        'norm_mix': gain((DEPTH, D)),
        'norm_ffn2': gain((DEPTH, D)),
        'ffn1_w_gate': nrm((DEPTH, D, F), D ** -0.5),
        'ffn1_w_up': nrm((DEPTH, D, F), D ** -0.5),
        'ffn1_w_down': nrm((DEPTH, F, D), F ** -0.5),
        'ffn2_w_gate': nrm((DEPTH, D, F), D ** -0.5),
        'ffn2_w_up': nrm((DEPTH, D, F), D ** -0.5),
        'ffn2_w_down': nrm((DEPTH, F, D), F ** -0.5),
        'ev_w_in': nrm((NE, D, EVEN_IN), D ** -0.5),
        'mla_q_norm': gain((NE, Q_RANK)),
        'mla_w_uq': nrm((NE, Q_RANK, MLA_HEADS * (QK_NOPE + QK_ROPE)), Q_RANK ** -0.5),
        'mla_kv_norm': gain((NE, KV_RANK)),
        'mla_w_ukv': nrm((NE, KV_RANK, MLA_HEADS * (QK_NOPE + V_DIM)), KV_RANK ** -0.5),
        's5_lambda_re': -0.5 + nrm((NE, 2, S5_GROUPS, S5_STATE), 0.01),
        's5_lambda_im': math.pi * jnp.arange(S5_STATE, dtype=f32) + nrm((NE, 2, S5_GROUPS, S5_STATE), 0.01),
        's5_log_step': unif((NE, 2, S5_GROUPS), math.log(S5_MIN_STEP), math.log(S5_MAX_STEP)),
        's5_b_re': nrm((NE, 2, S5_GROUPS, S5_STATE, S5_GROUP), (2 * S5_GROUP) ** -0.5),
        's5_b_im': nrm((NE, 2, S5_GROUPS, S5_STATE, S5_GROUP), (2 * S5_GROUP) ** -0.5),
        's5_c_re': nrm((NE, 2, S5_GROUPS, S5_GROUP, S5_STATE), S5_STATE ** -0.5),
        's5_c_im': nrm((NE, 2, S5_GROUPS, S5_GROUP, S5_STATE), S5_STATE ** -0.5),
        's5_d': nrm((NE, S5_CH), 0.5),
        's5_w_glu': nrm((NE, S5_CH, S5_CH), S5_CH ** -0.5),
        's5_b_glu': nrm((NE, S5_CH), 0.02),
        'ev_w_out': nrm((NE, EVEN_MIX, D), EVEN_MIX ** -0.5),
        'od_w_in': nrm((NO, D, ODD_IN), D ** -0.5),
        'hy_conv_w': nrm((NO, HY_SHORT, 3 * HY_CH), HY_SHORT ** -0.5),
        'hy_conv_b': nrm((NO, 3 * HY_CH), 0.02),
        'hy_filt_w1': nrm((NO, HY_EMB, HY_FILT_HIDDEN), HY_EMB ** -0.5),
        'hy_filt_b1': nrm((NO, HY_FILT_HIDDEN), 0.02),
        'hy_filt_w2': nrm((NO, HY_FILT_HIDDEN, HY_FILT_HIDDEN), HY_FILT_HIDDEN ** -0.5),
        'hy_filt_b2': nrm((NO, HY_FILT_HIDDEN), 0.02),
        'hy_filt_w3': nrm((NO, HY_FILT_HIDDEN, 2 * HY_CH), 0.02 * HY_FILT_HIDDEN ** -0.5),
        'hy_sin_freq': gain((NO, 2, HY_FILT_HIDDEN)),
        'hy_bias': nrm((NO, HY_CH), 0.5),
        'lru_conv_w': nrm((NO, LRU_CONV, LRU_CH), LRU_CONV ** -0.5),
        'lru_conv_b': nrm((NO, LRU_CH), 0.02),
        'lru_w_a': nrm((NO, 2, LRU_BLOCKS, LRU_BLOCK, LRU_BLOCK), LRU_BLOCK ** -0.5),
        'lru_b_a': nrm((NO, 2, LRU_CH), 0.02),
        'lru_w_x': nrm((NO, 2, LRU_BLOCKS, LRU_BLOCK, LRU_BLOCK), LRU_BLOCK ** -0.5),
        'lru_b_x': nrm((NO, 2, LRU_CH), 0.02),
        'lru_lambda': jnp.log(lru_a0) - jnp.log1p(-lru_a0),
        'od_w_out': nrm((NO, ODD_MIX, D), ODD_MIX ** -0.5),
        'final_norm': gain((D,)),
    }


def reference(x, c, ctx, c_ctx, mod_w, mod_b, norm_ffn1, norm_mix, norm_ffn2,
              ffn1_w_gate, ffn1_w_up, ffn1_w_down, ffn2_w_gate, ffn2_w_up, ffn2_w_down,
              ev_w_in, mla_q_norm, mla_w_uq, mla_kv_norm, mla_w_ukv,
              s5_lambda_re, s5_lambda_im, s5_log_step, s5_b_re, s5_b_im, s5_c_re, s5_c_im,
              s5_d, s5_w_glu, s5_b_glu, ev_w_out,
              od_w_in, hy_conv_w, hy_conv_b, hy_filt_w1, hy_filt_b1, hy_filt_w2, hy_filt_b2,
              hy_filt_w3, hy_sin_freq, hy_bias,
              lru_conv_w, lru_conv_b, lru_w_a, lru_b_a, lru_w_x, lru_b_x, lru_lambda, od_w_out,
              final_norm):
    n_lat = x.shape[1]
    cos, sin = grid_rope_tables(n_lat)
    lat, cx = x, ctx
    for i in range(DEPTH):
        last = i == DEPTH - 1
        m_lat = adaln_params(c, mod_w[i], mod_b[i])
        m_ctx = adaln_params(c_ctx[None, :], mod_w[i], mod_b[i])
        ffn1 = (ffn1_w_gate[i], ffn1_w_up[i], ffn1_w_down[i])
        ffn2 = (ffn2_w_gate[i], ffn2_w_up[i], ffn2_w_down[i])
        lat = ffn_half_step(lat, m_lat, 0, norm_ffn1[i], *ffn1)
        cx = ffn_half_step(cx, m_ctx, 0, norm_ffn1[i], *ffn1)
        h_lat = modulate(rmsnorm(lat, norm_mix[i]), m_lat[:, :, 3], m_lat[:, :, 4])
        h_ctx = modulate(rmsnorm(cx, norm_mix[i]), m_ctx[:, :, 3], m_ctx[:, :, 4])
        j = i // 2
        if i % 2 == 0:
            y_ctx, y_lat = even_mixer(h_ctx, h_lat, cos, sin, ev_w_in[j], mla_q_norm[j], mla_w_uq[j],
                                      mla_kv_norm[j], mla_w_ukv[j], s5_lambda_re[j], s5_lambda_im[j],
                                      s5_log_step[j], s5_b_re[j], s5_b_im[j], s5_c_re[j], s5_c_im[j],
                                      s5_d[j], s5_w_glu[j], s5_b_glu[j], ev_w_out[j], not last)
        else:
            y_ctx, y_lat = odd_mixer(h_ctx, h_lat, od_w_in[j], hy_conv_w[j], hy_conv_b[j], hy_filt_w1[j],
                                     hy_filt_b1[j], hy_filt_w2[j], hy_filt_b2[j], hy_filt_w3[j],
                                     hy_sin_freq[j], hy_bias[j], lru_conv_w[j], lru_conv_b[j], lru_w_a[j],
                                     lru_b_a[j], lru_w_x[j], lru_b_x[j], lru_lambda[j], od_w_out[j], not last)
        lat = lat + m_lat[:, :, 5] * y_lat
        lat = ffn_half_step(lat, m_lat, 2, norm_ffn2[i], *ffn2)
        if not last:
            cx = cx + m_ctx[:, :, 5] * y_ctx
            cx = ffn_half_step(cx, m_ctx, 2, norm_ffn2[i], *ffn2)
    return rmsnorm(lat, final_norm)
```

```python
from contextlib import ExitStack
import numpy as np
import concourse.bass as bass
import concourse.mybir as mybir
from concourse.bass_utils import run_bass_kernel_spmd

F32 = mybir.dt.float32
BF16 = mybir.dt.bfloat16
AF = mybir.ActivationFunctionType
ALU = mybir.AluOpType
AX = mybir.AxisListType

ENGS = ('pe', 'act', 'dve', 'pool', 'sp')
NSLOT = 12


class Buf:
    __slots__ = ('t', 'w', 'r', 'name')

    def __init__(self, t, name):
        self.t = t
        self.w = None
        self.r = []
        self.name = name

    def __getitem__(self, idx):
        return self.t[idx]


class Prog:
    def __init__(self, same_engine_sync=True):
        self.nc = bass.Bass("TRN2", target_bir_lowering=False)
        self.ops = {e: [] for e in ENGS}
        self.cnt = {e: 0 for e in ENGS}
        self.seen = {e: {} for e in ENGS}
        self.dma_n = {e: 0 for e in ENGS}
        self.slot_val = {}
        self.stack = ExitStack()
        self.same = same_engine_sync
        self.nbuf = 0
        self.out_tokens = []
        self.scopes = []
        self.closers = []
        self.barrier = []

    def dram(self, name, shape, dt=F32, kind="ExternalInput"):
        t = self.nc.dram_tensor(name, list(shape), dt, kind=kind).ap()
        return Buf(t, name)

    def _ctx(self):
        return self.scopes[-1][0] if self.scopes else self.stack

    def _reg(self, b):
        b.r = list(self.barrier)
        if self.scopes:
            self.scopes[-1][1].append(b)
        return b

    def sb(self, shape, dt=F32, name=None):
        self.nbuf += 1
        name = (name or "sb") + f"_{self.nbuf}"
        t = self._ctx().enter_context(self.nc.sbuf_tensor(name, list(shape), dt))
        return self._reg(Buf(t, name))

    def ps(self, shape, dt=F32, name=None):
        self.nbuf += 1
        name = (name or "ps") + f"_{self.nbuf}"
        t = self._ctx().enter_context(self.nc.psum_tensor(name, list(shape), dt))
        return self._reg(Buf(t, name))

    def view(self, b):
        return Buf(b.t, b.name)

    def push(self):
        self.scopes.append((ExitStack(), []))

    def pop(self):
        st, bufs = self.scopes.pop()
        m = {}
        for b in bufs:
            for tok in ([b.w] if b.w else []) + b.r:
                if m.get(tok[0], 0) < tok[1]:
                    m[tok[0]] = tok[1]
        for k, v in self.barrier:
            if m.get(k, 0) < v:
                m[k] = v
        self.barrier = list(m.items())
        st.close()

    def _deps(self, eng, reads, writes):
        deps = {}

        def add(tok):
            if tok is None:
                return
            k, v = tok
            if deps.get(k, 0) < v:
                deps[k] = v
        for b in reads:
            add(b.w)
        for b in writes:
            add(b.w)
            for t in b.r:
                add(t)
        out = []
        seen = self.seen[eng]
        for k, v in deps.items():
            if k == eng and (eng == 'pe' or not self.same):
                continue
            if seen.get(k, 0) >= v:
                continue
            seen[k] = v
            out.append((k, v))
        return out

    def _commit(self, tok, reads, writes):
        for b in reads:
            if not any(b is w for w in writes):
                b.r.append(tok)
        for b in writes:
            b.w = tok
            b.r = []

    def op(self, eng, fn, reads=(), writes=()):
        waits = self._deps(eng, reads, writes)
        self.cnt[eng] += 1
        tok = (eng, self.cnt[eng])
        self.ops[eng].append((waits, fn, (eng, 1)))
        self._commit(tok, reads, writes)
        return tok

    def dma(self, out_ap, in_ap, reads=(), writes=(), q='sp', is_output=False, **kw):
        i = self.dma_n[q]
        self.dma_n[q] += 1
        slot = ('d', q, i % NSLOT)
        waits = self._deps(q, reads, writes)
        prev = self.slot_val.get(slot, 0)
        if prev and self.seen[q].get(slot, 0) < prev:
            self.seen[q][slot] = prev
            waits.append((slot, prev))
        val = prev + 16
        self.slot_val[slot] = val
        tok = (slot, val)

        def fn(e, out_ap=out_ap, in_ap=in_ap, kw=kw):
            return e.dma_start(out=out_ap, in_=in_ap, **kw)
        self.ops[q].append((waits, fn, (slot, 16)))
        self._commit(tok, reads, writes)
        if is_output:
            self.out_tokens.append(tok)
        return tok

    def build(self):
        nc = self.nc
        fin = {}
        for k, v in self.out_tokens:
            fin[k] = max(fin.get(k, 0), v)
        keys = set()
        for e in ENGS:
            for waits, fn, inc in self.ops[e]:
                keys.add(inc[0])
                for k, v in waits:
                    keys.add(k)
        sems = {}
        for k in sorted(keys, key=str):
            nm = k if isinstance(k, str) else f"d_{k[1]}_{k[2]}"
            sems[k] = self.stack.enter_context(nc.semaphore("s_" + nm))
        ops = self.ops
        with nc.Block() as block:
            def mk(ename):
                def body(e):
                    for waits, fn, inc in ops[ename]:
                        for k, v in waits:
                            e.wait_ge(sems[k], v)
                        ins = fn(e)
                        ins.then_inc(sems[inc[0]], inc[1])
                    if ename == 'sp':
                        for k, v in fin.items():
                            e.wait_ge(sems[k], v)
                return body
            if ops['sp'] or fin:
                block.sync(mk('sp'))
            if ops['pe']:
                block.tensor(mk('pe'))
            if ops['act']:
                block.scalar(mk('act'))
            if ops['dve']:
                block.vector(mk('dve'))
            if ops['pool']:
                block.gpsimd(mk('pool'))
        self.stack.close()
        return nc

    def mm(self, out, lhsT, rhs, start=True, stop=True, reads=(), writes=()):
        return self.op('pe', lambda e: e.matmul(out, lhsT, rhs, start=start, stop=stop), reads, writes)

    def tr(self, out, in_, ident, reads=(), writes=()):
        return self.op('pe', lambda e: e.transpose(out, in_, ident), reads, writes)

    def act(self, out, in_, func, bias=None, scale=None, accum_out=None, reads=(), writes=(), eng='act'):
        kw = {}
        if bias is not None:
            kw['bias'] = bias
        if scale is not None:
            kw['scale'] = scale
        if accum_out is not None:
            kw['accum_out'] = accum_out
        return self.op(eng, lambda e: e.activation(out, in_, func, **kw), reads, writes)

    def tt(self, out, in0, in1, op, reads=(), writes=(), eng='dve'):
        return self.op(eng, lambda e: e.tensor_tensor(out, in0, in1, op), reads, writes)

    def ts(self, out, in0, s1, op0, s2=None, op1=None, accum_out=None, reads=(), writes=(), eng='dve'):
        kw = {}
        if op1 is not None:
            kw['op1'] = op1
        if accum_out is not None:
            kw['accum_out'] = accum_out
        return self.op(eng, lambda e: e.tensor_scalar(out, in0, s1, s2, op0, **kw), reads, writes)

    def stt(self, out, in0, scalar, in1, op0, op1, reads=(), writes=(), eng='dve'):
        eng = 'dve'
        return self.op(eng, lambda e: e.scalar_tensor_tensor(out, in0, scalar, in1, op0, op1), reads, writes)

    def copy(self, out, in_, reads=(), writes=(), eng='dve'):
        if eng == 'act':
            return self.op(eng, lambda e: e.copy(out, in_), reads, writes)
        return self.op(eng, lambda e: e.tensor_copy(out, in_), reads, writes)

    def memset(self, ap, val, writes=(), eng='pool'):
        return self.op(eng, lambda e: e.memset(ap, val), (), writes)

    def scan(self, out, d0, d1, init, reads=(), writes=(), eng='dve'):
        return self.op(eng, lambda e: e.tensor_tensor_scan(out, d0, d1, init, ALU.mult, ALU.add), reads, writes)


def run(prog_nc, in_maps, trace=False):
    res = run_bass_kernel_spmd(prog_nc, in_maps, core_ids=list(range(len(in_maps))), trace=trace)
    return res


D = 1024
FH = 2816
NF = 22
EPS = 1e-6
NT_CORE = 33
BLOCKS = [(4 * i, 4, 0) for i in range(8)] + [(32, 1, 1)]


def wview(w, p=128):
    return w.t.rearrange("(c p) f -> p c f", p=p)


def load_w(P, dst, src, q='pool'):
    v = wview(src)
    for c in range(v.shape[1]):
        P.dma(dst[:, c, :], v[:, c, :], writes=[dst], q=q)


class Ctx:
    pass


def consts(P):
    K = Ctx()
    idd = P.dram("ident", [128, 128])
    K.ident = P.sb([128, 128], F32, "ident")
    P.dma(K.ident[:], idd[:], writes=[K.ident])
    K.identb = P.sb([128, 128], BF16, "identb")
    P.copy(K.identb[:], K.ident[:], reads=[K.ident], writes=[K.identb])
    K.ones = P.sb([128, 128], F32, "ones")
    P.memset(K.ones[:], 1.0, writes=[K.ones])
    K.onesb = P.sb([128, 128], BF16, "onesb")
    P.memset(K.onesb[:], 1.0, writes=[K.onesb])
    K.pb = [P.ps([128, 512], F32, f"pb{i}") for i in range(6)]
    K.pbT = [P.ps([128, 1024], BF16, f"pbT{i}") for i in range(2)]
    return K


def load_fm(P, K, dstbuf, dst, src2d, C):
    tmp = P.sb([C, 128], F32, "lfm")
    P.dma(tmp[:], src2d, writes=[tmp])
    pb = K.pb[5]
    P.tr(pb[:, 0:C], tmp[:], K.ident[0:C, 0:C], reads=[tmp, K.ident], writes=[pb])
    P.copy(dst, pb[:, 0:C], reads=[pb], writes=[dstbuf])


def adaln(P, K, cnd, mod_w, mod_b, gate_ks):
    mfm = P.sb([128, 72, 2], F32, "mfm")
    gbc = {k: P.sb([128, 2, 1024], F32, f"gbc{k}") for k in gate_ks}
    P.push()
    sc = P.sb([128, 2, 8], F32, "sc")
    load_fm(P, K, sc, sc[:].rearrange("p n c -> p (n c)"), cnd.t.rearrange("n (c p) -> (n c) p", p=128), 16)
    scs = P.sb([128, 2, 8], F32, "scs")
    P.act(scs[:], sc[:], AF.Silu, reads=[sc], writes=[scs])
    scb = P.sb([128, 8, 2, 128], F32, "scb")
    for c in range(8):
        for n in range(2):
            P.ts(scb[:, c, n, :], K.ones[:], scs[:, n, c:c + 1], ALU.mult, reads=[K.ones, scs], writes=[scb])
    mbf = P.sb([128, 72], F32, "mbf")
    load_fm(P, K, mbf, mbf[:], mod_b.t.rearrange("(c p) -> c p", p=128), 72)
    mbb = P.sb([128, len(gate_ks), 1024], F32, "mbb")
    for i, k in enumerate(gate_ks):
        P.dma(mbb[:, i, :], mod_b.t[k * 1024:(k + 1) * 1024].partition_broadcast(128), writes=[mbb])
    wv = wview(mod_w)
    wbuf = [P.sb([128, 8, 1024], F32, f"modw{i}") for i in range(2)]
    for k in range(9):
        wb = wbuf[k % 2]
        for c in range(8):
            P.dma(wb[:, c, :], wv[:, c, k * 1024:(k + 1) * 1024], writes=[wb])
        for fc in range(8):
            pb = K.pb[fc % 2]
            for c in range(8):
                P.mm(pb[:, 0:2], wb[:, c, fc * 128:(fc + 1) * 128], scs[:, :, c], start=(c == 0), stop=(c == 7),
                     reads=[wb, scs], writes=[pb])
            ch = k * 8 + fc
            P.ts(mfm[:, ch, :], pb[:, 0:2], mbf[:, ch:ch + 1], ALU.add, reads=[pb, mbf], writes=[mfm])
        if k in gate_ks:
            gi = gate_ks.index(k)
            fac = 1.0 if k == 5 else 0.5
            for n in range(2):
                for hf in range(2):
                    pb = K.pb[2 + (n * 2 + hf) % 2]
                    for c in range(8):
                        P.mm(pb[:], scb[:, c, n, :], wb[:, c, hf * 512:(hf + 1) * 512], start=(c == 0), stop=(c == 7),
                             reads=[scb, wb], writes=[pb])
                    o = gbc[k][:, n, hf * 512:(hf + 1) * 512]
                    P.tt(o, pb[:], mbb[:, gi, hf * 512:(hf + 1) * 512], ALU.add, reads=[pb, mbb], writes=[gbc[k]])
                    if fac != 1.0:
                        P.ts(o, o, fac, ALU.mult, reads=[gbc[k]], writes=[gbc[k]])
    P.pop()
    return mfm, gbc


def norm_params(P, K, mfm, g_dram, k0, name):
    g = P.sb([128, 8], F32, name + "g")
    load_fm(P, K, g, g[:], g_dram.t.rearrange("(c p) -> c p", p=128), 8)
    sce = P.sb([128, 8, 2], F32, name + "sce")
    sh = P.sb([128, 8, 2], F32, name + "sh")
    for n in range(2):
        P.ts(sce[:, :, n], mfm[:, (k0 + 1) * 8:(k0 + 2) * 8, n], 1.0, ALU.add, reads=[mfm], writes=[sce])
        P.tt(sce[:, :, n], sce[:, :, n], g[:], ALU.mult, reads=[sce, g], writes=[sce])
        P.copy(sh[:, :, n], mfm[:, k0 * 8:(k0 + 1) * 8, n], reads=[mfm], writes=[sh])
    return sce, sh


def rstd_from_ss(P, out, ss, inv_n, reads, writes):
    P.ts(out, ss, inv_n, ALU.mult, EPS, ALU.add, reads=reads, writes=writes)
    P.act(out, out, AF.Sqrt, reads=writes, writes=writes)
    P.op('dve', lambda e: e.reciprocal(out, out), reads=writes, writes=writes)


def norm_T(P, K, W, xs, nt, n, sce, sh, hT):
    P.memset(W.ss[:], 0.0, writes=[W.ss], eng='dve')
    for t in range(nt):
        P.act(W.xn[t][:], xs[t][:], AF.Square, accum_out=W.ss[:, t:t + 1], reads=[xs[t]], writes=[W.xn[t], W.ss])
    rstd_from_ss(P, W.rstd[:], W.ss[:], 1.0 / D, [W.ss], [W.rstd])
    for t in range(nt):
        P.op('act', lambda e, t=t: e.mul(W.xn[t][:], xs[t][:], W.rstd[:, t:t + 1]), reads=[xs[t], W.rstd], writes=[W.xn[t]])
    for c in range(8):
        pb = K.pbT[c % 2]
        for t in range(nt):
            P.tr(pb[:, t * 128:(t + 1) * 128], W.xn[t][:, c * 128:(c + 1) * 128], K.identb[:],
                 reads=[W.xn[t], K.identb], writes=[pb])
        P.ts(hT[:, c, 0:nt * 128], pb[:, 0:nt * 128], sce[:, c, n:n + 1], ALU.mult, sh[:, c, n:n + 1], ALU.add,
             reads=[pb, sce, sh], writes=[hT])


class FFNW:
    pass


def ffn_alloc(P):
    W = FFNW()
    W.wg = P.sb([128, 8, FH], BF16, "wg")
    W.wu = P.sb([128, 8, FH], BF16, "wu")
    W.wd = P.sb([128, NF, D], BF16, "wd")
    W.xn = [P.sb([128, D], BF16, f"xn{t}") for t in range(4)]
    W.ss = P.sb([128, 4], F32, "ss")
    W.rstd = P.sb([128, 4], F32, "rstd")
    W.hT = P.sb([128, 8, 512], BF16, "hT")
    W.actT = P.sb([128, NF, 512], BF16, "actT")
    W.a = [P.sb([128, 512], BF16, f"a{i}") for i in range(2)]
    return W


def ffn_load(P, W, wg, wu, wd):
    load_w(P, W.wg, wg)
    load_w(P, W.wu, wu)
    load_w(P, W.wd, wd)


def ffn_block(P, K, W, xs, nt, n, sce, sh, gbc):
    N = nt * 128
    norm_T(P, K, W, xs, nt, n, sce, sh, W.hT)
    for f in range(NF):
        pg = K.pb[(f % 2) * 2]
        pu = K.pb[(f % 2) * 2 + 1]
        for c in range(8):
            P.mm(pg[:, 0:N], W.wg[:, c, f * 128:(f + 1) * 128], W.hT[:, c, 0:N], start=(c == 0), stop=(c == 7),
                 reads=[W.wg, W.hT], writes=[pg])
        for c in range(8):
            P.mm(pu[:, 0:N], W.wu[:, c, f * 128:(f + 1) * 128], W.hT[:, c, 0:N], start=(c == 0), stop=(c == 7),
                 reads=[W.wu, W.hT], writes=[pu])
        a = W.a[f % 2]
        P.act(a[:, 0:N], pg[:, 0:N], AF.Silu, reads=[pg], writes=[a])
        P.tt(W.actT[:, f, 0:N], a[:, 0:N], pu[:, 0:N], ALU.mult, reads=[a, pu], writes=[W.actT])
    i = 0
    for t in range(nt):
        for hf in range(2):
            py = K.pb[4 + i % 2]
            i += 1
            for f in range(NF):
                P.mm(py[:], W.actT[:, f, t * 128:(t + 1) * 128], W.wd[:, f, hf * 512:(hf + 1) * 512],
                     start=(f == 0), stop=(f == NF - 1), reads=[W.actT, W.wd], writes=[py])
            tb = W.xn[i % 2]
            tmp = tb[:].bitcast(F32)
            P.tt(tmp, py[:], gbc[:, n, hf * 512:(hf + 1) * 512], ALU.mult, reads=[py, gbc], writes=[tb])
            xo = xs[t][:, hf * 512:(hf + 1) * 512]
            P.tt(xo, xo, tmp, ALU.add, reads=[xs[t], tb], writes=[xs[t]], eng='pool')


QR, KVR, ROPE, S5C = 384, 256, 32, 512
NH = 8
EV_IN = QR + KVR + ROPE + S5C
EV_EXT = EV_IN + 192


def rot(K):
    K.rr = getattr(K, 'rr', -1) + 1
    return K.pb[K.rr % 6]


def evin_alloc(P):
    W = FFNW()
    W.win = P.sb([128, 8, EV_EXT], BF16, "win")
    W.wuq = P.sb([128, 3, NH * 96], BF16, "wuq")
    W.wuqs = P.sb([128, 3, NH * 96], BF16, "wuqs")
    W.wk = P.sb([128, 2, NH * 64], BF16, "wk")
    W.wv = P.sb([128, 2, 512], BF16, "wv")
    W.gq = P.sb([128, 3], F32, "gq")
    W.gkv = P.sb([128, 2], F32, "gkv")
    W.cos = P.sb([96, 4096], F32, "cos")
    W.sin = P.sb([96, 4096], F32, "sin")
    W.xn = [P.sb([128, D], BF16, f"xn{t}") for t in range(4)]
    W.ss = P.sb([128, 4], F32, "ss")
    W.rstd = P.sb([128, 4], F32, "rstd")
    W.hT = P.sb([128, 8, 512], BF16, "hT")
    W.cq = P.sb([128, 3, 512], F32, "cq")
    W.sq = P.sb([128, 512], F32, "sq")
    W.rbc = P.sb([128, 512], F32, "rbc")
    W.cqn = P.sb([128, 3, 512], BF16, "cqn")
    W.ckvn = P.sb([128, 2, 512], BF16, "ckvn")
    W.krr = P.sb([96, 512], F32, "krr")
    W.t1 = P.sb([96, 512], F32, "t1")
    W.t2 = P.sb([96, 512], F32, "t2")
    W.st = [P.sb([128, 512], F32, f"st{i}") for i in range(3)]
    W.sti = 0
    return W


def stage(W):
    W.sti += 1
    return W.st[W.sti % 3]


def lowrank_norm(P, K, W, col0, nch, nfeat, g, N, outn):
    pss = rot(K)
    for c3 in range(nch):
        pc = rot(K)
        for c in range(8):
            P.mm(pc[:, 0:N], W.win[:, c, col0 + c3 * 128:col0 + (c3 + 1) * 128], W.hT[:, c, 0:N], start=(c == 0), stop=(c == 7),
                 reads=[W.win, W.hT], writes=[pc])
        P.copy(W.cq[:, c3, 0:N], pc[:, 0:N], reads=[pc], writes=[W.cq], eng='act')
        P.act(W.sq[:, 0:N], pc[:, 0:N], AF.Square, reads=[pc], writes=[W.sq])
        P.mm(pss[:, 0:N], K.ones[:], W.sq[:, 0:N], start=(c3 == 0), stop=(c3 == nch - 1), reads=[K.ones, W.sq], writes=[pss])
    rstd_from_ss(P, W.rbc[:, 0:N], pss[:, 0:N], 1.0 / nfeat, [pss], [W.rbc])
    for c3 in range(nch):
        P.stt(outn[:, c3, 0:N], W.cq[:, c3, 0:N], g[:, c3:c3 + 1], W.rbc[:, 0:N], ALU.mult, ALU.mult,
              reads=[W.cq, g, W.rbc], writes=[outn])


def rope_rows(P, W, out, a, b, tok0, N, reads, writes, roped):
    if not roped:
        P.copy(out, a, reads=reads, writes=writes)
        return
    P.tt(W.t1[64:96, 0:N], a, W.cos[64:96, tok0:tok0 + N], ALU.mult, reads=reads + [W.cos], writes=[W.t1])
    P.tt(W.t2[64:96, 0:N], b, W.sin[64:96, tok0:tok0 + N], ALU.mult, reads=reads + [W.sin], writes=[W.t2])
    P.tt(out, W.t1[64:96, 0:N], W.t2[64:96, 0:N], ALU.add, reads=[W.t1, W.t2], writes=writes, eng='pool')


def evin_block(P, K, W, xs, t0, nt, n, sce, sh, QT, KT, V, ST):
    N = nt * 128
    tok0 = t0 * 128
    roped = (n == 0)
    norm_T(P, K, W, xs, nt, n, sce, sh, W.hT)
    lowrank_norm(P, K, W, 0, 3, QR, W.gq, N, W.cqn)
    lowrank_norm(P, K, W, QR, 2, KVR, W.gkv, N, W.ckvn)
    pa, pbs = rot(K), rot(K)
    for c in range(8):
        P.mm(pa[0:96, 0:N], W.win[:, c, EV_IN:EV_IN + 96], W.hT[:, c, 0:N], start=(c == 0), stop=(c == 7), reads=[W.win, W.hT], writes=[pa])
    if roped:
        for c in range(8):
            P.mm(pbs[0:96, 0:N], W.win[:, c, EV_IN + 96:EV_IN + 192], W.hT[:, c, 0:N], start=(c == 0), stop=(c == 7), reads=[W.win, W.hT], writes=[pbs])
    rope_rows(P, W, W.krr[64:96, 0:N], pa[64:96, 0:N], pbs[64:96, 0:N], tok0, N, [pa, pbs], [W.krr], roped)
    for c4 in range(4):
        pc = rot(K)
        for c in range(8):
            P.mm(pc[:, 0:N], W.win[:, c, 672 + c4 * 128:672 + (c4 + 1) * 128], W.hT[:, c, 0:N], start=(c == 0), stop=(c == 7),
                 reads=[W.win, W.hT], writes=[pc])
        st = stage(W)
        P.copy(st[:, 0:N], pc[:, 0:N], reads=[pc], writes=[st], eng='act')
        P.dma(ST[c4 * 128:(c4 + 1) * 128, tok0:tok0 + N], st[:, 0:N], reads=[st], writes=[P.view(ST)], is_output=True)
    for h in range(NH):
        pq, pqs = rot(K), rot(K)
        for c in range(3):
            P.mm(pq[0:96, 0:N], W.wuq[:, c, h * 96:(h + 1) * 96], W.cqn[:, c, 0:N], start=(c == 0), stop=(c == 2), reads=[W.wuq, W.cqn], writes=[pq])
        if roped:
            for c in range(3):
                P.mm(pqs[0:96, 0:N], W.wuqs[:, c, h * 96:(h + 1) * 96], W.cqn[:, c, 0:N], start=(c == 0), stop=(c == 2), reads=[W.wuqs, W.cqn], writes=[pqs])
        st = stage(W)
        P.copy(st[0:64, 0:N], pq[0:64, 0:N], reads=[pq], writes=[st], eng='act')
        rope_rows(P, W, st[64:96, 0:N], pq[64:96, 0:N], pqs[64:96, 0:N], tok0, N, [pq, pqs], [st], roped)
        P.dma(QT[h, :, tok0:tok0 + N], st[0:96, 0:N], reads=[st], writes=[P.view(QT)], is_output=True)
        pk = rot(K)
        for c in range(2):
            P.mm(pk[0:64, 0:N], W.wk[:, c, h * 64:(h + 1) * 64], W.ckvn[:, c, 0:N], start=(c == 0), stop=(c == 1), reads=[W.wk, W.ckvn], writes=[pk])
        st = stage(W)
        P.copy(st[0:64, 0:N], pk[0:64, 0:N], reads=[pk], writes=[st], eng='act')
        P.copy(st[64:96, 0:N], W.krr[64:96, 0:N], reads=[W.krr], writes=[st], eng='pool')
        P.dma(KT[h, :, tok0:tok0 + N], st[0:96, 0:N], reads=[st], writes=[P.view(KT)], is_output=True)
    for t in range(nt):
        pv = rot(K)
        for c in range(2):
            P.mm(pv[:], W.ckvn[:, c, t * 128:(t + 1) * 128], W.wv[:, c, :], start=(c == 0), stop=(c == 1), reads=[W.ckvn, W.wv], writes=[pv])
        st = stage(W)
        P.copy(st[:], pv[:], reads=[pv], writes=[st])
        P.dma(V[tok0 + t * 128:tok0 + (t + 1) * 128, :], st[:], reads=[st], writes=[P.view(V)], is_output=True)


def host_rope_tables():
    n = 8192
    row = np.repeat(np.arange(n // 64, dtype=np.float32), 64)
    col = np.tile(np.arange(64, dtype=np.float32), n // 64)
    nf = 8
    inv = (np.float32(10000.0) ** (-np.arange(nf, dtype=np.float32) / np.float32(nf))).astype(np.float32)
    ang = np.concatenate([row[:, None] * inv, col[:, None] * inv], -1).astype(np.float32)
    c, s = np.cos(ang).astype(np.float32), np.sin(ang).astype(np.float32)
    cos = np.zeros((96, n), np.float32)
    sin = np.zeros((96, n), np.float32)
    cos[64:80] = c.T
    cos[80:96] = c.T
    sin[64:80] = -s.T
    sin[80:96] = s.T
    return cos, sin


def build_L1():
    P = Prog()
    xin = P.dram("xin", [NT_CORE * 128, D]); cnd = P.dram("cnd", [2, D]); mod_w = P.dram("mod_w", [D, 9 * D]); mod_b = P.dram("mod_b", [9 * D])
    g1 = P.dram("g1", [D]); gm = P.dram("gm", [D]); wg = P.dram("wg", [D, FH]); wu = P.dram("wu", [D, FH]); wd = P.dram("wd", [FH, D])
    win = P.dram("win", [D, EV_EXT]); wuq = P.dram("wuq", [QR, NH * 96]); wuqs = P.dram("wuqs", [QR, NH * 96])
    wk = P.dram("wk", [KVR, NH * 64]); wv = P.dram("wv", [KVR, 512]); gq = P.dram("gq", [QR]); gkv = P.dram("gkv", [KVR])
    cos = P.dram("cos", [96, 4096]); sin = P.dram("sin", [96, 4096])
    xout = P.dram("xout", [NT_CORE * 128, D], kind="ExternalOutput")
    QT = P.dram("QT", [NH, 96, NT_CORE * 128], kind="ExternalOutput")
    KT = P.dram("KT", [NH, 96, NT_CORE * 128], kind="ExternalOutput")
    V = P.dram("V", [NT_CORE * 128, 512], kind="ExternalOutput")
    ST = P.dram("ST", [512, NT_CORE * 128], kind="ExternalOutput")
    K = consts(P)
    mfm, gbc = adaln(P, K, cnd, mod_w, mod_b, [2])
    sce, sh = norm_params(P, K, mfm, g1, 0, "n1")
    scem, shm = norm_params(P, K, mfm, gm, 3, "nm")
    xs = [P.sb([128, D], F32, f"x{t}") for t in range(4)]
    xviews = [P.view(xout) for _ in BLOCKS]
    P.push()
    W = ffn_alloc(P)
    ffn_load(P, W, wg, wu, wd)
    for bi, (t0, nt, n) in enumerate(BLOCKS):
        for t in range(nt):
            P.dma(xs[t][:], xin[(t0 + t) * 128:(t0 + t + 1) * 128, :], writes=[xs[t]])
        ffn_block(P, K, W, xs, nt, n, sce, sh, gbc[2])
        for t in range(nt):
            P.dma(xout[(t0 + t) * 128:(t0 + t + 1) * 128, :], xs[t][:], reads=[xs[t]], writes=[xviews[bi]], is_output=True)
    P.pop()
    P.push()
    W = evin_alloc(P)
    load_w(P, W.win, win); load_w(P, W.wuq, wuq); load_w(P, W.wuqs, wuqs); load_w(P, W.wk, wk); load_w(P, W.wv, wv)
    load_fm(P, K, W.gq, W.gq[:], gq.t.rearrange("(c p) -> c p", p=128), 3)
    load_fm(P, K, W.gkv, W.gkv[:], gkv.t.rearrange("(c p) -> c p", p=128), 2)
    P.dma(W.cos[64:96, :], cos[64:96, :], writes=[W.cos]); P.dma(W.sin[64:96, :], sin[64:96, :], writes=[W.sin])
    for bi, (t0, nt, n) in enumerate(BLOCKS):
        for t in range(nt):
            P.dma(xs[t][:], xout[(t0 + t) * 128:(t0 + t + 1) * 128, :], reads=[xviews[bi]], writes=[xs[t]])
        evin_block(P, K, W, xs, t0, nt, n, scem, shm, QT, KT, V, ST)
    P.pop()
    return P.build()


def host_L1(d, li=0):
    cos, sin = host_rope_tables()
    w_in = d['ev_w_in'][0]
    kr = w_in[:, 640:672]
    krs = np.concatenate([kr[:, 16:32], kr[:, 0:16]], 1)
    z64 = np.zeros((D, 64), np.float32)
    win = np.concatenate([w_in, z64, kr, z64, krs], 1)
    wuq = d['mla_w_uq'][0]
    wq3 = wuq.reshape(QR, NH, 96)
    wuqs = np.zeros((QR, NH, 96), np.float32)
    wuqs[:, :, 64:80] = wq3[:, :, 80:96]
    wuqs[:, :, 80:96] = wq3[:, :, 64:80]
    wukv = d['mla_w_ukv'][0].reshape(KVR, NH, 128)
    wk = np.ascontiguousarray(wukv[:, :, 0:64]).reshape(KVR, NH * 64)
    wv = np.ascontiguousarray(wukv[:, :, 64:128]).reshape(KVR, 512)
    ins = []
    for j in range(8):
        b, hf = j // 2, j % 2
        x = np.concatenate([d['x'][b, hf * 4096:(hf + 1) * 4096], d['ctx'][b, hf * 128:(hf + 1) * 128]], 0)
        ins.append({"xin": x, "cnd": np.stack([d['c'][b], d['c_ctx']]), "mod_w": d['mod_w'][li], "mod_b": d['mod_b'][li],
                    "g1": d['norm_ffn1'][li], "gm": d['norm_mix'][li],
                    "wg": d['ffn1_w_gate'][li], "wu": d['ffn1_w_up'][li], "wd": d['ffn1_w_down'][li],
                    "win": win, "wuq": wuq, "wuqs": wuqs.reshape(QR, NH * 96), "wk": wk, "wv": wv,
                    "gq": d['mla_q_norm'][0], "gkv": d['mla_kv_norm'][0],
                    "cos": np.ascontiguousarray(cos[:, hf * 4096:(hf + 1) * 4096]), "sin": np.ascontiguousarray(sin[:, hf * 4096:(hf + 1) * 4096]),
                    "ident": np.eye(128, dtype=np.float32)})
    return ins


NKT = 66
NQ = 8192 + 256
SCALE = 96 ** -0.5


def build_L2():
    P = Prog()
    QTd = P.dram("QT", [4, 96, NQ]); KTd = P.dram("KT", [4, 96, NKT * 128]); Vd = P.dram("V", [NKT * 128, 4, 64])
    Od = P.dram("O", [NQ, 256], kind="ExternalOutput")
    QT = P.sb([96, 4, NQ], BF16, "QT"); KT = P.sb([96, 4, NKT * 128], BF16, "KT"); V = P.sb([128, NKT, 4, 65], BF16, "V")
    P.memset(V[:, :, :, 64:65], 1.0, writes=[V])
    Vv = Vd.t.rearrange("(k p) h d -> p k h d", p=128)
    for h in range(4):
        for s in range(0, NKT * 128, 2112):
            P.dma(KT[:, h, s:s + 2112], KTd[h, :, s:s + 2112], writes=[KT], q='pool')
            P.dma(QT[:, h, s:s + 2112], QTd[h, :, s:s + 2112], writes=[QT], q='pool')
    for kt in range(NKT):
        P.dma(V[:, kt, :, 0:64], Vv[:, kt, :, :], writes=[V], q='pool')
    po = [P.ps([128, 512], F32, f"po{i}") for i in range(4)]
    pss = [P.ps([128, 512], F32, f"pss{i}") for i in range(4)]
    pt = [P.sb([128, 512], BF16, f"pt{i}") for i in range(3)]
    ost = [P.sb([128, 256], F32, f"ost{i}") for i in range(8)]
    rc = P.sb([128, 8], F32, "rc")
    blocks = [(qb * 512, 4, 0, NKT) for qb in range(16)] + [(8192, 2, 0, 2)]
    it = 0
    for bi, (q0, nq, k0, k1) in enumerate(blocks):
        N = nq * 128
        osts = ost[(bi % 2) * 4:(bi % 2) * 4 + 4]
        for h in range(4):
            for kt in range(k0, k1):
                ps = pss[it % 4]
                ptt = pt[it % 3]
                it += 1
                P.mm(ps[:, 0:N], KT[:, h, kt * 128:(kt + 1) * 128], QT[:, h, q0:q0 + N], reads=[KT, QT], writes=[ps])
                P.act(ptt[:, 0:N], ps[:, 0:N], AF.Exp, scale=SCALE, reads=[ps], writes=[ptt])
                for i in range(nq):
                    P.mm(po[i][:, 0:65], ptt[:, i * 128:(i + 1) * 128], V[:, kt, h, :], start=(kt == k0), stop=(kt == k1 - 1),
                         reads=[ptt, V], writes=[po[i]])
            for i in range(nq):
                P.op('dve', lambda e, i=i: e.reciprocal(rc[:, i:i + 1], po[i][:, 64:65]), reads=[po[i]], writes=[rc])
                P.ts(osts[i][:, h * 64:(h + 1) * 64], po[i][:, 0:64], rc[:, i:i + 1], ALU.mult, reads=[po[i], rc], writes=[osts[i]])
        for i in range(nq):
            P.dma(Od[q0 + i * 128:q0 + (i + 1) * 128, :], osts[i][:], reads=[osts[i]], writes=[P.view(Od)], is_output=True)
    return P.build()


def host_L2(r1):
    ins = []
    for j in range(8):
        b, hg = j // 2, j % 2
        a, c = r1[2 * b], r1[2 * b + 1]
        hs = slice(4 * hg, 4 * hg + 4)
        QT = np.concatenate([a['QT'][hs, :, :4096], c['QT'][hs, :, :4096], a['QT'][hs, :, 4096:], c['QT'][hs, :, 4096:]], 2)
        KT = np.concatenate([a['KT'][hs, :, 4096:], c['KT'][hs, :, 4096:], a['KT'][hs, :, :4096], c['KT'][hs, :, :4096]], 2)
        Vf = np.concatenate([a['V'][4096:], c['V'][4096:], a['V'][:4096], c['V'][:4096]], 0).reshape(NKT * 128, 8, 64)
        ins.append({"QT": np.ascontiguousarray(QT), "KT": np.ascontiguousarray(KT), "V": np.ascontiguousarray(Vf[:, hs])})
    return ins


MAGIC = 12582912.0
TWO_PI = 6.283185307179586
PI = 3.141592653589793
LC = 512
NTOK5 = 256 + 8192


def sincos(P, out_sin, out_cos, x, tmp_k, tmp_z, bufs_r, bufs_w, shift_only=None):
    for out, sh in ((out_sin, 0.0), (out_cos, PI / 2)):
        if out is None:
            continue
        src = x
        if sh:
            P.ts(tmp_z, x, sh, ALU.add, reads=bufs_r + bufs_w, writes=bufs_w)
            src = tmp_z
        P.ts(tmp_k, src, 1.0 / TWO_PI, ALU.mult, MAGIC, ALU.add, reads=bufs_r + bufs_w, writes=bufs_w)
        P.ts(tmp_k, tmp_k, -MAGIC, ALU.add, reads=bufs_w, writes=bufs_w)
        P.stt(tmp_z, tmp_k, -TWO_PI, src, ALU.mult, ALU.add, reads=bufs_r + bufs_w, writes=bufs_w)
        P.ts(tmp_z, tmp_z, PI, ALU.min, -PI, ALU.max, reads=bufs_w, writes=bufs_w)
        P.act(out, tmp_z, AF.Sin, reads=bufs_w, writes=bufs_w)


def build_L3(dbg=False):
    P = Prog()
    uT = P.dram("uT", [256, NTOK5])
    prm = P.dram("prm", [128, 3, 2, 8])
    Bm = P.dram("Bm", [128, 2, 2, 8, 128])
    Cm = P.dram("Cm", [128, 2, 2, 8, 128])
    dsk = P.dram("dsk", [2, 128])
    tau = P.dram("tau", [128, LC])
    identd = P.dram("ident", [128, 128])
    Y = P.dram("Y", [256, NTOK5], kind="ExternalOutput")
    ident = P.sb([128, 128], F32, "ident"); P.dma(ident[:], identd[:], writes=[ident])
    pbs = [P.ps([128, 512], F32, f"pb{i}") for i in range(8)]
    rr = [0]

    def rot():
        rr[0] += 1
        return pbs[rr[0] % 8]
    ub = P.sb([128, 2, NTOK5], BF16, "ub")
    for ct in range(2):
        for s in range(0, NTOK5, 2112):
            P.dma(ub[:, ct, s:s + 2112], uT[ct * 128:(ct + 1) * 128, s:s + 2112], writes=[ub], q='pool')
    Bb = P.sb([128, 2, 2, 8, 128], BF16, "Bb"); Cb = P.sb([128, 2, 2, 8, 128], BF16, "Cb")
    for d in range(2):
        P.dma(Bb[:, d], Bm[:, d], writes=[Bb], q='pool')
        P.dma(Cb[:, d], Cm[:, d], writes=[Cb], q='pool')
    for d in range(2):
        P.ts(Cb[:, d, 1], Cb[:, d, 1], -1.0, ALU.mult, reads=[Cb], writes=[Cb])
    tau1 = P.sb([128, LC], F32, "tau"); P.dma(tau1[:], tau[:], writes=[tau1])
    dcol = P.sb([128, 2], F32, "dcol")
    dtmp = P.sb([2, 128], F32, "dtmp"); P.dma(dtmp[:], dsk[:], writes=[dtmp])
    pb = rot()
    P.tr(pb[:, 0:2], dtmp[:], ident[0:2, 0:2], reads=[dtmp, ident], writes=[pb])
    P.copy(dcol[:], pb[:, 0:2], reads=[pb], writes=[dcol])
    yacc = P.sb([128, 2, NTOK5], F32, "yacc")
    for ct in range(2):
        for s in range(0, NTOK5, 2112):
            P.ts(yacc[:, ct, s:s + 2112], ub[:, ct, s:s + 2112], dcol[:, ct:ct + 1], ALU.mult, reads=[ub, dcol], writes=[yacc], eng='pool')
    pr = P.sb([128, 3, 16], F32, "pr"); P.dma(pr[:], prm.t.rearrange("p a d s -> p a (d s)"), writes=[pr])
    sm = P.sb([128, 16, 16], F32, "sm")
    P.memset(sm[:], 0.0, writes=[sm])
    S = lambda i: sm[:, i, :]
    R, Wr = [pr, sm], [sm]
    lre, lim, lst = pr[:, 0, :], pr[:, 1, :], pr[:, 2, :]
    P.act(S(0), lst, AF.Exp, reads=R, writes=Wr)
    P.tt(S(1), lre, S(0), ALU.mult, reads=R, writes=Wr)
    P.tt(S(2), lim, S(0), ALU.mult, reads=R, writes=Wr)
    P.act(S(3), S(1), AF.Exp, reads=R, writes=Wr)
    sincos(P, S(4), S(5), S(2), S(6), S(7), R, Wr)
    P.tt(S(8), S(3), S(5), ALU.mult, reads=R, writes=Wr)
    P.ts(S(8), S(8), -1.0, ALU.add, reads=R, writes=Wr)
    P.tt(S(9), S(3), S(4), ALU.mult, reads=R, writes=Wr)
    P.tt(S(6), lre, lre, ALU.mult, reads=R, writes=Wr)
    P.tt(S(7), lim, lim, ALU.mult, reads=R, writes=Wr)
    P.tt(S(6), S(6), S(7), ALU.add, reads=R, writes=Wr)
    P.op('dve', lambda e: e.reciprocal(S(6), S(6)), reads=R, writes=Wr)
    P.tt(S(10), S(8), lre, ALU.mult, reads=R, writes=Wr)
    P.tt(S(7), S(9), lim, ALU.mult, reads=R, writes=Wr)
    P.tt(S(10), S(10), S(7), ALU.add, reads=R, writes=Wr)
    P.tt(S(10), S(10), S(6), ALU.mult, reads=R, writes=Wr)
    P.tt(S(11), S(9), lre, ALU.mult, reads=R, writes=Wr)
    P.tt(S(7), S(8), lim, ALU.mult, reads=R, writes=Wr)
    P.tt(S(11), S(11), S(7), ALU.subtract, reads=R, writes=Wr)
    P.tt(S(11), S(11), S(6), ALU.mult, reads=R, writes=Wr)
    P.ts(S(12), S(10), -1.0, ALU.mult, reads=R, writes=Wr)
    TH, RR, CFR, CFI, NCFR = 2, 3, 10, 11, 12
    if dbg:
        dsm = P.dram("dsm", [128, 16, 16], kind="ExternalOutput")
        P.dma(dsm[:], sm[:], reads=[sm], writes=[dsm], is_output=True)
        dtab = P.dram("dtab", [5, 128, LC], kind="ExternalOutput")
        dk = P.dram("dk", [6, 128, LC], kind="ExternalOutput")
    tabs = [[P.sb([128, LC], F32, f"tab{i}_{j}") for j in range(5)] for i in range(2)]
    tmpa = P.sb([128, LC], F32, "tmpa"); tmpb = P.sb([128, LC], F32, "tmpb")
    ones = P.sb([128, LC], F32, "ones"); P.memset(ones[:], 1.0, writes=[ones])
    wk = [[P.sb([128, LC], F32, f"wk{i}_{j}") for j in range(8)] for i in range(2)]
    hb = [[P.sb([128, LC], BF16, f"hb{i}_{j}") for j in range(2)] for i in range(2)]
    ini = P.sb([128, 4], F32, "ini")
    chunks_f = [(0, 256)] + [(256 + i * LC, LC) for i in range(16)]
    ci = 0
    for d in range(2):
        for st in range(8):
            col = d * 8 + st
            ct = st // 4
            Ere, Eim, Tre, Tim, rf = tabs[col % 2]
            tb = tabs[col % 2]
            P.ts(tmpa[:], tau1[:], sm[:, TH, col:col + 1], ALU.mult, reads=[tau1, sm], writes=[tmpa])
            sincos(P, Eim[:], Ere[:], tmpa[:], tmpb[:], Tre[:], [tmpa], [tmpb, Tre, Eim, Ere])
            P.ts(tmpb[:], Ere[:], sm[:, CFR, col:col + 1], ALU.mult, reads=[Ere, sm], writes=[tmpb])
            P.stt(Tre[:], Eim[:], sm[:, CFI, col:col + 1], tmpb[:], ALU.mult, ALU.add, reads=[Eim, sm, tmpb], writes=[Tre])
            P.ts(tmpb[:], Ere[:], sm[:, CFI, col:col + 1], ALU.mult, reads=[Ere, sm], writes=[tmpb])
            P.stt(Tim[:], Eim[:], sm[:, NCFR, col:col + 1], tmpb[:], ALU.mult, ALU.add, reads=[Eim, sm, tmpb], writes=[Tim])
            P.ts(rf[:], ones[:], sm[:, RR, col:col + 1], ALU.mult, reads=[ones, sm], writes=[rf])
            if dbg and col == 0:
                for i5 in range(5):
                    P.dma(dtab[i5], tb[i5][:], reads=[tb[i5]], writes=[P.view(dtab)], is_output=True)
            seq = [chunks_f[0]] + (chunks_f[1:] if d == 0 else chunks_f[:0:-1])
            for qi, (c0, n) in enumerate(seq):
                w = wk[ci % 2]
                hh = hb[ci % 2]
                ci += 1
                kinr, kini, kr, ki, t1, t2, t3, t4 = w
                if d == 0:
                    tsl = lambda a: a[:, 0:n]
                    dsl = lambda a: a[:, 0:n]
                    last = n - 1
                else:
                    tsl = lambda a: a[:, n - 1::-1] if n < LC else a[:, ::-1]
                    dsl = lambda a: a[:, n - 1::-1] if n < LC else a[:, ::-1]
                    last = 0
                pr_, pi_ = rot(), rot()
                P.mm(pr_[:, 0:n], Bb[:, d, 0, st, :], ub[:, ct, c0:c0 + n], reads=[Bb, ub], writes=[pr_])
                P.mm(pi_[:, 0:n], Bb[:, d, 1, st, :], ub[:, ct, c0:c0 + n], reads=[Bb, ub], writes=[pi_])
                P.tt(t1[:, 0:n], pr_[:, 0:n], tsl(Tre), ALU.mult, reads=[pr_, Tre], writes=[t1])
                P.tt(t2[:, 0:n], pi_[:, 0:n], tsl(Tim), ALU.mult, reads=[pi_, Tim], writes=[t2])
                P.tt(kinr[:, 0:n], t1[:, 0:n], t2[:, 0:n], ALU.subtract, reads=[t1, t2], writes=[kinr], eng='pool')
                P.tt(t3[:, 0:n], pi_[:, 0:n], tsl(Tre), ALU.mult, reads=[pi_, Tre], writes=[t3])
                P.tt(t4[:, 0:n], pr_[:, 0:n], tsl(Tim), ALU.mult, reads=[pr_, Tim], writes=[t4])
                P.tt(kini[:, 0:n], t3[:, 0:n], t4[:, 0:n], ALU.add, reads=[t3, t4], writes=[kini], eng='pool')
                i_re = 0.0 if qi == 0 else ini[:, 0:1]
                i_im = 0.0 if qi == 0 else ini[:, 1:2]
                P.scan(dsl(kr), rf[:, 0:n], dsl(kinr), i_re, reads=[rf, kinr, ini], writes=[kr])
                P.scan(dsl(ki), rf[:, 0:n], dsl(kini), i_im, reads=[rf, kini, ini], writes=[ki])
                if qi < len(seq) - 1:
                    krl, kil = kr[:, last:last + 1], ki[:, last:last + 1]
                    erl, eil = Ere[:, n - 1:n], Eim[:, n - 1:n]
                    P.tt(ini[:, 2:3], kil, eil, ALU.mult, reads=[ki, Eim], writes=[ini])
                    P.stt(ini[:, 0:1], krl, erl, ini[:, 2:3], ALU.mult, ALU.subtract, reads=[kr, Ere, ini], writes=[ini])
                    P.tt(ini[:, 3:4], kil, erl, ALU.mult, reads=[ki, Ere], writes=[ini])
                    P.stt(ini[:, 1:2], krl, eil, ini[:, 3:4], ALU.mult, ALU.add, reads=[kr, Eim, ini], writes=[ini])
                P.tt(t1[:, 0:n], kr[:, 0:n], tsl(Ere), ALU.mult, reads=[kr, Ere], writes=[t1], eng='pool')
                P.tt(t2[:, 0:n], ki[:, 0:n], tsl(Eim), ALU.mult, reads=[ki, Eim], writes=[t2], eng='pool')
                P.tt(hh[0][:, 0:n], t1[:, 0:n], t2[:, 0:n], ALU.subtract, reads=[t1, t2], writes=[hh[0]], eng='pool')
                P.tt(t3[:, 0:n], kr[:, 0:n], tsl(Eim), ALU.mult, reads=[kr, Eim], writes=[t3], eng='pool')
                P.tt(t4[:, 0:n], ki[:, 0:n], tsl(Ere), ALU.mult, reads=[ki, Ere], writes=[t4])
                P.tt(hh[1][:, 0:n], t3[:, 0:n], t4[:, 0:n], ALU.add, reads=[t3, t4], writes=[hh[1]])
                if dbg and col == 0 and qi == 0:
                    for i6, bb in enumerate([kinr, kini, kr, ki]):
                        P.dma(dk[i6][:, 0:256], bb[:, 0:256], reads=[bb], writes=[P.view(dk)], is_output=True)
                py = rot()
                P.mm(py[:, 0:n], Cb[:, d, 0, st, :], hh[0][:, 0:n], start=True, stop=False, reads=[Cb, hh[0]], writes=[py])
                P.mm(py[:, 0:n], Cb[:, d, 1, st, :], hh[1][:, 0:n], start=False, stop=True, reads=[Cb, hh[1]], writes=[py])
                P.tt(yacc[:, ct, c0:c0 + n], yacc[:, ct, c0:c0 + n], py[:, 0:n], ALU.add, reads=[py], writes=[yacc])
    for ct in range(2):
        for s in range(0, NTOK5, 2112):
            P.dma(Y[ct * 128:(ct + 1) * 128, s:s + 2112], yacc[:, ct, s:s + 2112], reads=[yacc], writes=[P.view(Y)], is_output=True)
    return P.build()


def host_L3(d, r1):
    ins = []
    lre = d['s5_lambda_re'][0]; lim = d['s5_lambda_im'][0]; lst = d['s5_log_step'][0]
    bre = d['s5_b_re'][0]; bim = d['s5_b_im'][0]; cre = d['s5_c_re'][0]; cim = d['s5_c_im'][0]
    tau = np.broadcast_to(np.arange(1, LC + 1, dtype=np.float32)[None, :], (128, LC)).copy()
    for j in range(8):
        b, gh = j // 2, j % 2
        a, c = r1[2 * b], r1[2 * b + 1]
        rows = slice(256 * gh, 256 * gh + 256)
        uT = np.concatenate([a['ST'][rows, 4096:], c['ST'][rows, 4096:], a['ST'][rows, :4096], c['ST'][rows, :4096]], 1)
        prm = np.zeros((128, 3, 2, 8), np.float32)
        Bm = np.zeros((128, 2, 2, 8, 128), np.float32)
        Cm = np.zeros((128, 2, 2, 8, 128), np.float32)
        for dr in range(2):
            for st in range(8):
                for gm in range(2):
                    g = 16 * gh + 2 * st + gm
                    ps = slice(gm * 64, gm * 64 + 64)
                    prm[ps, 0, dr, st] = lre[dr, g]; prm[ps, 1, dr, st] = lim[dr, g]; prm[ps, 2, dr, st] = lst[dr, g]
                    gl = (2 * st + gm) % 8
                    ks = slice(gl * 16, gl * 16 + 16)
                    Bm[ks, dr, 0, st, ps] = bre[dr, g].T
                    Bm[ks, dr, 1, st, ps] = bim[dr, g].T
                    Cm[ps, dr, 0, st, ks] = cre[dr, g].T
                    Cm[ps, dr, 1, st, ks] = cim[dr, g].T
        ins.append({"uT": np.ascontiguousarray(uT), "prm": prm, "Bm": Bm, "Cm": Cm,
                    "dsk": np.ascontiguousarray(d['s5_d'][0][rows].reshape(2, 128)), "tau": tau, "ident": np.eye(128, dtype=np.float32)})
    return ins


def xio(P, xs, nt, t0, src, sview=None, dst=None, dview=None, load=True, out=False):
    for t in range(nt):
        rows = slice((t0 + t) * 128, (t0 + t + 1) * 128)
        if load:
            P.dma(xs[t][:], src[rows, :], reads=[sview] if sview is not None else [], writes=[xs[t]])
        else:
            P.dma(dst[rows, :], xs[t][:], reads=[xs[t]], writes=[dview], is_output=out)


def pass_ffn(P, K, xs, src, sviews, dst, dviews, wg, wu, wd, sce, sh, gbc, out=False, post=None, blocks=None):
    P.push()
    W = ffn_alloc(P)
    ffn_load(P, W, wg, wu, wd)
    for bi, (t0, nt, n) in enumerate(blocks or BLOCKS):
        xio(P, xs, nt, t0, src, sviews[bi] if sviews else None)
        ffn_block(P, K, W, xs, nt, n, sce, sh, gbc)
        if post is not None:
            post(W, xs, nt)
        xio(P, xs, nt, t0, None, None, dst, dviews[bi], load=False, out=out)
    P.pop()


def pass_mixout(P, K, xs, src, sviews, dst, dviews, wout, gbc, catsrc, glu=None, blocks=None):
    P.push()
    wo = P.sb([128, 8, D], BF16, "wout")
    load_w(P, wo, wout)
    cat = P.sb([128, 8, 512], BF16, "cat")
    tmp = P.sb([128, 512], F32, "tmpm")
    if glu is not None:
        wgl = P.sb([128, 4, 512], BF16, "wglu")
        load_w(P, wgl, glu[0])
        bgl = P.sb([128, 4], F32, "bglu")
        load_fm(P, K, bgl, bgl[:], glu[1].t.rearrange("(c p) -> c p", p=128), 4)
        yp = P.sb([128, 4, 512], F32, "yp")
        yg = P.sb([128, 4, 512], BF16, "yg")
        sg = P.sb([128, 512], BF16, "sg")
    for bi, (t0, nt, n) in enumerate(blocks or BLOCKS):
        N = nt * 128
        tok = slice(t0 * 128, t0 * 128 + N)
        xio(P, xs, nt, t0, src, sviews[bi] if sviews else None)
        for kc in range(8):
            if glu is not None and kc >= 4:
                P.dma(yp[:, kc - 4, 0:N], catsrc[kc][:, tok], writes=[yp])
            else:
                P.dma(cat[:, kc, 0:N], catsrc[kc][:, tok], writes=[cat], q='pool')
        if glu is not None:
            for kc in range(4):
                P.act(yg[:, kc, 0:N], yp[:, kc, 0:N], AF.Gelu, reads=[yp], writes=[yg])
            for oc in range(4):
                pb = rot(K)
                for kc in range(4):
                    P.mm(pb[:, 0:N], wgl[:, kc, oc * 128:(oc + 1) * 128], yg[:, kc, 0:N], start=(kc == 0), stop=(kc == 3),
                         reads=[wgl, yg], writes=[pb])
                P.act(sg[:, 0:N], pb[:, 0:N], AF.Sigmoid, bias=bgl[:, oc:oc + 1], reads=[pb, bgl], writes=[sg])
                P.tt(cat[:, 4 + oc, 0:N], sg[:, 0:N], yg[:, oc, 0:N], ALU.mult, reads=[sg, yg], writes=[cat])
        for t in range(nt):
            for hf in range(2):
                py = rot(K)
                for kc in range(8):
                    P.mm(py[:], cat[:, kc, t * 128:(t + 1) * 128], wo[:, kc, hf * 512:(hf + 1) * 512], start=(kc == 0), stop=(kc == 7),
                         reads=[cat, wo], writes=[py])
                P.tt(tmp[:], py[:], gbc[:, n, hf * 512:(hf + 1) * 512], ALU.mult, reads=[py, gbc], writes=[tmp])
                xo = xs[t][:, hf * 512:(hf + 1) * 512]
                P.tt(xo, xo, tmp[:], ALU.add, reads=[xs[t], tmp], writes=[xs[t]], eng='pool')
        xio(P, xs, nt, t0, None, None, dst, dviews[bi], load=False)
    P.pop()


OD_IN = 2560


def pass_inproj_odd(P, K, xs, src, sviews, win, sce, sh, UT):
    P.push()
    W = FFNW()
    W.win = P.sb([128, 8, OD_IN], BF16, "winod")
    load_w(P, W.win, win)
    W.xn = [P.sb([128, D], BF16, f"xn{t}") for t in range(4)]
    W.ss = P.sb([128, 4], F32, "ss"); W.rstd = P.sb([128, 4], F32, "rstd")
    W.hT = P.sb([128, 8, 512], BF16, "hT")
    W.st = [P.sb([128, 512], F32, f"st{i}") for i in range(3)]
    W.sti = 0
    for bi, (t0, nt, n) in enumerate(BLOCKS):
        N = nt * 128
        xio(P, xs, nt, t0, src, sviews[bi] if sviews else None)
        norm_T(P, K, W, xs, nt, n, sce, sh, W.hT)
        for oc in range(OD_IN // 128):
            pc = rot(K)
            for c in range(8):
                P.mm(pc[:, 0:N], W.win[:, c, oc * 128:(oc + 1) * 128], W.hT[:, c, 0:N], start=(c == 0), stop=(c == 7),
                     reads=[W.win, W.hT], writes=[pc])
            st = stage(W)
            P.copy(st[:, 0:N], pc[:, 0:N], reads=[pc], writes=[st], eng=('act' if oc % 2 else 'dve'))
            P.dma(UT[oc * 128:(oc + 1) * 128, t0 * 128:t0 * 128 + N], st[:, 0:N], reads=[st], writes=[P.view(UT)], is_output=True)
    P.pop()


def build_L4():
    P = Prog()
    NTOK = NT_CORE * 128
    xin = P.dram("xin", [NTOK, D]); cnd = P.dram("cnd", [2, D])
    mw0 = P.dram("mw0", [D, 9 * D]); mb0 = P.dram("mb0", [9 * D]); mw1 = P.dram("mw1", [D, 9 * D]); mb1 = P.dram("mb1", [9 * D])
    OT = P.dram("OT", [512, NTOK]); YT = P.dram("YT", [512, NTOK])
    wglu = P.dram("wglu", [512, 512]); bglu = P.dram("bglu", [512]); wout = P.dram("wout", [D, D])
    g2 = P.dram("g2", [D]); wg2 = P.dram("wg2", [D, FH]); wu2 = P.dram("wu2", [D, FH]); wd2 = P.dram("wd2", [FH, D])
    g1 = P.dram("g1", [D]); wg1 = P.dram("wg1", [D, FH]); wu1 = P.dram("wu1", [D, FH]); wd1 = P.dram("wd1", [FH, D])
    gm = P.dram("gm", [D]); win = P.dram("win", [D, OD_IN])
    sa = P.dram("scr_a", [NTOK, D], kind="Internal"); sbb = P.dram("scr_b", [NTOK, D], kind="Internal")
    xout = P.dram("xout", [NTOK, D], kind="ExternalOutput")
    UT = P.dram("UT", [OD_IN, NTOK], kind="ExternalOutput")
    K = consts(P)
    xs = [P.sb([128, D], F32, f"x{t}") for t in range(4)]
    va = [P.view(sa) for _ in BLOCKS]; vb = [P.view(sbb) for _ in BLOCKS]; vo = [P.view(xout) for _ in BLOCKS]
    P.push()
    mfm, gbc = adaln(P, K, cnd, mw0, mb0, [5, 8])
    sce2, sh2 = norm_params(P, K, mfm, g2, 6, "n2")
    cats = [OT[kc * 128:(kc + 1) * 128, :] for kc in range(4)] + [YT[kc * 128:(kc + 1) * 128, :] for kc in range(4)]
    pass_mixout(P, K, xs, xin, None, sa, va, wout, gbc[5], cats, glu=(wglu, bglu))
    pass_ffn(P, K, xs, sa, va, sbb, vb, wg2, wu2, wd2, sce2, sh2, gbc[8])
    P.pop()
    mfm1, gbc1 = adaln(P, K, cnd, mw1, mb1, [2])
    sce1, sh1 = norm_params(P, K, mfm1, g1, 0, "n1b")
    scem, shm = norm_params(P, K, mfm1, gm, 3, "nmb")
    pass_ffn(P, K, xs, sbb, vb, xout, vo, wg1, wu1, wd1, sce1, sh1, gbc1[2], out=True)
    pass_inproj_odd(P, K, xs, xout, vo, win, scem, shm, UT)
    return P.build()


def tok_cols(full_b, hf, nlat=8192, nctx=256):
    return np.concatenate([full_b[:, nctx + hf * 4096:nctx + (hf + 1) * 4096], full_b[:, hf * 128:(hf + 1) * 128]], 1)


def host_L4(d, r1, r2, r3):
    ins = []
    for j in range(8):
        b, hf = j // 2, j % 2
        O = np.concatenate([r2[2 * b]['O'], r2[2 * b + 1]['O']], 1)
        Oc = np.concatenate([O[hf * 4096:(hf + 1) * 4096], O[8192 + hf * 128:8192 + (hf + 1) * 128]], 0)
        Yf = np.concatenate([r3[2 * b]['Y'], r3[2 * b + 1]['Y']], 0)
        ins.append({"xin": r1[j]['xout'], "cnd": np.stack([d['c'][b], d['c_ctx']]),
                    "mw0": d['mod_w'][0], "mb0": d['mod_b'][0], "mw1": d['mod_w'][1], "mb1": d['mod_b'][1],
                    "OT": np.ascontiguousarray(Oc.T), "YT": np.ascontiguousarray(tok_cols(Yf, hf)),
                    "wglu": d['s5_w_glu'][0], "bglu": d['s5_b_glu'][0], "wout": d['ev_w_out'][0],
                    "g2": d['norm_ffn2'][0], "wg2": d['ffn2_w_gate'][0], "wu2": d['ffn2_w_up'][0], "wd2": d['ffn2_w_down'][0],
                    "g1": d['norm_ffn1'][1], "wg1": d['ffn1_w_gate'][1], "wu1": d['ffn1_w_up'][1], "wd1": d['ffn1_w_down'][1],
                    "gm": d['norm_mix'][1], "win": d['od_w_in'][0], "ident": np.eye(128, dtype=np.float32)})
    return ins


def build_L6():
    P = Prog()
    NL, NC = 8192, 256
    xT = P.dram("xT", [256, NC + NL]); gT = P.dram("gT", [256, NL])
    cw = P.dram("cw", [128, 2, 4]); vec = P.dram("vec", [128, 7, 2, 2])
    Wm = P.dram("Wm", [128, 2, 2, 2, 128])
    RT = P.dram("RT", [256, NL], kind="ExternalOutput")
    pbs = [P.ps([128, 512], F32, f"pb{i}") for i in range(8)]
    rr = [0]

    def rot():
        rr[0] += 1
        return pbs[rr[0] % 8]
    Wb = P.sb([128, 2, 2, 2, 128], BF16, "Wb")
    P.dma(Wb[:], Wm[:], writes=[Wb], q='pool')
    cws = P.sb([128, 2, 4], F32, "cws"); P.dma(cws[:], cw[:], writes=[cws])
    vs = P.sb([128, 7, 2, 2], F32, "vs"); P.dma(vs[:], vec[:], writes=[vs])
    c8 = P.sb([128, 2, 2], F32, "c8")
    P.act(c8[:], vs[:, 3], AF.Exp, scale=-1.0, reads=[vs], writes=[c8])
    P.act(c8[:], c8[:], AF.Ln, bias=1.0, reads=[c8], writes=[c8])
    P.ts(c8[:], c8[:], -8.0, ALU.mult, reads=[c8], writes=[c8])
    OFFC, OFFL = 2, 2 + NC + 1 + 2
    TOT = OFFL + NL + 1
    xc = P.sb([128, 2, NC + NL], F32, "xc"); xcb = P.sb([128, 2, NC + NL], BF16, "xcb")
    P.push()
    xp = P.sb([128, 2, TOT], F32, "xp")
    P.memset(xp[:, :, 0:2], 0.0, writes=[xp]); P.memset(xp[:, :, OFFC + NC:OFFL], 0.0, writes=[xp]); P.memset(xp[:, :, OFFL + NL:TOT], 0.0, writes=[xp])
    for ct in range(2):
        P.dma(xp[:, ct, OFFC:OFFC + NC], xT[ct * 128:(ct + 1) * 128, 0:NC], writes=[xp])
        for s in range(0, NL, 2048):
            P.dma(xp[:, ct, OFFL + s:OFFL + s + 2048], xT[ct * 128:(ct + 1) * 128, NC + s:NC + s + 2048], writes=[xp])
    for ct in range(2):
        for (o_in, o_out, n) in [(OFFC, 0, NC)] + [(OFFL + s, NC + s, 2048) for s in range(0, NL, 2048)]:
            dst = xc[:, ct, o_out:o_out + n]
            eng = 'dve' if ct == 0 else 'pool'
            P.ts(dst, xp[:, ct, o_in + 1:o_in + 1 + n], cws[:, ct, 3:4], ALU.mult, vs[:, 0, 0, ct:ct + 1], ALU.add, reads=[xp, cws, vs], writes=[xc], eng=eng)
            for k, sh in ((2, 0), (1, -1), (0, -2)):
                P.stt(dst, xp[:, ct, o_in + sh:o_in + sh + n], cws[:, ct, k:k + 1], dst, ALU.mult, ALU.add, reads=[xp, cws, xc], writes=[xc], eng=eng)
            P.copy(xcb[:, ct, o_out:o_out + n], dst, reads=[xc], writes=[xcb], eng='act')
    P.pop()
    yacc = P.sb([128, 2, NL], F32, "yacc")
    wk = [[P.sb([128, 512], F32, f"lw{i}_{j}") for j in range(5)] for i in range(2)]
    chunks = [(0, NC)] + [(NC + i * 512, 512) for i in range(16)]
    ci = 0
    for d in range(2):
        for ct in range(2):
            seq = [chunks[0]] + (chunks[1:] if d == 0 else chunks[:0:-1])
            prev_h = None
            for qi, (c0, n) in enumerate(seq):
                a_, ig, bc, bin_, h = wk[ci % 2]
                ci += 1
                rv = (lambda ap: ap[:, 0:n]) if d == 0 else (lambda ap: ap[:, n - 1::-1] if n < 512 else ap[:, ::-1])
                pa, px = rot(), rot()
                P.mm(pa[:, 0:n], Wb[:, 0, d, ct, :], xcb[:, ct, c0:c0 + n], reads=[Wb, xcb], writes=[pa])
                P.mm(px[:, 0:n], Wb[:, 1, d, ct, :], xcb[:, ct, c0:c0 + n], reads=[Wb, xcb], writes=[px])
                P.act(a_[:, 0:n], pa[:, 0:n], AF.Sigmoid, bias=vs[:, 1, d, ct:ct + 1], reads=[pa, vs], writes=[a_])
                P.act(ig[:, 0:n], px[:, 0:n], AF.Sigmoid, bias=vs[:, 2, d, ct:ct + 1], reads=[px, vs], writes=[ig])
                P.act(a_[:, 0:n], a_[:, 0:n], AF.Exp, scale=c8[:, d, ct:ct + 1], reads=[a_, c8], writes=[a_])
                P.tt(bc[:, 0:n], a_[:, 0:n], a_[:, 0:n], ALU.mult, reads=[a_], writes=[bc], eng='pool')
                P.act(bc[:, 0:n], bc[:, 0:n], AF.Sqrt, scale=-1.0, bias=1.0, reads=[bc], writes=[bc])
                P.tt(bin_[:, 0:n], ig[:, 0:n], xc[:, ct, c0:c0 + n], ALU.mult, reads=[ig, xc], writes=[bin_], eng='pool')
                P.tt(bin_[:, 0:n], bin_[:, 0:n], bc[:, 0:n], ALU.mult, reads=[bin_, bc], writes=[bin_])
                init = 0.0 if qi == 0 else prev_h
                rds = [a_, bin_] + ([prev_hb] if qi else [])
                P.scan(rv(h), rv(a_), rv(bin_), init, reads=rds, writes=[h])
                last = n - 1 if d == 0 else 0
                prev_h, prev_hb = h[:, last:last + 1], h
                if qi > 0:
                    o = c0 - NC
                    if d == 0:
                        P.copy(yacc[:, ct, o:o + n], h[:, 0:n], reads=[h], writes=[yacc], eng='pool')
                    else:
                        P.tt(yacc[:, ct, o:o + n], yacc[:, ct, o:o + n], h[:, 0:n], ALU.add, reads=[h], writes=[yacc], eng='pool')
    gt = [P.sb([128, 2048], F32, f"gt{i}") for i in range(2)]
    i = 0
    for ct in range(2):
        for s in range(0, NL, 2048):
            g = gt[i % 2]; i += 1
            P.dma(g[:], gT[ct * 128:(ct + 1) * 128, s:s + 2048], writes=[g])
            P.act(g[:], g[:], AF.Gelu, reads=[g], writes=[g])
            P.tt(g[:], g[:], yacc[:, ct, s:s + 2048], ALU.mult, reads=[g, yacc], writes=[g])
            P.dma(RT[ct * 128:(ct + 1) * 128, s:s + 2048], g[:], reads=[g], writes=[P.view(RT)], is_output=True)
    return P.build()


def host_L6(d, r4):
    ins = []
    cwf = d['lru_conv_w'][0]; cbf = d['lru_conv_b'][0]
    for j in range(8):
        b, chh = j // 2, j % 2
        a, c = r4[2 * b], r4[2 * b + 1]
        rows = slice(1536 + 256 * chh, 1536 + 256 * chh + 256)
        grows = slice(2048 + 256 * chh, 2048 + 256 * chh + 256)
        xT = np.concatenate([a['UT'][rows, 4096:], c['UT'][rows, 4096:], a['UT'][rows, :4096], c['UT'][rows, :4096]], 1)
        gT = np.concatenate([a['UT'][grows, :4096], c['UT'][grows, :4096]], 1)
        chs = slice(256 * chh, 256 * chh + 256)
        cw = np.ascontiguousarray(cwf[:, chs].reshape(4, 2, 128).transpose(2, 1, 0))
        vec = np.zeros((128, 7, 2, 2), np.float32)
        vec[:, 0, 0, :] = cbf[chs].reshape(2, 128).T
        for dr in range(2):
            vec[:, 1, dr, :] = d['lru_b_a'][0][dr, chs].reshape(2, 128).T
            vec[:, 2, dr, :] = d['lru_b_x'][0][dr, chs].reshape(2, 128).T
            vec[:, 3, dr, :] = d['lru_lambda'][0][dr, chs].reshape(2, 128).T
        Wm = np.zeros((128, 2, 2, 2, 128), np.float32)
        for ai, wsrc in enumerate([d['lru_w_a'][0], d['lru_w_x'][0]]):
            for dr in range(2):
                for ct in range(2):
                    for hb in range(2):
                        blk = 4 * chh + 2 * ct + hb
                        Wm[hb * 64:(hb + 1) * 64, ai, dr, ct, hb * 64:(hb + 1) * 64] = wsrc[dr, blk]
        ins.append({"xT": np.ascontiguousarray(xT), "gT": np.ascontiguousarray(gT), "cw": cw, "vec": vec, "Wm": Wm})
    return ins


NFFT = 16384
HY_G = 4


def hyena_consts():
    n = 8192
    t = np.linspace(0.0, 1.0, n, dtype=np.float32)[:, None]
    w = (2.0 * np.pi * np.arange(n, dtype=np.float32)[:, None] / n).astype(np.float32)
    bands = np.linspace(1e-4, 15, 16, dtype=np.float32)[None, :]
    z = np.concatenate([t, np.cos(bands * w), -np.sin(bands * w)], -1).astype(np.float32)
    idx = np.concatenate([np.arange(n), [0], np.arange(n - 1, 0, -1)])
    zc = np.ascontiguousarray(z[idx].T)
    tcirc = t[idx, 0].copy()
    tcirc[n] = 1.0e4
    tc = np.broadcast_to(tcirc[None, :], (128, NFFT)).copy()
    hmin, hmax = np.log(1e-2) / 1.5, np.log(1e-2) / 0.3
    deltas = np.abs(np.linspace(hmin, hmax, 512, dtype=np.float32))
    k = np.arange(128)
    ang = 2.0 * np.pi * np.outer(k, k) / 128.0
    Wr, Wi = np.cos(ang).astype(np.float32), (-np.sin(ang)).astype(np.float32)
    angt = 2.0 * np.pi * np.outer(k, k) / NFFT
    twr, twi = np.cos(angt).astype(np.float32), (-np.sin(angt)).astype(np.float32)
    rep = lambda m: np.ascontiguousarray(np.tile(m, (1, HY_G)))
    C = {"zc": zc, "tc": tc, "W1": np.concatenate([Wr, Wi], 1), "CW1": np.concatenate([Wr, -Wi], 1), "CW2": np.concatenate([Wi, Wr], 1),
         "W3": np.stack([Wr, Wi, -Wi], 1), "tw": np.stack([rep(twr), rep(twi), rep(-twi)], 1)}
    return C, deltas


def build_L5():
    P = Prog()
    NL = 8192
    hT = P.dram("hT", [3, 256, NL])
    cw = P.dram("cw", [128, 3, 2, 3]); cb = P.dram("cb", [128, 3, 2])
    zc = P.dram("zc", [33, NFFT]); tc = P.dram("tc", [128, NFFT])
    w1 = P.dram("w1", [33, 64]); w2 = P.dram("w2", [64, 64]); bf = P.dram("bf", [64, 4]); w3 = P.dram("w3", [64, 2, 256])
    chv = P.dram("chv", [128, 2, 2])
    W1d = P.dram("W1", [128, 256]); CW1d = P.dram("CW1", [128, 256]); CW2d = P.dram("CW2", [128, 256])
    W3d = P.dram("W3", [128, 3, 128]); twd = P.dram("tw", [128, 3, 512])
    Fs = P.dram("Fs", [256, NFFT], kind="Internal"); Zs = P.dram("Zs", [256, NL], kind="Internal")
    X0s = P.dram("X0s", [256, NL], kind="Internal"); Ys = P.dram("Ys", [256, NL], kind="Internal")
    HYT = P.dram("HYT", [256, NL], kind="ExternalOutput")
    pbs = [P.ps([128, 1024], F32, f"pq{i}") for i in range(4)]
    rr = [0]

    def rot():
        rr[0] += 1
        return pbs[rr[0] % 4]
    cws = P.sb([128, 3, 2, 3], F32, "cws"); P.dma(cws[:], cw[:], writes=[cws])
    cbs = P.sb([128, 3, 2], F32, "cbs"); P.dma(cbs[:], cb[:], writes=[cbs])
    chs = P.sb([128, 2, 2], F32, "chs"); P.dma(chs[:], chv[:], writes=[chs])
    zviews, xviews = [], []
    P.push()
    xp = [P.sb([128, NL + 2], F32, f"xp{i}") for i in range(2)]
    cv = [P.sb([128, NL], F32, f"cv{i}") for i in range(2)]
    for b_ in xp:
        P.memset(b_[:, 0:1], 0.0, writes=[b_]); P.memset(b_[:, NL + 1:NL + 2], 0.0, writes=[b_])
    k = 0
    for ct in range(2):
        for part in (1, 2, 0):
            x_ = xp[k % 2]; k += 1
            for s in range(0, NL, 2048):
                P.dma(x_[:, 1 + s:1 + s + 2048], hT[part, ct * 128:(ct + 1) * 128, s:s + 2048], writes=[x_])
            dst = cv[0] if part != 2 else cv[1]
            eng = 'dve' if part != 2 else 'pool'
            for s in range(0, NL, 2048):
                o = dst[:, s:s + 2048]
                P.ts(o, x_[:, s:s + 2048], cws[:, part, ct, 0:1], ALU.mult, cbs[:, part, ct:ct + 1], ALU.add, reads=[x_, cws, cbs], writes=[dst], eng=eng)
                P.stt(o, x_[:, s + 1:s + 1 + 2048], cws[:, part, ct, 1:2], o, ALU.mult, ALU.add, reads=[x_, cws, dst], writes=[dst], eng=eng)
                P.stt(o, x_[:, s + 2:s + 2 + 2048], cws[:, part, ct, 2:3], o, ALU.mult, ALU.add, reads=[x_, cws, dst], writes=[dst], eng=eng)
            if part == 2:
                P.tt(cv[1][:], cv[1][:], cv[0][:], ALU.mult, reads=[cv[0], cv[1]], writes=[cv[1]])
                v_ = P.view(Zs); zviews.append(v_)
                P.dma(Zs[ct * 128:(ct + 1) * 128, :], cv[1][:], reads=[cv[1]], writes=[v_])
            if part == 0:
                v_ = P.view(X0s); xviews.append(v_)
                P.dma(X0s[ct * 128:(ct + 1) * 128, :], cv[0][:], reads=[cv[0]], writes=[v_])
    P.pop()
    fviews = []
    P.push()
    w1s = P.sb([33, 64], F32, "w1s"); P.dma(w1s[:], w1[:], writes=[w1s])
    w2s = P.sb([64, 64], F32, "w2s"); P.dma(w2s[:], w2[:], writes=[w2s])
    w3s = P.sb([64, 2, 256], F32, "w3s"); P.dma(w3s[:], w3[:], writes=[w3s])
    bfs = P.sb([64, 4], F32, "bfs"); P.dma(bfs[:], bf[:], writes=[bfs])
    zq = [P.sb([33, 512], F32, f"zq{i}") for i in range(2)]
    tq = [P.sb([128, 512], F32, f"tq{i}") for i in range(2)]
    ar = P.sb([64, 512], F32, "ar"); tk = P.sb([64, 512], F32, "tk"); tz = P.sb([64, 512], F32, "tz")
    h1 = P.sb([64, 512], F32, "h1"); h2 = P.sb([64, 512], F32, "h2")
    dec = [P.sb([128, 512], F32, f"dec{i}") for i in range(2)]
    fst = [P.sb([128, 512], F32, f"fst{i}") for i in range(2)]
    for q in range(NFFT // 512):
        z_ = zq[q % 2]; t_ = tq[q % 2]
        cols = slice(q * 512, (q + 1) * 512)
        P.dma(z_[:], zc[:, cols], writes=[z_]); P.dma(t_[:], tc[:, cols], writes=[t_])
        p1 = rot()
        P.mm(p1[0:64, 0:512], w1s[:], z_[:], reads=[w1s, z_], writes=[p1])
        P.ts(ar[:], p1[0:64, 0:512], bfs[:, 0:1], ALU.add, bfs[:, 1:2], ALU.mult, reads=[p1, bfs], writes=[ar])
        sincos(P, h1[:], None, ar[:], tk[:], tz[:], [ar], [tk, tz, h1])
        p2 = rot()
        P.mm(p2[0:64, 0:512], w2s[:], h1[:], reads=[w2s, h1], writes=[p2])
        P.ts(ar[:], p2[0:64, 0:512], bfs[:, 2:3], ALU.add, bfs[:, 3:4], ALU.mult, reads=[p2, bfs], writes=[ar])
        sincos(P, h2[:], None, ar[:], tk[:], tz[:], [ar], [tk, tz, h2])
        dr = 0 if q < 16 else 1
        for ct in range(2):
            p3 = rot()
            P.mm(p3[:, 0:512], w3s[:, dr, ct * 128:(ct + 1) * 128], h2[:], reads=[w3s, h2], writes=[p3])
            d_ = dec[ct]; f_ = fst[ct]
            P.act(d_[:], t_[:], AF.Exp, scale=chs[:, 0, ct:ct + 1], reads=[t_, chs], writes=[d_])
            P.tt(f_[:], p3[:, 0:512], d_[:], ALU.mult, reads=[p3, d_], writes=[f_])
            v_ = P.view(Fs); fviews.append(v_)
            P.dma(Fs[ct * 128:(ct + 1) * 128, cols], f_[:], reads=[f_], writes=[v_])
    P.pop()
    yviews = []
    P.push()
    W1 = P.sb([128, 256], F32, "W1"); P.dma(W1[:], W1d[:], writes=[W1])
    CW1 = P.sb([128, 256], F32, "CW1"); P.dma(CW1[:], CW1d[:], writes=[CW1])
    CW2 = P.sb([128, 256], F32, "CW2"); P.dma(CW2[:], CW2d[:], writes=[CW2])
    W3 = P.sb([128, 3, 128], F32, "W3"); P.dma(W3[:], W3d[:], writes=[W3])
    tw = P.sb([128, 3, 512], F32, "tw"); P.dma(tw[:], twd[:], writes=[tw])
    Wr, Wi, nWi = W3[:, 0, :], W3[:, 1, :], W3[:, 2, :]
    twr, twi, ntwi = tw[:, 0, :], tw[:, 1, :], tw[:, 2, :]
    xg = [P.sb([64, HY_G, 128], F32, f"xg{i}") for i in range(2)]
    fg = [P.sb([128, HY_G, 128], F32, f"fg{i}") for i in range(2)]
    T = [P.sb([128, 512], F32, f"T{i}") for i in range(4)]
    Ap = [[P.sb([128, 512], F32, f"Ap{i}{j}") for j in range(2)] for i in range(2)]
    Hs = [P.sb([128, 512], F32, f"H{j}") for j in range(2)]
    Yc = [P.sb([128, 512], F32, f"Y{j}") for j in range(2)]
    Zp = [P.sb([128, 512], F32, f"Zp{j}") for j in range(2)]
    ysb = [P.sb([64, 512], F32, f"ysb{i}") for i in range(2)]

    def cmul(o_re, o_im, a_re, a_im, b_re, b_im, ra, rb, wo):
        P.tt(T[0][:], a_re, b_re, ALU.mult, reads=ra + rb, writes=[T[0]])
        P.tt(T[1][:], a_im, b_im, ALU.mult, reads=ra + rb, writes=[T[1]])
        P.tt(o_re, T[0][:], T[1][:], ALU.subtract, reads=[T[0], T[1]], writes=[wo[0]], eng='pool')
        P.tt(T[2][:], a_re, b_im, ALU.mult, reads=ra + rb, writes=[T[2]])
        P.tt(T[3][:], a_im, b_re, ALU.mult, reads=ra + rb, writes=[T[3]])
        P.tt(o_im, T[2][:], T[3][:], ALU.add, reads=[T[2], T[3]], writes=[wo[1]], eng='pool')

    def v4(ap512):
        return ap512

    def fwd_fft(src, Ka, A):
        psA = rot()
        for ch in range(HY_G):
            P.mm(psA[:, ch * 256:(ch + 1) * 256], src[0:Ka, ch, :], W1[0:Ka, :], reads=[src, W1], writes=[psA])
        pv = psA[:].rearrange("p (c r k) -> p c r k", c=HY_G, r=2)
        t4 = lambda ap: ap.rearrange("p (c k) -> p c k", c=HY_G)
        P.tt(t4(T[0][:]), pv[:, :, 0, :], t4(twr), ALU.mult, reads=[psA, tw], writes=[T[0]])
        P.tt(t4(T[1][:]), pv[:, :, 1, :], t4(twi), ALU.mult, reads=[psA, tw], writes=[T[1]])
        P.tt(A[0][:], T[0][:], T[1][:], ALU.subtract, reads=[T[0], T[1]], writes=[A[0]], eng='pool')
        P.tt(t4(T[2][:]), pv[:, :, 0, :], t4(twi), ALU.mult, reads=[psA, tw], writes=[T[2]])
        P.tt(t4(T[3][:]), pv[:, :, 1, :], t4(twr), ALU.mult, reads=[psA, tw], writes=[T[3]])
        P.tt(A[1][:], T[2][:], T[3][:], ALU.add, reads=[T[2], T[3]], writes=[A[1]], eng='pool')
        psX = rot()
        P.mm(psX[:, 0:512], Wr, A[0][:], start=True, stop=False, reads=[W3, A[0]], writes=[psX])
        P.mm(psX[:, 0:512], nWi, A[1][:], start=False, stop=True, reads=[W3, A[1]], writes=[psX])
        P.mm(psX[:, 512:1024], Wi, A[0][:], start=True, stop=False, reads=[W3, A[0]], writes=[psX])
        P.mm(psX[:, 512:1024], Wr, A[1][:], start=False, stop=True, reads=[W3, A[1]], writes=[psX])
        return psX

    ng = 256 // HY_G
    for g in range(ng):
        ch0 = g * HY_G
        ct = ch0 // 128
        x_ = xg[g % 2]; f_ = fg[g % 2]
        P.dma(x_[:], Zs[ch0:ch0 + HY_G, :].rearrange("c (a b) -> a c b", b=128), reads=zviews, writes=[x_])
        P.dma(f_[:], Fs[ch0:ch0 + HY_G, :].rearrange("c (a b) -> a c b", b=128), reads=fviews, writes=[f_])
        psH = fwd_fft(f_, 128, Ap[0])
        P.copy(Hs[0][:], psH[:, 0:512], reads=[psH], writes=[Hs[0]], eng='act')
        P.copy(Hs[1][:], psH[:, 512:1024], reads=[psH], writes=[Hs[1]], eng='act')
        psX = fwd_fft(x_, 64, Ap[1])
        cmul(Yc[0][:], Yc[1][:], psX[:, 0:512], psX[:, 512:1024], Hs[0][:], Hs[1][:], [psX], [Hs[0], Hs[1]], Yc)
        psZ = rot()
        for ch in range(HY_G):
            P.mm(psZ[:, ch * 256:(ch + 1) * 256], Yc[0][:, ch * 128:(ch + 1) * 128], CW1[:], start=True, stop=False, reads=[Yc[0], CW1], writes=[psZ])
            P.mm(psZ[:, ch * 256:(ch + 1) * 256], Yc[1][:, ch * 128:(ch + 1) * 128], CW2[:], start=False, stop=True, reads=[Yc[1], CW2], writes=[psZ])
        pz = psZ[:].rearrange("p (c r k) -> p c r k", c=HY_G, r=2)
        t4 = lambda ap: ap.rearrange("p (c k) -> p c k", c=HY_G)
        cmul(t4(Zp[0][:]), t4(Zp[1][:]), pz[:, :, 0, :], pz[:, :, 1, :], t4(twr), t4(ntwi), [psZ], [tw], Zp)
        psy = rot()
        P.mm(psy[0:64, 0:512], W3[:, 0, 0:64], Zp[0][:], start=True, stop=False, reads=[W3, Zp[0]], writes=[psy])
        P.mm(psy[0:64, 0:512], W3[:, 1, 0:64], Zp[1][:], start=False, stop=True, reads=[W3, Zp[1]], writes=[psy])
        y_ = ysb[g % 2]
        P.op('act', lambda e, y_=y_, psy=psy: e.mul(y_[:], psy[0:64, 0:512], 1.0 / NFFT), reads=[psy], writes=[y_])
        v_ = P.view(Ys); yviews.append(v_)
        P.dma(Ys[ch0:ch0 + HY_G, :].rearrange("c (a b) -> a c b", b=128), y_[:].rearrange("a (c b) -> a c b", c=HY_G), reads=[y_], writes=[v_])
    P.pop()
    P.push()
    bufs = [[P.sb([128, 2048], F32, f"o{i}{j}") for j in range(3)] for i in range(2)]
    k = 0
    for ct in range(2):
        for s in range(0, NL, 2048):
            yb, zb, xb = bufs[k % 2]; k += 1
            rows = slice(ct * 128, (ct + 1) * 128)
            P.dma(yb[:], Ys[rows, s:s + 2048], reads=yviews, writes=[yb])
            P.dma(zb[:], Zs[rows, s:s + 2048], reads=zviews, writes=[zb])
            P.dma(xb[:], X0s[rows, s:s + 2048], reads=xviews, writes=[xb])
            P.stt(yb[:], zb[:], chs[:, 1, ct:ct + 1], yb[:], ALU.mult, ALU.add, reads=[zb, chs, yb], writes=[yb])
            P.tt(yb[:], yb[:], xb[:], ALU.mult, reads=[yb, xb], writes=[yb], eng='pool')
            P.dma(HYT[rows, s:s + 2048], yb[:], reads=[yb], writes=[P.view(HYT)], is_output=True)
    P.pop()
    return P.build()


def host_L5(d, r4):
    C, deltas = hyena_consts()
    ins = []
    cwf = d['hy_conv_w'][0]; cbf = d['hy_conv_b'][0]
    for j in range(8):
        b, chh = j // 2, j % 2
        a, c = r4[2 * b], r4[2 * b + 1]
        hT = np.zeros((3, 256, 8192), np.float32)
        cw = np.zeros((128, 3, 2, 3), np.float32); cb = np.zeros((128, 3, 2), np.float32)
        for part in range(3):
            rows = slice(512 * part + 256 * chh, 512 * part + 256 * chh + 256)
            hT[part] = np.concatenate([a['UT'][rows, :4096], c['UT'][rows, :4096]], 1)
            cw[:, part] = cwf[:, rows].reshape(3, 2, 128).transpose(2, 1, 0)
            cb[:, part] = cbf[rows].reshape(2, 128).T
        chs = slice(256 * chh, 256 * chh + 256)
        chv = np.zeros((128, 2, 2), np.float32)
        chv[:, 0, :] = -deltas[chs].reshape(2, 128).T
        chv[:, 1, :] = d['hy_bias'][0][chs].reshape(2, 128).T
        bf = np.stack([d['hy_filt_b1'][0], d['hy_sin_freq'][0][0], d['hy_filt_b2'][0], d['hy_sin_freq'][0][1]], 1)
        w3 = np.ascontiguousarray(d['hy_filt_w3'][0].reshape(64, 2, 512)[:, :, chs])
        m = {"hT": hT, "cw": cw, "cb": cb, "w1": d['hy_filt_w1'][0], "w2": d['hy_filt_w2'][0], "bf": np.ascontiguousarray(bf), "w3": w3, "chv": chv}
        m.update(C)
        ins.append(m)
    return ins


def build_L7():
    P = Prog()
    NTOK = 4096
    LB = BLOCKS[:8]
    xin = P.dram("xin", [NTOK, D]); cnd = P.dram("cnd", [2, D])
    mw = P.dram("mw", [D, 9 * D]); mb = P.dram("mb", [9 * D])
    HR = P.dram("HR", [1024, NTOK]); wout = P.dram("wout", [D, D])
    g2 = P.dram("g2", [D]); wg2 = P.dram("wg2", [D, FH]); wu2 = P.dram("wu2", [D, FH]); wd2 = P.dram("wd2", [FH, D])
    gf = P.dram("gf", [D])
    sa = P.dram("scr_a", [NTOK, D], kind="Internal"); sbb = P.dram("scr_b", [NTOK, D], kind="Internal")
    out = P.dram("out", [NTOK, D], kind="ExternalOutput")
    K = consts(P)
    xs = [P.sb([128, D], F32, f"x{t}") for t in range(4)]
    va = [P.view(sa) for _ in LB]; vb = [P.view(sbb) for _ in LB]
    mfm, gbc = adaln(P, K, cnd, mw, mb, [5, 8])
    sce2, sh2 = norm_params(P, K, mfm, g2, 6, "n2")
    cats = [HR[kc * 128:(kc + 1) * 128, :] for kc in range(8)]
    pass_mixout(P, K, xs, xin, None, sa, va, wout, gbc[5], cats, blocks=LB)
    pass_ffn(P, K, xs, sa, va, sbb, vb, wg2, wu2, wd2, sce2, sh2, gbc[8], blocks=LB)
    P.push()
    gfb = P.sb([128, D], F32, "gfb")
    P.dma(gfb[:], gf.t.partition_broadcast(128), writes=[gfb])
    ss = P.sb([128, 4], F32, "ssf"); rstd = P.sb([128, 4], F32, "rstdf")
    junk = P.sb([128, D], BF16, "junk")
    for bi, (t0, nt, n) in enumerate(LB):
        xio(P, xs, nt, t0, sbb, vb[bi])
        P.memset(ss[:], 0.0, writes=[ss], eng='dve')
        for t in range(nt):
            P.act(junk[:], xs[t][:], AF.Square, accum_out=ss[:, t:t + 1], reads=[xs[t]], writes=[junk, ss])
        rstd_from_ss(P, rstd[:], ss[:], 1.0 / D, [ss], [rstd])
        for t in range(nt):
            P.ts(xs[t][:], xs[t][:], rstd[:, t:t + 1], ALU.mult, reads=[xs[t], rstd], writes=[xs[t]])
            P.tt(xs[t][:], xs[t][:], gfb[:], ALU.mult, reads=[xs[t], gfb], writes=[xs[t]], eng='pool')
        xio(P, xs, nt, t0, None, None, out, P.view(out), load=False, out=True)
    P.pop()
    return P.build()


def host_L7(d, r4, r5, r6):
    ins = []
    for j in range(8):
        b, hf = j // 2, j % 2
        cols = slice(hf * 4096, (hf + 1) * 4096)
        HR = np.concatenate([r5[2 * b]['HYT'][:, cols], r5[2 * b + 1]['HYT'][:, cols], r6[2 * b]['RT'][:, cols], r6[2 * b + 1]['RT'][:, cols]], 0)
        ins.append({"xin": np.ascontiguousarray(r4[j]['xout'][:4096]), "cnd": np.stack([d['c'][b], d['c_ctx']]),
                    "mw": d['mod_w'][1], "mb": d['mod_b'][1], "HR": np.ascontiguousarray(HR), "wout": d['od_w_out'][0],
                    "g2": d['norm_ffn2'][1], "wg2": d['ffn2_w_gate'][1], "wu2": d['ffn2_w_up'][1], "wd2": d['ffn2_w_down'][1],
                    "gf": d['final_norm'], "ident": np.eye(128, dtype=np.float32)})
    return ins


_NC = {}


def _get(name, fn):
    if name not in _NC:
        _NC[name] = fn()
    return _NC[name]


def _run(name, fn, ins):
    nc = _get(name, fn)
    res = run_bass_kernel_spmd(nc, ins, core_ids=list(range(8)))
    return res.results


def kernel(**inputs):
    d = {k: np.ascontiguousarray(np.asarray(v)) for k, v in inputs.items()}
    r1 = _run("L1", build_L1, host_L1(d))
    r2 = _run("L2", build_L2, host_L2(r1))
    r3 = _run("L3", build_L3, host_L3(d, r1))
    r4 = _run("L4", build_L4, host_L4(d, r1, r2, r3))
    r5 = _run("L5", build_L5, host_L5(d, r4))
    r6 = _run("L6", build_L6, host_L6(d, r4))
    r7 = _run("L7", build_L7, host_L7(d, r4, r5, r6))
    out = np.zeros((4, 8192, 1024), np.float32)
    for j in range(8):
        b, hf = j // 2, j % 2
        out[b, hf * 4096:(hf + 1) * 4096] = r7[j]['out']
    return out
```

```python
from contextlib import ExitStack
import numpy as np
import concourse.bass as bass
import concourse.mybir as mybir
from concourse.bass_utils import run_bass_kernel_spmd

F32 = mybir.dt.float32
F32R = mybir.dt.float32r
BF16 = mybir.dt.bfloat16
AF = mybir.ActivationFunctionType
ALU = mybir.AluOpType
AX = mybir.AxisListType

ENGS = ('pe', 'act', 'dve', 'pool', 'sp')
NSLOT = 12


class Buf:
    __slots__ = ('t', 'w', 'r', 'name')

    def __init__(self, t, name):
        self.t = t
        self.w = None
        self.r = []
        self.name = name

    def __getitem__(self, idx):
        return self.t[idx]


class Prog:
    def __init__(self, same_engine_sync=True):
        self.nc = bass.Bass("TRN2", target_bir_lowering=False)
        self.ops = {e: [] for e in ENGS}
        self.cnt = {e: 0 for e in ENGS}
        self.seen = {e: {} for e in ENGS}
        self.dma_n = {e: 0 for e in ENGS}
        self.slot_val = {}
        self.stack = ExitStack()
        self.same = same_engine_sync
        self.nbuf = 0
        self.out_tokens = []
        self.scopes = []
        self.closers = []
        self.barrier = []

    def dram(self, name, shape, dt=F32, kind="ExternalInput"):
        t = self.nc.dram_tensor(name, list(shape), dt, kind=kind).ap()
        return Buf(t, name)

    def _ctx(self):
        return self.scopes[-1][0] if self.scopes else self.stack

    def _reg(self, b):
        b.r = list(self.barrier)
        if self.scopes:
            self.scopes[-1][1].append(b)
        return b

    def sb(self, shape, dt=F32, name=None):
        self.nbuf += 1
        name = (name or "sb") + f"_{self.nbuf}"
        t = self._ctx().enter_context(self.nc.sbuf_tensor(name, list(shape), dt))
        return self._reg(Buf(t, name))

    def ps(self, shape, dt=F32, name=None):
        self.nbuf += 1
        name = (name or "ps") + f"_{self.nbuf}"
        t = self._ctx().enter_context(self.nc.psum_tensor(name, list(shape), dt))
        return self._reg(Buf(t, name))

    def view(self, b):
        return Buf(b.t, b.name)

    def push(self):
        self.scopes.append((ExitStack(), []))

    def pop(self):
        st, bufs = self.scopes.pop()
        m = {}
        for b in bufs:
            for tok in ([b.w] if b.w else []) + b.r:
                if m.get(tok[0], 0) < tok[1]:
                    m[tok[0]] = tok[1]
        for k, v in self.barrier:
            if m.get(k, 0) < v:
                m[k] = v
        self.barrier = list(m.items())
        st.close()

    def _deps(self, eng, reads, writes):
        deps = {}

        def add(tok):
            if tok is None:
                return
            k, v = tok
            if deps.get(k, 0) < v:
                deps[k] = v
        for b in reads:
            add(b.w)
        for b in writes:
            add(b.w)
            for t in b.r:
                add(t)
        out = []
        seen = self.seen[eng]
        for k, v in deps.items():
            if k == eng and (eng == 'pe' or not self.same):
                continue
            if seen.get(k, 0) >= v:
                continue
            seen[k] = v
            out.append((k, v))
        return out

    def _commit(self, tok, reads, writes):
        for b in reads:
            if not any(b is w for w in writes):
                b.r.append(tok)
        for b in writes:
            b.w = tok
            b.r = []

    def op(self, eng, fn, reads=(), writes=()):
        waits = self._deps(eng, reads, writes)
        self.cnt[eng] += 1
        tok = (eng, self.cnt[eng])
        self.ops[eng].append((waits, fn, (eng, 1)))
        self._commit(tok, reads, writes)
        return tok

    def dma(self, out_ap, in_ap, reads=(), writes=(), q='sp', is_output=False, **kw):
        i = self.dma_n[q]
        self.dma_n[q] += 1
        slot = ('d', q, i % NSLOT)
        waits = self._deps(q, reads, writes)
        prev = self.slot_val.get(slot, 0)
        if prev and self.seen[q].get(slot, 0) < prev:
            self.seen[q][slot] = prev
            waits.append((slot, prev))
        val = prev + 16
        self.slot_val[slot] = val
        tok = (slot, val)

        def fn(e, out_ap=out_ap, in_ap=in_ap, kw=kw):
            return e.dma_start(out=out_ap, in_=in_ap, **kw)
        self.ops[q].append((waits, fn, (slot, 16)))
        self._commit(tok, reads, writes)
        if is_output:
            self.out_tokens.append(tok)
        return tok

    def build(self):
        nc = self.nc
        fin = {}
        for k, v in self.out_tokens:
            fin[k] = max(fin.get(k, 0), v)
        keys = set()
        for e in ENGS:
            for waits, fn, inc in self.ops[e]:
                keys.add(inc[0])
                for k, v in waits:
                    keys.add(k)
        sems = {}
        for k in sorted(keys, key=str):
            nm = k if isinstance(k, str) else f"d_{k[1]}_{k[2]}"
            sems[k] = self.stack.enter_context(nc.semaphore("s_" + nm))
        ops = self.ops
        with nc.Block() as block:
            def mk(ename):
                def body(e):
                    for waits, fn, inc in ops[ename]:
                        for k, v in waits:
                            e.wait_ge(sems[k], v)
                        ins = fn(e)
                        ins.then_inc(sems[inc[0]], inc[1])
                    if ename == 'sp':
                        for k, v in fin.items():
                            e.wait_ge(sems[k], v)
                return body
            if ops['sp'] or fin:
                block.sync(mk('sp'))
            if ops['pe']:
                block.tensor(mk('pe'))
            if ops['act']:
                block.scalar(mk('act'))
            if ops['dve']:
                block.vector(mk('dve'))
            if ops['pool']:
                block.gpsimd(mk('pool'))
        self.stack.close()
        return nc

    def mm(self, out, lhsT, rhs, start=True, stop=True, reads=(), writes=()):
        return self.op('pe', lambda e: e.matmul(out, lhsT, rhs, start=start, stop=stop), reads, writes)

    def tr(self, out, in_, ident, reads=(), writes=()):
        return self.op('pe', lambda e: e.transpose(out, in_, ident), reads, writes)

    def act(self, out, in_, func, bias=None, scale=None, accum_out=None, reads=(), writes=(), eng='act'):
        kw = {}
        if bias is not None:
            kw['bias'] = bias
        if scale is not None:
            kw['scale'] = scale
        if accum_out is not None:
            kw['accum_out'] = accum_out
        return self.op(eng, lambda e: e.activation(out, in_, func, **kw), reads, writes)

    def tt(self, out, in0, in1, op, reads=(), writes=(), eng='dve'):
        return self.op(eng, lambda e: e.tensor_tensor(out, in0, in1, op), reads, writes)

    def ts(self, out, in0, s1, op0, s2=None, op1=None, accum_out=None, reads=(), writes=(), eng='dve'):
        kw = {}
        if op1 is not None:
            kw['op1'] = op1
        if accum_out is not None:
            kw['accum_out'] = accum_out
        return self.op(eng, lambda e: e.tensor_scalar(out, in0, s1, s2, op0, **kw), reads, writes)

    def stt(self, out, in0, scalar, in1, op0, op1, reads=(), writes=(), eng='dve'):
        eng = 'dve'
        return self.op(eng, lambda e: e.scalar_tensor_tensor(out, in0, scalar, in1, op0, op1), reads, writes)

    def copy(self, out, in_, reads=(), writes=(), eng='dve'):
        if eng == 'act':
            return self.op(eng, lambda e: e.copy(out, in_), reads, writes)
        return self.op(eng, lambda e: e.tensor_copy(out, in_), reads, writes)

    def memset(self, ap, val, writes=(), eng='pool'):
        return self.op(eng, lambda e: e.memset(ap, val), (), writes)

    def scan(self, out, d0, d1, init, reads=(), writes=(), eng='dve'):
        return self.op(eng, lambda e: e.tensor_tensor_scan(out, d0, d1, init, ALU.mult, ALU.add), reads, writes)


def run(prog_nc, in_maps, trace=False):
    res = run_bass_kernel_spmd(prog_nc, in_maps, core_ids=list(range(len(in_maps))), trace=trace)
    return res


D = 1024
FH = 2816
NF = 22
EPS = 1e-6
NT_CORE = 33
BLOCKS = [(4 * i, 4, 0) for i in range(8)] + [(32, 1, 1)]


def wview(w, p=128):
    return w.t.rearrange("(c p) f -> p c f", p=p)


def load_w(P, dst, src, q='pool'):
    v = wview(src)
    for c in range(v.shape[1]):
        P.dma(dst[:, c, :], v[:, c, :], writes=[dst], q=q)


class Ctx:
    pass


def consts(P):
    K = Ctx()
    idd = P.dram("ident", [128, 128])
    K.ident = P.sb([128, 128], F32, "ident")
    P.dma(K.ident[:], idd[:], writes=[K.ident])
    K.identb = P.sb([128, 128], BF16, "identb")
    P.copy(K.identb[:], K.ident[:], reads=[K.ident], writes=[K.identb])
    K.ones = P.sb([128, 128], F32, "ones")
    P.memset(K.ones[:], 1.0, writes=[K.ones])
    K.onesb = P.sb([128, 128], BF16, "onesb")
    P.memset(K.onesb[:], 1.0, writes=[K.onesb])
    K.pb = [P.ps([128, 512], F32, f"pb{i}") for i in range(6)]
    K.pbT = [P.ps([128, 1024], BF16, f"pbT{i}") for i in range(2)]
    return K


def load_fm(P, K, dstbuf, dst, src2d, C):
    tmp = P.sb([C, 128], F32, "lfm")
    P.dma(tmp[:], src2d, writes=[tmp])
    pb = K.pb[5]
    P.tr(pb[:, 0:C], tmp[:], K.ident[0:C, 0:C], reads=[tmp, K.ident], writes=[pb])
    P.copy(dst, pb[:, 0:C], reads=[pb], writes=[dstbuf])


def adaln(P, K, cnd, mod_w, mod_b, gate_ks):
    mfm = P.sb([128, 72, 2], F32, "mfm")
    gbc = {k: P.sb([128, 2, 1024], F32, f"gbc{k}") for k in gate_ks}
    P.push()
    sc = P.sb([128, 2, 8], F32, "sc")
    load_fm(P, K, sc, sc[:].rearrange("p n c -> p (n c)"), cnd.t.rearrange("n (c p) -> (n c) p", p=128), 16)
    scs = P.sb([128, 2, 8], F32, "scs")
    P.act(scs[:], sc[:], AF.Silu, reads=[sc], writes=[scs])
    scb = P.sb([128, 8, 2, 128], F32, "scb")
    for c in range(8):
        for n in range(2):
            P.ts(scb[:, c, n, :], K.ones[:], scs[:, n, c:c + 1], ALU.mult, reads=[K.ones, scs], writes=[scb])
    mbf = P.sb([128, 72], F32, "mbf")
    load_fm(P, K, mbf, mbf[:], mod_b.t.rearrange("(c p) -> c p", p=128), 72)
    mbb = P.sb([128, len(gate_ks), 1024], F32, "mbb")
    for i, k in enumerate(gate_ks):
        P.dma(mbb[:, i, :], mod_b.t[k * 1024:(k + 1) * 1024].partition_broadcast(128), writes=[mbb])
    wv = wview(mod_w)
    wbuf = [P.sb([128, 8, 1024], F32, f"modw{i}") for i in range(2)]
    for k in range(9):
        wb = wbuf[k % 2]
        for c in range(8):
            P.dma(wb[:, c, :], wv[:, c, k * 1024:(k + 1) * 1024], writes=[wb])
        for fc in range(8):
            pb = K.pb[fc % 2]
            for c in range(8):
                P.mm(pb[:, 0:2], wb[:, c, fc * 128:(fc + 1) * 128], scs[:, :, c], start=(c == 0), stop=(c == 7),
                     reads=[wb, scs], writes=[pb])
            ch = k * 8 + fc
            P.ts(mfm[:, ch, :], pb[:, 0:2], mbf[:, ch:ch + 1], ALU.add, reads=[pb, mbf], writes=[mfm])
        if k in gate_ks:
            gi = gate_ks.index(k)
            fac = 1.0 if k == 5 else 0.5
            for n in range(2):
                for hf in range(2):
                    pb = K.pb[2 + (n * 2 + hf) % 2]
                    for c in range(8):
                        P.mm(pb[:], scb[:, c, n, :], wb[:, c, hf * 512:(hf + 1) * 512], start=(c == 0), stop=(c == 7),
                             reads=[scb, wb], writes=[pb])
                    o = gbc[k][:, n, hf * 512:(hf + 1) * 512]
                    P.tt(o, pb[:], mbb[:, gi, hf * 512:(hf + 1) * 512], ALU.add, reads=[pb, mbb], writes=[gbc[k]])
                    if fac != 1.0:
                        P.ts(o, o, fac, ALU.mult, reads=[gbc[k]], writes=[gbc[k]])
    P.pop()
    return mfm, gbc


def norm_params(P, K, mfm, g_dram, k0, name):
    g = P.sb([128, 8], F32, name + "g")
    load_fm(P, K, g, g[:], g_dram.t.rearrange("(c p) -> c p", p=128), 8)
    sce = P.sb([128, 8, 2], F32, name + "sce")
    sh = P.sb([128, 8, 2], F32, name + "sh")
    for n in range(2):
        P.ts(sce[:, :, n], mfm[:, (k0 + 1) * 8:(k0 + 2) * 8, n], 1.0, ALU.add, reads=[mfm], writes=[sce])
        P.tt(sce[:, :, n], sce[:, :, n], g[:], ALU.mult, reads=[sce, g], writes=[sce])
        P.copy(sh[:, :, n], mfm[:, k0 * 8:(k0 + 1) * 8, n], reads=[mfm], writes=[sh])
    return sce, sh


def rstd_from_ss(P, out, ss, inv_n, reads, writes):
    P.ts(out, ss, inv_n, ALU.mult, EPS, ALU.add, reads=reads, writes=writes)
    P.act(out, out, AF.Sqrt, reads=writes, writes=writes)
    P.op('dve', lambda e: e.reciprocal(out, out), reads=writes, writes=writes)


def norm_T(P, K, W, xs, nt, n, sce, sh, hT):
    P.memset(W.ss[:], 0.0, writes=[W.ss], eng='dve')
    for t in range(nt):
        P.act(W.xn[t][:], xs[t][:], AF.Square, accum_out=W.ss[:, t:t + 1], reads=[xs[t]], writes=[W.xn[t], W.ss])
    rstd_from_ss(P, W.rstd[:], W.ss[:], 1.0 / D, [W.ss], [W.rstd])
    for t in range(nt):
        P.op('act', lambda e, t=t: e.mul(W.xn[t][:], xs[t][:], W.rstd[:, t:t + 1]), reads=[xs[t], W.rstd], writes=[W.xn[t]])
    for c in range(8):
        pb = K.pbT[c % 2]
        for t in range(nt):
            P.tr(pb[:, t * 128:(t + 1) * 128], W.xn[t][:, c * 128:(c + 1) * 128], K.identb[:],
                 reads=[W.xn[t], K.identb], writes=[pb])
        P.ts(hT[:, c, 0:nt * 128], pb[:, 0:nt * 128], sce[:, c, n:n + 1], ALU.mult, sh[:, c, n:n + 1], ALU.add,
             reads=[pb, sce, sh], writes=[hT])


class FFNW:
    pass


def ffn_alloc(P):
    W = FFNW()
    W.wg = P.sb([128, 8, FH], BF16, "wg")
    W.wu = P.sb([128, 8, FH], BF16, "wu")
    W.wd = P.sb([128, NF, D], BF16, "wd")
    W.xn = [P.sb([128, D], BF16, f"xn{t}") for t in range(4)]
    W.ss = P.sb([128, 4], F32, "ss")
    W.rstd = P.sb([128, 4], F32, "rstd")
    W.hT = P.sb([128, 8, 512], BF16, "hT")
    W.actT = P.sb([128, NF, 512], BF16, "actT")
    W.a = [P.sb([128, 512], BF16, f"a{i}") for i in range(2)]
    return W


def ffn_load(P, W, wg, wu, wd):
    load_w(P, W.wg, wg)
    load_w(P, W.wu, wu)
    load_w(P, W.wd, wd)


def ffn_block(P, K, W, xs, nt, n, sce, sh, gbc):
    N = nt * 128
    norm_T(P, K, W, xs, nt, n, sce, sh, W.hT)
    for f in range(NF):
        pg = K.pb[(f % 2) * 2]
        pu = K.pb[(f % 2) * 2 + 1]
        for c in range(8):
            P.mm(pg[:, 0:N], W.wg[:, c, f * 128:(f + 1) * 128], W.hT[:, c, 0:N], start=(c == 0), stop=(c == 7),
                 reads=[W.wg, W.hT], writes=[pg])
        for c in range(8):
            P.mm(pu[:, 0:N], W.wu[:, c, f * 128:(f + 1) * 128], W.hT[:, c, 0:N], start=(c == 0), stop=(c == 7),
                 reads=[W.wu, W.hT], writes=[pu])
        a = W.a[f % 2]
        P.act(a[:, 0:N], pg[:, 0:N], AF.Silu, reads=[pg], writes=[a])
        P.tt(W.actT[:, f, 0:N], a[:, 0:N], pu[:, 0:N], ALU.mult, reads=[a, pu], writes=[W.actT])
    i = 0
    for t in range(nt):
        for hf in range(2):
            py = K.pb[4 + i % 2]
            i += 1
            for f in range(NF):
                P.mm(py[:], W.actT[:, f, t * 128:(t + 1) * 128], W.wd[:, f, hf * 512:(hf + 1) * 512],
                     start=(f == 0), stop=(f == NF - 1), reads=[W.actT, W.wd], writes=[py])
            tb = W.xn[i % 2]
            tmp = tb[:].bitcast(F32)
            P.tt(tmp, py[:], gbc[:, n, hf * 512:(hf + 1) * 512], ALU.mult, reads=[py, gbc], writes=[tb])
            xo = xs[t][:, hf * 512:(hf + 1) * 512]
            P.tt(xo, xo, tmp, ALU.add, reads=[xs[t], tb], writes=[xs[t]], eng='pool')


QR, KVR, ROPE, S5C = 384, 256, 32, 512
NH = 8
EV_IN = QR + KVR + ROPE + S5C
EV_EXT = EV_IN + 192


def rot(K):
    K.rr = getattr(K, 'rr', -1) + 1
    return K.pb[K.rr % 6]


def evin_alloc(P):
    W = FFNW()
    W.win = P.sb([128, 8, EV_EXT], BF16, "win")
    W.wuq = P.sb([128, 3, NH * 96], BF16, "wuq")
    W.wuqs = P.sb([128, 3, NH * 96], BF16, "wuqs")
    W.wk = P.sb([128, 2, NH * 64], BF16, "wk")
    W.wv = P.sb([128, 2, 512], BF16, "wv")
    W.gq = P.sb([128, 3], F32, "gq")
    W.gkv = P.sb([128, 2], F32, "gkv")
    W.cos = P.sb([96, 4096], F32, "cos")
    W.sin = P.sb([96, 4096], F32, "sin")
    W.xn = [P.sb([128, D], BF16, f"xn{t}") for t in range(4)]
    W.ss = P.sb([128, 4], F32, "ss")
    W.rstd = P.sb([128, 4], F32, "rstd")
    W.hT = P.sb([128, 8, 512], BF16, "hT")
    W.cq = P.sb([128, 3, 512], F32, "cq")
    W.sq = P.sb([128, 512], F32, "sq")
    W.rbc = P.sb([128, 512], F32, "rbc")
    W.cqn = P.sb([128, 3, 512], BF16, "cqn")
    W.ckvn = P.sb([128, 2, 512], BF16, "ckvn")
    W.krr = P.sb([96, 512], F32, "krr")
    W.t1 = P.sb([96, 512], F32, "t1")
    W.t2 = P.sb([96, 512], F32, "t2")
    W.st = [P.sb([128, 512], F32, f"st{i}") for i in range(3)]
    W.sti = 0
    return W


def stage(W):
    W.sti += 1
    return W.st[W.sti % 3]


def lowrank_norm(P, K, W, col0, nch, nfeat, g, N, outn):
    pss = rot(K)
    for c3 in range(nch):
        pc = rot(K)
        for c in range(8):
            P.mm(pc[:, 0:N], W.win[:, c, col0 + c3 * 128:col0 + (c3 + 1) * 128], W.hT[:, c, 0:N], start=(c == 0), stop=(c == 7),
                 reads=[W.win, W.hT], writes=[pc])
        P.copy(W.cq[:, c3, 0:N], pc[:, 0:N], reads=[pc], writes=[W.cq], eng='act')
        P.act(W.sq[:, 0:N], pc[:, 0:N], AF.Square, reads=[pc], writes=[W.sq])
        P.mm(pss[:, 0:N], K.ones[:], W.sq[:, 0:N], start=(c3 == 0), stop=(c3 == nch - 1), reads=[K.ones, W.sq], writes=[pss])
    rstd_from_ss(P, W.rbc[:, 0:N], pss[:, 0:N], 1.0 / nfeat, [pss], [W.rbc])
    for c3 in range(nch):
        P.stt(outn[:, c3, 0:N], W.cq[:, c3, 0:N], g[:, c3:c3 + 1], W.rbc[:, 0:N], ALU.mult, ALU.mult,
              reads=[W.cq, g, W.rbc], writes=[outn])


def rope_rows(P, W, out, a, b, tok0, N, reads, writes, roped):
    if not roped:
        P.copy(out, a, reads=reads, writes=writes)
        return
    P.tt(W.t1[64:96, 0:N], a, W.cos[64:96, tok0:tok0 + N], ALU.mult, reads=reads + [W.cos], writes=[W.t1])
    P.tt(W.t2[64:96, 0:N], b, W.sin[64:96, tok0:tok0 + N], ALU.mult, reads=reads + [W.sin], writes=[W.t2])
    P.tt(out, W.t1[64:96, 0:N], W.t2[64:96, 0:N], ALU.add, reads=[W.t1, W.t2], writes=writes, eng='pool')


def evin_block(P, K, W, xs, t0, nt, n, sce, sh, QT, KT, V, ST):
    N = nt * 128
    tok0 = t0 * 128
    roped = (n == 0)
    norm_T(P, K, W, xs, nt, n, sce, sh, W.hT)
    lowrank_norm(P, K, W, 0, 3, QR, W.gq, N, W.cqn)
    lowrank_norm(P, K, W, QR, 2, KVR, W.gkv, N, W.ckvn)
    pa, pbs = rot(K), rot(K)
    for c in range(8):
        P.mm(pa[0:96, 0:N], W.win[:, c, EV_IN:EV_IN + 96], W.hT[:, c, 0:N], start=(c == 0), stop=(c == 7), reads=[W.win, W.hT], writes=[pa])
    if roped:
        for c in range(8):
            P.mm(pbs[0:96, 0:N], W.win[:, c, EV_IN + 96:EV_IN + 192], W.hT[:, c, 0:N], start=(c == 0), stop=(c == 7), reads=[W.win, W.hT], writes=[pbs])
    rope_rows(P, W, W.krr[64:96, 0:N], pa[64:96, 0:N], pbs[64:96, 0:N], tok0, N, [pa, pbs], [W.krr], roped)
    for c4 in range(4):
        pc = rot(K)
        for c in range(8):
            P.mm(pc[:, 0:N], W.win[:, c, 672 + c4 * 128:672 + (c4 + 1) * 128], W.hT[:, c, 0:N], start=(c == 0), stop=(c == 7),
                 reads=[W.win, W.hT], writes=[pc])
        st = stage(W)
        P.copy(st[:, 0:N], pc[:, 0:N], reads=[pc], writes=[st], eng='act')
        P.dma(ST[c4 * 128:(c4 + 1) * 128, tok0:tok0 + N], st[:, 0:N], reads=[st], writes=[P.view(ST)], is_output=True)
    for h in range(NH):
        pq, pqs = rot(K), rot(K)
        for c in range(3):
            P.mm(pq[0:96, 0:N], W.wuq[:, c, h * 96:(h + 1) * 96], W.cqn[:, c, 0:N], start=(c == 0), stop=(c == 2), reads=[W.wuq, W.cqn], writes=[pq])
        if roped:
            for c in range(3):
                P.mm(pqs[0:96, 0:N], W.wuqs[:, c, h * 96:(h + 1) * 96], W.cqn[:, c, 0:N], start=(c == 0), stop=(c == 2), reads=[W.wuqs, W.cqn], writes=[pqs])
        st = stage(W)
        P.copy(st[0:64, 0:N], pq[0:64, 0:N], reads=[pq], writes=[st], eng='act')
        rope_rows(P, W, st[64:96, 0:N], pq[64:96, 0:N], pqs[64:96, 0:N], tok0, N, [pq, pqs], [st], roped)
        P.dma(QT[h, :, tok0:tok0 + N], st[0:96, 0:N], reads=[st], writes=[P.view(QT)], is_output=True)
        pk = rot(K)
        for c in range(2):
            P.mm(pk[0:64, 0:N], W.wk[:, c, h * 64:(h + 1) * 64], W.ckvn[:, c, 0:N], start=(c == 0), stop=(c == 1), reads=[W.wk, W.ckvn], writes=[pk])
        st = stage(W)
        P.copy(st[0:64, 0:N], pk[0:64, 0:N], reads=[pk], writes=[st], eng='act')
        P.copy(st[64:96, 0:N], W.krr[64:96, 0:N], reads=[W.krr], writes=[st], eng='pool')
        P.dma(KT[h, :, tok0:tok0 + N], st[0:96, 0:N], reads=[st], writes=[P.view(KT)], is_output=True)
    for t in range(nt):
        pv = rot(K)
        for c in range(2):
            P.mm(pv[:], W.ckvn[:, c, t * 128:(t + 1) * 128], W.wv[:, c, :], start=(c == 0), stop=(c == 1), reads=[W.ckvn, W.wv], writes=[pv])
        st = stage(W)
        P.copy(st[:], pv[:], reads=[pv], writes=[st])
        P.dma(V[tok0 + t * 128:tok0 + (t + 1) * 128, :], st[:], reads=[st], writes=[P.view(V)], is_output=True)


def host_rope_tables():
    n = 8192
    row = np.repeat(np.arange(n // 64, dtype=np.float32), 64)
    col = np.tile(np.arange(64, dtype=np.float32), n // 64)
    nf = 8
    inv = (np.float32(10000.0) ** (-np.arange(nf, dtype=np.float32) / np.float32(nf))).astype(np.float32)
    ang = np.concatenate([row[:, None] * inv, col[:, None] * inv], -1).astype(np.float32)
    c, s = np.cos(ang).astype(np.float32), np.sin(ang).astype(np.float32)
    cos = np.zeros((96, n), np.float32)
    sin = np.zeros((96, n), np.float32)
    cos[64:80] = c.T
    cos[80:96] = c.T
    sin[64:80] = -s.T
    sin[80:96] = s.T
    return cos, sin


def build_L1():
    P = Prog()
    xin = P.dram("xin", [NT_CORE * 128, D]); cnd = P.dram("cnd", [2, D]); mod_w = P.dram("mod_w", [D, 9 * D]); mod_b = P.dram("mod_b", [9 * D])
    g1 = P.dram("g1", [D]); gm = P.dram("gm", [D]); wg = P.dram("wg", [D, FH]); wu = P.dram("wu", [D, FH]); wd = P.dram("wd", [FH, D])
    win = P.dram("win", [D, EV_EXT]); wuq = P.dram("wuq", [QR, NH * 96]); wuqs = P.dram("wuqs", [QR, NH * 96])
    wk = P.dram("wk", [KVR, NH * 64]); wv = P.dram("wv", [KVR, 512]); gq = P.dram("gq", [QR]); gkv = P.dram("gkv", [KVR])
    cos = P.dram("cos", [96, 4096]); sin = P.dram("sin", [96, 4096])
    xout = P.dram("xout", [NT_CORE * 128, D], kind="ExternalOutput")
    QT = P.dram("QT", [NH, 96, NT_CORE * 128], kind="ExternalOutput")
    KT = P.dram("KT", [NH, 96, NT_CORE * 128], kind="ExternalOutput")
    V = P.dram("V", [NT_CORE * 128, 512], kind="ExternalOutput")
    ST = P.dram("ST", [512, NT_CORE * 128], kind="ExternalOutput")
    K = consts(P)
    mfm, gbc = adaln(P, K, cnd, mod_w, mod_b, [2])
    sce, sh = norm_params(P, K, mfm, g1, 0, "n1")
    scem, shm = norm_params(P, K, mfm, gm, 3, "nm")
    xs = [P.sb([128, D], F32, f"x{t}") for t in range(4)]
    xviews = [P.view(xout) for _ in BLOCKS]
    P.push()
    W = ffn_alloc(P)
    ffn_load(P, W, wg, wu, wd)
    for bi, (t0, nt, n) in enumerate(BLOCKS):
        for t in range(nt):
            P.dma(xs[t][:], xin[(t0 + t) * 128:(t0 + t + 1) * 128, :], writes=[xs[t]])
        ffn_block(P, K, W, xs, nt, n, sce, sh, gbc[2])
        for t in range(nt):
            P.dma(xout[(t0 + t) * 128:(t0 + t + 1) * 128, :], xs[t][:], reads=[xs[t]], writes=[xviews[bi]], is_output=True)
    P.pop()
    P.push()
    W = evin_alloc(P)
    load_w(P, W.win, win); load_w(P, W.wuq, wuq); load_w(P, W.wuqs, wuqs); load_w(P, W.wk, wk); load_w(P, W.wv, wv)
    load_fm(P, K, W.gq, W.gq[:], gq.t.rearrange("(c p) -> c p", p=128), 3)
    load_fm(P, K, W.gkv, W.gkv[:], gkv.t.rearrange("(c p) -> c p", p=128), 2)
    P.dma(W.cos[64:96, :], cos[64:96, :], writes=[W.cos]); P.dma(W.sin[64:96, :], sin[64:96, :], writes=[W.sin])
    for bi, (t0, nt, n) in enumerate(BLOCKS):
        for t in range(nt):
            P.dma(xs[t][:], xout[(t0 + t) * 128:(t0 + t + 1) * 128, :], reads=[xviews[bi]], writes=[xs[t]])
        evin_block(P, K, W, xs, t0, nt, n, scem, shm, QT, KT, V, ST)
    P.pop()
    return P.build()


def host_L1(d, li=0):
    cos, sin = host_rope_tables()
    w_in = d['ev_w_in'][0]
    kr = w_in[:, 640:672]
    krs = np.concatenate([kr[:, 16:32], kr[:, 0:16]], 1)
    z64 = np.zeros((D, 64), np.float32)
    win = np.concatenate([w_in, z64, kr, z64, krs], 1)
    wuq = d['mla_w_uq'][0]
    wq3 = wuq.reshape(QR, NH, 96)
    wuqs = np.zeros((QR, NH, 96), np.float32)
    wuqs[:, :, 64:80] = wq3[:, :, 80:96]
    wuqs[:, :, 80:96] = wq3[:, :, 64:80]
    wukv = d['mla_w_ukv'][0].reshape(KVR, NH, 128)
    wk = np.ascontiguousarray(wukv[:, :, 0:64]).reshape(KVR, NH * 64)
    wv = np.ascontiguousarray(wukv[:, :, 64:128]).reshape(KVR, 512)
    ins = []
    for j in range(8):
        b, hf = j // 2, j % 2
        x = np.concatenate([d['x'][b, hf * 4096:(hf + 1) * 4096], d['ctx'][b, hf * 128:(hf + 1) * 128]], 0)
        ins.append({"xin": x, "cnd": np.stack([d['c'][b], d['c_ctx']]), "mod_w": d['mod_w'][li], "mod_b": d['mod_b'][li],
                    "g1": d['norm_ffn1'][li], "gm": d['norm_mix'][li],
                    "wg": d['ffn1_w_gate'][li], "wu": d['ffn1_w_up'][li], "wd": d['ffn1_w_down'][li],
                    "win": win, "wuq": wuq, "wuqs": wuqs.reshape(QR, NH * 96), "wk": wk, "wv": wv,
                    "gq": d['mla_q_norm'][0], "gkv": d['mla_kv_norm'][0],
                    "cos": np.ascontiguousarray(cos[:, hf * 4096:(hf + 1) * 4096]), "sin": np.ascontiguousarray(sin[:, hf * 4096:(hf + 1) * 4096]),
                    "ident": np.eye(128, dtype=np.float32)})
    return ins


NKT = 66
NQ = 8192 + 256
SCALE = 96 ** -0.5


def build_L2():
    P = Prog()
    QTd = P.dram("QT", [4, 96, NQ]); KTd = P.dram("KT", [4, 96, NKT * 128]); Vd = P.dram("V", [NKT * 128, 4, 64])
    Od = P.dram("O", [NQ, 256], kind="ExternalOutput")
    QT = P.sb([96, 4, NQ], BF16, "QT"); KT = P.sb([96, 4, NKT * 128], BF16, "KT"); V = P.sb([128, NKT, 4, 65], BF16, "V")
    P.memset(V[:, :, :, 64:65], 1.0, writes=[V])
    Vv = Vd.t.rearrange("(k p) h d -> p k h d", p=128)
    for h in range(4):
        for s in range(0, NKT * 128, 2112):
            P.dma(KT[:, h, s:s + 2112], KTd[h, :, s:s + 2112], writes=[KT], q='pool')
            P.dma(QT[:, h, s:s + 2112], QTd[h, :, s:s + 2112], writes=[QT], q='pool')
    for kt in range(NKT):
        P.dma(V[:, kt, :, 0:64], Vv[:, kt, :, :], writes=[V], q='pool')
    po = [P.ps([128, 512], F32, f"po{i}") for i in range(4)]
    pss = [P.ps([128, 512], F32, f"pss{i}") for i in range(4)]
    pt = [P.sb([128, 512], BF16, f"pt{i}") for i in range(3)]
    ost = [P.sb([128, 256], F32, f"ost{i}") for i in range(8)]
    rc = P.sb([128, 8], F32, "rc")
    blocks = [(qb * 512, 4, 0, NKT) for qb in range(16)] + [(8192, 2, 0, 2)]
    pt = pt + [P.sb([128, 512], BF16, "pt3")]
    iters = []
    for bi, (q0, nq, k0, k1) in enumerate(blocks):
        for h in range(4):
            for kt in range(k0, k1):
                iters.append((bi, q0, nq, k0, k1, h, kt))
    LAG = 2

    def front(idx):
        bi, q0, nq, k0, k1, h, kt = iters[idx]
        N = nq * 128
        ps = pss[idx % 4]
        ptt = pt[idx % 4]
        P.mm(ps[:, 0:N], KT[:, h, kt * 128:(kt + 1) * 128], QT[:, h, q0:q0 + N], reads=[KT, QT], writes=[ps])
        P.act(ptt[:, 0:N], ps[:, 0:N], AF.Exp, scale=SCALE, reads=[ps], writes=[ptt])

    def back(idx):
        bi, q0, nq, k0, k1, h, kt = iters[idx]
        ptt = pt[idx % 4]
        osts = ost[(bi % 2) * 4:(bi % 2) * 4 + 4]
        for i in range(nq):
            P.mm(po[i][:, 0:65], ptt[:, i * 128:(i + 1) * 128], V[:, kt, h, :], start=(kt == k0), stop=(kt == k1 - 1),
                 reads=[ptt, V], writes=[po[i]])
        if kt == k1 - 1:
            for i in range(nq):
                P.op('dve', lambda e, i=i: e.reciprocal(rc[:, i:i + 1], po[i][:, 64:65]), reads=[po[i]], writes=[rc])
                P.ts(osts[i][:, h * 64:(h + 1) * 64], po[i][:, 0:64], rc[:, i:i + 1], ALU.mult, reads=[po[i], rc], writes=[osts[i]])
            if h == 3:
                for i in range(nq):
                    P.dma(Od[q0 + i * 128:q0 + (i + 1) * 128, :], osts[i][:], reads=[osts[i]], writes=[P.view(Od)], is_output=True)

    for idx in range(len(iters) + LAG):
        if idx < len(iters):
            front(idx)
        if idx >= LAG:
            back(idx - LAG)
    return P.build()


def host_L2(r1):
    ins = []
    for j in range(8):
        b, hg = j // 2, j % 2
        a, c = r1[2 * b], r1[2 * b + 1]
        hs = slice(4 * hg, 4 * hg + 4)
        QT = np.concatenate([a['QT'][hs, :, :4096], c['QT'][hs, :, :4096], a['QT'][hs, :, 4096:], c['QT'][hs, :, 4096:]], 2)
        KT = np.concatenate([a['KT'][hs, :, 4096:], c['KT'][hs, :, 4096:], a['KT'][hs, :, :4096], c['KT'][hs, :, :4096]], 2)
        Vf = np.concatenate([a['V'][4096:], c['V'][4096:], a['V'][:4096], c['V'][:4096]], 0).reshape(NKT * 128, 8, 64)
        ins.append({"QT": np.ascontiguousarray(QT), "KT": np.ascontiguousarray(KT), "V": np.ascontiguousarray(Vf[:, hs])})
    return ins


MAGIC = 12582912.0
TWO_PI = 6.283185307179586
PI = 3.141592653589793
LC = 512
NTOK5 = 256 + 8192


def sincos(P, out_sin, out_cos, x, tmp_k, tmp_z, bufs_r, bufs_w, shift_only=None):
    for out, sh in ((out_sin, 0.0), (out_cos, PI / 2)):
        if out is None:
            continue
        src = x
        if sh:
            P.ts(tmp_z, x, sh, ALU.add, reads=bufs_r + bufs_w, writes=bufs_w)
            src = tmp_z
        P.ts(tmp_k, src, 1.0 / TWO_PI, ALU.mult, MAGIC, ALU.add, reads=bufs_r + bufs_w, writes=bufs_w)
        P.ts(tmp_k, tmp_k, -MAGIC, ALU.add, reads=bufs_w, writes=bufs_w)
        P.stt(tmp_z, tmp_k, -TWO_PI, src, ALU.mult, ALU.add, reads=bufs_r + bufs_w, writes=bufs_w)
        P.ts(tmp_z, tmp_z, PI, ALU.min, -PI, ALU.max, reads=bufs_w, writes=bufs_w)
        P.act(out, tmp_z, AF.Sin, reads=bufs_w, writes=bufs_w)


def build_L3(dbg=False, same=True):
    P = Prog(same_engine_sync=same)
    uT = P.dram("uT", [256, NTOK5])
    prm = P.dram("prm", [128, 3, 2, 8])
    Bm = P.dram("Bm", [128, 2, 2, 8, 128])
    Cm = P.dram("Cm", [128, 2, 2, 8, 128])
    dsk = P.dram("dsk", [2, 128])
    tau = P.dram("tau", [128, LC])
    identd = P.dram("ident", [128, 128])
    Y = P.dram("Y", [256, NTOK5], kind="ExternalOutput")
    ident = P.sb([128, 128], F32, "ident"); P.dma(ident[:], identd[:], writes=[ident])
    pbs = [P.ps([128, 512], F32, f"pb{i}") for i in range(8)]
    rr = [0]

    def rot():
        rr[0] += 1
        return pbs[rr[0] % 8]
    ub = P.sb([128, 2, NTOK5], BF16, "ub")
    for ct in range(2):
        for s in range(0, NTOK5, 2112):
            P.dma(ub[:, ct, s:s + 2112], uT[ct * 128:(ct + 1) * 128, s:s + 2112], writes=[ub], q='pool')
    Bb = P.sb([128, 2, 2, 8, 128], BF16, "Bb"); Cb = P.sb([128, 2, 2, 8, 128], BF16, "Cb")
    for d in range(2):
        P.dma(Bb[:, d], Bm[:, d], writes=[Bb], q='pool')
        P.dma(Cb[:, d], Cm[:, d], writes=[Cb], q='pool')
    for d in range(2):
        P.ts(Cb[:, d, 1], Cb[:, d, 1], -1.0, ALU.mult, reads=[Cb], writes=[Cb])
    tau1 = P.sb([128, LC], F32, "tau"); P.dma(tau1[:], tau[:], writes=[tau1])
    dcol = P.sb([128, 2], F32, "dcol")
    dtmp = P.sb([2, 128], F32, "dtmp"); P.dma(dtmp[:], dsk[:], writes=[dtmp])
    pb = rot()
    P.tr(pb[:, 0:2], dtmp[:], ident[0:2, 0:2], reads=[dtmp, ident], writes=[pb])
    P.copy(dcol[:], pb[:, 0:2], reads=[pb], writes=[dcol])
    yacc = P.sb([128, 2, NTOK5], F32, "yacc")
    for ct in range(2):
        for s in range(0, NTOK5, 2112):
            P.ts(yacc[:, ct, s:s + 2112], ub[:, ct, s:s + 2112], dcol[:, ct:ct + 1], ALU.mult, reads=[ub, dcol], writes=[yacc], eng='pool')
    pr = P.sb([128, 3, 16], F32, "pr"); P.dma(pr[:], prm.t.rearrange("p a d s -> p a (d s)"), writes=[pr])
    sm = P.sb([128, 16, 16], F32, "sm")
    P.memset(sm[:], 0.0, writes=[sm])
    S = lambda i: sm[:, i, :]
    R, Wr = [pr, sm], [sm]
    lre, lim, lst = pr[:, 0, :], pr[:, 1, :], pr[:, 2, :]
    P.act(S(0), lst, AF.Exp, reads=R, writes=Wr)
    P.tt(S(1), lre, S(0), ALU.mult, reads=R, writes=Wr)
    P.tt(S(2), lim, S(0), ALU.mult, reads=R, writes=Wr)
    P.act(S(3), S(1), AF.Exp, reads=R, writes=Wr)
    sincos(P, S(4), S(5), S(2), S(6), S(7), R, Wr)
    P.tt(S(8), S(3), S(5), ALU.mult, reads=R, writes=Wr)
    P.ts(S(8), S(8), -1.0, ALU.add, reads=R, writes=Wr)
    P.tt(S(9), S(3), S(4), ALU.mult, reads=R, writes=Wr)
    P.tt(S(6), lre, lre, ALU.mult, reads=R, writes=Wr)
    P.tt(S(7), lim, lim, ALU.mult, reads=R, writes=Wr)
    P.tt(S(6), S(6), S(7), ALU.add, reads=R, writes=Wr)
    P.op('dve', lambda e: e.reciprocal(S(6), S(6)), reads=R, writes=Wr)
    P.tt(S(10), S(8), lre, ALU.mult, reads=R, writes=Wr)
    P.tt(S(7), S(9), lim, ALU.mult, reads=R, writes=Wr)
    P.tt(S(10), S(10), S(7), ALU.add, reads=R, writes=Wr)
    P.tt(S(10), S(10), S(6), ALU.mult, reads=R, writes=Wr)
    P.tt(S(11), S(9), lre, ALU.mult, reads=R, writes=Wr)
    P.tt(S(7), S(8), lim, ALU.mult, reads=R, writes=Wr)
    P.tt(S(11), S(11), S(7), ALU.subtract, reads=R, writes=Wr)
    P.tt(S(11), S(11), S(6), ALU.mult, reads=R, writes=Wr)
    P.ts(S(12), S(10), -1.0, ALU.mult, reads=R, writes=Wr)
    TH, RR, CFR, CFI, NCFR = 2, 3, 10, 11, 12
    if dbg:
        dsm = P.dram("dsm", [128, 16, 16], kind="ExternalOutput")
        P.dma(dsm[:], sm[:], reads=[sm], writes=[dsm], is_output=True)
        dtab = P.dram("dtab", [5, 128, LC], kind="ExternalOutput")
        dk = P.dram("dk", [6, 128, LC], kind="ExternalOutput")
    tabs = [[P.sb([128, LC], F32, f"tab{i}_{j}") for j in range(5)] for i in range(2)]
    tmpa = P.sb([128, LC], F32, "tmpa"); tmpb = P.sb([128, LC], F32, "tmpb")
    ones = P.sb([128, LC], F32, "ones"); P.memset(ones[:], 1.0, writes=[ones])
    wk = [[P.sb([128, LC], F32, f"wk{i}_{j}") for j in range(8)] for i in range(2)]
    hb = [[P.sb([128, LC], BF16, f"hb{i}_{j}") for j in range(2)] for i in range(2)]
    ini = P.sb([128, 4], F32, "ini")
    chunks_f = [(0, 256)] + [(256 + i * LC, LC) for i in range(16)]
    ci = 0
    for d in range(2):
        for st in range(8):
            col = d * 8 + st
            ct = st // 4
            Ere, Eim, Tre, Tim, rf = tabs[col % 2]
            tb = tabs[col % 2]
            P.ts(tmpa[:], tau1[:], sm[:, TH, col:col + 1], ALU.mult, reads=[tau1, sm], writes=[tmpa])
            sincos(P, Eim[:], Ere[:], tmpa[:], tmpb[:], Tre[:], [tmpa], [tmpb, Tre, Eim, Ere])
            P.ts(tmpb[:], Ere[:], sm[:, CFR, col:col + 1], ALU.mult, reads=[Ere, sm], writes=[tmpb])
            P.stt(Tre[:], Eim[:], sm[:, CFI, col:col + 1], tmpb[:], ALU.mult, ALU.add, reads=[Eim, sm, tmpb], writes=[Tre])
            P.ts(tmpb[:], Ere[:], sm[:, CFI, col:col + 1], ALU.mult, reads=[Ere, sm], writes=[tmpb])
            P.stt(Tim[:], Eim[:], sm[:, NCFR, col:col + 1], tmpb[:], ALU.mult, ALU.add, reads=[Eim, sm, tmpb], writes=[Tim])
            P.ts(rf[:], ones[:], sm[:, RR, col:col + 1], ALU.mult, reads=[ones, sm], writes=[rf])
            if dbg and col == 0:
                for i5 in range(5):
                    P.dma(dtab[i5], tb[i5][:], reads=[tb[i5]], writes=[P.view(dtab)], is_output=True)
            seq = [chunks_f[0]] + (chunks_f[1:] if d == 0 else chunks_f[:0:-1])
            for qi, (c0, n) in enumerate(seq):
                w = wk[ci % 2]
                hh = hb[ci % 2]
                ci += 1
                kinr, kini, kr, ki, t1, t2, t3, t4 = w
                if d == 0:
                    tsl = lambda a: a[:, 0:n]
                    dsl = lambda a: a[:, 0:n]
                    last = n - 1
                else:
                    tsl = lambda a: a[:, n - 1::-1] if n < LC else a[:, ::-1]
                    dsl = lambda a: a[:, n - 1::-1] if n < LC else a[:, ::-1]
                    last = 0
                pr_, pi_ = rot(), rot()
                P.mm(pr_[:, 0:n], Bb[:, d, 0, st, :], ub[:, ct, c0:c0 + n], reads=[Bb, ub], writes=[pr_])
                P.mm(pi_[:, 0:n], Bb[:, d, 1, st, :], ub[:, ct, c0:c0 + n], reads=[Bb, ub], writes=[pi_])
                P.tt(t1[:, 0:n], pr_[:, 0:n], tsl(Tre), ALU.mult, reads=[pr_, Tre], writes=[t1])
                P.tt(t2[:, 0:n], pi_[:, 0:n], tsl(Tim), ALU.mult, reads=[pi_, Tim], writes=[t2])
                P.tt(kinr[:, 0:n], t1[:, 0:n], t2[:, 0:n], ALU.subtract, reads=[t1, t2], writes=[kinr], eng='pool')
                P.tt(t3[:, 0:n], pi_[:, 0:n], tsl(Tre), ALU.mult, reads=[pi_, Tre], writes=[t3])
                P.tt(t4[:, 0:n], pr_[:, 0:n], tsl(Tim), ALU.mult, reads=[pr_, Tim], writes=[t4])
                P.tt(kini[:, 0:n], t3[:, 0:n], t4[:, 0:n], ALU.add, reads=[t3, t4], writes=[kini], eng='pool')
                i_re = 0.0 if qi == 0 else ini[:, 0:1]
                i_im = 0.0 if qi == 0 else ini[:, 1:2]
                P.scan(dsl(kr), rf[:, 0:n], dsl(kinr), i_re, reads=[rf, kinr, ini], writes=[kr])
                P.scan(dsl(ki), rf[:, 0:n], dsl(kini), i_im, reads=[rf, kini, ini], writes=[ki])
                if qi < len(seq) - 1:
                    krl, kil = kr[:, last:last + 1], ki[:, last:last + 1]
                    erl, eil = Ere[:, n - 1:n], Eim[:, n - 1:n]
                    P.tt(ini[:, 2:3], kil, eil, ALU.mult, reads=[ki, Eim], writes=[ini])
                    P.stt(ini[:, 0:1], krl, erl, ini[:, 2:3], ALU.mult, ALU.subtract, reads=[kr, Ere, ini], writes=[ini])
                    P.tt(ini[:, 3:4], kil, erl, ALU.mult, reads=[ki, Ere], writes=[ini])
                    P.stt(ini[:, 1:2], krl, eil, ini[:, 3:4], ALU.mult, ALU.add, reads=[kr, Eim, ini], writes=[ini])
                P.tt(t1[:, 0:n], kr[:, 0:n], tsl(Ere), ALU.mult, reads=[kr, Ere], writes=[t1], eng='pool')
                P.tt(t2[:, 0:n], ki[:, 0:n], tsl(Eim), ALU.mult, reads=[ki, Eim], writes=[t2], eng='pool')
                P.tt(hh[0][:, 0:n], t1[:, 0:n], t2[:, 0:n], ALU.subtract, reads=[t1, t2], writes=[hh[0]], eng='pool')
                P.tt(t3[:, 0:n], kr[:, 0:n], tsl(Eim), ALU.mult, reads=[kr, Eim], writes=[t3], eng='pool')
                P.tt(t4[:, 0:n], ki[:, 0:n], tsl(Ere), ALU.mult, reads=[ki, Ere], writes=[t4])
                P.tt(hh[1][:, 0:n], t3[:, 0:n], t4[:, 0:n], ALU.add, reads=[t3, t4], writes=[hh[1]])
                if dbg and col == 0 and qi == 0:
                    for i6, bb in enumerate([kinr, kini, kr, ki]):
                        P.dma(dk[i6][:, 0:256], bb[:, 0:256], reads=[bb], writes=[P.view(dk)], is_output=True)
                py = rot()
                P.mm(py[:, 0:n], Cb[:, d, 0, st, :], hh[0][:, 0:n], start=True, stop=False, reads=[Cb, hh[0]], writes=[py])
                P.mm(py[:, 0:n], Cb[:, d, 1, st, :], hh[1][:, 0:n], start=False, stop=True, reads=[Cb, hh[1]], writes=[py])
                P.tt(yacc[:, ct, c0:c0 + n], yacc[:, ct, c0:c0 + n], py[:, 0:n], ALU.add, reads=[py], writes=[yacc])
    for ct in range(2):
        for s in range(0, NTOK5, 2112):
            P.dma(Y[ct * 128:(ct + 1) * 128, s:s + 2112], yacc[:, ct, s:s + 2112], reads=[yacc], writes=[P.view(Y)], is_output=True)
    return P.build()


def host_L3(d, r1):
    ins = []
    lre = d['s5_lambda_re'][0]; lim = d['s5_lambda_im'][0]; lst = d['s5_log_step'][0]
    bre = d['s5_b_re'][0]; bim = d['s5_b_im'][0]; cre = d['s5_c_re'][0]; cim = d['s5_c_im'][0]
    tau = np.broadcast_to(np.arange(1, LC + 1, dtype=np.float32)[None, :], (128, LC)).copy()
    for j in range(8):
        b, gh = j // 2, j % 2
        a, c = r1[2 * b], r1[2 * b + 1]
        rows = slice(256 * gh, 256 * gh + 256)
        uT = np.concatenate([a['ST'][rows, 4096:], c['ST'][rows, 4096:], a['ST'][rows, :4096], c['ST'][rows, :4096]], 1)
        prm = np.zeros((128, 3, 2, 8), np.float32)
        Bm = np.zeros((128, 2, 2, 8, 128), np.float32)
        Cm = np.zeros((128, 2, 2, 8, 128), np.float32)
        for dr in range(2):
            for st in range(8):
                for gm in range(2):
                    g = 16 * gh + 2 * st + gm
                    ps = slice(gm * 64, gm * 64 + 64)
                    prm[ps, 0, dr, st] = lre[dr, g]; prm[ps, 1, dr, st] = lim[dr, g]; prm[ps, 2, dr, st] = lst[dr, g]
                    gl = (2 * st + gm) % 8
                    ks = slice(gl * 16, gl * 16 + 16)
                    Bm[ks, dr, 0, st, ps] = bre[dr, g].T
                    Bm[ks, dr, 1, st, ps] = bim[dr, g].T
                    Cm[ps, dr, 0, st, ks] = cre[dr, g].T
                    Cm[ps, dr, 1, st, ks] = cim[dr, g].T
        ins.append({"uT": np.ascontiguousarray(uT), "prm": prm, "Bm": Bm, "Cm": Cm,
                    "dsk": np.ascontiguousarray(d['s5_d'][0][rows].reshape(2, 128)), "tau": tau, "ident": np.eye(128, dtype=np.float32)})
    return ins


def xio(P, xs, nt, t0, src, sview=None, dst=None, dview=None, load=True, out=False):
    for t in range(nt):
        rows = slice((t0 + t) * 128, (t0 + t + 1) * 128)
        if load:
            P.dma(xs[t][:], src[rows, :], reads=[sview] if sview is not None else [], writes=[xs[t]])
        else:
            P.dma(dst[rows, :], xs[t][:], reads=[xs[t]], writes=[dview], is_output=out)


def pass_ffn(P, K, xs, src, sviews, dst, dviews, wg, wu, wd, sce, sh, gbc, out=False, post=None, blocks=None):
    P.push()
    W = ffn_alloc(P)
    ffn_load(P, W, wg, wu, wd)
    for bi, (t0, nt, n) in enumerate(blocks or BLOCKS):
        xio(P, xs, nt, t0, src, sviews[bi] if sviews else None)
        ffn_block(P, K, W, xs, nt, n, sce, sh, gbc)
        if post is not None:
            post(W, xs, nt)
        xio(P, xs, nt, t0, None, None, dst, dviews[bi], load=False, out=out)
    P.pop()


def pass_mixout(P, K, xs, src, sviews, dst, dviews, wout, gbc, catsrc, glu=None, blocks=None):
    P.push()
    wo = P.sb([128, 8, D], BF16, "wout")
    load_w(P, wo, wout)
    cat = P.sb([128, 8, 512], BF16, "cat")
    tmp = P.sb([128, 512], F32, "tmpm")
    if glu is not None:
        wgl = P.sb([128, 4, 512], BF16, "wglu")
        load_w(P, wgl, glu[0])
        bgl = P.sb([128, 4], F32, "bglu")
        load_fm(P, K, bgl, bgl[:], glu[1].t.rearrange("(c p) -> c p", p=128), 4)
        yp = P.sb([128, 4, 512], F32, "yp")
        yg = P.sb([128, 4, 512], BF16, "yg")
        sg = P.sb([128, 512], BF16, "sg")
    for bi, (t0, nt, n) in enumerate(blocks or BLOCKS):
        N = nt * 128
        tok = slice(t0 * 128, t0 * 128 + N)
        xio(P, xs, nt, t0, src, sviews[bi] if sviews else None)
        for kc in range(8):
            if glu is not None and kc >= 4:
                P.dma(yp[:, kc - 4, 0:N], catsrc[kc][:, tok], writes=[yp])
            else:
                P.dma(cat[:, kc, 0:N], catsrc[kc][:, tok], writes=[cat], q='pool')
        if glu is not None:
            for kc in range(4):
                P.act(yg[:, kc, 0:N], yp[:, kc, 0:N], AF.Gelu, reads=[yp], writes=[yg])
            for oc in range(4):
                pb = rot(K)
                for kc in range(4):
                    P.mm(pb[:, 0:N], wgl[:, kc, oc * 128:(oc + 1) * 128], yg[:, kc, 0:N], start=(kc == 0), stop=(kc == 3),
                         reads=[wgl, yg], writes=[pb])
                P.act(sg[:, 0:N], pb[:, 0:N], AF.Sigmoid, bias=bgl[:, oc:oc + 1], reads=[pb, bgl], writes=[sg])
                P.tt(cat[:, 4 + oc, 0:N], sg[:, 0:N], yg[:, oc, 0:N], ALU.mult, reads=[sg, yg], writes=[cat])
        for t in range(nt):
            for hf in range(2):
                py = rot(K)
                for kc in range(8):
                    P.mm(py[:], cat[:, kc, t * 128:(t + 1) * 128], wo[:, kc, hf * 512:(hf + 1) * 512], start=(kc == 0), stop=(kc == 7),
                         reads=[cat, wo], writes=[py])
                P.tt(tmp[:], py[:], gbc[:, n, hf * 512:(hf + 1) * 512], ALU.mult, reads=[py, gbc], writes=[tmp])
                xo = xs[t][:, hf * 512:(hf + 1) * 512]
                P.tt(xo, xo, tmp[:], ALU.add, reads=[xs[t], tmp], writes=[xs[t]], eng='pool')
        xio(P, xs, nt, t0, None, None, dst, dviews[bi], load=False)
    P.pop()


OD_IN = 2560


def pass_inproj_odd(P, K, xs, src, sviews, win, sce, sh, UT):
    P.push()
    W = FFNW()
    W.win = P.sb([128, 8, OD_IN], BF16, "winod")
    load_w(P, W.win, win)
    W.xn = [P.sb([128, D], BF16, f"xn{t}") for t in range(4)]
    W.ss = P.sb([128, 4], F32, "ss"); W.rstd = P.sb([128, 4], F32, "rstd")
    W.hT = P.sb([128, 8, 512], BF16, "hT")
    W.st = [P.sb([128, 512], F32, f"st{i}") for i in range(3)]
    W.sti = 0
    for bi, (t0, nt, n) in enumerate(BLOCKS):
        N = nt * 128
        xio(P, xs, nt, t0, src, sviews[bi] if sviews else None)
        norm_T(P, K, W, xs, nt, n, sce, sh, W.hT)
        for oc in range(OD_IN // 128):
            pc = rot(K)
            for c in range(8):
                P.mm(pc[:, 0:N], W.win[:, c, oc * 128:(oc + 1) * 128], W.hT[:, c, 0:N], start=(c == 0), stop=(c == 7),
                     reads=[W.win, W.hT], writes=[pc])
            st = stage(W)
            P.copy(st[:, 0:N], pc[:, 0:N], reads=[pc], writes=[st], eng=('act' if oc % 2 else 'dve'))
            P.dma(UT[oc * 128:(oc + 1) * 128, t0 * 128:t0 * 128 + N], st[:, 0:N], reads=[st], writes=[P.view(UT)], is_output=True)
    P.pop()


def build_L4():
    P = Prog()
    NTOK = NT_CORE * 128
    xin = P.dram("xin", [NTOK, D]); cnd = P.dram("cnd", [2, D])
    mw0 = P.dram("mw0", [D, 9 * D]); mb0 = P.dram("mb0", [9 * D]); mw1 = P.dram("mw1", [D, 9 * D]); mb1 = P.dram("mb1", [9 * D])
    OT = P.dram("OT", [512, NTOK]); YT = P.dram("YT", [512, NTOK])
    wglu = P.dram("wglu", [512, 512]); bglu = P.dram("bglu", [512]); wout = P.dram("wout", [D, D])
    g2 = P.dram("g2", [D]); wg2 = P.dram("wg2", [D, FH]); wu2 = P.dram("wu2", [D, FH]); wd2 = P.dram("wd2", [FH, D])
    g1 = P.dram("g1", [D]); wg1 = P.dram("wg1", [D, FH]); wu1 = P.dram("wu1", [D, FH]); wd1 = P.dram("wd1", [FH, D])
    gm = P.dram("gm", [D]); win = P.dram("win", [D, OD_IN])
    sa = P.dram("scr_a", [NTOK, D], kind="Internal"); sbb = P.dram("scr_b", [NTOK, D], kind="Internal")
    xout = P.dram("xout", [NTOK, D], kind="ExternalOutput")
    UT = P.dram("UT", [OD_IN, NTOK], kind="ExternalOutput")
    K = consts(P)
    xs = [P.sb([128, D], F32, f"x{t}") for t in range(4)]
    va = [P.view(sa) for _ in BLOCKS]; vb = [P.view(sbb) for _ in BLOCKS]; vo = [P.view(xout) for _ in BLOCKS]
    P.push()
    mfm, gbc = adaln(P, K, cnd, mw0, mb0, [5, 8])
    sce2, sh2 = norm_params(P, K, mfm, g2, 6, "n2")
    cats = [OT[kc * 128:(kc + 1) * 128, :] for kc in range(4)] + [YT[kc * 128:(kc + 1) * 128, :] for kc in range(4)]
    pass_mixout(P, K, xs, xin, None, sa, va, wout, gbc[5], cats, glu=(wglu, bglu))
    pass_ffn(P, K, xs, sa, va, sbb, vb, wg2, wu2, wd2, sce2, sh2, gbc[8])
    P.pop()
    mfm1, gbc1 = adaln(P, K, cnd, mw1, mb1, [2])
    sce1, sh1 = norm_params(P, K, mfm1, g1, 0, "n1b")
    scem, shm = norm_params(P, K, mfm1, gm, 3, "nmb")
    pass_ffn(P, K, xs, sbb, vb, xout, vo, wg1, wu1, wd1, sce1, sh1, gbc1[2], out=True)
    pass_inproj_odd(P, K, xs, xout, vo, win, scem, shm, UT)
    return P.build()


def tok_cols(full_b, hf, nlat=8192, nctx=256):
    return np.concatenate([full_b[:, nctx + hf * 4096:nctx + (hf + 1) * 4096], full_b[:, hf * 128:(hf + 1) * 128]], 1)


def host_L4(d, r1, r2, r3):
    ins = []
    for j in range(8):
        b, hf = j // 2, j % 2
        O = np.concatenate([r2[2 * b]['O'], r2[2 * b + 1]['O']], 1)
        Oc = np.concatenate([O[hf * 4096:(hf + 1) * 4096], O[8192 + hf * 128:8192 + (hf + 1) * 128]], 0)
        Yf = np.concatenate([r3[2 * b]['Y'], r3[2 * b + 1]['Y']], 0)
        ins.append({"xin": r1[j]['xout'], "cnd": np.stack([d['c'][b], d['c_ctx']]),
                    "mw0": d['mod_w'][0], "mb0": d['mod_b'][0], "mw1": d['mod_w'][1], "mb1": d['mod_b'][1],
                    "OT": np.ascontiguousarray(Oc.T), "YT": np.ascontiguousarray(tok_cols(Yf, hf)),
                    "wglu": d['s5_w_glu'][0], "bglu": d['s5_b_glu'][0], "wout": d['ev_w_out'][0],
                    "g2": d['norm_ffn2'][0], "wg2": d['ffn2_w_gate'][0], "wu2": d['ffn2_w_up'][0], "wd2": d['ffn2_w_down'][0],
                    "g1": d['norm_ffn1'][1], "wg1": d['ffn1_w_gate'][1], "wu1": d['ffn1_w_up'][1], "wd1": d['ffn1_w_down'][1],
                    "gm": d['norm_mix'][1], "win": d['od_w_in'][0], "ident": np.eye(128, dtype=np.float32)})
    return ins


def build_L6():
    P = Prog()
    NL, NC = 8192, 256
    xT = P.dram("xT", [256, NC + NL]); gT = P.dram("gT", [256, NL])
    cw = P.dram("cw", [128, 2, 4]); vec = P.dram("vec", [128, 7, 2, 2])
    Wm = P.dram("Wm", [128, 2, 2, 2, 128])
    RT = P.dram("RT", [256, NL], kind="ExternalOutput")
    pbs = [P.ps([128, 512], F32, f"pb{i}") for i in range(8)]
    rr = [0]

    def rot():
        rr[0] += 1
        return pbs[rr[0] % 8]
    Wb = P.sb([128, 2, 2, 2, 128], BF16, "Wb")
    P.dma(Wb[:], Wm[:], writes=[Wb], q='pool')
    cws = P.sb([128, 2, 4], F32, "cws"); P.dma(cws[:], cw[:], writes=[cws])
    vs = P.sb([128, 7, 2, 2], F32, "vs"); P.dma(vs[:], vec[:], writes=[vs])
    c8 = P.sb([128, 2, 2], F32, "c8")
    P.act(c8[:], vs[:, 3], AF.Exp, scale=-1.0, reads=[vs], writes=[c8])
    P.act(c8[:], c8[:], AF.Ln, bias=1.0, reads=[c8], writes=[c8])
    P.ts(c8[:], c8[:], -8.0, ALU.mult, reads=[c8], writes=[c8])
    OFFC, OFFL = 2, 2 + NC + 1 + 2
    TOT = OFFL + NL + 1
    xc = P.sb([128, 2, NC + NL], F32, "xc"); xcb = P.sb([128, 2, NC + NL], BF16, "xcb")
    P.push()
    xp = P.sb([128, 2, TOT], F32, "xp")
    P.memset(xp[:, :, 0:2], 0.0, writes=[xp]); P.memset(xp[:, :, OFFC + NC:OFFL], 0.0, writes=[xp]); P.memset(xp[:, :, OFFL + NL:TOT], 0.0, writes=[xp])
    for ct in range(2):
        P.dma(xp[:, ct, OFFC:OFFC + NC], xT[ct * 128:(ct + 1) * 128, 0:NC], writes=[xp])
        for s in range(0, NL, 2048):
            P.dma(xp[:, ct, OFFL + s:OFFL + s + 2048], xT[ct * 128:(ct + 1) * 128, NC + s:NC + s + 2048], writes=[xp])
    for ct in range(2):
        for (o_in, o_out, n) in [(OFFC, 0, NC)] + [(OFFL + s, NC + s, 2048) for s in range(0, NL, 2048)]:
            dst = xc[:, ct, o_out:o_out + n]
            eng = 'dve' if ct == 0 else 'pool'
            P.ts(dst, xp[:, ct, o_in + 1:o_in + 1 + n], cws[:, ct, 3:4], ALU.mult, vs[:, 0, 0, ct:ct + 1], ALU.add, reads=[xp, cws, vs], writes=[xc], eng=eng)
            for k, sh in ((2, 0), (1, -1), (0, -2)):
                P.stt(dst, xp[:, ct, o_in + sh:o_in + sh + n], cws[:, ct, k:k + 1], dst, ALU.mult, ALU.add, reads=[xp, cws, xc], writes=[xc], eng=eng)
            P.copy(xcb[:, ct, o_out:o_out + n], dst, reads=[xc], writes=[xcb], eng='act')
    P.pop()
    yacc = P.sb([128, 2, NL], F32, "yacc")
    wk = [[P.sb([128, 512], F32, f"lw{i}_{j}") for j in range(5)] for i in range(2)]
    chunks = [(0, NC)] + [(NC + i * 512, 512) for i in range(16)]
    ci = 0
    for d in range(2):
        for ct in range(2):
            seq = [chunks[0]] + (chunks[1:] if d == 0 else chunks[:0:-1])
            prev_h = None
            for qi, (c0, n) in enumerate(seq):
                a_, ig, bc, bin_, h = wk[ci % 2]
                ci += 1
                rv = (lambda ap: ap[:, 0:n]) if d == 0 else (lambda ap: ap[:, n - 1::-1] if n < 512 else ap[:, ::-1])
                pa, px = rot(), rot()
                P.mm(pa[:, 0:n], Wb[:, 0, d, ct, :], xcb[:, ct, c0:c0 + n], reads=[Wb, xcb], writes=[pa])
                P.mm(px[:, 0:n], Wb[:, 1, d, ct, :], xcb[:, ct, c0:c0 + n], reads=[Wb, xcb], writes=[px])
                P.act(a_[:, 0:n], pa[:, 0:n], AF.Sigmoid, bias=vs[:, 1, d, ct:ct + 1], reads=[pa, vs], writes=[a_])
                P.act(ig[:, 0:n], px[:, 0:n], AF.Sigmoid, bias=vs[:, 2, d, ct:ct + 1], reads=[px, vs], writes=[ig])
                P.act(a_[:, 0:n], a_[:, 0:n], AF.Exp, scale=c8[:, d, ct:ct + 1], reads=[a_, c8], writes=[a_])
                P.tt(bc[:, 0:n], a_[:, 0:n], a_[:, 0:n], ALU.mult, reads=[a_], writes=[bc], eng='pool')
                P.act(bc[:, 0:n], bc[:, 0:n], AF.Sqrt, scale=-1.0, bias=1.0, reads=[bc], writes=[bc])
                P.tt(bin_[:, 0:n], ig[:, 0:n], xc[:, ct, c0:c0 + n], ALU.mult, reads=[ig, xc], writes=[bin_], eng='pool')
                P.tt(bin_[:, 0:n], bin_[:, 0:n], bc[:, 0:n], ALU.mult, reads=[bin_, bc], writes=[bin_])
                init = 0.0 if qi == 0 else prev_h
                rds = [a_, bin_] + ([prev_hb] if qi else [])
                P.scan(rv(h), rv(a_), rv(bin_), init, reads=rds, writes=[h])
                last = n - 1 if d == 0 else 0
                prev_h, prev_hb = h[:, last:last + 1], h
                if qi > 0:
                    o = c0 - NC
                    if d == 0:
                        P.copy(yacc[:, ct, o:o + n], h[:, 0:n], reads=[h], writes=[yacc], eng='pool')
                    else:
                        P.tt(yacc[:, ct, o:o + n], yacc[:, ct, o:o + n], h[:, 0:n], ALU.add, reads=[h], writes=[yacc], eng='pool')
    gt = [P.sb([128, 2048], F32, f"gt{i}") for i in range(2)]
    i = 0
    for ct in range(2):
        for s in range(0, NL, 2048):
            g = gt[i % 2]; i += 1
            P.dma(g[:], gT[ct * 128:(ct + 1) * 128, s:s + 2048], writes=[g])
            P.act(g[:], g[:], AF.Gelu, reads=[g], writes=[g])
            P.tt(g[:], g[:], yacc[:, ct, s:s + 2048], ALU.mult, reads=[g, yacc], writes=[g])
            P.dma(RT[ct * 128:(ct + 1) * 128, s:s + 2048], g[:], reads=[g], writes=[P.view(RT)], is_output=True)
    return P.build()


def host_L6(d, r4):
    ins = []
    cwf = d['lru_conv_w'][0]; cbf = d['lru_conv_b'][0]
    for j in range(8):
        b, chh = j // 2, j % 2
        a, c = r4[2 * b], r4[2 * b + 1]
        rows = slice(1536 + 256 * chh, 1536 + 256 * chh + 256)
        grows = slice(2048 + 256 * chh, 2048 + 256 * chh + 256)
        xT = np.concatenate([a['UT'][rows, 4096:], c['UT'][rows, 4096:], a['UT'][rows, :4096], c['UT'][rows, :4096]], 1)
        gT = np.concatenate([a['UT'][grows, :4096], c['UT'][grows, :4096]], 1)
        chs = slice(256 * chh, 256 * chh + 256)
        cw = np.ascontiguousarray(cwf[:, chs].reshape(4, 2, 128).transpose(2, 1, 0))
        vec = np.zeros((128, 7, 2, 2), np.float32)
        vec[:, 0, 0, :] = cbf[chs].reshape(2, 128).T
        for dr in range(2):
            vec[:, 1, dr, :] = d['lru_b_a'][0][dr, chs].reshape(2, 128).T
            vec[:, 2, dr, :] = d['lru_b_x'][0][dr, chs].reshape(2, 128).T
            vec[:, 3, dr, :] = d['lru_lambda'][0][dr, chs].reshape(2, 128).T
        Wm = np.zeros((128, 2, 2, 2, 128), np.float32)
        for ai, wsrc in enumerate([d['lru_w_a'][0], d['lru_w_x'][0]]):
            for dr in range(2):
                for ct in range(2):
                    for hb in range(2):
                        blk = 4 * chh + 2 * ct + hb
                        Wm[hb * 64:(hb + 1) * 64, ai, dr, ct, hb * 64:(hb + 1) * 64] = wsrc[dr, blk]
        ins.append({"xT": np.ascontiguousarray(xT), "gT": np.ascontiguousarray(gT), "cw": cw, "vec": vec, "Wm": Wm})
    return ins


NFFT = 16384
HY_G = 4


def hyena_consts():
    n = 8192
    t = np.linspace(0.0, 1.0, n, dtype=np.float32)[:, None]
    w = (2.0 * np.pi * np.arange(n, dtype=np.float32)[:, None] / n).astype(np.float32)
    bands = np.linspace(1e-4, 15, 16, dtype=np.float32)[None, :]
    z = np.concatenate([t, np.cos(bands * w), -np.sin(bands * w)], -1).astype(np.float32)
    idx = np.concatenate([np.arange(n), [0], np.arange(n - 1, 0, -1)])
    zc = np.ascontiguousarray(z[idx].T)
    tcirc = t[idx, 0].copy()
    tcirc[n] = 1.0e4
    tc = np.broadcast_to(tcirc[None, :], (128, NFFT)).copy()
    hmin, hmax = np.log(1e-2) / 1.5, np.log(1e-2) / 0.3
    deltas = np.abs(np.linspace(hmin, hmax, 512, dtype=np.float32))
    k = np.arange(128)
    ang = 2.0 * np.pi * np.outer(k, k) / 128.0
    Wr, Wi = np.cos(ang).astype(np.float32), (-np.sin(ang)).astype(np.float32)
    angt = 2.0 * np.pi * np.outer(k, k) / NFFT
    twr, twi = np.cos(angt).astype(np.float32), (-np.sin(angt)).astype(np.float32)
    rep = lambda m: np.ascontiguousarray(np.tile(m, (1, HY_G)))
    C = {"zc": zc, "tc": tc, "W1": np.concatenate([Wr, Wi], 1), "CW1": np.concatenate([Wr, -Wi], 1), "CW2": np.concatenate([Wi, Wr], 1),
         "W3": np.stack([Wr, Wi, -Wi], 1), "tw": np.stack([rep(twr), rep(twi), rep(-twi)], 1)}
    return C, deltas


def build_L5():
    P = Prog()
    NL = 8192
    hT = P.dram("hT", [3, 256, NL])
    cw = P.dram("cw", [128, 3, 2, 3]); cb = P.dram("cb", [128, 3, 2])
    zc = P.dram("zc", [33, NFFT]); tc = P.dram("tc", [128, NFFT])
    w1 = P.dram("w1", [33, 64]); w2 = P.dram("w2", [64, 64]); bf = P.dram("bf", [64, 4]); w3 = P.dram("w3", [64, 2, 256])
    chv = P.dram("chv", [128, 2, 2])
    W1d = P.dram("W1", [128, 256]); CW1d = P.dram("CW1", [128, 256]); CW2d = P.dram("CW2", [128, 256])
    W3d = P.dram("W3", [128, 3, 128]); twd = P.dram("tw", [128, 3, 512])
    Fs = P.dram("Fs", [256, NFFT], kind="Internal"); Zs = P.dram("Zs", [256, NL], kind="Internal")
    X0s = P.dram("X0s", [256, NL], kind="Internal"); Ys = P.dram("Ys", [256, NL], kind="Internal")
    HYT = P.dram("HYT", [256, NL], kind="ExternalOutput")
    pbs = [P.ps([128, 1024], F32, f"pq{i}") for i in range(4)]
    rr = [0]

    def rot():
        rr[0] += 1
        return pbs[rr[0] % 4]
    cws = P.sb([128, 3, 2, 3], F32, "cws"); P.dma(cws[:], cw[:], writes=[cws])
    cbs = P.sb([128, 3, 2], F32, "cbs"); P.dma(cbs[:], cb[:], writes=[cbs])
    chs = P.sb([128, 2, 2], F32, "chs"); P.dma(chs[:], chv[:], writes=[chs])
    zviews, xviews = [], []
    P.push()
    xp = [P.sb([128, NL + 2], F32, f"xp{i}") for i in range(2)]
    cv = [P.sb([128, NL], F32, f"cv{i}") for i in range(2)]
    for b_ in xp:
        P.memset(b_[:, 0:1], 0.0, writes=[b_]); P.memset(b_[:, NL + 1:NL + 2], 0.0, writes=[b_])
    k = 0
    for ct in range(2):
        for part in (1, 2, 0):
            x_ = xp[k % 2]; k += 1
            for s in range(0, NL, 2048):
                P.dma(x_[:, 1 + s:1 + s + 2048], hT[part, ct * 128:(ct + 1) * 128, s:s + 2048], writes=[x_])
            dst = cv[0] if part != 2 else cv[1]
            eng = 'dve' if part != 2 else 'pool'
            for s in range(0, NL, 2048):
                o = dst[:, s:s + 2048]
                P.ts(o, x_[:, s:s + 2048], cws[:, part, ct, 0:1], ALU.mult, cbs[:, part, ct:ct + 1], ALU.add, reads=[x_, cws, cbs], writes=[dst], eng=eng)
                P.stt(o, x_[:, s + 1:s + 1 + 2048], cws[:, part, ct, 1:2], o, ALU.mult, ALU.add, reads=[x_, cws, dst], writes=[dst], eng=eng)
                P.stt(o, x_[:, s + 2:s + 2 + 2048], cws[:, part, ct, 2:3], o, ALU.mult, ALU.add, reads=[x_, cws, dst], writes=[dst], eng=eng)
            if part == 2:
                P.tt(cv[1][:], cv[1][:], cv[0][:], ALU.mult, reads=[cv[0], cv[1]], writes=[cv[1]])
                v_ = P.view(Zs); zviews.append(v_)
                P.dma(Zs[ct * 128:(ct + 1) * 128, :], cv[1][:], reads=[cv[1]], writes=[v_])
            if part == 0:
                v_ = P.view(X0s); xviews.append(v_)
                P.dma(X0s[ct * 128:(ct + 1) * 128, :], cv[0][:], reads=[cv[0]], writes=[v_])
    P.pop()
    fviews = []
    P.push()
    w1s = P.sb([33, 64], F32, "w1s"); P.dma(w1s[:], w1[:], writes=[w1s])
    w2s = P.sb([64, 64], F32, "w2s"); P.dma(w2s[:], w2[:], writes=[w2s])
    w3s = P.sb([64, 2, 256], F32, "w3s"); P.dma(w3s[:], w3[:], writes=[w3s])
    bfs = P.sb([64, 4], F32, "bfs"); P.dma(bfs[:], bf[:], writes=[bfs])
    zq = [P.sb([33, 512], F32, f"zq{i}") for i in range(2)]
    tq = [P.sb([128, 512], F32, f"tq{i}") for i in range(2)]
    ar = P.sb([64, 512], F32, "ar"); tk = P.sb([64, 512], F32, "tk"); tz = P.sb([64, 512], F32, "tz")
    h1 = P.sb([64, 512], F32, "h1"); h2 = P.sb([64, 512], F32, "h2")
    dec = [P.sb([128, 512], F32, f"dec{i}") for i in range(2)]
    fst = [P.sb([128, 512], F32, f"fst{i}") for i in range(2)]
    for q in range(NFFT // 512):
        z_ = zq[q % 2]; t_ = tq[q % 2]
        cols = slice(q * 512, (q + 1) * 512)
        P.dma(z_[:], zc[:, cols], writes=[z_]); P.dma(t_[:], tc[:, cols], writes=[t_])
        p1 = rot()
        P.mm(p1[0:64, 0:512], w1s[:], z_[:], reads=[w1s, z_], writes=[p1])
        P.ts(ar[:], p1[0:64, 0:512], bfs[:, 0:1], ALU.add, bfs[:, 1:2], ALU.mult, reads=[p1, bfs], writes=[ar])
        sincos(P, h1[:], None, ar[:], tk[:], tz[:], [ar], [tk, tz, h1])
        p2 = rot()
        P.mm(p2[0:64, 0:512], w2s[:], h1[:], reads=[w2s, h1], writes=[p2])
        P.ts(ar[:], p2[0:64, 0:512], bfs[:, 2:3], ALU.add, bfs[:, 3:4], ALU.mult, reads=[p2, bfs], writes=[ar])
        sincos(P, h2[:], None, ar[:], tk[:], tz[:], [ar], [tk, tz, h2])
        dr = 0 if q < 16 else 1
        for ct in range(2):
            p3 = rot()
            P.mm(p3[:, 0:512], w3s[:, dr, ct * 128:(ct + 1) * 128], h2[:], reads=[w3s, h2], writes=[p3])
            d_ = dec[ct]; f_ = fst[ct]
            P.act(d_[:], t_[:], AF.Exp, scale=chs[:, 0, ct:ct + 1], reads=[t_, chs], writes=[d_])
            P.tt(f_[:], p3[:, 0:512], d_[:], ALU.mult, reads=[p3, d_], writes=[f_])
            v_ = P.view(Fs); fviews.append(v_)
            P.dma(Fs[ct * 128:(ct + 1) * 128, cols], f_[:], reads=[f_], writes=[v_])
    P.pop()
    yviews = []
    P.push()
    def ld_r(name, shape, src):
        t32 = P.sb(shape, F32, name + "f"); P.dma(t32[:], src[:], writes=[t32])
        tr = P.sb(shape, F32R, name)
        P.copy(tr[:], t32[:], reads=[t32], writes=[tr], eng='act')
        return tr
    W1 = ld_r("W1", [128, 256], W1d); CW1 = ld_r("CW1", [128, 256], CW1d); CW2 = ld_r("CW2", [128, 256], CW2d)
    W3 = ld_r("W3", [128, 3, 128], W3d)
    tw = P.sb([128, 3, 512], F32, "tw"); P.dma(tw[:], twd[:], writes=[tw])
    Wr, Wi, nWi = W3[:, 0, :], W3[:, 1, :], W3[:, 2, :]
    twr, twi, ntwi = tw[:, 0, :], tw[:, 1, :], tw[:, 2, :]
    xg = [P.sb([64, HY_G, 128], F32, f"xg{i}") for i in range(2)]
    fg = [P.sb([128, HY_G, 128], F32, f"fg{i}") for i in range(2)]
    T = [P.sb([128, 512], F32, f"T{i}") for i in range(4)]
    Ap = [[P.sb([128, 512], F32R, f"Ap{i}{j}") for j in range(2)] for i in range(2)]
    Hs = [P.sb([128, 512], F32, f"H{j}") for j in range(2)]
    Yc = [P.sb([128, 512], F32R, f"Y{j}") for j in range(2)]
    Zp = [P.sb([128, 512], F32R, f"Zp{j}") for j in range(2)]
    xgr = [P.sb([64, HY_G, 128], F32R, f"xgr{i}") for i in range(2)]
    fgr = [P.sb([128, HY_G, 128], F32R, f"fgr{i}") for i in range(2)]
    ysb = [P.sb([64, 512], F32, f"ysb{i}") for i in range(2)]

    def cmul(o_re, o_im, a_re, a_im, b_re, b_im, ra, rb, wo):
        P.tt(T[0][:], a_re, b_re, ALU.mult, reads=ra + rb, writes=[T[0]])
        P.tt(T[1][:], a_im, b_im, ALU.mult, reads=ra + rb, writes=[T[1]])
        P.tt(o_re, T[0][:], T[1][:], ALU.subtract, reads=[T[0], T[1]], writes=[wo[0]], eng='pool')
        P.tt(T[2][:], a_re, b_im, ALU.mult, reads=ra + rb, writes=[T[2]])
        P.tt(T[3][:], a_im, b_re, ALU.mult, reads=ra + rb, writes=[T[3]])
        P.tt(o_im, T[2][:], T[3][:], ALU.add, reads=[T[2], T[3]], writes=[wo[1]], eng='pool')

    def v4(ap512):
        return ap512

    def fwd_fft(src, Ka, A):
        psA = rot()
        for ch in range(HY_G):
            P.mm(psA[:, ch * 256:(ch + 1) * 256], src[0:Ka, ch, :], W1[0:Ka, :], reads=[src, W1], writes=[psA])
        pv = psA[:].rearrange("p (c r k) -> p c r k", c=HY_G, r=2)
        t4 = lambda ap: ap.rearrange("p (c k) -> p c k", c=HY_G)
        P.tt(t4(T[0][:]), pv[:, :, 0, :], t4(twr), ALU.mult, reads=[psA, tw], writes=[T[0]])
        P.tt(t4(T[1][:]), pv[:, :, 1, :], t4(twi), ALU.mult, reads=[psA, tw], writes=[T[1]])
        P.tt(A[0][:], T[0][:], T[1][:], ALU.subtract, reads=[T[0], T[1]], writes=[A[0]], eng='pool')
        P.tt(t4(T[2][:]), pv[:, :, 0, :], t4(twi), ALU.mult, reads=[psA, tw], writes=[T[2]])
        P.tt(t4(T[3][:]), pv[:, :, 1, :], t4(twr), ALU.mult, reads=[psA, tw], writes=[T[3]])
        P.tt(A[1][:], T[2][:], T[3][:], ALU.add, reads=[T[2], T[3]], writes=[A[1]], eng='pool')
        psX = rot()
        P.mm(psX[:, 0:512], Wr, A[0][:], start=True, stop=False, reads=[W3, A[0]], writes=[psX])
        P.mm(psX[:, 0:512], nWi, A[1][:], start=False, stop=True, reads=[W3, A[1]], writes=[psX])
        P.mm(psX[:, 512:1024], Wi, A[0][:], start=True, stop=False, reads=[W3, A[0]], writes=[psX])
        P.mm(psX[:, 512:1024], Wr, A[1][:], start=False, stop=True, reads=[W3, A[1]], writes=[psX])
        return psX

    ng = 256 // HY_G
    for g in range(ng):
        ch0 = g * HY_G
        ct = ch0 // 128
        x_ = xg[g % 2]; f_ = fg[g % 2]
        P.dma(x_[:], Zs[ch0:ch0 + HY_G, :].rearrange("c (a b) -> a c b", b=128), reads=zviews, writes=[x_])
        P.dma(f_[:], Fs[ch0:ch0 + HY_G, :].rearrange("c (a b) -> a c b", b=128), reads=fviews, writes=[f_])
        xr_ = xgr[g % 2]; fr_ = fgr[g % 2]
        P.copy(fr_[:], f_[:], reads=[f_], writes=[fr_], eng='act')
        P.copy(xr_[:], x_[:], reads=[x_], writes=[xr_], eng='act')
        f_, x_ = fr_, xr_
        psH = fwd_fft(f_, 128, Ap[0])
        P.copy(Hs[0][:], psH[:, 0:512], reads=[psH], writes=[Hs[0]], eng='act')
        P.copy(Hs[1][:], psH[:, 512:1024], reads=[psH], writes=[Hs[1]], eng='act')
        psX = fwd_fft(x_, 64, Ap[1])
        cmul(Yc[0][:], Yc[1][:], psX[:, 0:512], psX[:, 512:1024], Hs[0][:], Hs[1][:], [psX], [Hs[0], Hs[1]], Yc)
        psZ = rot()
        for ch in range(HY_G):
            P.mm(psZ[:, ch * 256:(ch + 1) * 256], Yc[0][:, ch * 128:(ch + 1) * 128], CW1[:], start=True, stop=False, reads=[Yc[0], CW1], writes=[psZ])
            P.mm(psZ[:, ch * 256:(ch + 1) * 256], Yc[1][:, ch * 128:(ch + 1) * 128], CW2[:], start=False, stop=True, reads=[Yc[1], CW2], writes=[psZ])
        pz = psZ[:].rearrange("p (c r k) -> p c r k", c=HY_G, r=2)
        t4 = lambda ap: ap.rearrange("p (c k) -> p c k", c=HY_G)
        cmul(t4(Zp[0][:]), t4(Zp[1][:]), pz[:, :, 0, :], pz[:, :, 1, :], t4(twr), t4(ntwi), [psZ], [tw], Zp)
        psy = rot()
        P.mm(psy[0:64, 0:512], W3[:, 0, 0:64], Zp[0][:], start=True, stop=False, reads=[W3, Zp[0]], writes=[psy])
        P.mm(psy[0:64, 0:512], W3[:, 1, 0:64], Zp[1][:], start=False, stop=True, reads=[W3, Zp[1]], writes=[psy])
        y_ = ysb[g % 2]
        P.op('act', lambda e, y_=y_, psy=psy: e.mul(y_[:], psy[0:64, 0:512], 1.0 / NFFT), reads=[psy], writes=[y_])
        v_ = P.view(Ys); yviews.append(v_)
        P.dma(Ys[ch0:ch0 + HY_G, :].rearrange("c (a b) -> a c b", b=128), y_[:].rearrange("a (c b) -> a c b", c=HY_G), reads=[y_], writes=[v_])
    P.pop()
    P.push()
    bufs = [[P.sb([128, 2048], F32, f"o{i}{j}") for j in range(3)] for i in range(2)]
    k = 0
    for ct in range(2):
        for s in range(0, NL, 2048):
            yb, zb, xb = bufs[k % 2]; k += 1
            rows = slice(ct * 128, (ct + 1) * 128)
            P.dma(yb[:], Ys[rows, s:s + 2048], reads=yviews, writes=[yb])
            P.dma(zb[:], Zs[rows, s:s + 2048], reads=zviews, writes=[zb])
            P.dma(xb[:], X0s[rows, s:s + 2048], reads=xviews, writes=[xb])
            P.stt(yb[:], zb[:], chs[:, 1, ct:ct + 1], yb[:], ALU.mult, ALU.add, reads=[zb, chs, yb], writes=[yb])
            P.tt(yb[:], yb[:], xb[:], ALU.mult, reads=[yb, xb], writes=[yb], eng='pool')
            P.dma(HYT[rows, s:s + 2048], yb[:], reads=[yb], writes=[P.view(HYT)], is_output=True)
    P.pop()
    return P.build()


def host_L5(d, r4):
    C, deltas = hyena_consts()
    ins = []
    cwf = d['hy_conv_w'][0]; cbf = d['hy_conv_b'][0]
    for j in range(8):
        b, chh = j // 2, j % 2
        a, c = r4[2 * b], r4[2 * b + 1]
        hT = np.zeros((3, 256, 8192), np.float32)
        cw = np.zeros((128, 3, 2, 3), np.float32); cb = np.zeros((128, 3, 2), np.float32)
        for part in range(3):
            rows = slice(512 * part + 256 * chh, 512 * part + 256 * chh + 256)
            hT[part] = np.concatenate([a['UT'][rows, :4096], c['UT'][rows, :4096]], 1)
            cw[:, part] = cwf[:, rows].reshape(3, 2, 128).transpose(2, 1, 0)
            cb[:, part] = cbf[rows].reshape(2, 128).T
        chs = slice(256 * chh, 256 * chh + 256)
        chv = np.zeros((128, 2, 2), np.float32)
        chv[:, 0, :] = -deltas[chs].reshape(2, 128).T
        chv[:, 1, :] = d['hy_bias'][0][chs].reshape(2, 128).T
        bf = np.stack([d['hy_filt_b1'][0], d['hy_sin_freq'][0][0], d['hy_filt_b2'][0], d['hy_sin_freq'][0][1]], 1)
        w3 = np.ascontiguousarray(d['hy_filt_w3'][0].reshape(64, 2, 512)[:, :, chs])
        m = {"hT": hT, "cw": cw, "cb": cb, "w1": d['hy_filt_w1'][0], "w2": d['hy_filt_w2'][0], "bf": np.ascontiguousarray(bf), "w3": w3, "chv": chv}
        m.update(C)
        ins.append(m)
    return ins


def build_L7():
    P = Prog()
    NTOK = 4096
    LB = BLOCKS[:8]
    xin = P.dram("xin", [NTOK, D]); cnd = P.dram("cnd", [2, D])
    mw = P.dram("mw", [D, 9 * D]); mb = P.dram("mb", [9 * D])
    HR = P.dram("HR", [1024, NTOK]); wout = P.dram("wout", [D, D])
    g2 = P.dram("g2", [D]); wg2 = P.dram("wg2", [D, FH]); wu2 = P.dram("wu2", [D, FH]); wd2 = P.dram("wd2", [FH, D])
    gf = P.dram("gf", [D])
    sa = P.dram("scr_a", [NTOK, D], kind="Internal"); sbb = P.dram("scr_b", [NTOK, D], kind="Internal")
    out = P.dram("out", [NTOK, D], kind="ExternalOutput")
    K = consts(P)
    xs = [P.sb([128, D], F32, f"x{t}") for t in range(4)]
    va = [P.view(sa) for _ in LB]; vb = [P.view(sbb) for _ in LB]
    mfm, gbc = adaln(P, K, cnd, mw, mb, [5, 8])
    sce2, sh2 = norm_params(P, K, mfm, g2, 6, "n2")
    cats = [HR[kc * 128:(kc + 1) * 128, :] for kc in range(8)]
    pass_mixout(P, K, xs, xin, None, sa, va, wout, gbc[5], cats, blocks=LB)
    pass_ffn(P, K, xs, sa, va, sbb, vb, wg2, wu2, wd2, sce2, sh2, gbc[8], blocks=LB)
    P.push()
    gfb = P.sb([128, D], F32, "gfb")
    P.dma(gfb[:], gf.t.partition_broadcast(128), writes=[gfb])
    ss = P.sb([128, 4], F32, "ssf"); rstd = P.sb([128, 4], F32, "rstdf")
    junk = P.sb([128, D], BF16, "junk")
    for bi, (t0, nt, n) in enumerate(LB):
        xio(P, xs, nt, t0, sbb, vb[bi])
        P.memset(ss[:], 0.0, writes=[ss], eng='dve')
        for t in range(nt):
            P.act(junk[:], xs[t][:], AF.Square, accum_out=ss[:, t:t + 1], reads=[xs[t]], writes=[junk, ss])
        rstd_from_ss(P, rstd[:], ss[:], 1.0 / D, [ss], [rstd])
        for t in range(nt):
            P.ts(xs[t][:], xs[t][:], rstd[:, t:t + 1], ALU.mult, reads=[xs[t], rstd], writes=[xs[t]])
            P.tt(xs[t][:], xs[t][:], gfb[:], ALU.mult, reads=[xs[t], gfb], writes=[xs[t]], eng='pool')
        xio(P, xs, nt, t0, None, None, out, P.view(out), load=False, out=True)
    P.pop()
    return P.build()


def host_L7(d, r4, r5, r6):
    ins = []
    for j in range(8):
        b, hf = j // 2, j % 2
        cols = slice(hf * 4096, (hf + 1) * 4096)
        HR = np.concatenate([r5[2 * b]['HYT'][:, cols], r5[2 * b + 1]['HYT'][:, cols], r6[2 * b]['RT'][:, cols], r6[2 * b + 1]['RT'][:, cols]], 0)
        ins.append({"xin": np.ascontiguousarray(r4[j]['xout'][:4096]), "cnd": np.stack([d['c'][b], d['c_ctx']]),
                    "mw": d['mod_w'][1], "mb": d['mod_b'][1], "HR": np.ascontiguousarray(HR), "wout": d['od_w_out'][0],
                    "g2": d['norm_ffn2'][1], "wg2": d['ffn2_w_gate'][1], "wu2": d['ffn2_w_up'][1], "wd2": d['ffn2_w_down'][1],
                    "gf": d['final_norm'], "ident": np.eye(128, dtype=np.float32)})
    return ins


_NC = {}


def _get(name, fn):
    if name not in _NC:
        _NC[name] = fn()
    return _NC[name]


def _run(name, fn, ins):
    nc = _get(name, fn)
    res = run_bass_kernel_spmd(nc, ins, core_ids=list(range(8)))
    return res.results


def kernel(**inputs):
    d = {k: np.ascontiguousarray(np.asarray(v)) for k, v in inputs.items()}
    r1 = _run("L1", build_L1, host_L1(d))
    r2 = _run("L2", build_L2, host_L2(r1))
    r3 = _run("L3", build_L3, host_L3(d, r1))
    r4 = _run("L4", build_L4, host_L4(d, r1, r2, r3))
    r5 = _run("L5", build_L5, host_L5(d, r4))
    r6 = _run("L6", build_L6, host_L6(d, r4))
    r7 = _run("L7", build_L7, host_L7(d, r4, r5, r6))
    out = np.zeros((4, 8192, 1024), np.float32)
    for j in range(8):
        b, hf = j // 2, j % 2
        out[b, hf * 4096:(hf + 1) * 4096] = r7[j]['out']
    return out
```

```python
from contextlib import ExitStack
import numpy as np
import concourse.bass as bass
import concourse.mybir as mybir
from concourse.bass_utils import run_bass_kernel_spmd

F32 = mybir.dt.float32
F32R = mybir.dt.float32r
BF16 = mybir.dt.bfloat16
AF = mybir.ActivationFunctionType
ALU = mybir.AluOpType
AX = mybir.AxisListType

ENGS = ('pe', 'act', 'dve', 'pool', 'sp')
NSLOT = 12


class Buf:
    __slots__ = ('t', 'w', 'r', 'name')

    def __init__(self, t, name):
        self.t = t
        self.w = None
        self.r = []
        self.name = name

    def __getitem__(self, idx):
        return self.t[idx]


class Prog:
    def __init__(self, same_engine_sync=True):
        self.nc = bass.Bass("TRN2", target_bir_lowering=False)
        self.ops = {e: [] for e in ENGS}
        self.cnt = {e: 0 for e in ENGS}
        self.seen = {e: {} for e in ENGS}
        self.dma_n = {e: 0 for e in ENGS}
        self.slot_val = {}
        self.stack = ExitStack()
        self.same = same_engine_sync
        self.nbuf = 0
        self.out_tokens = []
        self.scopes = []
        self.closers = []
        self.barrier = []

    def dram(self, name, shape, dt=F32, kind="ExternalInput"):
        t = self.nc.dram_tensor(name, list(shape), dt, kind=kind).ap()
        return Buf(t, name)

    def _ctx(self):
        return self.scopes[-1][0] if self.scopes else self.stack

    def _reg(self, b):
        b.r = list(self.barrier)
        if self.scopes:
            self.scopes[-1][1].append(b)
        return b

    def sb(self, shape, dt=F32, name=None):
        self.nbuf += 1
        name = (name or "sb") + f"_{self.nbuf}"
        t = self._ctx().enter_context(self.nc.sbuf_tensor(name, list(shape), dt))
        return self._reg(Buf(t, name))

    def ps(self, shape, dt=F32, name=None):
        self.nbuf += 1
        name = (name or "ps") + f"_{self.nbuf}"
        t = self._ctx().enter_context(self.nc.psum_tensor(name, list(shape), dt))
        return self._reg(Buf(t, name))

    def view(self, b):
        return Buf(b.t, b.name)

    def push(self):
        self.scopes.append((ExitStack(), []))

    def pop(self):
        st, bufs = self.scopes.pop()
        m = {}
        for b in bufs:
            for tok in ([b.w] if b.w else []) + b.r:
                if m.get(tok[0], 0) < tok[1]:
                    m[tok[0]] = tok[1]
        for k, v in self.barrier:
            if m.get(k, 0) < v:
                m[k] = v
        self.barrier = list(m.items())
        st.close()

    def _deps(self, eng, reads, writes):
        deps = {}

        def add(tok):
            if tok is None:
                return
            k, v = tok
            if deps.get(k, 0) < v:
                deps[k] = v
        for b in reads:
            add(b.w)
        for b in writes:
            add(b.w)
            for t in b.r:
                add(t)
        out = []
        seen = self.seen[eng]
        for k, v in deps.items():
            if k == eng and (eng == 'pe' or not self.same):
                continue
            if seen.get(k, 0) >= v:
                continue
            seen[k] = v
            out.append((k, v))
        return out

    def _commit(self, tok, reads, writes):
        for b in reads:
            if not any(b is w for w in writes):
                b.r.append(tok)
        for b in writes:
            b.w = tok
            b.r = []

    def op(self, eng, fn, reads=(), writes=()):
        waits = self._deps(eng, reads, writes)
        self.cnt[eng] += 1
        tok = (eng, self.cnt[eng])
        self.ops[eng].append((waits, fn, (eng, 1)))
        self._commit(tok, reads, writes)
        return tok

    def dma(self, out_ap, in_ap, reads=(), writes=(), q='sp', is_output=False, **kw):
        i = self.dma_n[q]
        self.dma_n[q] += 1
        slot = ('d', q, i % NSLOT)
        waits = self._deps(q, reads, writes)
        prev = self.slot_val.get(slot, 0)
        if prev and self.seen[q].get(slot, 0) < prev:
            self.seen[q][slot] = prev
            waits.append((slot, prev))
        val = prev + 16
        self.slot_val[slot] = val
        tok = (slot, val)

        def fn(e, out_ap=out_ap, in_ap=in_ap, kw=kw):
            return e.dma_start(out=out_ap, in_=in_ap, **kw)
        self.ops[q].append((waits, fn, (slot, 16)))
        self._commit(tok, reads, writes)
        if is_output:
            self.out_tokens.append(tok)
        return tok

    def build(self):
        nc = self.nc
        fin = {}
        for k, v in self.out_tokens:
            fin[k] = max(fin.get(k, 0), v)
        keys = set()
        for e in ENGS:
            for waits, fn, inc in self.ops[e]:
                keys.add(inc[0])
                for k, v in waits:
                    keys.add(k)
        sems = {}
        for k in sorted(keys, key=str):
            nm = k if isinstance(k, str) else f"d_{k[1]}_{k[2]}"
            sems[k] = self.stack.enter_context(nc.semaphore("s_" + nm))
        ops = self.ops
        with nc.Block() as block:
            def mk(ename):
                def body(e):
                    for waits, fn, inc in ops[ename]:
                        for k, v in waits:
                            e.wait_ge(sems[k], v)
                        ins = fn(e)
                        ins.then_inc(sems[inc[0]], inc[1])
                    if ename == 'sp':
                        for k, v in fin.items():
                            e.wait_ge(sems[k], v)
                return body
            if ops['sp'] or fin:
                block.sync(mk('sp'))
            if ops['pe']:
                block.tensor(mk('pe'))
            if ops['act']:
                block.scalar(mk('act'))
            if ops['dve']:
                block.vector(mk('dve'))
            if ops['pool']:
                block.gpsimd(mk('pool'))
        self.stack.close()
        return nc

    def mm(self, out, lhsT, rhs, start=True, stop=True, reads=(), writes=()):
        return self.op('pe', lambda e: e.matmul(out, lhsT, rhs, start=start, stop=stop), reads, writes)

    def tr(self, out, in_, ident, reads=(), writes=()):
        return self.op('pe', lambda e: e.transpose(out, in_, ident), reads, writes)

    def act(self, out, in_, func, bias=None, scale=None, accum_out=None, reads=(), writes=(), eng='act'):
        kw = {}
        if bias is not None:
            kw['bias'] = bias
        if scale is not None:
            kw['scale'] = scale
        if accum_out is not None:
            kw['accum_out'] = accum_out
        return self.op(eng, lambda e: e.activation(out, in_, func, **kw), reads, writes)

    def tt(self, out, in0, in1, op, reads=(), writes=(), eng='dve'):
        return self.op(eng, lambda e: e.tensor_tensor(out, in0, in1, op), reads, writes)

    def ts(self, out, in0, s1, op0, s2=None, op1=None, accum_out=None, reads=(), writes=(), eng='dve'):
        kw = {}
        if op1 is not None:
            kw['op1'] = op1
        if accum_out is not None:
            kw['accum_out'] = accum_out
        return self.op(eng, lambda e: e.tensor_scalar(out, in0, s1, s2, op0, **kw), reads, writes)

    def stt(self, out, in0, scalar, in1, op0, op1, reads=(), writes=(), eng='dve'):
        eng = 'dve'
        return self.op(eng, lambda e: e.scalar_tensor_tensor(out, in0, scalar, in1, op0, op1), reads, writes)

    def copy(self, out, in_, reads=(), writes=(), eng='dve'):
        if eng == 'act':
            return self.op(eng, lambda e: e.copy(out, in_), reads, writes)
        return self.op(eng, lambda e: e.tensor_copy(out, in_), reads, writes)

    def memset(self, ap, val, writes=(), eng='pool'):
        return self.op(eng, lambda e: e.memset(ap, val), (), writes)

    def scan(self, out, d0, d1, init, reads=(), writes=(), eng='dve'):
        return self.op(eng, lambda e: e.tensor_tensor_scan(out, d0, d1, init, ALU.mult, ALU.add), reads, writes)


def run(prog_nc, in_maps, trace=False):
    res = run_bass_kernel_spmd(prog_nc, in_maps, core_ids=list(range(len(in_maps))), trace=trace)
    return res


D = 1024
FH = 2816
NF = 22
EPS = 1e-6
NT_CORE = 33
BLOCKS = [(4 * i, 4, 0) for i in range(8)] + [(32, 1, 1)]


def wview(w, p=128):
    return w.t.rearrange("(c p) f -> p c f", p=p)


def load_w(P, dst, src, q='pool'):
    v = wview(src)
    for c in range(v.shape[1]):
        P.dma(dst[:, c, :], v[:, c, :], writes=[dst], q=q)


class Ctx:
    pass


def consts(P):
    K = Ctx()
    idd = P.dram("ident", [128, 128])
    K.ident = P.sb([128, 128], F32, "ident")
    P.dma(K.ident[:], idd[:], writes=[K.ident])
    K.identb = P.sb([128, 128], BF16, "identb")
    P.copy(K.identb[:], K.ident[:], reads=[K.ident], writes=[K.identb])
    K.ones = P.sb([128, 128], F32, "ones")
    P.memset(K.ones[:], 1.0, writes=[K.ones])
    K.onesb = P.sb([128, 128], BF16, "onesb")
    P.memset(K.onesb[:], 1.0, writes=[K.onesb])
    K.pb = [P.ps([128, 512], F32, f"pb{i}") for i in range(6)]
    K.pbT = [P.ps([128, 1024], BF16, f"pbT{i}") for i in range(2)]
    return K


def load_fm(P, K, dstbuf, dst, src2d, C):
    tmp = P.sb([C, 128], F32, "lfm")
    P.dma(tmp[:], src2d, writes=[tmp])
    pb = K.pb[5]
    P.tr(pb[:, 0:C], tmp[:], K.ident[0:C, 0:C], reads=[tmp, K.ident], writes=[pb])
    P.copy(dst, pb[:, 0:C], reads=[pb], writes=[dstbuf])


def adaln(P, K, cnd, mod_w, mod_b, gate_ks, ks=None):
    mfm = P.sb([128, 72, 2], F32, "mfm")
    gbc = {k: P.sb([128, 2, 1024], F32, f"gbc{k}") for k in gate_ks}
    P.push()
    sc = P.sb([128, 2, 8], F32, "sc")
    load_fm(P, K, sc, sc[:].rearrange("p n c -> p (n c)"), cnd.t.rearrange("n (c p) -> (n c) p", p=128), 16)
    scs = P.sb([128, 2, 8], F32, "scs")
    P.act(scs[:], sc[:], AF.Silu, reads=[sc], writes=[scs])
    scb = P.sb([128, 8, 2, 128], F32, "scb")
    for c in range(8):
        for n in range(2):
            P.ts(scb[:, c, n, :], K.ones[:], scs[:, n, c:c + 1], ALU.mult, reads=[K.ones, scs], writes=[scb])
    mbf = P.sb([128, 72], F32, "mbf")
    load_fm(P, K, mbf, mbf[:], mod_b.t.rearrange("(c p) -> c p", p=128), 72)
    mbb = P.sb([128, len(gate_ks), 1024], F32, "mbb")
    for i, k in enumerate(gate_ks):
        P.dma(mbb[:, i, :], mod_b.t[k * 1024:(k + 1) * 1024].partition_broadcast(128), writes=[mbb])
    wv = wview(mod_w)
    wbuf = [P.sb([128, 8, 1024], F32, f"modw{i}") for i in range(2)]
    for ki_, k in enumerate(ks if ks is not None else range(9)):
        wb = wbuf[ki_ % 2]
        for c in range(8):
            P.dma(wb[:, c, :], wv[:, c, k * 1024:(k + 1) * 1024], writes=[wb])
        for fc in range(8):
            pb = K.pb[fc % 2]
            for c in range(8):
                P.mm(pb[:, 0:2], wb[:, c, fc * 128:(fc + 1) * 128], scs[:, :, c], start=(c == 0), stop=(c == 7),
                     reads=[wb, scs], writes=[pb])
            ch = k * 8 + fc
            P.ts(mfm[:, ch, :], pb[:, 0:2], mbf[:, ch:ch + 1], ALU.add, reads=[pb, mbf], writes=[mfm])
        if k in gate_ks:
            gi = gate_ks.index(k)
            fac = 1.0 if k == 5 else 0.5
            for n in range(2):
                for hf in range(2):
                    pb = K.pb[2 + (n * 2 + hf) % 2]
                    for c in range(8):
                        P.mm(pb[:], scb[:, c, n, :], wb[:, c, hf * 512:(hf + 1) * 512], start=(c == 0), stop=(c == 7),
                             reads=[scb, wb], writes=[pb])
                    o = gbc[k][:, n, hf * 512:(hf + 1) * 512]
                    P.tt(o, pb[:], mbb[:, gi, hf * 512:(hf + 1) * 512], ALU.add, reads=[pb, mbb], writes=[gbc[k]])
                    if fac != 1.0:
                        P.ts(o, o, fac, ALU.mult, reads=[gbc[k]], writes=[gbc[k]])
    P.pop()
    return mfm, gbc


def norm_params(P, K, mfm, g_dram, k0, name):
    g = P.sb([128, 8], F32, name + "g")
    load_fm(P, K, g, g[:], g_dram.t.rearrange("(c p) -> c p", p=128), 8)
    sce = P.sb([128, 8, 2], F32, name + "sce")
    sh = P.sb([128, 8, 2], F32, name + "sh")
    for n in range(2):
        P.ts(sce[:, :, n], mfm[:, (k0 + 1) * 8:(k0 + 2) * 8, n], 1.0, ALU.add, reads=[mfm], writes=[sce])
        P.tt(sce[:, :, n], sce[:, :, n], g[:], ALU.mult, reads=[sce, g], writes=[sce])
        P.copy(sh[:, :, n], mfm[:, k0 * 8:(k0 + 1) * 8, n], reads=[mfm], writes=[sh])
    return sce, sh


def rstd_from_ss(P, out, ss, inv_n, reads, writes):
    P.ts(out, ss, inv_n, ALU.mult, EPS, ALU.add, reads=reads, writes=writes)
    P.act(out, out, AF.Sqrt, reads=writes, writes=writes)
    P.op('dve', lambda e: e.reciprocal(out, out), reads=writes, writes=writes)


def norm_T(P, K, W, xs, nt, n, sce, sh, hT):
    P.memset(W.ss[:], 0.0, writes=[W.ss], eng='dve')
    for t in range(nt):
        P.act(W.xn[t][:], xs[t][:], AF.Square, accum_out=W.ss[:, t:t + 1], reads=[xs[t]], writes=[W.xn[t], W.ss])
    rstd_from_ss(P, W.rstd[:], W.ss[:], 1.0 / D, [W.ss], [W.rstd])
    for t in range(nt):
        P.op('act', lambda e, t=t: e.mul(W.xn[t][:], xs[t][:], W.rstd[:, t:t + 1]), reads=[xs[t], W.rstd], writes=[W.xn[t]])
    for c in range(8):
        pb = K.pbT[c % 2]
        for t in range(nt):
            P.tr(pb[:, t * 128:(t + 1) * 128], W.xn[t][:, c * 128:(c + 1) * 128], K.identb[:],
                 reads=[W.xn[t], K.identb], writes=[pb])
        P.ts(hT[:, c, 0:nt * 128], pb[:, 0:nt * 128], sce[:, c, n:n + 1], ALU.mult, sh[:, c, n:n + 1], ALU.add,
             reads=[pb, sce, sh], writes=[hT])


class FFNW:
    pass


def ffn_alloc(P):
    W = FFNW()
    W.wg = [P.sb([128, 8, 256], BF16, f"wg{g}") for g in range(NF // 2)]
    W.wu = [P.sb([128, 8, 256], BF16, f"wu{g}") for g in range(NF // 2)]
    W.wd = [P.sb([128, 2, D], BF16, f"wd{g}") for g in range(NF // 2)]
    W.xn = [P.sb([128, D], BF16, f"xn{t}") for t in range(4)]
    W.ss = P.sb([128, 4], F32, "ss")
    W.rstd = P.sb([128, 4], F32, "rstd")
    W.hT = P.sb([128, 8, 512], BF16, "hT")
    W.actT = P.sb([128, NF, 512], BF16, "actT")
    W.a = [P.sb([128, 512], BF16, f"a{i}") for i in range(2)]
    return W


def ffn_load(P, W, wg, wu, wd):
    vg, vu, vd = wview(wg), wview(wu), wview(wd)
    for g in range(NF // 2):
        cs = slice(g * 256, (g + 1) * 256)
        for src, dst in ((vg, W.wg[g]), (vu, W.wu[g])):
            for c0 in (0, 4):
                P.dma(dst[:, c0:c0 + 4, :], src[:, c0:c0 + 4, cs], writes=[dst], q='pool')
    for g in range(NF // 2):
        P.dma(W.wd[g][:], vd[:, 2 * g:2 * g + 2, :], writes=[W.wd[g]], q='pool')


def ffn_block(P, K, W, xs, nt, n, sce, sh, gbc):
    N = nt * 128
    norm_T(P, K, W, xs, nt, n, sce, sh, W.hT)
    for f in range(NF):
        pg = K.pb[(f % 2) * 2]
        pu = K.pb[(f % 2) * 2 + 1]
        for c in range(8):
            P.mm(pg[:, 0:N], W.wg[f // 2][:, c, (f % 2) * 128:(f % 2 + 1) * 128], W.hT[:, c, 0:N], start=(c == 0), stop=(c == 7),
                 reads=[W.wg[f // 2], W.hT], writes=[pg])
        for c in range(8):
            P.mm(pu[:, 0:N], W.wu[f // 2][:, c, (f % 2) * 128:(f % 2 + 1) * 128], W.hT[:, c, 0:N], start=(c == 0), stop=(c == 7),
                 reads=[W.wu[f // 2], W.hT], writes=[pu])
        a = W.a[f % 2]
        P.act(a[:, 0:N], pg[:, 0:N], AF.Silu, reads=[pg], writes=[a])
        P.tt(W.actT[:, f, 0:N], a[:, 0:N], pu[:, 0:N], ALU.mult, reads=[a, pu], writes=[W.actT])
    i = 0
    for t in range(nt):
        for hf in range(2):
            py = K.pb[4 + i % 2]
            i += 1
            for f in range(NF):
                P.mm(py[:], W.actT[:, f, t * 128:(t + 1) * 128], W.wd[f // 2][:, f % 2, hf * 512:(hf + 1) * 512],
                     start=(f == 0), stop=(f == NF - 1), reads=[W.actT, W.wd[f // 2]], writes=[py])
            tb = W.xn[i % 2]
            tmp = tb[:].bitcast(F32)
            P.tt(tmp, py[:], gbc[:, n, hf * 512:(hf + 1) * 512], ALU.mult, reads=[py, gbc], writes=[tb])
            xo = xs[t][:, hf * 512:(hf + 1) * 512]
            P.tt(xo, xo, tmp, ALU.add, reads=[xs[t], tb], writes=[xs[t]], eng='pool')


QR, KVR, ROPE, S5C = 384, 256, 32, 512
NH = 8
EV_IN = QR + KVR + ROPE + S5C
EV_EXT = EV_IN + 192


def rot(K):
    K.rr = getattr(K, 'rr', -1) + 1
    return K.pb[K.rr % 6]


def evin_alloc(P):
    W = FFNW()
    W.win = P.sb([128, 8, EV_EXT], BF16, "win")
    W.wuq = P.sb([128, 3, NH * 96], BF16, "wuq")
    W.wuqs = P.sb([128, 3, NH * 96], BF16, "wuqs")
    W.wk = P.sb([128, 2, NH * 64], BF16, "wk")
    W.wv = P.sb([128, 2, 512], BF16, "wv")
    W.gq = P.sb([128, 3], F32, "gq")
    W.gkv = P.sb([128, 2], F32, "gkv")
    W.cos = P.sb([96, 4096], F32, "cos")
    W.sin = P.sb([96, 4096], F32, "sin")
    W.xn = [P.sb([128, D], BF16, f"xn{t}") for t in range(4)]
    W.ss = P.sb([128, 4], F32, "ss")
    W.rstd = P.sb([128, 4], F32, "rstd")
    W.hT = P.sb([128, 8, 512], BF16, "hT")
    W.cq = P.sb([128, 3, 512], F32, "cq")
    W.sq = P.sb([128, 512], F32, "sq")
    W.rbc = P.sb([128, 512], F32, "rbc")
    W.cqn = P.sb([128, 3, 512], BF16, "cqn")
    W.ckvn = P.sb([128, 2, 512], BF16, "ckvn")
    W.krr = P.sb([96, 512], F32, "krr")
    W.t1 = P.sb([96, 512], F32, "t1")
    W.t2 = P.sb([96, 512], F32, "t2")
    W.st = [P.sb([128, 512], F32, f"st{i}") for i in range(3)]
    W.sti = 0
    return W


def stage(W):
    W.sti += 1
    return W.st[W.sti % 3]


def lowrank_norm(P, K, W, col0, nch, nfeat, g, N, outn):
    pss = rot(K)
    for c3 in range(nch):
        pc = rot(K)
        for c in range(8):
            P.mm(pc[:, 0:N], W.win[:, c, col0 + c3 * 128:col0 + (c3 + 1) * 128], W.hT[:, c, 0:N], start=(c == 0), stop=(c == 7),
                 reads=[W.win, W.hT], writes=[pc])
        P.copy(W.cq[:, c3, 0:N], pc[:, 0:N], reads=[pc], writes=[W.cq], eng='act')
        P.act(W.sq[:, 0:N], pc[:, 0:N], AF.Square, reads=[pc], writes=[W.sq])
        P.mm(pss[:, 0:N], K.ones[:], W.sq[:, 0:N], start=(c3 == 0), stop=(c3 == nch - 1), reads=[K.ones, W.sq], writes=[pss])
    rstd_from_ss(P, W.rbc[:, 0:N], pss[:, 0:N], 1.0 / nfeat, [pss], [W.rbc])
    for c3 in range(nch):
        P.stt(outn[:, c3, 0:N], W.cq[:, c3, 0:N], g[:, c3:c3 + 1], W.rbc[:, 0:N], ALU.mult, ALU.mult,
              reads=[W.cq, g, W.rbc], writes=[outn])


def rope_rows(P, W, out, a, b, tok0, N, reads, writes, roped):
    if not roped:
        P.copy(out, a, reads=reads, writes=writes)
        return
    P.tt(W.t1[64:96, 0:N], a, W.cos[64:96, tok0:tok0 + N], ALU.mult, reads=reads + [W.cos], writes=[W.t1])
    P.tt(W.t2[64:96, 0:N], b, W.sin[64:96, tok0:tok0 + N], ALU.mult, reads=reads + [W.sin], writes=[W.t2])
    P.tt(out, W.t1[64:96, 0:N], W.t2[64:96, 0:N], ALU.add, reads=[W.t1, W.t2], writes=writes, eng='pool')


def evin_block(P, K, W, xs, t0, nt, n, sce, sh, QT, KT, V, ST):
    N = nt * 128
    tok0 = t0 * 128
    roped = (n == 0)
    norm_T(P, K, W, xs, nt, n, sce, sh, W.hT)
    lowrank_norm(P, K, W, 0, 3, QR, W.gq, N, W.cqn)
    lowrank_norm(P, K, W, QR, 2, KVR, W.gkv, N, W.ckvn)
    pa, pbs = rot(K), rot(K)
    for c in range(8):
        P.mm(pa[0:96, 0:N], W.win[:, c, EV_IN:EV_IN + 96], W.hT[:, c, 0:N], start=(c == 0), stop=(c == 7), reads=[W.win, W.hT], writes=[pa])
    if roped:
        for c in range(8):
            P.mm(pbs[0:96, 0:N], W.win[:, c, EV_IN + 96:EV_IN + 192], W.hT[:, c, 0:N], start=(c == 0), stop=(c == 7), reads=[W.win, W.hT], writes=[pbs])
    rope_rows(P, W, W.krr[64:96, 0:N], pa[64:96, 0:N], pbs[64:96, 0:N], tok0, N, [pa, pbs], [W.krr], roped)
    for c4 in range(4):
        pc = rot(K)
        for c in range(8):
            P.mm(pc[:, 0:N], W.win[:, c, 672 + c4 * 128:672 + (c4 + 1) * 128], W.hT[:, c, 0:N], start=(c == 0), stop=(c == 7),
                 reads=[W.win, W.hT], writes=[pc])
        st = stage(W)
        P.copy(st[:, 0:N], pc[:, 0:N], reads=[pc], writes=[st], eng='act')
        P.dma(ST[c4 * 128:(c4 + 1) * 128, tok0:tok0 + N], st[:, 0:N], reads=[st], writes=[P.view(ST)], is_output=True)
    for h in range(NH):
        pq, pqs = rot(K), rot(K)
        for c in range(3):
            P.mm(pq[0:96, 0:N], W.wuq[:, c, h * 96:(h + 1) * 96], W.cqn[:, c, 0:N], start=(c == 0), stop=(c == 2), reads=[W.wuq, W.cqn], writes=[pq])
        if roped:
            for c in range(3):
                P.mm(pqs[0:96, 0:N], W.wuqs[:, c, h * 96:(h + 1) * 96], W.cqn[:, c, 0:N], start=(c == 0), stop=(c == 2), reads=[W.wuqs, W.cqn], writes=[pqs])
        st = stage(W)
        P.copy(st[0:64, 0:N], pq[0:64, 0:N], reads=[pq], writes=[st], eng='act')
        rope_rows(P, W, st[64:96, 0:N], pq[64:96, 0:N], pqs[64:96, 0:N], tok0, N, [pq, pqs], [st], roped)
        P.dma(QT[h, :, tok0:tok0 + N], st[0:96, 0:N], reads=[st], writes=[P.view(QT)], is_output=True)
        pk = rot(K)
        for c in range(2):
            P.mm(pk[0:64, 0:N], W.wk[:, c, h * 64:(h + 1) * 64], W.ckvn[:, c, 0:N], start=(c == 0), stop=(c == 1), reads=[W.wk, W.ckvn], writes=[pk])
        st = stage(W)
        P.copy(st[0:64, 0:N], pk[0:64, 0:N], reads=[pk], writes=[st], eng='act')
        P.copy(st[64:96, 0:N], W.krr[64:96, 0:N], reads=[W.krr], writes=[st], eng='pool')
        P.dma(KT[h, :, tok0:tok0 + N], st[0:96, 0:N], reads=[st], writes=[P.view(KT)], is_output=True)
    for t in range(nt):
        pv = rot(K)
        for c in range(2):
            P.mm(pv[:], W.ckvn[:, c, t * 128:(t + 1) * 128], W.wv[:, c, :], start=(c == 0), stop=(c == 1), reads=[W.ckvn, W.wv], writes=[pv])
        st = stage(W)
        P.copy(st[:], pv[:], reads=[pv], writes=[st])
        P.dma(V[tok0 + t * 128:tok0 + (t + 1) * 128, :], st[:], reads=[st], writes=[P.view(V)], is_output=True)


def host_rope_tables():
    n = 8192
    row = np.repeat(np.arange(n // 64, dtype=np.float32), 64)
    col = np.tile(np.arange(64, dtype=np.float32), n // 64)
    nf = 8
    inv = (np.float32(10000.0) ** (-np.arange(nf, dtype=np.float32) / np.float32(nf))).astype(np.float32)
    ang = np.concatenate([row[:, None] * inv, col[:, None] * inv], -1).astype(np.float32)
    c, s = np.cos(ang).astype(np.float32), np.sin(ang).astype(np.float32)
    cos = np.zeros((96, n), np.float32)
    sin = np.zeros((96, n), np.float32)
    cos[64:80] = c.T
    cos[80:96] = c.T
    sin[64:80] = -s.T
    sin[80:96] = s.T
    return cos, sin


def build_L1():
    P = Prog()
    xin = P.dram("xin", [NT_CORE * 128, D]); cnd = P.dram("cnd", [2, D]); mod_w = P.dram("mod_w", [D, 9 * D]); mod_b = P.dram("mod_b", [9 * D])
    g1 = P.dram("g1", [D]); gm = P.dram("gm", [D]); wg = P.dram("wg", [D, FH]); wu = P.dram("wu", [D, FH]); wd = P.dram("wd", [FH, D])
    win = P.dram("win", [D, EV_EXT]); wuq = P.dram("wuq", [QR, NH * 96]); wuqs = P.dram("wuqs", [QR, NH * 96])
    wk = P.dram("wk", [KVR, NH * 64]); wv = P.dram("wv", [KVR, 512]); gq = P.dram("gq", [QR]); gkv = P.dram("gkv", [KVR])
    cos = P.dram("cos", [96, 4096]); sin = P.dram("sin", [96, 4096])
    xout = P.dram("xout", [NT_CORE * 128, D], kind="ExternalOutput")
    QT = P.dram("QT", [NH, 96, NT_CORE * 128], kind="ExternalOutput")
    KT = P.dram("KT", [NH, 96, NT_CORE * 128], kind="ExternalOutput")
    V = P.dram("V", [NT_CORE * 128, 512], kind="ExternalOutput")
    ST = P.dram("ST", [512, NT_CORE * 128], kind="ExternalOutput")
    K = consts(P)
    mfm, gbc = adaln(P, K, cnd, mod_w, mod_b, [2], ks=[0, 1, 2, 3, 4])
    sce, sh = norm_params(P, K, mfm, g1, 0, "n1")
    scem, shm = norm_params(P, K, mfm, gm, 3, "nm")
    xs = [P.sb([128, D], F32, f"x{t}") for t in range(4)]
    xviews = [P.view(xout) for _ in BLOCKS]
    P.push()
    W = ffn_alloc(P)
    ffn_load(P, W, wg, wu, wd)
    for bi, (t0, nt, n) in enumerate(BLOCKS):
        for t in range(nt):
            P.dma(xs[t][:], xin[(t0 + t) * 128:(t0 + t + 1) * 128, :], writes=[xs[t]])
        ffn_block(P, K, W, xs, nt, n, sce, sh, gbc[2])
        for t in range(nt):
            P.dma(xout[(t0 + t) * 128:(t0 + t + 1) * 128, :], xs[t][:], reads=[xs[t]], writes=[xviews[bi]], is_output=True)
    P.pop()
    P.push()
    W = evin_alloc(P)
    load_w(P, W.win, win); load_w(P, W.wuq, wuq); load_w(P, W.wuqs, wuqs); load_w(P, W.wk, wk); load_w(P, W.wv, wv)
    load_fm(P, K, W.gq, W.gq[:], gq.t.rearrange("(c p) -> c p", p=128), 3)
    load_fm(P, K, W.gkv, W.gkv[:], gkv.t.rearrange("(c p) -> c p", p=128), 2)
    P.dma(W.cos[64:96, :], cos[64:96, :], writes=[W.cos]); P.dma(W.sin[64:96, :], sin[64:96, :], writes=[W.sin])
    for bi, (t0, nt, n) in enumerate(BLOCKS):
        for t in range(nt):
            P.dma(xs[t][:], xout[(t0 + t) * 128:(t0 + t + 1) * 128, :], reads=[xviews[bi]], writes=[xs[t]])
        evin_block(P, K, W, xs, t0, nt, n, scem, shm, QT, KT, V, ST)
    P.pop()
    return P.build()


def host_L1(d, li=0):
    cos, sin = host_rope_tables()
    w_in = d['ev_w_in'][0]
    kr = w_in[:, 640:672]
    krs = np.concatenate([kr[:, 16:32], kr[:, 0:16]], 1)
    z64 = np.zeros((D, 64), np.float32)
    win = np.concatenate([w_in, z64, kr, z64, krs], 1)
    wuq = d['mla_w_uq'][0]
    wq3 = wuq.reshape(QR, NH, 96)
    wuqs = np.zeros((QR, NH, 96), np.float32)
    wuqs[:, :, 64:80] = wq3[:, :, 80:96]
    wuqs[:, :, 80:96] = wq3[:, :, 64:80]
    wukv = d['mla_w_ukv'][0].reshape(KVR, NH, 128)
    wk = np.ascontiguousarray(wukv[:, :, 0:64]).reshape(KVR, NH * 64)
    wv = np.ascontiguousarray(wukv[:, :, 64:128]).reshape(KVR, 512)
    ins = []
    for j in range(8):
        b, hf = j // 2, j % 2
        x = np.concatenate([d['x'][b, hf * 4096:(hf + 1) * 4096], d['ctx'][b, hf * 128:(hf + 1) * 128]], 0)
        ins.append({"xin": x, "cnd": np.stack([d['c'][b], d['c_ctx']]), "mod_w": d['mod_w'][li], "mod_b": d['mod_b'][li],
                    "g1": d['norm_ffn1'][li], "gm": d['norm_mix'][li],
                    "wg": d['ffn1_w_gate'][li], "wu": d['ffn1_w_up'][li], "wd": d['ffn1_w_down'][li],
                    "win": win, "wuq": wuq, "wuqs": wuqs.reshape(QR, NH * 96), "wk": wk, "wv": wv,
                    "gq": d['mla_q_norm'][0], "gkv": d['mla_kv_norm'][0],
                    "cos": np.ascontiguousarray(cos[:, hf * 4096:(hf + 1) * 4096]), "sin": np.ascontiguousarray(sin[:, hf * 4096:(hf + 1) * 4096]),
                    "ident": np.eye(128, dtype=np.float32)})
    return ins


NKT = 66
NQ = 8192 + 256
SCALE = 96 ** -0.5


def build_L2():
    P = Prog()
    QTd = P.dram("QT", [4, 96, NQ]); KTd = P.dram("KT", [4, 96, NKT * 128]); Vd = P.dram("V", [NKT * 128, 4, 64])
    Od = P.dram("O", [NQ, 256], kind="ExternalOutput")
    QT = P.sb([96, 4, NQ], BF16, "QT"); KT = P.sb([96, 4, NKT * 128], BF16, "KT"); V = P.sb([128, NKT, 4, 65], BF16, "V")
    P.memset(V[:, :, :, 64:65], 1.0, writes=[V])
    Vv = Vd.t.rearrange("(k p) h d -> p k h d", p=128)
    for h in range(4):
        for s in range(0, NKT * 128, 2112):
            P.dma(KT[:, h, s:s + 2112], KTd[h, :, s:s + 2112], writes=[KT], q='pool')
            P.dma(QT[:, h, s:s + 2112], QTd[h, :, s:s + 2112], writes=[QT], q='pool')
    for kt in range(NKT):
        P.dma(V[:, kt, :, 0:64], Vv[:, kt, :, :], writes=[V], q='pool')
    po = [P.ps([128, 512], F32, f"po{i}") for i in range(4)]
    pss = [P.ps([128, 512], F32, f"pss{i}") for i in range(4)]
    pt = [P.sb([128, 512], BF16, f"pt{i}") for i in range(3)]
    ost = [P.sb([128, 256], F32, f"ost{i}") for i in range(8)]
    rc = P.sb([128, 8], F32, "rc")
    blocks = [(qb * 512, 4, 0, NKT) for qb in range(16)] + [(8192, 2, 0, 2)]
    pt = pt + [P.sb([128, 512], BF16, "pt3")]
    iters = []
    for bi, (q0, nq, k0, k1) in enumerate(blocks):
        for h in range(4):
            for kt in range(k0, k1):
                iters.append((bi, q0, nq, k0, k1, h, kt))
    LAG = 2

    def front(idx):
        bi, q0, nq, k0, k1, h, kt = iters[idx]
        N = nq * 128
        ps = pss[idx % 4]
        ptt = pt[idx % 4]
        P.mm(ps[:, 0:N], KT[:, h, kt * 128:(kt + 1) * 128], QT[:, h, q0:q0 + N], reads=[KT, QT], writes=[ps])
        P.act(ptt[:, 0:N], ps[:, 0:N], AF.Exp, scale=SCALE, reads=[ps], writes=[ptt])

    def back(idx):
        bi, q0, nq, k0, k1, h, kt = iters[idx]
        ptt = pt[idx % 4]
        osts = ost[(bi % 2) * 4:(bi % 2) * 4 + 4]
        for i in range(nq):
            P.mm(po[i][:, 0:65], ptt[:, i * 128:(i + 1) * 128], V[:, kt, h, :], start=(kt == k0), stop=(kt == k1 - 1),
                 reads=[ptt, V], writes=[po[i]])
        if kt == k1 - 1:
            for i in range(nq):
                P.op('dve', lambda e, i=i: e.reciprocal(rc[:, i:i + 1], po[i][:, 64:65]), reads=[po[i]], writes=[rc])
                P.ts(osts[i][:, h * 64:(h + 1) * 64], po[i][:, 0:64], rc[:, i:i + 1], ALU.mult, reads=[po[i], rc], writes=[osts[i]])
            if h == 3:
                for i in range(nq):
                    P.dma(Od[q0 + i * 128:q0 + (i + 1) * 128, :], osts[i][:], reads=[osts[i]], writes=[P.view(Od)], is_output=True)

    for idx in range(len(iters) + LAG):
        if idx < len(iters):
            front(idx)
        if idx >= LAG:
            back(idx - LAG)
    return P.build()


def host_L2(r1):
    ins = []
    for j in range(8):
        b, hg = j // 2, j % 2
        a, c = r1[2 * b], r1[2 * b + 1]
        hs = slice(4 * hg, 4 * hg + 4)
        QT = np.concatenate([a['QT'][hs, :, :4096], c['QT'][hs, :, :4096], a['QT'][hs, :, 4096:], c['QT'][hs, :, 4096:]], 2)
        KT = np.concatenate([a['KT'][hs, :, 4096:], c['KT'][hs, :, 4096:], a['KT'][hs, :, :4096], c['KT'][hs, :, :4096]], 2)
        Vf = np.concatenate([a['V'][4096:], c['V'][4096:], a['V'][:4096], c['V'][:4096]], 0).reshape(NKT * 128, 8, 64)
        ins.append({"QT": np.ascontiguousarray(QT), "KT": np.ascontiguousarray(KT), "V": np.ascontiguousarray(Vf[:, hs])})
    return ins


MAGIC = 12582912.0
TWO_PI = 6.283185307179586
PI = 3.141592653589793
LC = 512
NTOK5 = 256 + 8192


def sincos(P, out_sin, out_cos, x, tmp_k, tmp_z, bufs_r, bufs_w, shift_only=None):
    for out, sh in ((out_sin, 0.0), (out_cos, PI / 2)):
        if out is None:
            continue
        src = x
        if sh:
            P.ts(tmp_z, x, sh, ALU.add, reads=bufs_r + bufs_w, writes=bufs_w)
            src = tmp_z
        P.ts(tmp_k, src, 1.0 / TWO_PI, ALU.mult, MAGIC, ALU.add, reads=bufs_r + bufs_w, writes=bufs_w)
        P.ts(tmp_k, tmp_k, -MAGIC, ALU.add, reads=bufs_w, writes=bufs_w)
        P.stt(tmp_z, tmp_k, -TWO_PI, src, ALU.mult, ALU.add, reads=bufs_r + bufs_w, writes=bufs_w)
        P.ts(tmp_z, tmp_z, PI, ALU.min, -PI, ALU.max, reads=bufs_w, writes=bufs_w)
        P.act(out, tmp_z, AF.Sin, reads=bufs_w, writes=bufs_w)


def build_L3(dbg=False, same=True):
    P = Prog(same_engine_sync=same)
    uT = P.dram("uT", [256, NTOK5])
    prm = P.dram("prm", [128, 3, 2, 8])
    Bm = P.dram("Bm", [128, 2, 2, 8, 128])
    Cm = P.dram("Cm", [128, 2, 2, 8, 128])
    dsk = P.dram("dsk", [2, 128])
    tau = P.dram("tau", [128, LC])
    identd = P.dram("ident", [128, 128])
    Y = P.dram("Y", [256, NTOK5], kind="ExternalOutput")
    ident = P.sb([128, 128], F32, "ident"); P.dma(ident[:], identd[:], writes=[ident])
    pbs = [P.ps([128, 512], F32, f"pb{i}") for i in range(8)]
    rr = [0]

    def rot():
        rr[0] += 1
        return pbs[rr[0] % 8]
    ub = P.sb([128, 2, NTOK5], BF16, "ub")
    for ct in range(2):
        for s in range(0, NTOK5, 2112):
            P.dma(ub[:, ct, s:s + 2112], uT[ct * 128:(ct + 1) * 128, s:s + 2112], writes=[ub], q='pool')
    Bb = P.sb([128, 2, 2, 8, 128], BF16, "Bb"); Cb = P.sb([128, 2, 2, 8, 128], BF16, "Cb")
    for d in range(2):
        P.dma(Bb[:, d], Bm[:, d], writes=[Bb], q='pool')
        P.dma(Cb[:, d], Cm[:, d], writes=[Cb], q='pool')
    for d in range(2):
        P.ts(Cb[:, d, 1], Cb[:, d, 1], -1.0, ALU.mult, reads=[Cb], writes=[Cb])
    tau1 = P.sb([128, LC], F32, "tau"); P.dma(tau1[:], tau[:], writes=[tau1])
    dcol = P.sb([128, 2], F32, "dcol")
    dtmp = P.sb([2, 128], F32, "dtmp"); P.dma(dtmp[:], dsk[:], writes=[dtmp])
    pb = rot()
    P.tr(pb[:, 0:2], dtmp[:], ident[0:2, 0:2], reads=[dtmp, ident], writes=[pb])
    P.copy(dcol[:], pb[:, 0:2], reads=[pb], writes=[dcol])
    yacc = P.sb([128, 2, NTOK5], F32, "yacc")
    for ct in range(2):
        for s in range(0, NTOK5, 2112):
            P.ts(yacc[:, ct, s:s + 2112], ub[:, ct, s:s + 2112], dcol[:, ct:ct + 1], ALU.mult, reads=[ub, dcol], writes=[yacc], eng='pool')
    pr = P.sb([128, 3, 16], F32, "pr"); P.dma(pr[:], prm.t.rearrange("p a d s -> p a (d s)"), writes=[pr])
    sm = P.sb([128, 16, 16], F32, "sm")
    P.memset(sm[:], 0.0, writes=[sm])
    S = lambda i: sm[:, i, :]
    R, Wr = [pr, sm], [sm]
    lre, lim, lst = pr[:, 0, :], pr[:, 1, :], pr[:, 2, :]
    P.act(S(0), lst, AF.Exp, reads=R, writes=Wr)
    P.tt(S(1), lre, S(0), ALU.mult, reads=R, writes=Wr)
    P.tt(S(2), lim, S(0), ALU.mult, reads=R, writes=Wr)
    P.act(S(3), S(1), AF.Exp, reads=R, writes=Wr)
    sincos(P, S(4), S(5), S(2), S(6), S(7), R, Wr)
    P.tt(S(8), S(3), S(5), ALU.mult, reads=R, writes=Wr)
    P.ts(S(8), S(8), -1.0, ALU.add, reads=R, writes=Wr)
    P.tt(S(9), S(3), S(4), ALU.mult, reads=R, writes=Wr)
    P.tt(S(6), lre, lre, ALU.mult, reads=R, writes=Wr)
    P.tt(S(7), lim, lim, ALU.mult, reads=R, writes=Wr)
    P.tt(S(6), S(6), S(7), ALU.add, reads=R, writes=Wr)
    P.op('dve', lambda e: e.reciprocal(S(6), S(6)), reads=R, writes=Wr)
    P.tt(S(10), S(8), lre, ALU.mult, reads=R, writes=Wr)
    P.tt(S(7), S(9), lim, ALU.mult, reads=R, writes=Wr)
    P.tt(S(10), S(10), S(7), ALU.add, reads=R, writes=Wr)
    P.tt(S(10), S(10), S(6), ALU.mult, reads=R, writes=Wr)
    P.tt(S(11), S(9), lre, ALU.mult, reads=R, writes=Wr)
    P.tt(S(7), S(8), lim, ALU.mult, reads=R, writes=Wr)
    P.tt(S(11), S(11), S(7), ALU.subtract, reads=R, writes=Wr)
    P.tt(S(11), S(11), S(6), ALU.mult, reads=R, writes=Wr)
    P.ts(S(12), S(10), -1.0, ALU.mult, reads=R, writes=Wr)
    TH, RR, CFR, CFI, NCFR = 2, 3, 10, 11, 12
    if dbg:
        dsm = P.dram("dsm", [128, 16, 16], kind="ExternalOutput")
        P.dma(dsm[:], sm[:], reads=[sm], writes=[dsm], is_output=True)
        dtab = P.dram("dtab", [5, 128, LC], kind="ExternalOutput")
        dk = P.dram("dk", [6, 128, LC], kind="ExternalOutput")
    tabs = [[P.sb([128, LC], F32, f"tab{i}_{j}") for j in range(5)] for i in range(2)]
    tmpa = P.sb([128, LC], F32, "tmpa"); tmpb = P.sb([128, LC], F32, "tmpb")
    ones = P.sb([128, LC], F32, "ones"); P.memset(ones[:], 1.0, writes=[ones])
    wk = [[P.sb([128, LC], F32, f"wk{i}_{j}") for j in range(8)] for i in range(2)]
    hb = [[P.sb([128, LC], BF16, f"hb{i}_{j}") for j in range(2)] for i in range(2)]
    ini = P.sb([128, 4], F32, "ini")
    chunks_f = [(0, 256)] + [(256 + i * LC, LC) for i in range(16)]
    wk = wk + [[P.sb([128, LC], F32, f"wk2_{j}") for j in range(8)]]
    hb = hb + [[P.sb([128, LC], BF16, f"hb2_{j}") for j in range(2)]]
    units = []
    for d in range(2):
        for st in range(8):
            seq = [chunks_f[0]] + (chunks_f[1:] if d == 0 else chunks_f[:0:-1])
            for qi, (c0, n) in enumerate(seq):
                units.append((d, st, qi, c0, n, qi == len(seq) - 1))
    NU = len(units)
    pys = {}

    def views(d, n):
        if d == 0:
            return (lambda a: a[:, 0:n]), n - 1
        return (lambda a: a[:, n - 1::-1] if n < LC else a[:, ::-1]), 0

    def gen_tables(d, st):
        col = d * 8 + st
        Ere, Eim, Tre, Tim, rf = tabs[col % 2]
        P.ts(tmpa[:], tau1[:], sm[:, TH, col:col + 1], ALU.mult, reads=[tau1, sm], writes=[tmpa])
        sincos(P, Eim[:], Ere[:], tmpa[:], tmpb[:], Tre[:], [tmpa], [tmpb, Tre, Eim, Ere])
        P.ts(tmpb[:], Ere[:], sm[:, CFR, col:col + 1], ALU.mult, reads=[Ere, sm], writes=[tmpb])
        P.stt(Tre[:], Eim[:], sm[:, CFI, col:col + 1], tmpb[:], ALU.mult, ALU.add, reads=[Eim, sm, tmpb], writes=[Tre])
        P.ts(tmpb[:], Ere[:], sm[:, CFI, col:col + 1], ALU.mult, reads=[Ere, sm], writes=[tmpb])
        P.stt(Tim[:], Eim[:], sm[:, NCFR, col:col + 1], tmpb[:], ALU.mult, ALU.add, reads=[Eim, sm, tmpb], writes=[Tim])
        P.ts(rf[:], ones[:], sm[:, RR, col:col + 1], ALU.mult, reads=[ones, sm], writes=[rf])

    def stA(u):
        d, st, qi, c0, n, lastq = units[u]
        col = d * 8 + st
        ct = st // 4
        if qi == 0:
            gen_tables(d, st)
        Ere, Eim, Tre, Tim, rf = tabs[col % 2]
        kinr, kini, kr, ki, t1, t2, t3, t4 = wk[u % 3]
        tsl, _ = views(d, n)
        pr_, pi_ = rot(), rot()
        P.mm(pr_[:, 0:n], Bb[:, d, 0, st, :], ub[:, ct, c0:c0 + n], reads=[Bb, ub], writes=[pr_])
        P.mm(pi_[:, 0:n], Bb[:, d, 1, st, :], ub[:, ct, c0:c0 + n], reads=[Bb, ub], writes=[pi_])
        P.tt(t1[:, 0:n], pr_[:, 0:n], tsl(Tre), ALU.mult, reads=[pr_, Tre], writes=[t1])
        P.tt(t2[:, 0:n], pi_[:, 0:n], tsl(Tim), ALU.mult, reads=[pi_, Tim], writes=[t2])
        P.tt(kinr[:, 0:n], t1[:, 0:n], t2[:, 0:n], ALU.subtract, reads=[t1, t2], writes=[kinr])
        P.tt(t3[:, 0:n], pi_[:, 0:n], tsl(Tre), ALU.mult, reads=[pi_, Tre], writes=[t3])
        P.tt(t4[:, 0:n], pr_[:, 0:n], tsl(Tim), ALU.mult, reads=[pr_, Tim], writes=[t4])
        P.tt(kini[:, 0:n], t3[:, 0:n], t4[:, 0:n], ALU.add, reads=[t3, t4], writes=[kini])

    def stB(u):
        d, st, qi, c0, n, lastq = units[u]
        col = d * 8 + st
        Ere, Eim, Tre, Tim, rf = tabs[col % 2]
        kinr, kini, kr, ki, t1, t2, t3, t4 = wk[u % 3]
        dsl, last = views(d, n)
        i_re = 0.0 if qi == 0 else ini[:, 0:1]
        i_im = 0.0 if qi == 0 else ini[:, 1:2]
        P.scan(dsl(kr), rf[:, 0:n], dsl(kinr), i_re, reads=[rf, kinr, ini], writes=[kr])
        P.scan(dsl(ki), rf[:, 0:n], dsl(kini), i_im, reads=[rf, kini, ini], writes=[ki])
        if not lastq:
            krl, kil = kr[:, last:last + 1], ki[:, last:last + 1]
            erl, eil = Ere[:, n - 1:n], Eim[:, n - 1:n]
            P.tt(ini[:, 2:3], kil, eil, ALU.mult, reads=[ki, Eim], writes=[ini])
            P.stt(ini[:, 0:1], krl, erl, ini[:, 2:3], ALU.mult, ALU.subtract, reads=[kr, Ere, ini], writes=[ini])
            P.tt(ini[:, 3:4], kil, erl, ALU.mult, reads=[ki, Ere], writes=[ini])
            P.stt(ini[:, 1:2], krl, eil, ini[:, 3:4], ALU.mult, ALU.add, reads=[kr, Eim, ini], writes=[ini])

    def stC(u):
        d, st, qi, c0, n, lastq = units[u]
        col = d * 8 + st
        Ere, Eim, Tre, Tim, rf = tabs[col % 2]
        kinr, kini, kr, ki, t1, t2, t3, t4 = wk[u % 3]
        hh = hb[u % 3]
        tsl, _ = views(d, n)
        P.tt(t1[:, 0:n], kr[:, 0:n], tsl(Ere), ALU.mult, reads=[kr, Ere], writes=[t1], eng='pool')
        P.tt(t2[:, 0:n], ki[:, 0:n], tsl(Eim), ALU.mult, reads=[ki, Eim], writes=[t2], eng='pool')
        P.tt(hh[0][:, 0:n], t1[:, 0:n], t2[:, 0:n], ALU.subtract, reads=[t1, t2], writes=[hh[0]])
        P.tt(t3[:, 0:n], kr[:, 0:n], tsl(Eim), ALU.mult, reads=[kr, Eim], writes=[t3], eng='pool')
        P.tt(t4[:, 0:n], ki[:, 0:n], tsl(Ere), ALU.mult, reads=[ki, Ere], writes=[t4])
        P.tt(hh[1][:, 0:n], t3[:, 0:n], t4[:, 0:n], ALU.add, reads=[t3, t4], writes=[hh[1]])

    def stD(u):
        d, st, qi, c0, n, lastq = units[u]
        ct = st // 4
        hh = hb[u % 3]
        py = rot()
        P.mm(py[:, 0:n], Cb[:, d, 0, st, :], hh[0][:, 0:n], start=True, stop=False, reads=[Cb, hh[0]], writes=[py])
        P.mm(py[:, 0:n], Cb[:, d, 1, st, :], hh[1][:, 0:n], start=False, stop=True, reads=[Cb, hh[1]], writes=[py])
        P.tt(yacc[:, ct, c0:c0 + n], yacc[:, ct, c0:c0 + n], py[:, 0:n], ALU.add, reads=[py], writes=[yacc])

    for s_ in range(NU + 2):
        if s_ < NU:
            stA(s_)
        if 0 <= s_ - 1 < NU:
            stB(s_ - 1)
            stC(s_ - 1)
        if 0 <= s_ - 2 < NU:
            stD(s_ - 2)
    for ct in range(2):
        for s in range(0, NTOK5, 2112):
            P.dma(Y[ct * 128:(ct + 1) * 128, s:s + 2112], yacc[:, ct, s:s + 2112], reads=[yacc], writes=[P.view(Y)], is_output=True)
    return P.build()


def host_L3(d, r1):
    ins = []
    lre = d['s5_lambda_re'][0]; lim = d['s5_lambda_im'][0]; lst = d['s5_log_step'][0]
    bre = d['s5_b_re'][0]; bim = d['s5_b_im'][0]; cre = d['s5_c_re'][0]; cim = d['s5_c_im'][0]
    tau = np.broadcast_to(np.arange(1, LC + 1, dtype=np.float32)[None, :], (128, LC)).copy()
    for j in range(8):
        b, gh = j // 2, j % 2
        a, c = r1[2 * b], r1[2 * b + 1]
        rows = slice(256 * gh, 256 * gh + 256)
        uT = np.concatenate([a['ST'][rows, 4096:], c['ST'][rows, 4096:], a['ST'][rows, :4096], c['ST'][rows, :4096]], 1)
        prm = np.zeros((128, 3, 2, 8), np.float32)
        Bm = np.zeros((128, 2, 2, 8, 128), np.float32)
        Cm = np.zeros((128, 2, 2, 8, 128), np.float32)
        for dr in range(2):
            for st in range(8):
                for gm in range(2):
                    g = 16 * gh + 2 * st + gm
                    ps = slice(gm * 64, gm * 64 + 64)
                    prm[ps, 0, dr, st] = lre[dr, g]; prm[ps, 1, dr, st] = lim[dr, g]; prm[ps, 2, dr, st] = lst[dr, g]
                    gl = (2 * st + gm) % 8
                    ks = slice(gl * 16, gl * 16 + 16)
                    Bm[ks, dr, 0, st, ps] = bre[dr, g].T
                    Bm[ks, dr, 1, st, ps] = bim[dr, g].T
                    Cm[ps, dr, 0, st, ks] = cre[dr, g].T
                    Cm[ps, dr, 1, st, ks] = cim[dr, g].T
        ins.append({"uT": np.ascontiguousarray(uT), "prm": prm, "Bm": Bm, "Cm": Cm,
                    "dsk": np.ascontiguousarray(d['s5_d'][0][rows].reshape(2, 128)), "tau": tau, "ident": np.eye(128, dtype=np.float32)})
    return ins


def xio(P, xs, nt, t0, src, sview=None, dst=None, dview=None, load=True, out=False):
    for t in range(nt):
        rows = slice((t0 + t) * 128, (t0 + t + 1) * 128)
        if load:
            P.dma(xs[t][:], src[rows, :], reads=[sview] if sview is not None else [], writes=[xs[t]])
        else:
            P.dma(dst[rows, :], xs[t][:], reads=[xs[t]], writes=[dview], is_output=out)


def pass_ffn(P, K, xs, src, sviews, dst, dviews, wg, wu, wd, sce, sh, gbc, out=False, post=None, blocks=None):
    P.push()
    W = ffn_alloc(P)
    ffn_load(P, W, wg, wu, wd)
    for bi, (t0, nt, n) in enumerate(blocks or BLOCKS):
        xio(P, xs, nt, t0, src, sviews[bi] if sviews else None)
        ffn_block(P, K, W, xs, nt, n, sce, sh, gbc)
        if post is not None:
            post(W, xs, nt)
        xio(P, xs, nt, t0, None, None, dst, dviews[bi], load=False, out=out)
    P.pop()


def pass_mixout(P, K, xs, src, sviews, dst, dviews, wout, gbc, catsrc, glu=None, blocks=None):
    P.push()
    wo = P.sb([128, 8, D], BF16, "wout")
    load_w(P, wo, wout)
    cat = P.sb([128, 8, 512], BF16, "cat")
    tmp = P.sb([128, 512], F32, "tmpm")
    if glu is not None:
        wgl = P.sb([128, 4, 512], BF16, "wglu")
        load_w(P, wgl, glu[0])
        bgl = P.sb([128, 4], F32, "bglu")
        load_fm(P, K, bgl, bgl[:], glu[1].t.rearrange("(c p) -> c p", p=128), 4)
        yp = P.sb([128, 4, 512], F32, "yp")
        yg = P.sb([128, 4, 512], BF16, "yg")
        sg = P.sb([128, 512], BF16, "sg")
    for bi, (t0, nt, n) in enumerate(blocks or BLOCKS):
        N = nt * 128
        tok = slice(t0 * 128, t0 * 128 + N)
        xio(P, xs, nt, t0, src, sviews[bi] if sviews else None)
        for kc in range(8):
            if glu is not None and kc >= 4:
                P.dma(yp[:, kc - 4, 0:N], catsrc[kc][:, tok], writes=[yp])
            else:
                P.dma(cat[:, kc, 0:N], catsrc[kc][:, tok], writes=[cat], q='pool')
        if glu is not None:
            for kc in range(4):
                P.act(yg[:, kc, 0:N], yp[:, kc, 0:N], AF.Gelu, reads=[yp], writes=[yg])
            for oc in range(4):
                pb = rot(K)
                for kc in range(4):
                    P.mm(pb[:, 0:N], wgl[:, kc, oc * 128:(oc + 1) * 128], yg[:, kc, 0:N], start=(kc == 0), stop=(kc == 3),
                         reads=[wgl, yg], writes=[pb])
                P.act(sg[:, 0:N], pb[:, 0:N], AF.Sigmoid, bias=bgl[:, oc:oc + 1], reads=[pb, bgl], writes=[sg])
                P.tt(cat[:, 4 + oc, 0:N], sg[:, 0:N], yg[:, oc, 0:N], ALU.mult, reads=[sg, yg], writes=[cat])
        for t in range(nt):
            for hf in range(2):
                py = rot(K)
                for kc in range(8):
                    P.mm(py[:], cat[:, kc, t * 128:(t + 1) * 128], wo[:, kc, hf * 512:(hf + 1) * 512], start=(kc == 0), stop=(kc == 7),
                         reads=[cat, wo], writes=[py])
                P.tt(tmp[:], py[:], gbc[:, n, hf * 512:(hf + 1) * 512], ALU.mult, reads=[py, gbc], writes=[tmp])
                xo = xs[t][:, hf * 512:(hf + 1) * 512]
                P.tt(xo, xo, tmp[:], ALU.add, reads=[xs[t], tmp], writes=[xs[t]], eng='pool')
        xio(P, xs, nt, t0, None, None, dst, dviews[bi], load=False)
    P.pop()


OD_IN = 2560


def pass_inproj_odd(P, K, xs, src, sviews, win, sce, sh, UT):
    P.push()
    W = FFNW()
    W.win = P.sb([128, 8, OD_IN], BF16, "winod")
    load_w(P, W.win, win)
    W.xn = [P.sb([128, D], BF16, f"xn{t}") for t in range(4)]
    W.ss = P.sb([128, 4], F32, "ss"); W.rstd = P.sb([128, 4], F32, "rstd")
    W.hT = P.sb([128, 8, 512], BF16, "hT")
    W.st = [P.sb([128, 512], F32, f"st{i}") for i in range(3)]
    W.sti = 0
    for bi, (t0, nt, n) in enumerate(BLOCKS):
        N = nt * 128
        xio(P, xs, nt, t0, src, sviews[bi] if sviews else None)
        norm_T(P, K, W, xs, nt, n, sce, sh, W.hT)
        for oc in range(OD_IN // 128):
            pc = rot(K)
            for c in range(8):
                P.mm(pc[:, 0:N], W.win[:, c, oc * 128:(oc + 1) * 128], W.hT[:, c, 0:N], start=(c == 0), stop=(c == 7),
                     reads=[W.win, W.hT], writes=[pc])
            st = stage(W)
            P.copy(st[:, 0:N], pc[:, 0:N], reads=[pc], writes=[st], eng=('act' if oc % 2 else 'dve'))
            P.dma(UT[oc * 128:(oc + 1) * 128, t0 * 128:t0 * 128 + N], st[:, 0:N], reads=[st], writes=[P.view(UT)], is_output=True)
    P.pop()


def build_L4():
    P = Prog()
    NTOK = NT_CORE * 128
    xin = P.dram("xin", [NTOK, D]); cnd = P.dram("cnd", [2, D])
    mw0 = P.dram("mw0", [D, 9 * D]); mb0 = P.dram("mb0", [9 * D]); mw1 = P.dram("mw1", [D, 9 * D]); mb1 = P.dram("mb1", [9 * D])
    OT = P.dram("OT", [512, NTOK]); YT = P.dram("YT", [512, NTOK])
    wglu = P.dram("wglu", [512, 512]); bglu = P.dram("bglu", [512]); wout = P.dram("wout", [D, D])
    g2 = P.dram("g2", [D]); wg2 = P.dram("wg2", [D, FH]); wu2 = P.dram("wu2", [D, FH]); wd2 = P.dram("wd2", [FH, D])
    g1 = P.dram("g1", [D]); wg1 = P.dram("wg1", [D, FH]); wu1 = P.dram("wu1", [D, FH]); wd1 = P.dram("wd1", [FH, D])
    gm = P.dram("gm", [D]); win = P.dram("win", [D, OD_IN])
    sa = P.dram("scr_a", [NTOK, D], kind="Internal"); sbb = P.dram("scr_b", [NTOK, D], kind="Internal")
    xout = P.dram("xout", [NTOK, D], kind="ExternalOutput")
    UT = P.dram("UT", [OD_IN, NTOK], kind="ExternalOutput")
    K = consts(P)
    xs = [P.sb([128, D], F32, f"x{t}") for t in range(4)]
    va = [P.view(sa) for _ in BLOCKS]; vb = [P.view(sbb) for _ in BLOCKS]; vo = [P.view(xout) for _ in BLOCKS]
    P.push()
    mfm, gbc = adaln(P, K, cnd, mw0, mb0, [5, 8], ks=[5, 6, 7, 8])
    sce2, sh2 = norm_params(P, K, mfm, g2, 6, "n2")
    cats = [OT[kc * 128:(kc + 1) * 128, :] for kc in range(4)] + [YT[kc * 128:(kc + 1) * 128, :] for kc in range(4)]
    pass_mixout(P, K, xs, xin, None, sa, va, wout, gbc[5], cats, glu=(wglu, bglu))
    pass_ffn(P, K, xs, sa, va, sbb, vb, wg2, wu2, wd2, sce2, sh2, gbc[8])
    P.pop()
    mfm1, gbc1 = adaln(P, K, cnd, mw1, mb1, [2], ks=[0, 1, 2, 3, 4])
    sce1, sh1 = norm_params(P, K, mfm1, g1, 0, "n1b")
    scem, shm = norm_params(P, K, mfm1, gm, 3, "nmb")
    pass_ffn(P, K, xs, sbb, vb, xout, vo, wg1, wu1, wd1, sce1, sh1, gbc1[2], out=True)
    pass_inproj_odd(P, K, xs, xout, vo, win, scem, shm, UT)
    return P.build()


def tok_cols(full_b, hf, nlat=8192, nctx=256):
    return np.concatenate([full_b[:, nctx + hf * 4096:nctx + (hf + 1) * 4096], full_b[:, hf * 128:(hf + 1) * 128]], 1)


def host_L4(d, r1, r2, r3):
    ins = []
    for j in range(8):
        b, hf = j // 2, j % 2
        O = np.concatenate([r2[2 * b]['O'], r2[2 * b + 1]['O']], 1)
        Oc = np.concatenate([O[hf * 4096:(hf + 1) * 4096], O[8192 + hf * 128:8192 + (hf + 1) * 128]], 0)
        Yf = np.concatenate([r3[2 * b]['Y'], r3[2 * b + 1]['Y']], 0)
        ins.append({"xin": r1[j]['xout'], "cnd": np.stack([d['c'][b], d['c_ctx']]),
                    "mw0": d['mod_w'][0], "mb0": d['mod_b'][0], "mw1": d['mod_w'][1], "mb1": d['mod_b'][1],
                    "OT": np.ascontiguousarray(Oc.T), "YT": np.ascontiguousarray(tok_cols(Yf, hf)),
                    "wglu": d['s5_w_glu'][0], "bglu": d['s5_b_glu'][0], "wout": d['ev_w_out'][0],
                    "g2": d['norm_ffn2'][0], "wg2": d['ffn2_w_gate'][0], "wu2": d['ffn2_w_up'][0], "wd2": d['ffn2_w_down'][0],
                    "g1": d['norm_ffn1'][1], "wg1": d['ffn1_w_gate'][1], "wu1": d['ffn1_w_up'][1], "wd1": d['ffn1_w_down'][1],
                    "gm": d['norm_mix'][1], "win": d['od_w_in'][0], "ident": np.eye(128, dtype=np.float32)})
    return ins


def build_L6():
    P = Prog()
    NL, NC = 8192, 256
    xT = P.dram("xT", [256, NC + NL]); gT = P.dram("gT", [256, NL])
    cw = P.dram("cw", [128, 2, 4]); vec = P.dram("vec", [128, 7, 2, 2])
    Wm = P.dram("Wm", [128, 2, 2, 2, 128])
    RT = P.dram("RT", [256, NL], kind="ExternalOutput")
    pbs = [P.ps([128, 512], F32, f"pb{i}") for i in range(8)]
    rr = [0]

    def rot():
        rr[0] += 1
        return pbs[rr[0] % 8]
    Wb = P.sb([128, 2, 2, 2, 128], BF16, "Wb")
    P.dma(Wb[:], Wm[:], writes=[Wb], q='pool')
    cws = P.sb([128, 2, 4], F32, "cws"); P.dma(cws[:], cw[:], writes=[cws])
    vs = P.sb([128, 7, 2, 2], F32, "vs"); P.dma(vs[:], vec[:], writes=[vs])
    c8 = P.sb([128, 2, 2], F32, "c8")
    P.act(c8[:], vs[:, 3], AF.Exp, scale=-1.0, reads=[vs], writes=[c8])
    P.act(c8[:], c8[:], AF.Ln, bias=1.0, reads=[c8], writes=[c8])
    P.ts(c8[:], c8[:], -8.0, ALU.mult, reads=[c8], writes=[c8])
    OFFC, OFFL = 2, 2 + NC + 1 + 2
    TOT = OFFL + NL + 1
    xc = P.sb([128, 2, NC + NL], F32, "xc"); xcb = P.sb([128, 2, NC + NL], BF16, "xcb")
    P.push()
    xp = P.sb([128, 2, TOT], F32, "xp")
    P.memset(xp[:, :, 0:2], 0.0, writes=[xp]); P.memset(xp[:, :, OFFC + NC:OFFL], 0.0, writes=[xp]); P.memset(xp[:, :, OFFL + NL:TOT], 0.0, writes=[xp])
    for ct in range(2):
        P.dma(xp[:, ct, OFFC:OFFC + NC], xT[ct * 128:(ct + 1) * 128, 0:NC], writes=[xp])
        for s in range(0, NL, 2048):
            P.dma(xp[:, ct, OFFL + s:OFFL + s + 2048], xT[ct * 128:(ct + 1) * 128, NC + s:NC + s + 2048], writes=[xp])
    for ct in range(2):
        for (o_in, o_out, n) in [(OFFC, 0, NC)] + [(OFFL + s, NC + s, 2048) for s in range(0, NL, 2048)]:
            dst = xc[:, ct, o_out:o_out + n]
            eng = 'dve' if ct == 0 else 'pool'
            P.ts(dst, xp[:, ct, o_in + 1:o_in + 1 + n], cws[:, ct, 3:4], ALU.mult, vs[:, 0, 0, ct:ct + 1], ALU.add, reads=[xp, cws, vs], writes=[xc], eng=eng)
            for k, sh in ((2, 0), (1, -1), (0, -2)):
                P.stt(dst, xp[:, ct, o_in + sh:o_in + sh + n], cws[:, ct, k:k + 1], dst, ALU.mult, ALU.add, reads=[xp, cws, xc], writes=[xc], eng=eng)
            P.copy(xcb[:, ct, o_out:o_out + n], dst, reads=[xc], writes=[xcb], eng='act')
    P.pop()
    yacc = P.sb([128, 2, NL], F32, "yacc")
    wk = [[P.sb([128, 512], F32, f"lw{i}_{j}") for j in range(5)] for i in range(2)]
    chunks = [(0, NC)] + [(NC + i * 512, 512) for i in range(16)]
    ci = 0
    for d in range(2):
        for ct in range(2):
            seq = [chunks[0]] + (chunks[1:] if d == 0 else chunks[:0:-1])
            prev_h = None
            for qi, (c0, n) in enumerate(seq):
                a_, ig, bc, bin_, h = wk[ci % 2]
                ci += 1
                rv = (lambda ap: ap[:, 0:n]) if d == 0 else (lambda ap: ap[:, n - 1::-1] if n < 512 else ap[:, ::-1])
                pa, px = rot(), rot()
                P.mm(pa[:, 0:n], Wb[:, 0, d, ct, :], xcb[:, ct, c0:c0 + n], reads=[Wb, xcb], writes=[pa])
                P.mm(px[:, 0:n], Wb[:, 1, d, ct, :], xcb[:, ct, c0:c0 + n], reads=[Wb, xcb], writes=[px])
                P.act(a_[:, 0:n], pa[:, 0:n], AF.Sigmoid, bias=vs[:, 1, d, ct:ct + 1], reads=[pa, vs], writes=[a_])
                P.act(ig[:, 0:n], px[:, 0:n], AF.Sigmoid, bias=vs[:, 2, d, ct:ct + 1], reads=[px, vs], writes=[ig])
                P.act(a_[:, 0:n], a_[:, 0:n], AF.Exp, scale=c8[:, d, ct:ct + 1], reads=[a_, c8], writes=[a_])
                P.tt(bc[:, 0:n], a_[:, 0:n], a_[:, 0:n], ALU.mult, reads=[a_], writes=[bc], eng='pool')
                P.act(bc[:, 0:n], bc[:, 0:n], AF.Sqrt, scale=-1.0, bias=1.0, reads=[bc], writes=[bc])
                P.tt(bin_[:, 0:n], ig[:, 0:n], xc[:, ct, c0:c0 + n], ALU.mult, reads=[ig, xc], writes=[bin_], eng='pool')
                P.tt(bin_[:, 0:n], bin_[:, 0:n], bc[:, 0:n], ALU.mult, reads=[bin_, bc], writes=[bin_])
                init = 0.0 if qi == 0 else prev_h
                rds = [a_, bin_] + ([prev_hb] if qi else [])
                P.scan(rv(h), rv(a_), rv(bin_), init, reads=rds, writes=[h])
                last = n - 1 if d == 0 else 0
                prev_h, prev_hb = h[:, last:last + 1], h
                if qi > 0:
                    o = c0 - NC
                    if d == 0:
                        P.copy(yacc[:, ct, o:o + n], h[:, 0:n], reads=[h], writes=[yacc], eng='pool')
                    else:
                        P.tt(yacc[:, ct, o:o + n], yacc[:, ct, o:o + n], h[:, 0:n], ALU.add, reads=[h], writes=[yacc], eng='pool')
    gt = [P.sb([128, 2048], F32, f"gt{i}") for i in range(2)]
    i = 0
    for ct in range(2):
        for s in range(0, NL, 2048):
            g = gt[i % 2]; i += 1
            P.dma(g[:], gT[ct * 128:(ct + 1) * 128, s:s + 2048], writes=[g])
            P.act(g[:], g[:], AF.Gelu, reads=[g], writes=[g])
            P.tt(g[:], g[:], yacc[:, ct, s:s + 2048], ALU.mult, reads=[g, yacc], writes=[g])
            P.dma(RT[ct * 128:(ct + 1) * 128, s:s + 2048], g[:], reads=[g], writes=[P.view(RT)], is_output=True)
    return P.build()


def host_L6(d, r4):
    ins = []
    cwf = d['lru_conv_w'][0]; cbf = d['lru_conv_b'][0]
    for j in range(8):
        b, chh = j // 2, j % 2
        a, c = r4[2 * b], r4[2 * b + 1]
        rows = slice(1536 + 256 * chh, 1536 + 256 * chh + 256)
        grows = slice(2048 + 256 * chh, 2048 + 256 * chh + 256)
        xT = np.concatenate([a['UT'][rows, 4096:], c['UT'][rows, 4096:], a['UT'][rows, :4096], c['UT'][rows, :4096]], 1)
        gT = np.concatenate([a['UT'][grows, :4096], c['UT'][grows, :4096]], 1)
        chs = slice(256 * chh, 256 * chh + 256)
        cw = np.ascontiguousarray(cwf[:, chs].reshape(4, 2, 128).transpose(2, 1, 0))
        vec = np.zeros((128, 7, 2, 2), np.float32)
        vec[:, 0, 0, :] = cbf[chs].reshape(2, 128).T
        for dr in range(2):
            vec[:, 1, dr, :] = d['lru_b_a'][0][dr, chs].reshape(2, 128).T
            vec[:, 2, dr, :] = d['lru_b_x'][0][dr, chs].reshape(2, 128).T
            vec[:, 3, dr, :] = d['lru_lambda'][0][dr, chs].reshape(2, 128).T
        Wm = np.zeros((128, 2, 2, 2, 128), np.float32)
        for ai, wsrc in enumerate([d['lru_w_a'][0], d['lru_w_x'][0]]):
            for dr in range(2):
                for ct in range(2):
                    for hb in range(2):
                        blk = 4 * chh + 2 * ct + hb
                        Wm[hb * 64:(hb + 1) * 64, ai, dr, ct, hb * 64:(hb + 1) * 64] = wsrc[dr, blk]
        ins.append({"xT": np.ascontiguousarray(xT), "gT": np.ascontiguousarray(gT), "cw": cw, "vec": vec, "Wm": Wm})
    return ins


NFFT = 16384
HY_G = 4


def hyena_consts():
    n = 8192
    t = np.linspace(0.0, 1.0, n, dtype=np.float32)[:, None]
    w = (2.0 * np.pi * np.arange(n, dtype=np.float32)[:, None] / n).astype(np.float32)
    bands = np.linspace(1e-4, 15, 16, dtype=np.float32)[None, :]
    z = np.concatenate([t, np.cos(bands * w), -np.sin(bands * w)], -1).astype(np.float32)
    idx = np.concatenate([np.arange(n), [0], np.arange(n - 1, 0, -1)])
    zc = np.ascontiguousarray(z[idx].T)
    tcirc = t[idx, 0].copy()
    tcirc[n] = 1.0e4
    tc = np.broadcast_to(tcirc[None, :], (128, NFFT)).copy()
    hmin, hmax = np.log(1e-2) / 1.5, np.log(1e-2) / 0.3
    deltas = np.abs(np.linspace(hmin, hmax, 512, dtype=np.float32))
    k = np.arange(128)
    ang = 2.0 * np.pi * np.outer(k, k) / 128.0
    Wr, Wi = np.cos(ang).astype(np.float32), (-np.sin(ang)).astype(np.float32)
    angt = 2.0 * np.pi * np.outer(k, k) / NFFT
    twr, twi = np.cos(angt).astype(np.float32), (-np.sin(angt)).astype(np.float32)
    rep = lambda m: np.ascontiguousarray(np.tile(m, (1, HY_G)))
    C = {"zc": zc, "tc": tc, "W1": np.concatenate([Wr, Wi], 1), "CW1": np.concatenate([Wr, -Wi], 1), "CW2": np.concatenate([Wi, Wr], 1),
         "W3": np.stack([Wr, Wi, -Wi], 1), "tw": np.stack([rep(twr), rep(twi), rep(-twi)], 1)}
    return C, deltas


def build_L5():
    P = Prog()
    NL = 8192
    hT = P.dram("hT", [3, 256, NL])
    cw = P.dram("cw", [128, 3, 2, 3]); cb = P.dram("cb", [128, 3, 2])
    zc = P.dram("zc", [33, NFFT]); tc = P.dram("tc", [128, NFFT])
    w1 = P.dram("w1", [33, 64]); w2 = P.dram("w2", [64, 64]); bf = P.dram("bf", [64, 4]); w3 = P.dram("w3", [64, 2, 256])
    chv = P.dram("chv", [128, 2, 2])
    W1d = P.dram("W1", [128, 256]); CW1d = P.dram("CW1", [128, 256]); CW2d = P.dram("CW2", [128, 256])
    W3d = P.dram("W3", [128, 3, 128]); twd = P.dram("tw", [128, 3, 512])
    Fs = P.dram("Fs", [256, NFFT], kind="Internal"); Zs = P.dram("Zs", [256, NL], kind="Internal")
    X0s = P.dram("X0s", [256, NL], kind="Internal"); Ys = P.dram("Ys", [256, NL], kind="Internal")
    HYT = P.dram("HYT", [256, NL], kind="ExternalOutput")
    pbs = [P.ps([128, 1024], F32, f"pq{i}") for i in range(4)]
    rr = [0]

    def rot():
        rr[0] += 1
        return pbs[rr[0] % 4]
    cws = P.sb([128, 3, 2, 3], F32, "cws"); P.dma(cws[:], cw[:], writes=[cws])
    cbs = P.sb([128, 3, 2], F32, "cbs"); P.dma(cbs[:], cb[:], writes=[cbs])
    chs = P.sb([128, 2, 2], F32, "chs"); P.dma(chs[:], chv[:], writes=[chs])
    zviews, xviews = [], []
    P.push()
    xp = [P.sb([128, NL + 2], F32, f"xp{i}") for i in range(2)]
    cv = [P.sb([128, NL], F32, f"cv{i}") for i in range(2)]
    for b_ in xp:
        P.memset(b_[:, 0:1], 0.0, writes=[b_]); P.memset(b_[:, NL + 1:NL + 2], 0.0, writes=[b_])
    k = 0
    for ct in range(2):
        for part in (1, 2, 0):
            x_ = xp[k % 2]; k += 1
            for s in range(0, NL, 2048):
                P.dma(x_[:, 1 + s:1 + s + 2048], hT[part, ct * 128:(ct + 1) * 128, s:s + 2048], writes=[x_])
            dst = cv[0] if part != 2 else cv[1]
            eng = 'dve' if part != 2 else 'pool'
            for s in range(0, NL, 2048):
                o = dst[:, s:s + 2048]
                P.ts(o, x_[:, s:s + 2048], cws[:, part, ct, 0:1], ALU.mult, cbs[:, part, ct:ct + 1], ALU.add, reads=[x_, cws, cbs], writes=[dst], eng=eng)
                P.stt(o, x_[:, s + 1:s + 1 + 2048], cws[:, part, ct, 1:2], o, ALU.mult, ALU.add, reads=[x_, cws, dst], writes=[dst], eng=eng)
                P.stt(o, x_[:, s + 2:s + 2 + 2048], cws[:, part, ct, 2:3], o, ALU.mult, ALU.add, reads=[x_, cws, dst], writes=[dst], eng=eng)
            if part == 2:
                P.tt(cv[1][:], cv[1][:], cv[0][:], ALU.mult, reads=[cv[0], cv[1]], writes=[cv[1]])
                v_ = P.view(Zs); zviews.append(v_)
                P.dma(Zs[ct * 128:(ct + 1) * 128, :], cv[1][:], reads=[cv[1]], writes=[v_])
            if part == 0:
                v_ = P.view(X0s); xviews.append(v_)
                P.dma(X0s[ct * 128:(ct + 1) * 128, :], cv[0][:], reads=[cv[0]], writes=[v_])
    P.pop()
    fviews = []
    P.push()
    w1s = P.sb([33, 64], F32, "w1s"); P.dma(w1s[:], w1[:], writes=[w1s])
    w2s = P.sb([64, 64], F32, "w2s"); P.dma(w2s[:], w2[:], writes=[w2s])
    w3s = P.sb([64, 2, 256], F32, "w3s"); P.dma(w3s[:], w3[:], writes=[w3s])
    bfs = P.sb([64, 4], F32, "bfs"); P.dma(bfs[:], bf[:], writes=[bfs])
    zq = [P.sb([33, 512], F32, f"zq{i}") for i in range(2)]
    tq = [P.sb([128, 512], F32, f"tq{i}") for i in range(2)]
    ar = P.sb([64, 512], F32, "ar"); tk = P.sb([64, 512], F32, "tk"); tz = P.sb([64, 512], F32, "tz")
    h1 = P.sb([64, 512], F32, "h1"); h2 = P.sb([64, 512], F32, "h2")
    dec = [P.sb([128, 512], F32, f"dec{i}") for i in range(2)]
    fst = [P.sb([128, 512], F32, f"fst{i}") for i in range(2)]
    for q in range(NFFT // 512):
        z_ = zq[q % 2]; t_ = tq[q % 2]
        cols = slice(q * 512, (q + 1) * 512)
        P.dma(z_[:], zc[:, cols], writes=[z_]); P.dma(t_[:], tc[:, cols], writes=[t_])
        p1 = rot()
        P.mm(p1[0:64, 0:512], w1s[:], z_[:], reads=[w1s, z_], writes=[p1])
        P.ts(ar[:], p1[0:64, 0:512], bfs[:, 0:1], ALU.add, bfs[:, 1:2], ALU.mult, reads=[p1, bfs], writes=[ar])
        sincos(P, h1[:], None, ar[:], tk[:], tz[:], [ar], [tk, tz, h1])
        p2 = rot()
        P.mm(p2[0:64, 0:512], w2s[:], h1[:], reads=[w2s, h1], writes=[p2])
        P.ts(ar[:], p2[0:64, 0:512], bfs[:, 2:3], ALU.add, bfs[:, 3:4], ALU.mult, reads=[p2, bfs], writes=[ar])
        sincos(P, h2[:], None, ar[:], tk[:], tz[:], [ar], [tk, tz, h2])
        dr = 0 if q < 16 else 1
        for ct in range(2):
            p3 = rot()
            P.mm(p3[:, 0:512], w3s[:, dr, ct * 128:(ct + 1) * 128], h2[:], reads=[w3s, h2], writes=[p3])
            d_ = dec[ct]; f_ = fst[ct]
            P.act(d_[:], t_[:], AF.Exp, scale=chs[:, 0, ct:ct + 1], reads=[t_, chs], writes=[d_])
            P.tt(f_[:], p3[:, 0:512], d_[:], ALU.mult, reads=[p3, d_], writes=[f_])
            v_ = P.view(Fs); fviews.append(v_)
            P.dma(Fs[ct * 128:(ct + 1) * 128, cols], f_[:], reads=[f_], writes=[v_])
    P.pop()
    yviews = []
    P.push()
    def ld_r(name, shape, src):
        t32 = P.sb(shape, F32, name + "f"); P.dma(t32[:], src[:], writes=[t32])
        tr = P.sb(shape, F32R, name)
        P.copy(tr[:], t32[:], reads=[t32], writes=[tr], eng='act')
        return tr
    W1 = ld_r("W1", [128, 256], W1d); CW1 = ld_r("CW1", [128, 256], CW1d); CW2 = ld_r("CW2", [128, 256], CW2d)
    W3 = ld_r("W3", [128, 3, 128], W3d)
    tw = P.sb([128, 3, 512], F32, "tw"); P.dma(tw[:], twd[:], writes=[tw])
    Wr, Wi, nWi = W3[:, 0, :], W3[:, 1, :], W3[:, 2, :]
    twr, twi, ntwi = tw[:, 0, :], tw[:, 1, :], tw[:, 2, :]
    xg = [P.sb([64, HY_G, 128], F32, f"xg{i}") for i in range(2)]
    fg = [P.sb([128, HY_G, 128], F32, f"fg{i}") for i in range(2)]
    T = [P.sb([128, 512], F32, f"T{i}") for i in range(4)]
    Ap = [[P.sb([128, 512], F32R, f"Ap{i}{j}") for j in range(2)] for i in range(2)]
    Hs = [P.sb([128, 512], F32, f"H{j}") for j in range(2)]
    Yc = [P.sb([128, 512], F32R, f"Y{j}") for j in range(2)]
    Zp = [P.sb([128, 512], F32R, f"Zp{j}") for j in range(2)]
    xgr = [P.sb([64, HY_G, 128], F32R, f"xgr{i}") for i in range(2)]
    fgr = [P.sb([128, HY_G, 128], F32R, f"fgr{i}") for i in range(2)]
    ysb = [P.sb([64, 512], F32, f"ysb{i}") for i in range(2)]

    def cmul(o_re, o_im, a_re, a_im, b_re, b_im, ra, rb, wo):
        P.tt(T[0][:], a_re, b_re, ALU.mult, reads=ra + rb, writes=[T[0]])
        P.tt(T[1][:], a_im, b_im, ALU.mult, reads=ra + rb, writes=[T[1]])
        P.tt(o_re, T[0][:], T[1][:], ALU.subtract, reads=[T[0], T[1]], writes=[wo[0]])
        P.tt(T[2][:], a_re, b_im, ALU.mult, reads=ra + rb, writes=[T[2]])
        P.tt(T[3][:], a_im, b_re, ALU.mult, reads=ra + rb, writes=[T[3]])
        P.tt(o_im, T[2][:], T[3][:], ALU.add, reads=[T[2], T[3]], writes=[wo[1]], eng='pool')

    def v4(ap512):
        return ap512

    def fwd_fft(src, Ka, A):
        psA = rot()
        for ch in range(HY_G):
            P.mm(psA[:, ch * 256:(ch + 1) * 256], src[0:Ka, ch, :], W1[0:Ka, :], reads=[src, W1], writes=[psA])
        pv = psA[:].rearrange("p (c r k) -> p c r k", c=HY_G, r=2)
        t4 = lambda ap: ap.rearrange("p (c k) -> p c k", c=HY_G)
        P.tt(t4(T[0][:]), pv[:, :, 0, :], t4(twr), ALU.mult, reads=[psA, tw], writes=[T[0]])
        P.tt(t4(T[1][:]), pv[:, :, 1, :], t4(twi), ALU.mult, reads=[psA, tw], writes=[T[1]])
        P.tt(A[0][:], T[0][:], T[1][:], ALU.subtract, reads=[T[0], T[1]], writes=[A[0]])
        P.tt(t4(T[2][:]), pv[:, :, 0, :], t4(twi), ALU.mult, reads=[psA, tw], writes=[T[2]])
        P.tt(t4(T[3][:]), pv[:, :, 1, :], t4(twr), ALU.mult, reads=[psA, tw], writes=[T[3]])
        P.tt(A[1][:], T[2][:], T[3][:], ALU.add, reads=[T[2], T[3]], writes=[A[1]], eng='pool')
        psX = rot()
        P.mm(psX[:, 0:512], Wr, A[0][:], start=True, stop=False, reads=[W3, A[0]], writes=[psX])
        P.mm(psX[:, 0:512], nWi, A[1][:], start=False, stop=True, reads=[W3, A[1]], writes=[psX])
        P.mm(psX[:, 512:1024], Wi, A[0][:], start=True, stop=False, reads=[W3, A[0]], writes=[psX])
        P.mm(psX[:, 512:1024], Wr, A[1][:], start=False, stop=True, reads=[W3, A[1]], writes=[psX])
        return psX

    ng = 256 // HY_G
    for g in range(ng):
        ch0 = g * HY_G
        ct = ch0 // 128
        x_ = xg[g % 2]; f_ = fg[g % 2]
        P.dma(x_[:], Zs[ch0:ch0 + HY_G, :].rearrange("c (a b) -> a c b", b=128), reads=zviews, writes=[x_])
        P.dma(f_[:], Fs[ch0:ch0 + HY_G, :].rearrange("c (a b) -> a c b", b=128), reads=fviews, writes=[f_])
        xr_ = xgr[g % 2]; fr_ = fgr[g % 2]
        P.copy(fr_[:], f_[:], reads=[f_], writes=[fr_], eng='act')
        P.copy(xr_[:], x_[:], reads=[x_], writes=[xr_], eng='act')
        f_, x_ = fr_, xr_
        psH = fwd_fft(f_, 128, Ap[0])
        P.copy(Hs[0][:], psH[:, 0:512], reads=[psH], writes=[Hs[0]], eng='act')
        P.copy(Hs[1][:], psH[:, 512:1024], reads=[psH], writes=[Hs[1]], eng='act')
        psX = fwd_fft(x_, 64, Ap[1])
        cmul(Yc[0][:], Yc[1][:], psX[:, 0:512], psX[:, 512:1024], Hs[0][:], Hs[1][:], [psX], [Hs[0], Hs[1]], Yc)
        psZ = rot()
        for ch in range(HY_G):
            P.mm(psZ[:, ch * 256:(ch + 1) * 256], Yc[0][:, ch * 128:(ch + 1) * 128], CW1[:], start=True, stop=False, reads=[Yc[0], CW1], writes=[psZ])
            P.mm(psZ[:, ch * 256:(ch + 1) * 256], Yc[1][:, ch * 128:(ch + 1) * 128], CW2[:], start=False, stop=True, reads=[Yc[1], CW2], writes=[psZ])
        pz = psZ[:].rearrange("p (c r k) -> p c r k", c=HY_G, r=2)
        t4 = lambda ap: ap.rearrange("p (c k) -> p c k", c=HY_G)
        cmul(t4(Zp[0][:]), t4(Zp[1][:]), pz[:, :, 0, :], pz[:, :, 1, :], t4(twr), t4(ntwi), [psZ], [tw], Zp)
        psy = rot()
        P.mm(psy[0:64, 0:512], W3[:, 0, 0:64], Zp[0][:], start=True, stop=False, reads=[W3, Zp[0]], writes=[psy])
        P.mm(psy[0:64, 0:512], W3[:, 1, 0:64], Zp[1][:], start=False, stop=True, reads=[W3, Zp[1]], writes=[psy])
        y_ = ysb[g % 2]
        P.op('act', lambda e, y_=y_, psy=psy: e.mul(y_[:], psy[0:64, 0:512], 1.0 / NFFT), reads=[psy], writes=[y_])
        v_ = P.view(Ys); yviews.append(v_)
        P.dma(Ys[ch0:ch0 + HY_G, :].rearrange("c (a b) -> a c b", b=128), y_[:].rearrange("a (c b) -> a c b", c=HY_G), reads=[y_], writes=[v_])
    P.pop()
    P.push()
    bufs = [[P.sb([128, 2048], F32, f"o{i}{j}") for j in range(3)] for i in range(2)]
    k = 0
    for ct in range(2):
        for s in range(0, NL, 2048):
            yb, zb, xb = bufs[k % 2]; k += 1
            rows = slice(ct * 128, (ct + 1) * 128)
            P.dma(yb[:], Ys[rows, s:s + 2048], reads=yviews, writes=[yb])
            P.dma(zb[:], Zs[rows, s:s + 2048], reads=zviews, writes=[zb])
            P.dma(xb[:], X0s[rows, s:s + 2048], reads=xviews, writes=[xb])
            P.stt(yb[:], zb[:], chs[:, 1, ct:ct + 1], yb[:], ALU.mult, ALU.add, reads=[zb, chs, yb], writes=[yb])
            P.tt(yb[:], yb[:], xb[:], ALU.mult, reads=[yb, xb], writes=[yb], eng='pool')
            P.dma(HYT[rows, s:s + 2048], yb[:], reads=[yb], writes=[P.view(HYT)], is_output=True)
    P.pop()
    return P.build()


def host_L5(d, r4):
    C, deltas = hyena_consts()
    ins = []
    cwf = d['hy_conv_w'][0]; cbf = d['hy_conv_b'][0]
    for j in range(8):
        b, chh = j // 2, j % 2
        a, c = r4[2 * b], r4[2 * b + 1]
        hT = np.zeros((3, 256, 8192), np.float32)
        cw = np.zeros((128, 3, 2, 3), np.float32); cb = np.zeros((128, 3, 2), np.float32)
        for part in range(3):
            rows = slice(512 * part + 256 * chh, 512 * part + 256 * chh + 256)
            hT[part] = np.concatenate([a['UT'][rows, :4096], c['UT'][rows, :4096]], 1)
            cw[:, part] = cwf[:, rows].reshape(3, 2, 128).transpose(2, 1, 0)
            cb[:, part] = cbf[rows].reshape(2, 128).T
        chs = slice(256 * chh, 256 * chh + 256)
        chv = np.zeros((128, 2, 2), np.float32)
        chv[:, 0, :] = -deltas[chs].reshape(2, 128).T
        chv[:, 1, :] = d['hy_bias'][0][chs].reshape(2, 128).T
        bf = np.stack([d['hy_filt_b1'][0], d['hy_sin_freq'][0][0], d['hy_filt_b2'][0], d['hy_sin_freq'][0][1]], 1)
        w3 = np.ascontiguousarray(d['hy_filt_w3'][0].reshape(64, 2, 512)[:, :, chs])
        m = {"hT": hT, "cw": cw, "cb": cb, "w1": d['hy_filt_w1'][0], "w2": d['hy_filt_w2'][0], "bf": np.ascontiguousarray(bf), "w3": w3, "chv": chv}
        m.update(C)
        ins.append(m)
    return ins


def build_L7():
    P = Prog()
    NTOK = 4096
    LB = BLOCKS[:8]
    xin = P.dram("xin", [NTOK, D]); cnd = P.dram("cnd", [2, D])
    mw = P.dram("mw", [D, 9 * D]); mb = P.dram("mb", [9 * D])
    HR = P.dram("HR", [1024, NTOK]); wout = P.dram("wout", [D, D])
    g2 = P.dram("g2", [D]); wg2 = P.dram("wg2", [D, FH]); wu2 = P.dram("wu2", [D, FH]); wd2 = P.dram("wd2", [FH, D])
    gf = P.dram("gf", [D])
    sa = P.dram("scr_a", [NTOK, D], kind="Internal"); sbb = P.dram("scr_b", [NTOK, D], kind="Internal")
    out = P.dram("out", [NTOK, D], kind="ExternalOutput")
    K = consts(P)
    xs = [P.sb([128, D], F32, f"x{t}") for t in range(4)]
    va = [P.view(sa) for _ in LB]; vb = [P.view(sbb) for _ in LB]
    mfm, gbc = adaln(P, K, cnd, mw, mb, [5, 8], ks=[5, 6, 7, 8])
    sce2, sh2 = norm_params(P, K, mfm, g2, 6, "n2")
    cats = [HR[kc * 128:(kc + 1) * 128, :] for kc in range(8)]
    pass_mixout(P, K, xs, xin, None, sa, va, wout, gbc[5], cats, blocks=LB)
    pass_ffn(P, K, xs, sa, va, sbb, vb, wg2, wu2, wd2, sce2, sh2, gbc[8], blocks=LB)
    P.push()
    gfb = P.sb([128, D], F32, "gfb")
    P.dma(gfb[:], gf.t.partition_broadcast(128), writes=[gfb])
    ss = P.sb([128, 4], F32, "ssf"); rstd = P.sb([128, 4], F32, "rstdf")
    junk = P.sb([128, D], BF16, "junk")
    for bi, (t0, nt, n) in enumerate(LB):
        xio(P, xs, nt, t0, sbb, vb[bi])
        P.memset(ss[:], 0.0, writes=[ss], eng='dve')
        for t in range(nt):
            P.act(junk[:], xs[t][:], AF.Square, accum_out=ss[:, t:t + 1], reads=[xs[t]], writes=[junk, ss])
        rstd_from_ss(P, rstd[:], ss[:], 1.0 / D, [ss], [rstd])
        for t in range(nt):
            P.ts(xs[t][:], xs[t][:], rstd[:, t:t + 1], ALU.mult, reads=[xs[t], rstd], writes=[xs[t]])
            P.tt(xs[t][:], xs[t][:], gfb[:], ALU.mult, reads=[xs[t], gfb], writes=[xs[t]], eng='pool')
        xio(P, xs, nt, t0, None, None, out, P.view(out), load=False, out=True)
    P.pop()
    return P.build()


def host_L7(d, r4, r5, r6):
    ins = []
    for j in range(8):
        b, hf = j // 2, j % 2
        cols = slice(hf * 4096, (hf + 1) * 4096)
        HR = np.concatenate([r5[2 * b]['HYT'][:, cols], r5[2 * b + 1]['HYT'][:, cols], r6[2 * b]['RT'][:, cols], r6[2 * b + 1]['RT'][:, cols]], 0)
        ins.append({"xin": np.ascontiguousarray(r4[j]['xout'][:4096]), "cnd": np.stack([d['c'][b], d['c_ctx']]),
                    "mw": d['mod_w'][1], "mb": d['mod_b'][1], "HR": np.ascontiguousarray(HR), "wout": d['od_w_out'][0],
                    "g2": d['norm_ffn2'][1], "wg2": d['ffn2_w_gate'][1], "wu2": d['ffn2_w_up'][1], "wd2": d['ffn2_w_down'][1],
                    "gf": d['final_norm'], "ident": np.eye(128, dtype=np.float32)})
    return ins


_NC = {}


def _get(name, fn):
    if name not in _NC:
        _NC[name] = fn()
    return _NC[name]


def _run(name, fn, ins):
    nc = _get(name, fn)
    res = run_bass_kernel_spmd(nc, ins, core_ids=list(range(8)))
    return res.results


def kernel(**inputs):
    d = {k: np.ascontiguousarray(np.asarray(v)) for k, v in inputs.items()}
    r1 = _run("L1", build_L1, host_L1(d))
    r2 = _run("L2", build_L2, host_L2(r1))
    r3 = _run("L3", build_L3, host_L3(d, r1))
    r4 = _run("L4", build_L4, host_L4(d, r1, r2, r3))
    r5 = _run("L5", build_L5, host_L5(d, r4))
    r6 = _run("L6", build_L6, host_L6(d, r4))
    r7 = _run("L7", build_L7, host_L7(d, r4, r5, r6))
    out = np.zeros((4, 8192, 1024), np.float32)
    for j in range(8):
        b, hf = j // 2, j % 2
        out[b, hf * 4096:(hf + 1) * 4096] = r7[j]['out']
    return out
```

```python
from contextlib import ExitStack
import numpy as np
import concourse.bass as bass
import concourse.mybir as mybir
from concourse.bass_utils import run_bass_kernel_spmd

F32 = mybir.dt.float32
F32R = mybir.dt.float32r
BF16 = mybir.dt.bfloat16
AF = mybir.ActivationFunctionType
ALU = mybir.AluOpType
AX = mybir.AxisListType

ENGS = ('pe', 'act', 'dve', 'pool', 'sp')
NSLOT = 12


class Buf:
    __slots__ = ('t', 'w', 'r', 'name')

    def __init__(self, t, name):
        self.t = t
        self.w = None
        self.r = []
        self.name = name

    def __getitem__(self, idx):
        return self.t[idx]


class Prog:
    def __init__(self, same_engine_sync=True):
        self.nc = bass.Bass("TRN2", target_bir_lowering=False)
        self.ops = {e: [] for e in ENGS}
        self.cnt = {e: 0 for e in ENGS}
        self.seen = {e: {} for e in ENGS}
        self.dma_n = {e: 0 for e in ENGS}
        self.slot_val = {}
        self.stack = ExitStack()
        self.same = same_engine_sync
        self.nbuf = 0
        self.out_tokens = []
        self.scopes = []
        self.closers = []
        self.barrier = []

    def dram(self, name, shape, dt=F32, kind="ExternalInput"):
        t = self.nc.dram_tensor(name, list(shape), dt, kind=kind).ap()
        return Buf(t, name)

    def _ctx(self):
        return self.scopes[-1][0] if self.scopes else self.stack

    def _reg(self, b):
        b.r = list(self.barrier)
        if self.scopes:
            self.scopes[-1][1].append(b)
        return b

    def sb(self, shape, dt=F32, name=None):
        self.nbuf += 1
        name = (name or "sb") + f"_{self.nbuf}"
        t = self._ctx().enter_context(self.nc.sbuf_tensor(name, list(shape), dt))
        return self._reg(Buf(t, name))

    def ps(self, shape, dt=F32, name=None):
        self.nbuf += 1
        name = (name or "ps") + f"_{self.nbuf}"
        t = self._ctx().enter_context(self.nc.psum_tensor(name, list(shape), dt))
        return self._reg(Buf(t, name))

    def view(self, b):
        return Buf(b.t, b.name)

    def push(self):
        self.scopes.append((ExitStack(), []))

    def pop(self):
        st, bufs = self.scopes.pop()
        m = {}
        for b in bufs:
            for tok in ([b.w] if b.w else []) + b.r:
                if m.get(tok[0], 0) < tok[1]:
                    m[tok[0]] = tok[1]
        for k, v in self.barrier:
            if m.get(k, 0) < v:
                m[k] = v
        self.barrier = list(m.items())
        st.close()

    def _deps(self, eng, reads, writes):
        deps = {}

        def add(tok):
            if tok is None:
                return
            k, v = tok
            if deps.get(k, 0) < v:
                deps[k] = v
        for b in reads:
            add(b.w)
        for b in writes:
            add(b.w)
            for t in b.r:
                add(t)
        out = []
        seen = self.seen[eng]
        for k, v in deps.items():
            if k == eng and (eng == 'pe' or not self.same):
                continue
            if seen.get(k, 0) >= v:
                continue
            seen[k] = v
            out.append((k, v))
        return out

    def _commit(self, tok, reads, writes):
        for b in reads:
            if not any(b is w for w in writes):
                b.r.append(tok)
        for b in writes:
            b.w = tok
            b.r = []

    def op(self, eng, fn, reads=(), writes=()):
        waits = self._deps(eng, reads, writes)
        self.cnt[eng] += 1
        tok = (eng, self.cnt[eng])
        self.ops[eng].append((waits, fn, (eng, 1)))
        self._commit(tok, reads, writes)
        return tok

    def dma(self, out_ap, in_ap, reads=(), writes=(), q='sp', is_output=False, **kw):
        i = self.dma_n[q]
        self.dma_n[q] += 1
        slot = ('d', q, i % NSLOT)
        waits = self._deps(q, reads, writes)
        prev = self.slot_val.get(slot, 0)
        if prev and self.seen[q].get(slot, 0) < prev:
            self.seen[q][slot] = prev
            waits.append((slot, prev))
        val = prev + 16
        self.slot_val[slot] = val
        tok = (slot, val)

        def fn(e, out_ap=out_ap, in_ap=in_ap, kw=kw):
            return e.dma_start(out=out_ap, in_=in_ap, **kw)
        self.ops[q].append((waits, fn, (slot, 16)))
        self._commit(tok, reads, writes)
        if is_output:
            self.out_tokens.append(tok)
        return tok

    def build(self):
        nc = self.nc
        fin = {}
        for k, v in self.out_tokens:
            fin[k] = max(fin.get(k, 0), v)
        keys = set()
        for e in ENGS:
            for waits, fn, inc in self.ops[e]:
                keys.add(inc[0])
                for k, v in waits:
                    keys.add(k)
        sems = {}
        for k in sorted(keys, key=str):
            nm = k if isinstance(k, str) else f"d_{k[1]}_{k[2]}"
            sems[k] = self.stack.enter_context(nc.semaphore("s_" + nm))
        ops = self.ops
        with nc.Block() as block:
            def mk(ename):
                def body(e):
                    for waits, fn, inc in ops[ename]:
                        for k, v in waits:
                            e.wait_ge(sems[k], v)
                        ins = fn(e)
                        ins.then_inc(sems[inc[0]], inc[1])
                    if ename == 'sp':
                        for k, v in fin.items():
                            e.wait_ge(sems[k], v)
                return body
            if ops['sp'] or fin:
                block.sync(mk('sp'))
            if ops['pe']:
                block.tensor(mk('pe'))
            if ops['act']:
                block.scalar(mk('act'))
            if ops['dve']:
                block.vector(mk('dve'))
            if ops['pool']:
                block.gpsimd(mk('pool'))
        self.stack.close()
        return nc

    def mm(self, out, lhsT, rhs, start=True, stop=True, reads=(), writes=()):
        return self.op('pe', lambda e: e.matmul(out, lhsT, rhs, start=start, stop=stop), reads, writes)

    def tr(self, out, in_, ident, reads=(), writes=()):
        return self.op('pe', lambda e: e.transpose(out, in_, ident), reads, writes)

    def act(self, out, in_, func, bias=None, scale=None, accum_out=None, reads=(), writes=(), eng='act'):
        kw = {}
        if bias is not None:
            kw['bias'] = bias
        if scale is not None:
            kw['scale'] = scale
        if accum_out is not None:
            kw['accum_out'] = accum_out
        return self.op(eng, lambda e: e.activation(out, in_, func, **kw), reads, writes)

    def tt(self, out, in0, in1, op, reads=(), writes=(), eng='dve'):
        return self.op(eng, lambda e: e.tensor_tensor(out, in0, in1, op), reads, writes)

    def ts(self, out, in0, s1, op0, s2=None, op1=None, accum_out=None, reads=(), writes=(), eng='dve'):
        kw = {}
        if op1 is not None:
            kw['op1'] = op1
        if accum_out is not None:
            kw['accum_out'] = accum_out
        return self.op(eng, lambda e: e.tensor_scalar(out, in0, s1, s2, op0, **kw), reads, writes)

    def stt(self, out, in0, scalar, in1, op0, op1, reads=(), writes=(), eng='dve'):
        eng = 'dve'
        return self.op(eng, lambda e: e.scalar_tensor_tensor(out, in0, scalar, in1, op0, op1), reads, writes)

    def copy(self, out, in_, reads=(), writes=(), eng='dve'):
        if eng == 'act':
            return self.op(eng, lambda e: e.copy(out, in_), reads, writes)
        return self.op(eng, lambda e: e.tensor_copy(out, in_), reads, writes)

    def memset(self, ap, val, writes=(), eng='pool'):
        return self.op(eng, lambda e: e.memset(ap, val), (), writes)

    def scan(self, out, d0, d1, init, reads=(), writes=(), eng='dve'):
        return self.op(eng, lambda e: e.tensor_tensor_scan(out, d0, d1, init, ALU.mult, ALU.add), reads, writes)


def run(prog_nc, in_maps, trace=False):
    res = run_bass_kernel_spmd(prog_nc, in_maps, core_ids=list(range(len(in_maps))), trace=trace)
    return res


D = 1024
FH = 2816
NF = 22
EPS = 1e-6
NT_CORE = 33
BLOCKS = [(4 * i, 4, 0) for i in range(8)] + [(32, 1, 1)]


def wview(w, p=128):
    return w.t.rearrange("(c p) f -> p c f", p=p)


def load_w(P, dst, src, q='pool'):
    v = wview(src)
    for c in range(v.shape[1]):
        P.dma(dst[:, c, :], v[:, c, :], writes=[dst], q=q)


class Ctx:
    pass


def consts(P):
    K = Ctx()
    idd = P.dram("ident", [128, 128])
    K.ident = P.sb([128, 128], F32, "ident")
    P.dma(K.ident[:], idd[:], writes=[K.ident])
    K.identb = P.sb([128, 128], BF16, "identb")
    P.copy(K.identb[:], K.ident[:], reads=[K.ident], writes=[K.identb])
    K.ones = P.sb([128, 128], F32, "ones")
    P.memset(K.ones[:], 1.0, writes=[K.ones])
    K.onesb = P.sb([128, 128], BF16, "onesb")
    P.memset(K.onesb[:], 1.0, writes=[K.onesb])
    K.pb = [P.ps([128, 512], F32, f"pb{i}") for i in range(6)]
    K.pbT = [P.ps([128, 1024], BF16, f"pbT{i}") for i in range(2)]
    return K


def load_fm(P, K, dstbuf, dst, src2d, C):
    tmp = P.sb([C, 128], F32, "lfm")
    P.dma(tmp[:], src2d, writes=[tmp])
    pb = K.pb[5]
    P.tr(pb[:, 0:C], tmp[:], K.ident[0:C, 0:C], reads=[tmp, K.ident], writes=[pb])
    P.copy(dst, pb[:, 0:C], reads=[pb], writes=[dstbuf])


def adaln(P, K, cnd, mod_w, mod_b, gate_ks, ks=None):
    mfm = P.sb([128, 72, 2], F32, "mfm")
    gbc = {k: P.sb([128, 2, 1024], F32, f"gbc{k}") for k in gate_ks}
    P.push()
    sc = P.sb([128, 2, 8], F32, "sc")
    load_fm(P, K, sc, sc[:].rearrange("p n c -> p (n c)"), cnd.t.rearrange("n (c p) -> (n c) p", p=128), 16)
    scs = P.sb([128, 2, 8], F32, "scs")
    P.act(scs[:], sc[:], AF.Silu, reads=[sc], writes=[scs])
    scb = P.sb([128, 8, 2, 128], F32, "scb")
    for c in range(8):
        for n in range(2):
            P.ts(scb[:, c, n, :], K.ones[:], scs[:, n, c:c + 1], ALU.mult, reads=[K.ones, scs], writes=[scb])
    mbf = P.sb([128, 72], F32, "mbf")
    load_fm(P, K, mbf, mbf[:], mod_b.t.rearrange("(c p) -> c p", p=128), 72)
    mbb = P.sb([128, len(gate_ks), 1024], F32, "mbb")
    for i, k in enumerate(gate_ks):
        P.dma(mbb[:, i, :], mod_b.t[k * 1024:(k + 1) * 1024].partition_broadcast(128), writes=[mbb])
    wv = wview(mod_w)
    wbuf = [P.sb([128, 8, 1024], F32, f"modw{i}") for i in range(2)]
    for ki_, k in enumerate(ks if ks is not None else range(9)):
        wb = wbuf[ki_ % 2]
        for c in range(8):
            P.dma(wb[:, c, :], wv[:, c, k * 1024:(k + 1) * 1024], writes=[wb])
        for fc in range(8):
            pb = K.pb[fc % 2]
            for c in range(8):
                P.mm(pb[:, 0:2], wb[:, c, fc * 128:(fc + 1) * 128], scs[:, :, c], start=(c == 0), stop=(c == 7),
                     reads=[wb, scs], writes=[pb])
            ch = k * 8 + fc
            P.ts(mfm[:, ch, :], pb[:, 0:2], mbf[:, ch:ch + 1], ALU.add, reads=[pb, mbf], writes=[mfm])
        if k in gate_ks:
            gi = gate_ks.index(k)
            fac = 1.0 if k == 5 else 0.5
            for n in range(2):
                for hf in range(2):
                    pb = K.pb[2 + (n * 2 + hf) % 2]
                    for c in range(8):
                        P.mm(pb[:], scb[:, c, n, :], wb[:, c, hf * 512:(hf + 1) * 512], start=(c == 0), stop=(c == 7),
                             reads=[scb, wb], writes=[pb])
                    o = gbc[k][:, n, hf * 512:(hf + 1) * 512]
                    P.tt(o, pb[:], mbb[:, gi, hf * 512:(hf + 1) * 512], ALU.add, reads=[pb, mbb], writes=[gbc[k]])
                    if fac != 1.0:
                        P.ts(o, o, fac, ALU.mult, reads=[gbc[k]], writes=[gbc[k]])
    P.pop()
    return mfm, gbc


def norm_params(P, K, mfm, g_dram, k0, name):
    g = P.sb([128, 8], F32, name + "g")
    load_fm(P, K, g, g[:], g_dram.t.rearrange("(c p) -> c p", p=128), 8)
    sce = P.sb([128, 8, 2], F32, name + "sce")
    sh = P.sb([128, 8, 2], F32, name + "sh")
    for n in range(2):
        P.ts(sce[:, :, n], mfm[:, (k0 + 1) * 8:(k0 + 2) * 8, n], 1.0, ALU.add, reads=[mfm], writes=[sce])
        P.tt(sce[:, :, n], sce[:, :, n], g[:], ALU.mult, reads=[sce, g], writes=[sce])
        P.copy(sh[:, :, n], mfm[:, k0 * 8:(k0 + 1) * 8, n], reads=[mfm], writes=[sh])
    return sce, sh


def rstd_from_ss(P, out, ss, inv_n, reads, writes):
    P.ts(out, ss, inv_n, ALU.mult, EPS, ALU.add, reads=reads, writes=writes)
    P.act(out, out, AF.Sqrt, reads=writes, writes=writes)
    P.op('dve', lambda e: e.reciprocal(out, out), reads=writes, writes=writes)


def norm_T(P, K, W, xs, nt, n, sce, sh, hT):
    P.memset(W.ss[:], 0.0, writes=[W.ss], eng='dve')
    for t in range(nt):
        P.act(W.xn[t][:], xs[t][:], AF.Square, accum_out=W.ss[:, t:t + 1], reads=[xs[t]], writes=[W.xn[t], W.ss])
    rstd_from_ss(P, W.rstd[:], W.ss[:], 1.0 / D, [W.ss], [W.rstd])
    for t in range(nt):
        P.op('act', lambda e, t=t: e.mul(W.xn[t][:], xs[t][:], W.rstd[:, t:t + 1]), reads=[xs[t], W.rstd], writes=[W.xn[t]])
    for c in range(8):
        pb = K.pbT[c % 2]
        for t in range(nt):
            P.tr(pb[:, t * 128:(t + 1) * 128], W.xn[t][:, c * 128:(c + 1) * 128], K.identb[:],
                 reads=[W.xn[t], K.identb], writes=[pb])
        P.ts(hT[:, c, 0:nt * 128], pb[:, 0:nt * 128], sce[:, c, n:n + 1], ALU.mult, sh[:, c, n:n + 1], ALU.add,
             reads=[pb, sce, sh], writes=[hT])


class FFNW:
    pass


def ffn_alloc(P):
    W = FFNW()
    W.wg = [P.sb([128, 8, 256], BF16, f"wg{g}") for g in range(NF // 2)]
    W.wu = [P.sb([128, 8, 256], BF16, f"wu{g}") for g in range(NF // 2)]
    W.wd = [P.sb([128, 2, D], BF16, f"wd{g}") for g in range(NF // 2)]
    W.xn = [P.sb([128, D], BF16, f"xn{t}") for t in range(4)]
    W.ss = P.sb([128, 4], F32, "ss")
    W.rstd = P.sb([128, 4], F32, "rstd")
    W.hT = P.sb([128, 8, 512], BF16, "hT")
    W.actT = P.sb([128, NF, 512], BF16, "actT")
    W.a = [P.sb([128, 512], BF16, f"a{i}") for i in range(2)]
    return W


def ffn_load(P, W, wg, wu, wd):
    vg, vu, vd = wview(wg), wview(wu), wview(wd)
    for g in range(NF // 2):
        cs = slice(g * 256, (g + 1) * 256)
        for src, dst in ((vg, W.wg[g]), (vu, W.wu[g])):
            for c0 in (0, 4):
                P.dma(dst[:, c0:c0 + 4, :], src[:, c0:c0 + 4, cs], writes=[dst], q='pool')
    for g in range(NF // 2):
        P.dma(W.wd[g][:], vd[:, 2 * g:2 * g + 2, :], writes=[W.wd[g]], q='pool')


def ffn_block(P, K, W, xs, nt, n, sce, sh, gbc):
    N = nt * 128
    norm_T(P, K, W, xs, nt, n, sce, sh, W.hT)
    for f in range(NF):
        pg = K.pb[(f % 2) * 2]
        pu = K.pb[(f % 2) * 2 + 1]
        for c in range(8):
            P.mm(pg[:, 0:N], W.wg[f // 2][:, c, (f % 2) * 128:(f % 2 + 1) * 128], W.hT[:, c, 0:N], start=(c == 0), stop=(c == 7),
                 reads=[W.wg[f // 2], W.hT], writes=[pg])
        for c in range(8):
            P.mm(pu[:, 0:N], W.wu[f // 2][:, c, (f % 2) * 128:(f % 2 + 1) * 128], W.hT[:, c, 0:N], start=(c == 0), stop=(c == 7),
                 reads=[W.wu[f // 2], W.hT], writes=[pu])
        a = W.a[f % 2]
        P.act(a[:, 0:N], pg[:, 0:N], AF.Silu, reads=[pg], writes=[a])
        P.tt(W.actT[:, f, 0:N], a[:, 0:N], pu[:, 0:N], ALU.mult, reads=[a, pu], writes=[W.actT])
    i = 0
    for t in range(nt):
        for hf in range(2):
            py = K.pb[4 + i % 2]
            i += 1
            for f in range(NF):
                P.mm(py[:], W.actT[:, f, t * 128:(t + 1) * 128], W.wd[f // 2][:, f % 2, hf * 512:(hf + 1) * 512],
                     start=(f == 0), stop=(f == NF - 1), reads=[W.actT, W.wd[f // 2]], writes=[py])
            tb = W.xn[i % 2]
            tmp = tb[:].bitcast(F32)
            P.tt(tmp, py[:], gbc[:, n, hf * 512:(hf + 1) * 512], ALU.mult, reads=[py, gbc], writes=[tb])
            xo = xs[t][:, hf * 512:(hf + 1) * 512]
            P.tt(xo, xo, tmp, ALU.add, reads=[xs[t], tb], writes=[xs[t]], eng='pool')


QR, KVR, ROPE, S5C = 384, 256, 32, 512
NH = 8
EV_IN = QR + KVR + ROPE + S5C
EV_EXT = EV_IN + 192


def rot(K):
    K.rr = getattr(K, 'rr', -1) + 1
    return K.pb[K.rr % 6]


def evin_alloc(P):
    W = FFNW()
    W.win = P.sb([128, 8, EV_EXT], BF16, "win")
    W.wuq = P.sb([128, 3, NH * 96], BF16, "wuq")
    W.wuqs = P.sb([128, 3, NH * 96], BF16, "wuqs")
    W.wk = P.sb([128, 2, NH * 64], BF16, "wk")
    W.wv = P.sb([128, 2, 512], BF16, "wv")
    W.gq = P.sb([128, 3], F32, "gq")
    W.gkv = P.sb([128, 2], F32, "gkv")
    W.cos = P.sb([96, 4096], F32, "cos")
    W.sin = P.sb([96, 4096], F32, "sin")
    W.xn = [P.sb([128, D], BF16, f"xn{t}") for t in range(4)]
    W.ss = P.sb([128, 4], F32, "ss")
    W.rstd = P.sb([128, 4], F32, "rstd")
    W.hT = P.sb([128, 8, 512], BF16, "hT")
    W.cq = P.sb([128, 3, 512], F32, "cq")
    W.sq = P.sb([128, 512], F32, "sq")
    W.rbc = P.sb([128, 512], F32, "rbc")
    W.cqn = P.sb([128, 3, 512], BF16, "cqn")
    W.ckvn = P.sb([128, 2, 512], BF16, "ckvn")
    W.krr = P.sb([96, 512], F32, "krr")
    W.t1 = P.sb([96, 512], F32, "t1")
    W.t2 = P.sb([96, 512], F32, "t2")
    W.st = [P.sb([128, 512], F32, f"st{i}") for i in range(3)]
    W.sti = 0
    return W


def stage(W):
    W.sti += 1
    return W.st[W.sti % 3]


def lowrank_norm(P, K, W, col0, nch, nfeat, g, N, outn):
    pss = rot(K)
    for c3 in range(nch):
        pc = rot(K)
        for c in range(8):
            P.mm(pc[:, 0:N], W.win[:, c, col0 + c3 * 128:col0 + (c3 + 1) * 128], W.hT[:, c, 0:N], start=(c == 0), stop=(c == 7),
                 reads=[W.win, W.hT], writes=[pc])
        P.copy(W.cq[:, c3, 0:N], pc[:, 0:N], reads=[pc], writes=[W.cq], eng='act')
        P.act(W.sq[:, 0:N], pc[:, 0:N], AF.Square, reads=[pc], writes=[W.sq])
        P.mm(pss[:, 0:N], K.ones[:], W.sq[:, 0:N], start=(c3 == 0), stop=(c3 == nch - 1), reads=[K.ones, W.sq], writes=[pss])
    rstd_from_ss(P, W.rbc[:, 0:N], pss[:, 0:N], 1.0 / nfeat, [pss], [W.rbc])
    for c3 in range(nch):
        P.stt(outn[:, c3, 0:N], W.cq[:, c3, 0:N], g[:, c3:c3 + 1], W.rbc[:, 0:N], ALU.mult, ALU.mult,
              reads=[W.cq, g, W.rbc], writes=[outn])


def rope_rows(P, W, out, a, b, tok0, N, reads, writes, roped):
    if not roped:
        P.copy(out, a, reads=reads, writes=writes)
        return
    P.tt(W.t1[64:96, 0:N], a, W.cos[64:96, tok0:tok0 + N], ALU.mult, reads=reads + [W.cos], writes=[W.t1])
    P.tt(W.t2[64:96, 0:N], b, W.sin[64:96, tok0:tok0 + N], ALU.mult, reads=reads + [W.sin], writes=[W.t2])
    P.tt(out, W.t1[64:96, 0:N], W.t2[64:96, 0:N], ALU.add, reads=[W.t1, W.t2], writes=writes, eng='pool')


def evin_block(P, K, W, xs, t0, nt, n, sce, sh, QT, KT, V, ST):
    N = nt * 128
    tok0 = t0 * 128
    roped = (n == 0)
    norm_T(P, K, W, xs, nt, n, sce, sh, W.hT)
    lowrank_norm(P, K, W, 0, 3, QR, W.gq, N, W.cqn)
    lowrank_norm(P, K, W, QR, 2, KVR, W.gkv, N, W.ckvn)
    pa, pbs = rot(K), rot(K)
    for c in range(8):
        P.mm(pa[0:96, 0:N], W.win[:, c, EV_IN:EV_IN + 96], W.hT[:, c, 0:N], start=(c == 0), stop=(c == 7), reads=[W.win, W.hT], writes=[pa])
    if roped:
        for c in range(8):
            P.mm(pbs[0:96, 0:N], W.win[:, c, EV_IN + 96:EV_IN + 192], W.hT[:, c, 0:N], start=(c == 0), stop=(c == 7), reads=[W.win, W.hT], writes=[pbs])
    rope_rows(P, W, W.krr[64:96, 0:N], pa[64:96, 0:N], pbs[64:96, 0:N], tok0, N, [pa, pbs], [W.krr], roped)
    for c4 in range(4):
        pc = rot(K)
        for c in range(8):
            P.mm(pc[:, 0:N], W.win[:, c, 672 + c4 * 128:672 + (c4 + 1) * 128], W.hT[:, c, 0:N], start=(c == 0), stop=(c == 7),
                 reads=[W.win, W.hT], writes=[pc])
        st = stage(W)
        P.copy(st[:, 0:N], pc[:, 0:N], reads=[pc], writes=[st], eng='act')
        P.dma(ST[c4 * 128:(c4 + 1) * 128, tok0:tok0 + N], st[:, 0:N], reads=[st], writes=[P.view(ST)], is_output=True)
    for h in range(NH):
        pq, pqs = rot(K), rot(K)
        for c in range(3):
            P.mm(pq[0:96, 0:N], W.wuq[:, c, h * 96:(h + 1) * 96], W.cqn[:, c, 0:N], start=(c == 0), stop=(c == 2), reads=[W.wuq, W.cqn], writes=[pq])
        if roped:
            for c in range(3):
                P.mm(pqs[0:96, 0:N], W.wuqs[:, c, h * 96:(h + 1) * 96], W.cqn[:, c, 0:N], start=(c == 0), stop=(c == 2), reads=[W.wuqs, W.cqn], writes=[pqs])
        st = stage(W)
        P.copy(st[0:64, 0:N], pq[0:64, 0:N], reads=[pq], writes=[st], eng='act')
        rope_rows(P, W, st[64:96, 0:N], pq[64:96, 0:N], pqs[64:96, 0:N], tok0, N, [pq, pqs], [st], roped)
        P.dma(QT[h, :, tok0:tok0 + N], st[0:96, 0:N], reads=[st], writes=[P.view(QT)], is_output=True)
        pk = rot(K)
        for c in range(2):
            P.mm(pk[0:64, 0:N], W.wk[:, c, h * 64:(h + 1) * 64], W.ckvn[:, c, 0:N], start=(c == 0), stop=(c == 1), reads=[W.wk, W.ckvn], writes=[pk])
        st = stage(W)
        P.copy(st[0:64, 0:N], pk[0:64, 0:N], reads=[pk], writes=[st], eng='act')
        P.copy(st[64:96, 0:N], W.krr[64:96, 0:N], reads=[W.krr], writes=[st], eng='pool')
        P.dma(KT[h, :, tok0:tok0 + N], st[0:96, 0:N], reads=[st], writes=[P.view(KT)], is_output=True)
    for t in range(nt):
        pv = rot(K)
        for c in range(2):
            P.mm(pv[:], W.ckvn[:, c, t * 128:(t + 1) * 128], W.wv[:, c, :], start=(c == 0), stop=(c == 1), reads=[W.ckvn, W.wv], writes=[pv])
        st = stage(W)
        P.copy(st[:], pv[:], reads=[pv], writes=[st])
        P.dma(V[tok0 + t * 128:tok0 + (t + 1) * 128, :], st[:], reads=[st], writes=[P.view(V)], is_output=True)


def host_rope_tables():
    n = 8192
    row = np.repeat(np.arange(n // 64, dtype=np.float32), 64)
    col = np.tile(np.arange(64, dtype=np.float32), n // 64)
    nf = 8
    inv = (np.float32(10000.0) ** (-np.arange(nf, dtype=np.float32) / np.float32(nf))).astype(np.float32)
    ang = np.concatenate([row[:, None] * inv, col[:, None] * inv], -1).astype(np.float32)
    c, s = np.cos(ang).astype(np.float32), np.sin(ang).astype(np.float32)
    cos = np.zeros((96, n), np.float32)
    sin = np.zeros((96, n), np.float32)
    cos[64:80] = c.T
    cos[80:96] = c.T
    sin[64:80] = -s.T
    sin[80:96] = s.T
    return cos, sin


def build_L1():
    P = Prog()
    xin = P.dram("xin", [NT_CORE * 128, D]); cnd = P.dram("cnd", [2, D]); mod_w = P.dram("mod_w", [D, 9 * D]); mod_b = P.dram("mod_b", [9 * D])
    g1 = P.dram("g1", [D]); gm = P.dram("gm", [D]); wg = P.dram("wg", [D, FH]); wu = P.dram("wu", [D, FH]); wd = P.dram("wd", [FH, D])
    win = P.dram("win", [D, EV_EXT]); wuq = P.dram("wuq", [QR, NH * 96]); wuqs = P.dram("wuqs", [QR, NH * 96])
    wk = P.dram("wk", [KVR, NH * 64]); wv = P.dram("wv", [KVR, 512]); gq = P.dram("gq", [QR]); gkv = P.dram("gkv", [KVR])
    cos = P.dram("cos", [96, 4096]); sin = P.dram("sin", [96, 4096])
    xout = P.dram("xout", [NT_CORE * 128, D], kind="ExternalOutput")
    QT = P.dram("QT", [NH, 96, NT_CORE * 128], kind="ExternalOutput")
    KT = P.dram("KT", [NH, 96, NT_CORE * 128], kind="ExternalOutput")
    V = P.dram("V", [NT_CORE * 128, 512], kind="ExternalOutput")
    ST = P.dram("ST", [512, NT_CORE * 128], kind="ExternalOutput")
    K = consts(P)
    mfm, gbc = adaln(P, K, cnd, mod_w, mod_b, [2], ks=[0, 1, 2, 3, 4])
    sce, sh = norm_params(P, K, mfm, g1, 0, "n1")
    scem, shm = norm_params(P, K, mfm, gm, 3, "nm")
    xs = [P.sb([128, D], F32, f"x{t}") for t in range(4)]
    xviews = [P.view(xout) for _ in BLOCKS]
    P.push()
    W = ffn_alloc(P)
    ffn_load(P, W, wg, wu, wd)
    for bi, (t0, nt, n) in enumerate(BLOCKS):
        for t in range(nt):
            P.dma(xs[t][:], xin[(t0 + t) * 128:(t0 + t + 1) * 128, :], writes=[xs[t]])
        ffn_block(P, K, W, xs, nt, n, sce, sh, gbc[2])
        for t in range(nt):
            P.dma(xout[(t0 + t) * 128:(t0 + t + 1) * 128, :], xs[t][:], reads=[xs[t]], writes=[xviews[bi]], is_output=True)
    P.pop()
    P.push()
    W = evin_alloc(P)
    load_w(P, W.win, win); load_w(P, W.wuq, wuq); load_w(P, W.wuqs, wuqs); load_w(P, W.wk, wk); load_w(P, W.wv, wv)
    load_fm(P, K, W.gq, W.gq[:], gq.t.rearrange("(c p) -> c p", p=128), 3)
    load_fm(P, K, W.gkv, W.gkv[:], gkv.t.rearrange("(c p) -> c p", p=128), 2)
    P.dma(W.cos[64:96, :], cos[64:96, :], writes=[W.cos]); P.dma(W.sin[64:96, :], sin[64:96, :], writes=[W.sin])
    for bi, (t0, nt, n) in enumerate(BLOCKS):
        for t in range(nt):
            P.dma(xs[t][:], xout[(t0 + t) * 128:(t0 + t + 1) * 128, :], reads=[xviews[bi]], writes=[xs[t]])
        evin_block(P, K, W, xs, t0, nt, n, scem, shm, QT, KT, V, ST)
    P.pop()
    return P.build()


def host_L1(d, li=0):
    cos, sin = host_rope_tables()
    w_in = d['ev_w_in'][0]
    kr = w_in[:, 640:672]
    krs = np.concatenate([kr[:, 16:32], kr[:, 0:16]], 1)
    z64 = np.zeros((D, 64), np.float32)
    win = np.concatenate([w_in, z64, kr, z64, krs], 1)
    wuq = d['mla_w_uq'][0]
    wq3 = wuq.reshape(QR, NH, 96)
    wuqs = np.zeros((QR, NH, 96), np.float32)
    wuqs[:, :, 64:80] = wq3[:, :, 80:96]
    wuqs[:, :, 80:96] = wq3[:, :, 64:80]
    wukv = d['mla_w_ukv'][0].reshape(KVR, NH, 128)
    wk = np.ascontiguousarray(wukv[:, :, 0:64]).reshape(KVR, NH * 64)
    wv = np.ascontiguousarray(wukv[:, :, 64:128]).reshape(KVR, 512)
    ins = []
    for j in range(8):
        b, hf = j // 2, j % 2
        x = np.concatenate([d['x'][b, hf * 4096:(hf + 1) * 4096], d['ctx'][b, hf * 128:(hf + 1) * 128]], 0)
        ins.append({"xin": x, "cnd": np.stack([d['c'][b], d['c_ctx']]), "mod_w": d['mod_w'][li], "mod_b": d['mod_b'][li],
                    "g1": d['norm_ffn1'][li], "gm": d['norm_mix'][li],
                    "wg": d['ffn1_w_gate'][li], "wu": d['ffn1_w_up'][li], "wd": d['ffn1_w_down'][li],
                    "win": win, "wuq": wuq, "wuqs": wuqs.reshape(QR, NH * 96), "wk": wk, "wv": wv,
                    "gq": d['mla_q_norm'][0], "gkv": d['mla_kv_norm'][0],
                    "cos": np.ascontiguousarray(cos[:, hf * 4096:(hf + 1) * 4096]), "sin": np.ascontiguousarray(sin[:, hf * 4096:(hf + 1) * 4096]),
                    "ident": np.eye(128, dtype=np.float32)})
    return ins


NKT = 66
NQ = 8192 + 256
SCALE = 96 ** -0.5


def build_L2():
    P = Prog()
    QTd = P.dram("QT", [4, 96, NQ]); KTd = P.dram("KT", [4, 96, NKT * 128]); Vd = P.dram("V", [NKT * 128, 4, 64])
    Od = P.dram("O", [NQ, 256], kind="ExternalOutput")
    QT = P.sb([96, 4, NQ], BF16, "QT"); KT = P.sb([96, 4, NKT * 128], BF16, "KT"); V = P.sb([128, NKT, 4, 65], BF16, "V")
    P.memset(V[:, :, :, 64:65], 1.0, writes=[V])
    Vv = Vd.t.rearrange("(k p) h d -> p k h d", p=128)
    for h in range(4):
        for s in range(0, NKT * 128, 2112):
            P.dma(KT[:, h, s:s + 2112], KTd[h, :, s:s + 2112], writes=[KT], q='pool')
            P.dma(QT[:, h, s:s + 2112], QTd[h, :, s:s + 2112], writes=[QT], q='pool')
    for kt in range(NKT):
        P.dma(V[:, kt, :, 0:64], Vv[:, kt, :, :], writes=[V], q='pool')
    po = [P.ps([128, 512], F32, f"po{i}") for i in range(4)]
    pss = [P.ps([128, 512], F32, f"pss{i}") for i in range(4)]
    pt = [P.sb([128, 512], BF16, f"pt{i}") for i in range(3)]
    ost = [P.sb([128, 256], F32, f"ost{i}") for i in range(8)]
    rc = P.sb([128, 8], F32, "rc")
    blocks = [(qb * 512, 4, 0, NKT) for qb in range(16)] + [(8192, 2, 0, 2)]
    pt = pt + [P.sb([128, 512], BF16, "pt3")]
    iters = []
    for bi, (q0, nq, k0, k1) in enumerate(blocks):
        for h in range(4):
            for kt in range(k0, k1):
                iters.append((bi, q0, nq, k0, k1, h, kt))
    LAG = 2

    def front(idx):
        bi, q0, nq, k0, k1, h, kt = iters[idx]
        N = nq * 128
        ps = pss[idx % 4]
        ptt = pt[idx % 4]
        P.mm(ps[:, 0:N], KT[:, h, kt * 128:(kt + 1) * 128], QT[:, h, q0:q0 + N], reads=[KT, QT], writes=[ps])
        P.act(ptt[:, 0:N], ps[:, 0:N], AF.Exp, scale=SCALE, reads=[ps], writes=[ptt])

    def back(idx):
        bi, q0, nq, k0, k1, h, kt = iters[idx]
        ptt = pt[idx % 4]
        osts = ost[(bi % 2) * 4:(bi % 2) * 4 + 4]
        for i in range(nq):
            P.mm(po[i][:, 0:65], ptt[:, i * 128:(i + 1) * 128], V[:, kt, h, :], start=(kt == k0), stop=(kt == k1 - 1),
                 reads=[ptt, V], writes=[po[i]])
        if kt == k1 - 1:
            for i in range(nq):
                P.op('dve', lambda e, i=i: e.reciprocal(rc[:, i:i + 1], po[i][:, 64:65]), reads=[po[i]], writes=[rc])
                P.ts(osts[i][:, h * 64:(h + 1) * 64], po[i][:, 0:64], rc[:, i:i + 1], ALU.mult, reads=[po[i], rc], writes=[osts[i]])
            if h == 3:
                for i in range(nq):
                    P.dma(Od[q0 + i * 128:q0 + (i + 1) * 128, :], osts[i][:], reads=[osts[i]], writes=[P.view(Od)], is_output=True)

    for idx in range(len(iters) + LAG):
        if idx < len(iters):
            front(idx)
        if idx >= LAG:
            back(idx - LAG)
    return P.build()


def host_L2(r1):
    ins = []
    for j in range(8):
        b, hg = j // 2, j % 2
        a, c = r1[2 * b], r1[2 * b + 1]
        hs = slice(4 * hg, 4 * hg + 4)
        QT = np.concatenate([a['QT'][hs, :, :4096], c['QT'][hs, :, :4096], a['QT'][hs, :, 4096:], c['QT'][hs, :, 4096:]], 2)
        KT = np.concatenate([a['KT'][hs, :, 4096:], c['KT'][hs, :, 4096:], a['KT'][hs, :, :4096], c['KT'][hs, :, :4096]], 2)
        Vf = np.concatenate([a['V'][4096:], c['V'][4096:], a['V'][:4096], c['V'][:4096]], 0).reshape(NKT * 128, 8, 64)
        ins.append({"QT": np.ascontiguousarray(QT), "KT": np.ascontiguousarray(KT), "V": np.ascontiguousarray(Vf[:, hs])})
    return ins


MAGIC = 12582912.0
TWO_PI = 6.283185307179586
PI = 3.141592653589793
LC = 512
NTOK5 = 256 + 8192


def sincos(P, out_sin, out_cos, x, tmp_k, tmp_z, bufs_r, bufs_w, shift_only=None):
    for out, sh in ((out_sin, 0.0), (out_cos, PI / 2)):
        if out is None:
            continue
        src = x
        if sh:
            P.ts(tmp_z, x, sh, ALU.add, reads=bufs_r + bufs_w, writes=bufs_w)
            src = tmp_z
        P.ts(tmp_k, src, 1.0 / TWO_PI, ALU.mult, MAGIC, ALU.add, reads=bufs_r + bufs_w, writes=bufs_w)
        P.ts(tmp_k, tmp_k, -MAGIC, ALU.add, reads=bufs_w, writes=bufs_w)
        P.stt(tmp_z, tmp_k, -TWO_PI, src, ALU.mult, ALU.add, reads=bufs_r + bufs_w, writes=bufs_w)
        P.ts(tmp_z, tmp_z, PI, ALU.min, -PI, ALU.max, reads=bufs_w, writes=bufs_w)
        P.act(out, tmp_z, AF.Sin, reads=bufs_w, writes=bufs_w)


def build_L3(dbg=False, same=True):
    P = Prog(same_engine_sync=same)
    uT = P.dram("uT", [256, NTOK5])
    prm = P.dram("prm", [128, 3, 2, 8])
    Bm = P.dram("Bm", [128, 2, 2, 8, 128])
    Cm = P.dram("Cm", [128, 2, 2, 8, 128])
    dsk = P.dram("dsk", [2, 128])
    tau = P.dram("tau", [128, LC])
    identd = P.dram("ident", [128, 128])
    Y = P.dram("Y", [256, NTOK5], kind="ExternalOutput")
    ident = P.sb([128, 128], F32, "ident"); P.dma(ident[:], identd[:], writes=[ident])
    pbs = [P.ps([128, 512], F32, f"pb{i}") for i in range(8)]
    rr = [0]

    def rot():
        rr[0] += 1
        return pbs[rr[0] % 8]
    ub = P.sb([128, 2, NTOK5], BF16, "ub")
    for ct in range(2):
        for s in range(0, NTOK5, 2112):
            P.dma(ub[:, ct, s:s + 2112], uT[ct * 128:(ct + 1) * 128, s:s + 2112], writes=[ub], q='pool')
    Bb = P.sb([128, 2, 2, 8, 128], BF16, "Bb"); Cb = P.sb([128, 2, 2, 8, 128], BF16, "Cb")
    for d in range(2):
        P.dma(Bb[:, d], Bm[:, d], writes=[Bb], q='pool')
        P.dma(Cb[:, d], Cm[:, d], writes=[Cb], q='pool')
    for d in range(2):
        P.ts(Cb[:, d, 1], Cb[:, d, 1], -1.0, ALU.mult, reads=[Cb], writes=[Cb])
    tau1 = P.sb([128, LC], F32, "tau"); P.dma(tau1[:], tau[:], writes=[tau1])
    dcol = P.sb([128, 2], F32, "dcol")
    dtmp = P.sb([2, 128], F32, "dtmp"); P.dma(dtmp[:], dsk[:], writes=[dtmp])
    pb = rot()
    P.tr(pb[:, 0:2], dtmp[:], ident[0:2, 0:2], reads=[dtmp, ident], writes=[pb])
    P.copy(dcol[:], pb[:, 0:2], reads=[pb], writes=[dcol])
    yacc = P.sb([128, 2, NTOK5], F32, "yacc")
    for ct in range(2):
        for s in range(0, NTOK5, 2112):
            P.ts(yacc[:, ct, s:s + 2112], ub[:, ct, s:s + 2112], dcol[:, ct:ct + 1], ALU.mult, reads=[ub, dcol], writes=[yacc], eng='pool')
    pr = P.sb([128, 3, 16], F32, "pr"); P.dma(pr[:], prm.t.rearrange("p a d s -> p a (d s)"), writes=[pr])
    sm = P.sb([128, 16, 16], F32, "sm")
    P.memset(sm[:], 0.0, writes=[sm])
    S = lambda i: sm[:, i, :]
    R, Wr = [pr, sm], [sm]
    lre, lim, lst = pr[:, 0, :], pr[:, 1, :], pr[:, 2, :]
    P.act(S(0), lst, AF.Exp, reads=R, writes=Wr)
    P.tt(S(1), lre, S(0), ALU.mult, reads=R, writes=Wr)
    P.tt(S(2), lim, S(0), ALU.mult, reads=R, writes=Wr)
    P.act(S(3), S(1), AF.Exp, reads=R, writes=Wr)
    sincos(P, S(4), S(5), S(2), S(6), S(7), R, Wr)
    P.tt(S(8), S(3), S(5), ALU.mult, reads=R, writes=Wr)
    P.ts(S(8), S(8), -1.0, ALU.add, reads=R, writes=Wr)
    P.tt(S(9), S(3), S(4), ALU.mult, reads=R, writes=Wr)
    P.tt(S(6), lre, lre, ALU.mult, reads=R, writes=Wr)
    P.tt(S(7), lim, lim, ALU.mult, reads=R, writes=Wr)
    P.tt(S(6), S(6), S(7), ALU.add, reads=R, writes=Wr)
    P.op('dve', lambda e: e.reciprocal(S(6), S(6)), reads=R, writes=Wr)
    P.tt(S(10), S(8), lre, ALU.mult, reads=R, writes=Wr)
    P.tt(S(7), S(9), lim, ALU.mult, reads=R, writes=Wr)
    P.tt(S(10), S(10), S(7), ALU.add, reads=R, writes=Wr)
    P.tt(S(10), S(10), S(6), ALU.mult, reads=R, writes=Wr)
    P.tt(S(11), S(9), lre, ALU.mult, reads=R, writes=Wr)
    P.tt(S(7), S(8), lim, ALU.mult, reads=R, writes=Wr)
    P.tt(S(11), S(11), S(7), ALU.subtract, reads=R, writes=Wr)
    P.tt(S(11), S(11), S(6), ALU.mult, reads=R, writes=Wr)
    P.ts(S(12), S(10), -1.0, ALU.mult, reads=R, writes=Wr)
    TH, RR, CFR, CFI, NCFR = 2, 3, 10, 11, 12
    if dbg:
        dsm = P.dram("dsm", [128, 16, 16], kind="ExternalOutput")
        P.dma(dsm[:], sm[:], reads=[sm], writes=[dsm], is_output=True)
        dtab = P.dram("dtab", [5, 128, LC], kind="ExternalOutput")
        dk = P.dram("dk", [6, 128, LC], kind="ExternalOutput")
    tabs = [[P.sb([128, LC], F32, f"tab{i}_{j}") for j in range(5)] for i in range(2)]
    tmpa = P.sb([128, LC], F32, "tmpa"); tmpb = P.sb([128, LC], F32, "tmpb")
    ones = P.sb([128, LC], F32, "ones"); P.memset(ones[:], 1.0, writes=[ones])
    wk = [[P.sb([128, LC], F32, f"wk{i}_{j}") for j in range(4)] for i in range(2)]
    hb = [[P.sb([128, LC], BF16, f"hb{i}_{j}") for j in range(2)] for i in range(2)]
    ini = P.sb([128, 4], F32, "ini")
    chunks_f = [(0, 256)] + [(256 + i * LC, LC) for i in range(16)]
    wk = wk + [[P.sb([128, LC], F32, f"wk2_{j}") for j in range(4)]]
    hb = hb + [[P.sb([128, LC], BF16, f"hb2_{j}") for j in range(2)]]
    tabsb = [[P.sb([128, LC], BF16, f"tabb{i}_{j}") for j in range(4)] for i in range(2)]
    wb16 = [[P.sb([128, LC], BF16, f"wb{i}_{j}") for j in range(8)] for i in range(3)]
    units = []
    for d in range(2):
        for st in range(8):
            seq = [chunks_f[0]] + (chunks_f[1:] if d == 0 else chunks_f[:0:-1])
            for qi, (c0, n) in enumerate(seq):
                units.append((d, st, qi, c0, n, qi == len(seq) - 1))
    NU = len(units)
    pys = {}

    def views(d, n):
        if d == 0:
            return (lambda a: a[:, 0:n]), n - 1
        return (lambda a: a[:, n - 1::-1] if n < LC else a[:, ::-1]), 0

    def gen_tables(d, st):
        col = d * 8 + st
        Ere, Eim, Tre, Tim, rf = tabs[col % 2]
        P.ts(tmpa[:], tau1[:], sm[:, TH, col:col + 1], ALU.mult, reads=[tau1, sm], writes=[tmpa])
        sincos(P, Eim[:], Ere[:], tmpa[:], tmpb[:], Tre[:], [tmpa], [tmpb, Tre, Eim, Ere])
        P.ts(tmpb[:], Ere[:], sm[:, CFR, col:col + 1], ALU.mult, reads=[Ere, sm], writes=[tmpb])
        P.stt(Tre[:], Eim[:], sm[:, CFI, col:col + 1], tmpb[:], ALU.mult, ALU.add, reads=[Eim, sm, tmpb], writes=[Tre])
        P.ts(tmpb[:], Ere[:], sm[:, CFI, col:col + 1], ALU.mult, reads=[Ere, sm], writes=[tmpb])
        P.stt(Tim[:], Eim[:], sm[:, NCFR, col:col + 1], tmpb[:], ALU.mult, ALU.add, reads=[Eim, sm, tmpb], writes=[Tim])
        P.ts(rf[:], ones[:], sm[:, RR, col:col + 1], ALU.mult, reads=[ones, sm], writes=[rf])
        for j_, src_ in enumerate((Ere, Eim, Tre, Tim)):
            P.copy(tabsb[col % 2][j_][:], src_[:], reads=[src_], writes=[tabsb[col % 2][j_]], eng='act')

    def stA(u):
        d, st, qi, c0, n, lastq = units[u]
        col = d * 8 + st
        ct = st // 4
        if qi == 0:
            gen_tables(d, st)
        Ereb, Eimb, Treb, Timb = tabsb[col % 2]
        kinr, kini, kr, ki = wk[u % 3][0:4]
        bur, bui, krb, kib, t1, t2, t3, t4 = wb16[u % 3]
        tsl, _ = views(d, n)
        pr_, pi_ = rot(), rot()
        P.mm(pr_[:, 0:n], Bb[:, d, 0, st, :], ub[:, ct, c0:c0 + n], reads=[Bb, ub], writes=[pr_])
        P.mm(pi_[:, 0:n], Bb[:, d, 1, st, :], ub[:, ct, c0:c0 + n], reads=[Bb, ub], writes=[pi_])
        P.copy(bur[:, 0:n], pr_[:, 0:n], reads=[pr_], writes=[bur], eng='act')
        P.copy(bui[:, 0:n], pi_[:, 0:n], reads=[pi_], writes=[bui], eng='act')
        P.tt(t1[:, 0:n], bur[:, 0:n], tsl(Treb), ALU.mult, reads=[bur, Treb], writes=[t1])
        P.tt(t2[:, 0:n], bui[:, 0:n], tsl(Timb), ALU.mult, reads=[bui, Timb], writes=[t2])
        P.tt(kinr[:, 0:n], t1[:, 0:n], t2[:, 0:n], ALU.subtract, reads=[t1, t2], writes=[kinr])
        P.tt(t3[:, 0:n], bui[:, 0:n], tsl(Treb), ALU.mult, reads=[bui, Treb], writes=[t3])
        P.tt(t4[:, 0:n], bur[:, 0:n], tsl(Timb), ALU.mult, reads=[bur, Timb], writes=[t4])
        P.tt(kini[:, 0:n], t3[:, 0:n], t4[:, 0:n], ALU.add, reads=[t3, t4], writes=[kini])

    def stB(u):
        d, st, qi, c0, n, lastq = units[u]
        col = d * 8 + st
        Ere, Eim, Tre, Tim, rf = tabs[col % 2]
        kinr, kini, kr, ki = wk[u % 3][0:4]
        dsl, last = views(d, n)
        i_re = 0.0 if qi == 0 else ini[:, 0:1]
        i_im = 0.0 if qi == 0 else ini[:, 1:2]
        P.scan(dsl(kr), rf[:, 0:n], dsl(kinr), i_re, reads=[rf, kinr, ini], writes=[kr])
        P.scan(dsl(ki), rf[:, 0:n], dsl(kini), i_im, reads=[rf, kini, ini], writes=[ki])
        if not lastq:
            krl, kil = kr[:, last:last + 1], ki[:, last:last + 1]
            erl, eil = Ere[:, n - 1:n], Eim[:, n - 1:n]
            P.tt(ini[:, 2:3], kil, eil, ALU.mult, reads=[ki, Eim], writes=[ini])
            P.stt(ini[:, 0:1], krl, erl, ini[:, 2:3], ALU.mult, ALU.subtract, reads=[kr, Ere, ini], writes=[ini])
            P.tt(ini[:, 3:4], kil, erl, ALU.mult, reads=[ki, Ere], writes=[ini])
            P.stt(ini[:, 1:2], krl, eil, ini[:, 3:4], ALU.mult, ALU.add, reads=[kr, Eim, ini], writes=[ini])

    def stC(u):
        d, st, qi, c0, n, lastq = units[u]
        col = d * 8 + st
        Ereb, Eimb, Treb, Timb = tabsb[col % 2]
        kinr, kini, kr, ki = wk[u % 3][0:4]
        bur, bui, krb, kib, t1, t2, t3, t4 = wb16[u % 3]
        hh = hb[u % 3]
        tsl, _ = views(d, n)
        P.copy(krb[:, 0:n], kr[:, 0:n], reads=[kr], writes=[krb], eng='act')
        P.copy(kib[:, 0:n], ki[:, 0:n], reads=[ki], writes=[kib], eng='act')
        P.tt(t1[:, 0:n], krb[:, 0:n], tsl(Ereb), ALU.mult, reads=[krb, Ereb], writes=[t1], eng='pool')
        P.tt(t2[:, 0:n], kib[:, 0:n], tsl(Eimb), ALU.mult, reads=[kib, Eimb], writes=[t2], eng='pool')
        P.tt(hh[0][:, 0:n], t1[:, 0:n], t2[:, 0:n], ALU.subtract, reads=[t1, t2], writes=[hh[0]], eng='pool')
        P.tt(t3[:, 0:n], krb[:, 0:n], tsl(Eimb), ALU.mult, reads=[krb, Eimb], writes=[t3])
        P.tt(t4[:, 0:n], kib[:, 0:n], tsl(Ereb), ALU.mult, reads=[kib, Ereb], writes=[t4])
        P.tt(hh[1][:, 0:n], t3[:, 0:n], t4[:, 0:n], ALU.add, reads=[t3, t4], writes=[hh[1]])

    def stD(u):
        d, st, qi, c0, n, lastq = units[u]
        ct = st // 4
        hh = hb[u % 3]
        py = rot()
        P.mm(py[:, 0:n], Cb[:, d, 0, st, :], hh[0][:, 0:n], start=True, stop=False, reads=[Cb, hh[0]], writes=[py])
        P.mm(py[:, 0:n], Cb[:, d, 1, st, :], hh[1][:, 0:n], start=False, stop=True, reads=[Cb, hh[1]], writes=[py])
        P.tt(yacc[:, ct, c0:c0 + n], yacc[:, ct, c0:c0 + n], py[:, 0:n], ALU.add, reads=[py], writes=[yacc])

    for s_ in range(NU + 2):
        if s_ < NU:
            stA(s_)
        if 0 <= s_ - 1 < NU:
            stB(s_ - 1)
            stC(s_ - 1)
        if 0 <= s_ - 2 < NU:
            stD(s_ - 2)
    for ct in range(2):
        for s in range(0, NTOK5, 2112):
            P.dma(Y[ct * 128:(ct + 1) * 128, s:s + 2112], yacc[:, ct, s:s + 2112], reads=[yacc], writes=[P.view(Y)], is_output=True)
    return P.build()


def host_L3(d, r1):
    ins = []
    lre = d['s5_lambda_re'][0]; lim = d['s5_lambda_im'][0]; lst = d['s5_log_step'][0]
    bre = d['s5_b_re'][0]; bim = d['s5_b_im'][0]; cre = d['s5_c_re'][0]; cim = d['s5_c_im'][0]
    tau = np.broadcast_to(np.arange(1, LC + 1, dtype=np.float32)[None, :], (128, LC)).copy()
    for j in range(8):
        b, gh = j // 2, j % 2
        a, c = r1[2 * b], r1[2 * b + 1]
        rows = slice(256 * gh, 256 * gh + 256)
        uT = np.concatenate([a['ST'][rows, 4096:], c['ST'][rows, 4096:], a['ST'][rows, :4096], c['ST'][rows, :4096]], 1)
        prm = np.zeros((128, 3, 2, 8), np.float32)
        Bm = np.zeros((128, 2, 2, 8, 128), np.float32)
        Cm = np.zeros((128, 2, 2, 8, 128), np.float32)
        for dr in range(2):
            for st in range(8):
                for gm in range(2):
                    g = 16 * gh + 2 * st + gm
                    ps = slice(gm * 64, gm * 64 + 64)
                    prm[ps, 0, dr, st] = lre[dr, g]; prm[ps, 1, dr, st] = lim[dr, g]; prm[ps, 2, dr, st] = lst[dr, g]
                    gl = (2 * st + gm) % 8
                    ks = slice(gl * 16, gl * 16 + 16)
                    Bm[ks, dr, 0, st, ps] = bre[dr, g].T
                    Bm[ks, dr, 1, st, ps] = bim[dr, g].T
                    Cm[ps, dr, 0, st, ks] = cre[dr, g].T
                    Cm[ps, dr, 1, st, ks] = cim[dr, g].T
        ins.append({"uT": np.ascontiguousarray(uT), "prm": prm, "Bm": Bm, "Cm": Cm,
                    "dsk": np.ascontiguousarray(d['s5_d'][0][rows].reshape(2, 128)), "tau": tau, "ident": np.eye(128, dtype=np.float32)})
    return ins


def xio(P, xs, nt, t0, src, sview=None, dst=None, dview=None, load=True, out=False):
    for t in range(nt):
        rows = slice((t0 + t) * 128, (t0 + t + 1) * 128)
        if load:
            P.dma(xs[t][:], src[rows, :], reads=[sview] if sview is not None else [], writes=[xs[t]])
        else:
            P.dma(dst[rows, :], xs[t][:], reads=[xs[t]], writes=[dview], is_output=out)


def pass_ffn(P, K, xs, src, sviews, dst, dviews, wg, wu, wd, sce, sh, gbc, out=False, post=None, blocks=None):
    P.push()
    W = ffn_alloc(P)
    ffn_load(P, W, wg, wu, wd)
    for bi, (t0, nt, n) in enumerate(blocks or BLOCKS):
        xio(P, xs, nt, t0, src, sviews[bi] if sviews else None)
        ffn_block(P, K, W, xs, nt, n, sce, sh, gbc)
        if post is not None:
            post(W, xs, nt)
        xio(P, xs, nt, t0, None, None, dst, dviews[bi], load=False, out=out)
    P.pop()


def pass_mixout(P, K, xs, src, sviews, dst, dviews, wout, gbc, catsrc, glu=None, blocks=None):
    P.push()
    wo = P.sb([128, 8, D], BF16, "wout")
    load_w(P, wo, wout)
    cat = P.sb([128, 8, 512], BF16, "cat")
    tmp = P.sb([128, 512], F32, "tmpm")
    if glu is not None:
        wgl = P.sb([128, 4, 512], BF16, "wglu")
        load_w(P, wgl, glu[0])
        bgl = P.sb([128, 4], F32, "bglu")
        load_fm(P, K, bgl, bgl[:], glu[1].t.rearrange("(c p) -> c p", p=128), 4)
        yp = P.sb([128, 4, 512], F32, "yp")
        yg = P.sb([128, 4, 512], BF16, "yg")
        sg = P.sb([128, 512], BF16, "sg")
    for bi, (t0, nt, n) in enumerate(blocks or BLOCKS):
        N = nt * 128
        tok = slice(t0 * 128, t0 * 128 + N)
        xio(P, xs, nt, t0, src, sviews[bi] if sviews else None)
        for kc in range(8):
            if glu is not None and kc >= 4:
                P.dma(yp[:, kc - 4, 0:N], catsrc[kc][:, tok], writes=[yp])
            else:
                P.dma(cat[:, kc, 0:N], catsrc[kc][:, tok], writes=[cat], q='pool')
        if glu is not None:
            for kc in range(4):
                P.act(yg[:, kc, 0:N], yp[:, kc, 0:N], AF.Gelu, reads=[yp], writes=[yg])
            for oc in range(4):
                pb = rot(K)
                for kc in range(4):
                    P.mm(pb[:, 0:N], wgl[:, kc, oc * 128:(oc + 1) * 128], yg[:, kc, 0:N], start=(kc == 0), stop=(kc == 3),
                         reads=[wgl, yg], writes=[pb])
                P.act(sg[:, 0:N], pb[:, 0:N], AF.Sigmoid, bias=bgl[:, oc:oc + 1], reads=[pb, bgl], writes=[sg])
                P.tt(cat[:, 4 + oc, 0:N], sg[:, 0:N], yg[:, oc, 0:N], ALU.mult, reads=[sg, yg], writes=[cat])
        for t in range(nt):
            for hf in range(2):
                py = rot(K)
                for kc in range(8):
                    P.mm(py[:], cat[:, kc, t * 128:(t + 1) * 128], wo[:, kc, hf * 512:(hf + 1) * 512], start=(kc == 0), stop=(kc == 7),
                         reads=[cat, wo], writes=[py])
                P.tt(tmp[:], py[:], gbc[:, n, hf * 512:(hf + 1) * 512], ALU.mult, reads=[py, gbc], writes=[tmp])
                xo = xs[t][:, hf * 512:(hf + 1) * 512]
                P.tt(xo, xo, tmp[:], ALU.add, reads=[xs[t], tmp], writes=[xs[t]], eng='pool')
        xio(P, xs, nt, t0, None, None, dst, dviews[bi], load=False)
    P.pop()


OD_IN = 2560


def pass_inproj_odd(P, K, xs, src, sviews, win, sce, sh, UT):
    P.push()
    W = FFNW()
    W.win = P.sb([128, 8, OD_IN], BF16, "winod")
    load_w(P, W.win, win)
    W.xn = [P.sb([128, D], BF16, f"xn{t}") for t in range(4)]
    W.ss = P.sb([128, 4], F32, "ss"); W.rstd = P.sb([128, 4], F32, "rstd")
    W.hT = P.sb([128, 8, 512], BF16, "hT")
    W.st = [P.sb([128, 512], F32, f"st{i}") for i in range(3)]
    W.sti = 0
    for bi, (t0, nt, n) in enumerate(BLOCKS):
        N = nt * 128
        xio(P, xs, nt, t0, src, sviews[bi] if sviews else None)
        norm_T(P, K, W, xs, nt, n, sce, sh, W.hT)
        for oc in range(OD_IN // 128):
            pc = rot(K)
            for c in range(8):
                P.mm(pc[:, 0:N], W.win[:, c, oc * 128:(oc + 1) * 128], W.hT[:, c, 0:N], start=(c == 0), stop=(c == 7),
                     reads=[W.win, W.hT], writes=[pc])
            st = stage(W)
            P.copy(st[:, 0:N], pc[:, 0:N], reads=[pc], writes=[st], eng=('act' if oc % 2 else 'dve'))
            P.dma(UT[oc * 128:(oc + 1) * 128, t0 * 128:t0 * 128 + N], st[:, 0:N], reads=[st], writes=[P.view(UT)], is_output=True)
    P.pop()


def build_L4():
    P = Prog()
    NTOK = NT_CORE * 128
    xin = P.dram("xin", [NTOK, D]); cnd = P.dram("cnd", [2, D])
    mw0 = P.dram("mw0", [D, 9 * D]); mb0 = P.dram("mb0", [9 * D]); mw1 = P.dram("mw1", [D, 9 * D]); mb1 = P.dram("mb1", [9 * D])
    OT = P.dram("OT", [512, NTOK]); YT = P.dram("YT", [512, NTOK])
    wglu = P.dram("wglu", [512, 512]); bglu = P.dram("bglu", [512]); wout = P.dram("wout", [D, D])
    g2 = P.dram("g2", [D]); wg2 = P.dram("wg2", [D, FH]); wu2 = P.dram("wu2", [D, FH]); wd2 = P.dram("wd2", [FH, D])
    g1 = P.dram("g1", [D]); wg1 = P.dram("wg1", [D, FH]); wu1 = P.dram("wu1", [D, FH]); wd1 = P.dram("wd1", [FH, D])
    gm = P.dram("gm", [D]); win = P.dram("win", [D, OD_IN])
    sa = P.dram("scr_a", [NTOK, D], kind="Internal"); sbb = P.dram("scr_b", [NTOK, D], kind="Internal")
    xout = P.dram("xout", [NTOK, D], kind="ExternalOutput")
    UT = P.dram("UT", [OD_IN, NTOK], kind="ExternalOutput")
    K = consts(P)
    xs = [P.sb([128, D], F32, f"x{t}") for t in range(4)]
    va = [P.view(sa) for _ in BLOCKS]; vb = [P.view(sbb) for _ in BLOCKS]; vo = [P.view(xout) for _ in BLOCKS]
    P.push()
    mfm, gbc = adaln(P, K, cnd, mw0, mb0, [5, 8], ks=[5, 6, 7, 8])
    sce2, sh2 = norm_params(P, K, mfm, g2, 6, "n2")
    cats = [OT[kc * 128:(kc + 1) * 128, :] for kc in range(4)] + [YT[kc * 128:(kc + 1) * 128, :] for kc in range(4)]
    pass_mixout(P, K, xs, xin, None, sa, va, wout, gbc[5], cats, glu=(wglu, bglu))
    pass_ffn(P, K, xs, sa, va, sbb, vb, wg2, wu2, wd2, sce2, sh2, gbc[8])
    P.pop()
    mfm1, gbc1 = adaln(P, K, cnd, mw1, mb1, [2], ks=[0, 1, 2, 3, 4])
    sce1, sh1 = norm_params(P, K, mfm1, g1, 0, "n1b")
    scem, shm = norm_params(P, K, mfm1, gm, 3, "nmb")
    pass_ffn(P, K, xs, sbb, vb, xout, vo, wg1, wu1, wd1, sce1, sh1, gbc1[2], out=True)
    pass_inproj_odd(P, K, xs, xout, vo, win, scem, shm, UT)
    return P.build()


def tok_cols(full_b, hf, nlat=8192, nctx=256):
    return np.concatenate([full_b[:, nctx + hf * 4096:nctx + (hf + 1) * 4096], full_b[:, hf * 128:(hf + 1) * 128]], 1)


def host_L4(d, r1, r2, r3):
    ins = []
    for j in range(8):
        b, hf = j // 2, j % 2
        O = np.concatenate([r2[2 * b]['O'], r2[2 * b + 1]['O']], 1)
        Oc = np.concatenate([O[hf * 4096:(hf + 1) * 4096], O[8192 + hf * 128:8192 + (hf + 1) * 128]], 0)
        Yf = np.concatenate([r3[2 * b]['Y'], r3[2 * b + 1]['Y']], 0)
        ins.append({"xin": r1[j]['xout'], "cnd": np.stack([d['c'][b], d['c_ctx']]),
                    "mw0": d['mod_w'][0], "mb0": d['mod_b'][0], "mw1": d['mod_w'][1], "mb1": d['mod_b'][1],
                    "OT": np.ascontiguousarray(Oc.T), "YT": np.ascontiguousarray(tok_cols(Yf, hf)),
                    "wglu": d['s5_w_glu'][0], "bglu": d['s5_b_glu'][0], "wout": d['ev_w_out'][0],
                    "g2": d['norm_ffn2'][0], "wg2": d['ffn2_w_gate'][0], "wu2": d['ffn2_w_up'][0], "wd2": d['ffn2_w_down'][0],
                    "g1": d['norm_ffn1'][1], "wg1": d['ffn1_w_gate'][1], "wu1": d['ffn1_w_up'][1], "wd1": d['ffn1_w_down'][1],
                    "gm": d['norm_mix'][1], "win": d['od_w_in'][0], "ident": np.eye(128, dtype=np.float32)})
    return ins


def build_L6():
    P = Prog()
    NL, NC = 8192, 256
    xT = P.dram("xT", [256, NC + NL]); gT = P.dram("gT", [256, NL])
    cw = P.dram("cw", [128, 2, 4]); vec = P.dram("vec", [128, 7, 2, 2])
    Wm = P.dram("Wm", [128, 2, 2, 2, 128])
    RT = P.dram("RT", [256, NL], kind="ExternalOutput")
    pbs = [P.ps([128, 512], F32, f"pb{i}") for i in range(8)]
    rr = [0]

    def rot():
        rr[0] += 1
        return pbs[rr[0] % 8]
    Wb = P.sb([128, 2, 2, 2, 128], BF16, "Wb")
    P.dma(Wb[:], Wm[:], writes=[Wb], q='pool')
    cws = P.sb([128, 2, 4], F32, "cws"); P.dma(cws[:], cw[:], writes=[cws])
    vs = P.sb([128, 7, 2, 2], F32, "vs"); P.dma(vs[:], vec[:], writes=[vs])
    c8 = P.sb([128, 2, 2], F32, "c8")
    P.act(c8[:], vs[:, 3], AF.Exp, scale=-1.0, reads=[vs], writes=[c8])
    P.act(c8[:], c8[:], AF.Ln, bias=1.0, reads=[c8], writes=[c8])
    P.ts(c8[:], c8[:], -8.0, ALU.mult, reads=[c8], writes=[c8])
    OFFC, OFFL = 2, 2 + NC + 1 + 2
    TOT = OFFL + NL + 1
    xc = P.sb([128, 2, NC + NL], F32, "xc"); xcb = P.sb([128, 2, NC + NL], BF16, "xcb")
    P.push()
    xp = P.sb([128, 2, TOT], F32, "xp")
    P.memset(xp[:, :, 0:2], 0.0, writes=[xp]); P.memset(xp[:, :, OFFC + NC:OFFL], 0.0, writes=[xp]); P.memset(xp[:, :, OFFL + NL:TOT], 0.0, writes=[xp])
    for ct in range(2):
        P.dma(xp[:, ct, OFFC:OFFC + NC], xT[ct * 128:(ct + 1) * 128, 0:NC], writes=[xp])
        for s in range(0, NL, 2048):
            P.dma(xp[:, ct, OFFL + s:OFFL + s + 2048], xT[ct * 128:(ct + 1) * 128, NC + s:NC + s + 2048], writes=[xp])
    for ct in range(2):
        for (o_in, o_out, n) in [(OFFC, 0, NC)] + [(OFFL + s, NC + s, 2048) for s in range(0, NL, 2048)]:
            dst = xc[:, ct, o_out:o_out + n]
            eng = 'dve' if ct == 0 else 'pool'
            P.ts(dst, xp[:, ct, o_in + 1:o_in + 1 + n], cws[:, ct, 3:4], ALU.mult, vs[:, 0, 0, ct:ct + 1], ALU.add, reads=[xp, cws, vs], writes=[xc], eng=eng)
            for k, sh in ((2, 0), (1, -1), (0, -2)):
                P.stt(dst, xp[:, ct, o_in + sh:o_in + sh + n], cws[:, ct, k:k + 1], dst, ALU.mult, ALU.add, reads=[xp, cws, xc], writes=[xc], eng=eng)
            P.copy(xcb[:, ct, o_out:o_out + n], dst, reads=[xc], writes=[xcb], eng='act')
    P.pop()
    yacc = P.sb([128, 2, NL], F32, "yacc")
    wk = [[P.sb([128, 512], F32, f"lw{i}_{j}") for j in range(5)] for i in range(2)]
    chunks = [(0, NC)] + [(NC + i * 512, 512) for i in range(16)]
    ci = 0
    for d in range(2):
        for ct in range(2):
            seq = [chunks[0]] + (chunks[1:] if d == 0 else chunks[:0:-1])
            prev_h = None
            for qi, (c0, n) in enumerate(seq):
                a_, ig, bc, bin_, h = wk[ci % 2]
                ci += 1
                rv = (lambda ap: ap[:, 0:n]) if d == 0 else (lambda ap: ap[:, n - 1::-1] if n < 512 else ap[:, ::-1])
                pa, px = rot(), rot()
                P.mm(pa[:, 0:n], Wb[:, 0, d, ct, :], xcb[:, ct, c0:c0 + n], reads=[Wb, xcb], writes=[pa])
                P.mm(px[:, 0:n], Wb[:, 1, d, ct, :], xcb[:, ct, c0:c0 + n], reads=[Wb, xcb], writes=[px])
                P.act(a_[:, 0:n], pa[:, 0:n], AF.Sigmoid, bias=vs[:, 1, d, ct:ct + 1], reads=[pa, vs], writes=[a_])
                P.act(ig[:, 0:n], px[:, 0:n], AF.Sigmoid, bias=vs[:, 2, d, ct:ct + 1], reads=[px, vs], writes=[ig])
                P.act(a_[:, 0:n], a_[:, 0:n], AF.Exp, scale=c8[:, d, ct:ct + 1], reads=[a_, c8], writes=[a_])
                P.tt(bc[:, 0:n], a_[:, 0:n], a_[:, 0:n], ALU.mult, reads=[a_], writes=[bc], eng='pool')
                P.act(bc[:, 0:n], bc[:, 0:n], AF.Sqrt, scale=-1.0, bias=1.0, reads=[bc], writes=[bc])
                P.tt(bin_[:, 0:n], ig[:, 0:n], xc[:, ct, c0:c0 + n], ALU.mult, reads=[ig, xc], writes=[bin_], eng='pool')
                P.tt(bin_[:, 0:n], bin_[:, 0:n], bc[:, 0:n], ALU.mult, reads=[bin_, bc], writes=[bin_])
                init = 0.0 if qi == 0 else prev_h
                rds = [a_, bin_] + ([prev_hb] if qi else [])
                P.scan(rv(h), rv(a_), rv(bin_), init, reads=rds, writes=[h])
                last = n - 1 if d == 0 else 0
                prev_h, prev_hb = h[:, last:last + 1], h
                if qi > 0:
                    o = c0 - NC
                    if d == 0:
                        P.copy(yacc[:, ct, o:o + n], h[:, 0:n], reads=[h], writes=[yacc], eng='pool')
                    else:
                        P.tt(yacc[:, ct, o:o + n], yacc[:, ct, o:o + n], h[:, 0:n], ALU.add, reads=[h], writes=[yacc], eng='pool')
    gt = [P.sb([128, 2048], F32, f"gt{i}") for i in range(2)]
    i = 0
    for ct in range(2):
        for s in range(0, NL, 2048):
            g = gt[i % 2]; i += 1
            P.dma(g[:], gT[ct * 128:(ct + 1) * 128, s:s + 2048], writes=[g])
            P.act(g[:], g[:], AF.Gelu, reads=[g], writes=[g])
            P.tt(g[:], g[:], yacc[:, ct, s:s + 2048], ALU.mult, reads=[g, yacc], writes=[g])
            P.dma(RT[ct * 128:(ct + 1) * 128, s:s + 2048], g[:], reads=[g], writes=[P.view(RT)], is_output=True)
    return P.build()


def host_L6(d, r4):
    ins = []
    cwf = d['lru_conv_w'][0]; cbf = d['lru_conv_b'][0]
    for j in range(8):
        b, chh = j // 2, j % 2
        a, c = r4[2 * b], r4[2 * b + 1]
        rows = slice(1536 + 256 * chh, 1536 + 256 * chh + 256)
        grows = slice(2048 + 256 * chh, 2048 + 256 * chh + 256)
        xT = np.concatenate([a['UT'][rows, 4096:], c['UT'][rows, 4096:], a['UT'][rows, :4096], c['UT'][rows, :4096]], 1)
        gT = np.concatenate([a['UT'][grows, :4096], c['UT'][grows, :4096]], 1)
        chs = slice(256 * chh, 256 * chh + 256)
        cw = np.ascontiguousarray(cwf[:, chs].reshape(4, 2, 128).transpose(2, 1, 0))
        vec = np.zeros((128, 7, 2, 2), np.float32)
        vec[:, 0, 0, :] = cbf[chs].reshape(2, 128).T
        for dr in range(2):
            vec[:, 1, dr, :] = d['lru_b_a'][0][dr, chs].reshape(2, 128).T
            vec[:, 2, dr, :] = d['lru_b_x'][0][dr, chs].reshape(2, 128).T
            vec[:, 3, dr, :] = d['lru_lambda'][0][dr, chs].reshape(2, 128).T
        Wm = np.zeros((128, 2, 2, 2, 128), np.float32)
        for ai, wsrc in enumerate([d['lru_w_a'][0], d['lru_w_x'][0]]):
            for dr in range(2):
                for ct in range(2):
                    for hb in range(2):
                        blk = 4 * chh + 2 * ct + hb
                        Wm[hb * 64:(hb + 1) * 64, ai, dr, ct, hb * 64:(hb + 1) * 64] = wsrc[dr, blk]
        ins.append({"xT": np.ascontiguousarray(xT), "gT": np.ascontiguousarray(gT), "cw": cw, "vec": vec, "Wm": Wm})
    return ins


NFFT = 16384
HY_G = 4


def hyena_consts():
    n = 8192
    t = np.linspace(0.0, 1.0, n, dtype=np.float32)[:, None]
    w = (2.0 * np.pi * np.arange(n, dtype=np.float32)[:, None] / n).astype(np.float32)
    bands = np.linspace(1e-4, 15, 16, dtype=np.float32)[None, :]
    z = np.concatenate([t, np.cos(bands * w), -np.sin(bands * w)], -1).astype(np.float32)
    idx = np.concatenate([np.arange(n), [0], np.arange(n - 1, 0, -1)])
    zc = np.ascontiguousarray(z[idx].T)
    tcirc = t[idx, 0].copy()
    tcirc[n] = 1.0e4
    tc = np.broadcast_to(tcirc[None, :], (128, NFFT)).copy()
    hmin, hmax = np.log(1e-2) / 1.5, np.log(1e-2) / 0.3
    deltas = np.abs(np.linspace(hmin, hmax, 512, dtype=np.float32))
    k = np.arange(128)
    ang = 2.0 * np.pi * np.outer(k, k) / 128.0
    Wr, Wi = np.cos(ang).astype(np.float32), (-np.sin(ang)).astype(np.float32)
    angt = 2.0 * np.pi * np.outer(k, k) / NFFT
    twr, twi = np.cos(angt).astype(np.float32), (-np.sin(angt)).astype(np.float32)
    rep = lambda m: np.ascontiguousarray(np.tile(m, (1, HY_G)))
    C = {"zc": zc, "tc": tc, "W1": np.concatenate([Wr, Wi], 1), "CW1": np.concatenate([Wr, -Wi], 1), "CW2": np.concatenate([Wi, Wr], 1),
         "W3": np.stack([Wr, Wi, -Wi], 1), "tw": np.stack([rep(twr), rep(twi), rep(-twi)], 1)}
    return C, deltas


def build_L5():
    P = Prog()
    NL = 8192
    hT = P.dram("hT", [3, 256, NL])
    cw = P.dram("cw", [128, 3, 2, 3]); cb = P.dram("cb", [128, 3, 2])
    zc = P.dram("zc", [33, NFFT]); tc = P.dram("tc", [128, NFFT])
    w1 = P.dram("w1", [33, 64]); w2 = P.dram("w2", [64, 64]); bf = P.dram("bf", [64, 4]); w3 = P.dram("w3", [64, 2, 256])
    chv = P.dram("chv", [128, 2, 2])
    W1d = P.dram("W1", [128, 256]); CW1d = P.dram("CW1", [128, 256]); CW2d = P.dram("CW2", [128, 256])
    W3d = P.dram("W3", [128, 3, 128]); twd = P.dram("tw", [128, 3, 512])
    Fs = P.dram("Fs", [256, NFFT], kind="Internal"); Zs = P.dram("Zs", [256, NL], kind="Internal")
    X0s = P.dram("X0s", [256, NL], kind="Internal"); Ys = P.dram("Ys", [256, NL], kind="Internal")
    HYT = P.dram("HYT", [256, NL], kind="ExternalOutput")
    pbs = [P.ps([128, 1024], F32, f"pq{i}") for i in range(4)]
    rr = [0]

    def rot():
        rr[0] += 1
        return pbs[rr[0] % 4]
    cws = P.sb([128, 3, 2, 3], F32, "cws"); P.dma(cws[:], cw[:], writes=[cws])
    cbs = P.sb([128, 3, 2], F32, "cbs"); P.dma(cbs[:], cb[:], writes=[cbs])
    chs = P.sb([128, 2, 2], F32, "chs"); P.dma(chs[:], chv[:], writes=[chs])
    zviews, xviews = [], []
    P.push()
    xp = [P.sb([128, NL + 2], F32, f"xp{i}") for i in range(2)]
    cv = [P.sb([128, NL], F32, f"cv{i}") for i in range(2)]
    for b_ in xp:
        P.memset(b_[:, 0:1], 0.0, writes=[b_]); P.memset(b_[:, NL + 1:NL + 2], 0.0, writes=[b_])
    k = 0
    for ct in range(2):
        for part in (1, 2, 0):
            x_ = xp[k % 2]; k += 1
            for s in range(0, NL, 2048):
                P.dma(x_[:, 1 + s:1 + s + 2048], hT[part, ct * 128:(ct + 1) * 128, s:s + 2048], writes=[x_])
            dst = cv[0] if part != 2 else cv[1]
            eng = 'dve' if part != 2 else 'pool'
            for s in range(0, NL, 2048):
                o = dst[:, s:s + 2048]
                P.ts(o, x_[:, s:s + 2048], cws[:, part, ct, 0:1], ALU.mult, cbs[:, part, ct:ct + 1], ALU.add, reads=[x_, cws, cbs], writes=[dst], eng=eng)
                P.stt(o, x_[:, s + 1:s + 1 + 2048], cws[:, part, ct, 1:2], o, ALU.mult, ALU.add, reads=[x_, cws, dst], writes=[dst], eng=eng)
                P.stt(o, x_[:, s + 2:s + 2 + 2048], cws[:, part, ct, 2:3], o, ALU.mult, ALU.add, reads=[x_, cws, dst], writes=[dst], eng=eng)
            if part == 2:
                P.tt(cv[1][:], cv[1][:], cv[0][:], ALU.mult, reads=[cv[0], cv[1]], writes=[cv[1]])
                v_ = P.view(Zs); zviews.append(v_)
                P.dma(Zs[ct * 128:(ct + 1) * 128, :], cv[1][:], reads=[cv[1]], writes=[v_])
            if part == 0:
                v_ = P.view(X0s); xviews.append(v_)
                P.dma(X0s[ct * 128:(ct + 1) * 128, :], cv[0][:], reads=[cv[0]], writes=[v_])
    P.pop()
    fviews = []
    P.push()
    w1s = P.sb([33, 64], F32, "w1s"); P.dma(w1s[:], w1[:], writes=[w1s])
    w2s = P.sb([64, 64], F32, "w2s"); P.dma(w2s[:], w2[:], writes=[w2s])
    w3s = P.sb([64, 2, 256], F32, "w3s"); P.dma(w3s[:], w3[:], writes=[w3s])
    bfs = P.sb([64, 4], F32, "bfs"); P.dma(bfs[:], bf[:], writes=[bfs])
    zq = [P.sb([33, 512], F32, f"zq{i}") for i in range(2)]
    tq = [P.sb([128, 512], F32, f"tq{i}") for i in range(2)]
    ar = P.sb([64, 512], F32, "ar"); tk = P.sb([64, 512], F32, "tk"); tz = P.sb([64, 512], F32, "tz")
    h1 = P.sb([64, 512], F32, "h1"); h2 = P.sb([64, 512], F32, "h2")
    dec = [P.sb([128, 512], F32, f"dec{i}") for i in range(2)]
    fst = [P.sb([128, 512], F32, f"fst{i}") for i in range(2)]
    for q in range(NFFT // 512):
        z_ = zq[q % 2]; t_ = tq[q % 2]
        cols = slice(q * 512, (q + 1) * 512)
        P.dma(z_[:], zc[:, cols], writes=[z_]); P.dma(t_[:], tc[:, cols], writes=[t_])
        p1 = rot()
        P.mm(p1[0:64, 0:512], w1s[:], z_[:], reads=[w1s, z_], writes=[p1])
        P.ts(ar[:], p1[0:64, 0:512], bfs[:, 0:1], ALU.add, bfs[:, 1:2], ALU.mult, reads=[p1, bfs], writes=[ar])
        sincos(P, h1[:], None, ar[:], tk[:], tz[:], [ar], [tk, tz, h1])
        p2 = rot()
        P.mm(p2[0:64, 0:512], w2s[:], h1[:], reads=[w2s, h1], writes=[p2])
        P.ts(ar[:], p2[0:64, 0:512], bfs[:, 2:3], ALU.add, bfs[:, 3:4], ALU.mult, reads=[p2, bfs], writes=[ar])
        sincos(P, h2[:], None, ar[:], tk[:], tz[:], [ar], [tk, tz, h2])
        dr = 0 if q < 16 else 1
        for ct in range(2):
            p3 = rot()
            P.mm(p3[:, 0:512], w3s[:, dr, ct * 128:(ct + 1) * 128], h2[:], reads=[w3s, h2], writes=[p3])
            d_ = dec[ct]; f_ = fst[ct]
            P.act(d_[:], t_[:], AF.Exp, scale=chs[:, 0, ct:ct + 1], reads=[t_, chs], writes=[d_])
            P.tt(f_[:], p3[:, 0:512], d_[:], ALU.mult, reads=[p3, d_], writes=[f_])
            v_ = P.view(Fs); fviews.append(v_)
            P.dma(Fs[ct * 128:(ct + 1) * 128, cols], f_[:], reads=[f_], writes=[v_])
    P.pop()
    yviews = []
    P.push()
    def ld_r(name, shape, src):
        t32 = P.sb(shape, F32, name + "f"); P.dma(t32[:], src[:], writes=[t32])
        tr = P.sb(shape, F32R, name)
        P.copy(tr[:], t32[:], reads=[t32], writes=[tr], eng='act')
        return tr
    W1 = ld_r("W1", [128, 256], W1d); CW1 = ld_r("CW1", [128, 256], CW1d); CW2 = ld_r("CW2", [128, 256], CW2d)
    W3 = ld_r("W3", [128, 3, 128], W3d)
    tw = P.sb([128, 3, 512], F32, "tw"); P.dma(tw[:], twd[:], writes=[tw])
    Wr, Wi, nWi = W3[:, 0, :], W3[:, 1, :], W3[:, 2, :]
    twr, twi, ntwi = tw[:, 0, :], tw[:, 1, :], tw[:, 2, :]
    xg = [P.sb([64, HY_G, 128], F32, f"xg{i}") for i in range(2)]
    fg = [P.sb([128, HY_G, 128], F32, f"fg{i}") for i in range(2)]
    T = [P.sb([128, 512], F32, f"T{i}") for i in range(4)]
    Ap = [[P.sb([128, 512], F32R, f"Ap{i}{j}") for j in range(2)] for i in range(2)]
    Hs = [P.sb([128, 512], F32, f"H{j}") for j in range(2)]
    Yc = [P.sb([128, 512], F32R, f"Y{j}") for j in range(2)]
    Zp = [P.sb([128, 512], F32R, f"Zp{j}") for j in range(2)]
    xgr = [P.sb([64, HY_G, 128], F32R, f"xgr{i}") for i in range(2)]
    fgr = [P.sb([128, HY_G, 128], F32R, f"fgr{i}") for i in range(2)]
    ysb = [P.sb([64, 512], F32, f"ysb{i}") for i in range(2)]

    def cmul(o_re, o_im, a_re, a_im, b_re, b_im, ra, rb, wo):
        P.tt(T[0][:], a_re, b_re, ALU.mult, reads=ra + rb, writes=[T[0]])
        P.tt(T[1][:], a_im, b_im, ALU.mult, reads=ra + rb, writes=[T[1]])
        P.tt(o_re, T[0][:], T[1][:], ALU.subtract, reads=[T[0], T[1]], writes=[wo[0]], eng='pool')
        P.tt(T[2][:], a_re, b_im, ALU.mult, reads=ra + rb, writes=[T[2]])
        P.tt(T[3][:], a_im, b_re, ALU.mult, reads=ra + rb, writes=[T[3]])
        P.tt(o_im, T[2][:], T[3][:], ALU.add, reads=[T[2], T[3]], writes=[wo[1]], eng='pool')

    def v4(ap512):
        return ap512

    def fwd_fft(src, Ka, A):
        psA = rot()
        for ch in range(HY_G):
            P.mm(psA[:, ch * 256:(ch + 1) * 256], src[0:Ka, ch, :], W1[0:Ka, :], reads=[src, W1], writes=[psA])
        pv = psA[:].rearrange("p (c r k) -> p c r k", c=HY_G, r=2)
        t4 = lambda ap: ap.rearrange("p (c k) -> p c k", c=HY_G)
        P.tt(t4(T[0][:]), pv[:, :, 0, :], t4(twr), ALU.mult, reads=[psA, tw], writes=[T[0]])
        P.tt(t4(T[1][:]), pv[:, :, 1, :], t4(twi), ALU.mult, reads=[psA, tw], writes=[T[1]])
        P.tt(A[0][:], T[0][:], T[1][:], ALU.subtract, reads=[T[0], T[1]], writes=[A[0]], eng='pool')
        P.tt(t4(T[2][:]), pv[:, :, 0, :], t4(twi), ALU.mult, reads=[psA, tw], writes=[T[2]])
        P.tt(t4(T[3][:]), pv[:, :, 1, :], t4(twr), ALU.mult, reads=[psA, tw], writes=[T[3]])
        P.tt(A[1][:], T[2][:], T[3][:], ALU.add, reads=[T[2], T[3]], writes=[A[1]], eng='pool')
        psX = rot()
        P.mm(psX[:, 0:512], Wr, A[0][:], start=True, stop=False, reads=[W3, A[0]], writes=[psX])
        P.mm(psX[:, 0:512], nWi, A[1][:], start=False, stop=True, reads=[W3, A[1]], writes=[psX])
        P.mm(psX[:, 512:1024], Wi, A[0][:], start=True, stop=False, reads=[W3, A[0]], writes=[psX])
        P.mm(psX[:, 512:1024], Wr, A[1][:], start=False, stop=True, reads=[W3, A[1]], writes=[psX])
        return psX

    ng = 256 // HY_G
    for g in range(ng):
        ch0 = g * HY_G
        ct = ch0 // 128
        x_ = xg[g % 2]; f_ = fg[g % 2]
        P.dma(x_[:], Zs[ch0:ch0 + HY_G, :].rearrange("c (a b) -> a c b", b=128), reads=zviews, writes=[x_])
        P.dma(f_[:], Fs[ch0:ch0 + HY_G, :].rearrange("c (a b) -> a c b", b=128), reads=fviews, writes=[f_])
        xr_ = xgr[g % 2]; fr_ = fgr[g % 2]
        P.copy(fr_[:], f_[:], reads=[f_], writes=[fr_], eng='act')
        P.copy(xr_[:], x_[:], reads=[x_], writes=[xr_], eng='act')
        f_, x_ = fr_, xr_
        psH = fwd_fft(f_, 128, Ap[0])
        P.copy(Hs[0][:], psH[:, 0:512], reads=[psH], writes=[Hs[0]], eng='act')
        P.copy(Hs[1][:], psH[:, 512:1024], reads=[psH], writes=[Hs[1]], eng='act')
        psX = fwd_fft(x_, 64, Ap[1])
        cmul(Yc[0][:], Yc[1][:], psX[:, 0:512], psX[:, 512:1024], Hs[0][:], Hs[1][:], [psX], [Hs[0], Hs[1]], Yc)
        psZ = rot()
        for ch in range(HY_G):
            P.mm(psZ[:, ch * 256:(ch + 1) * 256], Yc[0][:, ch * 128:(ch + 1) * 128], CW1[:], start=True, stop=False, reads=[Yc[0], CW1], writes=[psZ])
            P.mm(psZ[:, ch * 256:(ch + 1) * 256], Yc[1][:, ch * 128:(ch + 1) * 128], CW2[:], start=False, stop=True, reads=[Yc[1], CW2], writes=[psZ])
        pz = psZ[:].rearrange("p (c r k) -> p c r k", c=HY_G, r=2)
        t4 = lambda ap: ap.rearrange("p (c k) -> p c k", c=HY_G)
        cmul(t4(Zp[0][:]), t4(Zp[1][:]), pz[:, :, 0, :], pz[:, :, 1, :], t4(twr), t4(ntwi), [psZ], [tw], Zp)
        psy = rot()
        P.mm(psy[0:64, 0:512], W3[:, 0, 0:64], Zp[0][:], start=True, stop=False, reads=[W3, Zp[0]], writes=[psy])
        P.mm(psy[0:64, 0:512], W3[:, 1, 0:64], Zp[1][:], start=False, stop=True, reads=[W3, Zp[1]], writes=[psy])
        y_ = ysb[g % 2]
        P.op('act', lambda e, y_=y_, psy=psy: e.mul(y_[:], psy[0:64, 0:512], 1.0 / NFFT), reads=[psy], writes=[y_])
        v_ = P.view(Ys); yviews.append(v_)
        P.dma(Ys[ch0:ch0 + HY_G, :].rearrange("c (a b) -> a c b", b=128), y_[:].rearrange("a (c b) -> a c b", c=HY_G), reads=[y_], writes=[v_])
    P.pop()
    P.push()
    bufs = [[P.sb([128, 2048], F32, f"o{i}{j}") for j in range(3)] for i in range(2)]
    k = 0
    for ct in range(2):
        for s in range(0, NL, 2048):
            yb, zb, xb = bufs[k % 2]; k += 1
            rows = slice(ct * 128, (ct + 1) * 128)
            P.dma(yb[:], Ys[rows, s:s + 2048], reads=yviews, writes=[yb])
            P.dma(zb[:], Zs[rows, s:s + 2048], reads=zviews, writes=[zb])
            P.dma(xb[:], X0s[rows, s:s + 2048], reads=xviews, writes=[xb])
            P.stt(yb[:], zb[:], chs[:, 1, ct:ct + 1], yb[:], ALU.mult, ALU.add, reads=[zb, chs, yb], writes=[yb])
            P.tt(yb[:], yb[:], xb[:], ALU.mult, reads=[yb, xb], writes=[yb], eng='pool')
            P.dma(HYT[rows, s:s + 2048], yb[:], reads=[yb], writes=[P.view(HYT)], is_output=True)
    P.pop()
    return P.build()


def host_L5(d, r4):
    C, deltas = hyena_consts()
    ins = []
    cwf = d['hy_conv_w'][0]; cbf = d['hy_conv_b'][0]
    for j in range(8):
        b, chh = j // 2, j % 2
        a, c = r4[2 * b], r4[2 * b + 1]
        hT = np.zeros((3, 256, 8192), np.float32)
        cw = np.zeros((128, 3, 2, 3), np.float32); cb = np.zeros((128, 3, 2), np.float32)
        for part in range(3):
            rows = slice(512 * part + 256 * chh, 512 * part + 256 * chh + 256)
            hT[part] = np.concatenate([a['UT'][rows, :4096], c['UT'][rows, :4096]], 1)
            cw[:, part] = cwf[:, rows].reshape(3, 2, 128).transpose(2, 1, 0)
            cb[:, part] = cbf[rows].reshape(2, 128).T
        chs = slice(256 * chh, 256 * chh + 256)
        chv = np.zeros((128, 2, 2), np.float32)
        chv[:, 0, :] = -deltas[chs].reshape(2, 128).T
        chv[:, 1, :] = d['hy_bias'][0][chs].reshape(2, 128).T
        bf = np.stack([d['hy_filt_b1'][0], d['hy_sin_freq'][0][0], d['hy_filt_b2'][0], d['hy_sin_freq'][0][1]], 1)
        w3 = np.ascontiguousarray(d['hy_filt_w3'][0].reshape(64, 2, 512)[:, :, chs])
        m = {"hT": hT, "cw": cw, "cb": cb, "w1": d['hy_filt_w1'][0], "w2": d['hy_filt_w2'][0], "bf": np.ascontiguousarray(bf), "w3": w3, "chv": chv}
        m.update(C)
        ins.append(m)
    return ins


def build_L7():
    P = Prog()
    NTOK = 4096
    LB = BLOCKS[:8]
    xin = P.dram("xin", [NTOK, D]); cnd = P.dram("cnd", [2, D])
    mw = P.dram("mw", [D, 9 * D]); mb = P.dram("mb", [9 * D])
    HR = P.dram("HR", [1024, NTOK]); wout = P.dram("wout", [D, D])
    g2 = P.dram("g2", [D]); wg2 = P.dram("wg2", [D, FH]); wu2 = P.dram("wu2", [D, FH]); wd2 = P.dram("wd2", [FH, D])
    gf = P.dram("gf", [D])
    sa = P.dram("scr_a", [NTOK, D], kind="Internal"); sbb = P.dram("scr_b", [NTOK, D], kind="Internal")
    out = P.dram("out", [NTOK, D], kind="ExternalOutput")
    K = consts(P)
    xs = [P.sb([128, D], F32, f"x{t}") for t in range(4)]
    va = [P.view(sa) for _ in LB]; vb = [P.view(sbb) for _ in LB]
    mfm, gbc = adaln(P, K, cnd, mw, mb, [5, 8], ks=[5, 6, 7, 8])
    sce2, sh2 = norm_params(P, K, mfm, g2, 6, "n2")
    cats = [HR[kc * 128:(kc + 1) * 128, :] for kc in range(8)]
    pass_mixout(P, K, xs, xin, None, sa, va, wout, gbc[5], cats, blocks=LB)
    pass_ffn(P, K, xs, sa, va, sbb, vb, wg2, wu2, wd2, sce2, sh2, gbc[8], blocks=LB)
    P.push()
    gfb = P.sb([128, D], F32, "gfb")
    P.dma(gfb[:], gf.t.partition_broadcast(128), writes=[gfb])
    ss = P.sb([128, 4], F32, "ssf"); rstd = P.sb([128, 4], F32, "rstdf")
    junk = P.sb([128, D], BF16, "junk")
    for bi, (t0, nt, n) in enumerate(LB):
        xio(P, xs, nt, t0, sbb, vb[bi])
        P.memset(ss[:], 0.0, writes=[ss], eng='dve')
        for t in range(nt):
            P.act(junk[:], xs[t][:], AF.Square, accum_out=ss[:, t:t + 1], reads=[xs[t]], writes=[junk, ss])
        rstd_from_ss(P, rstd[:], ss[:], 1.0 / D, [ss], [rstd])
        for t in range(nt):
            P.ts(xs[t][:], xs[t][:], rstd[:, t:t + 1], ALU.mult, reads=[xs[t], rstd], writes=[xs[t]])
            P.tt(xs[t][:], xs[t][:], gfb[:], ALU.mult, reads=[xs[t], gfb], writes=[xs[t]], eng='pool')
        xio(P, xs, nt, t0, None, None, out, P.view(out), load=False, out=True)
    P.pop()
    return P.build()


def host_L7(d, r4, r5, r6):
    ins = []
    for j in range(8):
        b, hf = j // 2, j % 2
        cols = slice(hf * 4096, (hf + 1) * 4096)
        HR = np.concatenate([r5[2 * b]['HYT'][:, cols], r5[2 * b + 1]['HYT'][:, cols], r6[2 * b]['RT'][:, cols], r6[2 * b + 1]['RT'][:, cols]], 0)
        ins.append({"xin": np.ascontiguousarray(r4[j]['xout'][:4096]), "cnd": np.stack([d['c'][b], d['c_ctx']]),
                    "mw": d['mod_w'][1], "mb": d['mod_b'][1], "HR": np.ascontiguousarray(HR), "wout": d['od_w_out'][0],
                    "g2": d['norm_ffn2'][1], "wg2": d['ffn2_w_gate'][1], "wu2": d['ffn2_w_up'][1], "wd2": d['ffn2_w_down'][1],
                    "gf": d['final_norm'], "ident": np.eye(128, dtype=np.float32)})
    return ins


_NC = {}


def _get(name, fn):
    if name not in _NC:
        _NC[name] = fn()
    return _NC[name]


def _run(name, fn, ins):
    nc = _get(name, fn)
    res = run_bass_kernel_spmd(nc, ins, core_ids=list(range(8)))
    return res.results


def kernel(**inputs):
    d = {k: np.ascontiguousarray(np.asarray(v)) for k, v in inputs.items()}
    r1 = _run("L1", build_L1, host_L1(d))
    r2 = _run("L2", build_L2, host_L2(r1))
    r3 = _run("L3", build_L3, host_L3(d, r1))
    r4 = _run("L4", build_L4, host_L4(d, r1, r2, r3))
    r5 = _run("L5", build_L5, host_L5(d, r4))
    r6 = _run("L6", build_L6, host_L6(d, r4))
    r7 = _run("L7", build_L7, host_L7(d, r4, r5, r6))
    out = np.zeros((4, 8192, 1024), np.float32)
    for j in range(8):
        b, hf = j // 2, j % 2
        out[b, hf * 4096:(hf + 1) * 4096] = r7[j]['out']
    return out
```
